# Optimizing a Trainium2 kernel written in Bass

```python
import math
import functools
import jax
import jax.numpy as jnp
from jax import lax
import numpy as np

D_MODEL = 1024
BATCH = 8
SEQ = 4096
DEPTH = 2
DEC_BATCH = 16
DEC_SEQ = 32
PAST_LEN = 2048

CHUNK = 64
Q_BLOCK = 128
HEAD_DIM = 64
H_A = 4
H_B = 4
H_C = 4
W_A = H_A * 2 * HEAD_DIM
W_B = H_B * HEAD_DIM
W_C = H_C * HEAD_DIM
MIX_WIDTH = W_A + W_B + W_C
N_IN = 3 * W_A + 4 * W_B + 2 * H_B + 4 * W_C
CONV_W = 4
D_FF = 4 * D_MODEL
ROPE_THETA = 10000.0
ALPHA = (2 * DEPTH) ** 0.25
DEEPNORM_BETA = (8 * DEPTH) ** -0.25
LN_EPS = 1e-5
NORM_EPS = 1e-6

kernel_name = 'hybrid_streaming_encoder_step'

F32 = jnp.float32


def _split_points():
    sizes = (W_A, W_A, W_A, 3 * W_B, W_B, H_B, H_B, W_C, W_C, W_C, W_C)
    return [int(s) for s in np.cumsum(sizes)[:-1]]


def _layer_norm(x, g, b):
    xf = x.astype(F32)
    mu = jnp.mean(xf, -1, keepdims=True)
    var = jnp.mean(jnp.square(xf - mu), -1, keepdims=True)
    return ((xf - mu) * lax.rsqrt(var + LN_EPS) * g.astype(F32) + b.astype(F32)).astype(x.dtype)


def _rms(x):
    return x * lax.rsqrt(jnp.mean(jnp.square(x), -1, keepdims=True) + NORM_EPS)


def _l2norm(x):
    return x * lax.rsqrt(jnp.sum(jnp.square(x), -1, keepdims=True) + NORM_EPS)


def _rotary(x, pos):
    half = HEAD_DIM // 2
    inv_freq = ROPE_THETA ** (-jnp.arange(half, dtype=F32) / half)
    ang = pos.astype(F32)[:, None] * inv_freq[None, :]
    cos = jnp.cos(ang)[None, :, None, :]
    sin = jnp.sin(ang)[None, :, None, :]
    x1, x2 = x[..., :half], x[..., half:]
    return jnp.concatenate([x1 * cos - x2 * sin, x2 * cos + x1 * sin], -1)


def _diff_attention(q, k, v, q_pos, k_pos, lam):
    bsz, lq = q.shape[0], q.shape[1]
    lk = k.shape[1]
    qb = min(Q_BLOCK, lq)
    nb = lq // qb
    q_blocks = q.reshape(bsz, nb, qb, 2 * H_A, HEAD_DIM).transpose(1, 0, 2, 3, 4)
    pos_blocks = q_pos.reshape(nb, qb)
    k_chunk = k_pos // CHUNK

    def one_block(args):
        q_blk, pos_blk = args
        s = jnp.einsum('bqhd,bkhd->bhqk', q_blk, k) * (HEAD_DIM ** -0.5)
        visible = k_chunk[None, :] <= (pos_blk // CHUNK)[:, None]
        prob = jax.nn.softmax(jnp.where(visible, s, -jnp.inf), axis=-1)
        prob = prob.reshape(bsz, H_A, 2, qb, lk)
        return jnp.einsum('bhqk,bkhe->bqhe', prob[:, :, 0] - lam * prob[:, :, 1], v)

    out = lax.map(one_block, (q_blocks, pos_blocks))
    return out.transpose(1, 0, 2, 3, 4).reshape(bsz, lq, H_A, 2 * HEAD_DIM)


def _short_conv(u, buf, w):
    L = u.shape[1]
    full = jnp.concatenate([buf, u], axis=1)
    out = full[:, 0:L] * w[0]
    for j in range(1, CONV_W):
        out = out + full[:, j:j + L] * w[j]
    return jax.nn.silu(out), full[:, L:]


def _to_chunks(t, n, c):
    bsz, _, h = t.shape[0], t.shape[1], t.shape[2]
    return t.reshape(bsz, n, c, h, -1).transpose(1, 0, 3, 2, 4)


def _gated_delta_chunked(q, k, v, g, beta, s0):
    bsz, L, H, _ = q.shape
    c = min(CHUNK, L)
    n = L // c
    q, k, v = _to_chunks(q, n, c), _to_chunks(k, n, c), _to_chunks(v, n, c)
    g = g.reshape(bsz, n, c, H).transpose(1, 0, 3, 2)
    beta = beta.reshape(bsz, n, c, H).transpose(1, 0, 3, 2)[..., None]
    decay = jnp.cumsum(g, axis=-1)
    tril = jnp.tril(jnp.ones((c, c), bool))
    strict = jnp.tril(jnp.ones((c, c), bool), -1)
    diff = decay[..., :, None] - decay[..., None, :]
    lmask = jnp.where(tril, jnp.exp(jnp.where(tril, diff, 0.0)), 0.0)
    k_beta = k * beta
    a = jnp.where(strict, jnp.einsum('nbhid,nbhjd->nbhij', k_beta, k) * lmask, 0.0)
    t_mat = a + jnp.eye(c, dtype=F32)
    solve = functools.partial(lax.linalg.triangular_solve, left_side=True, lower=True,
                              unit_diagonal=True)
    u = solve(t_mat, v * beta)
    w = solve(t_mat, k_beta * jnp.exp(decay)[..., None])
    intra = jnp.where(tril, jnp.einsum('nbhid,nbhjd->nbhij', q, k) * lmask, 0.0)

    def step(S, xs):
        q_i, k_i, u_i, w_i, intra_i, dec_i = xs
        v_new = u_i - jnp.einsum('bhcd,bhde->bhce', w_i, S)
        o = (jnp.einsum('bhcd,bhde->bhce', q_i * jnp.exp(dec_i)[..., None], S)
             + jnp.einsum('bhij,bhje->bhie', intra_i, v_new))
        last = dec_i[..., -1:]
        S = (S * jnp.exp(last)[..., None]
             + jnp.einsum('bhcd,bhce->bhde', k_i * jnp.exp(last - dec_i)[..., None], v_new))
        return S, o

    S, o = lax.scan(step, s0, (q, k, u, w, intra, decay))
    return o.transpose(1, 0, 3, 2, 4).reshape(bsz, L, H, -1), S


def _retention_log_decay():
    return jnp.log(1.0 - 2.0 ** (-5.0 - jnp.arange(H_C, dtype=F32)))


def _retention_chunked(q, k, v, log_gamma, s0):
    bsz, L, H, _ = q.shape
    c = min(CHUNK, L)
    n = L // c
    q, k, v = _to_chunks(q, n, c), _to_chunks(k, n, c), _to_chunks(v, n, c)
    idx = jnp.arange(c, dtype=F32)
    dist = idx[:, None] - idx[None, :]
    dmask = jnp.where(dist >= 0, jnp.exp(log_gamma[:, None, None] * jnp.maximum(dist, 0.0)), 0.0)
    xi = jnp.exp(log_gamma[:, None] * (idx + 1.0))[..., None]
    zeta = jnp.exp(log_gamma[:, None] * (c - 1.0 - idx))[..., None]
    g_chunk = jnp.exp(log_gamma * c)[:, None, None]
    intra = jnp.einsum('nbhij,nbhje->nbhie',
                       jnp.einsum('nbhid,nbhjd->nbhij', q, k) * dmask, v)

    def step(S, xs):
        q_i, k_i, v_i, intra_i = xs
        o = intra_i + jnp.einsum('bhcd,bhde->bhce', q_i, S) * xi
        S = g_chunk * S + jnp.einsum('bhcd,bhce->bhde', k_i * zeta, v_i)
        return S, o

    S, o = lax.scan(step, s0, (q, k, v, intra))
    return o.transpose(1, 0, 3, 2, 4).reshape(bsz, L, H, -1), S


def _layer(x, past_k, past_v, s_delta, conv_buf, s_ret, lam_init, p):
    bsz, L, _ = x.shape
    dt = x.dtype
    pos0 = past_k.shape[1]
    q_pos = pos0 + jnp.arange(L, dtype=jnp.int32)
    h = jnp.matmul(x, p['w_in']).astype(F32)
    (qa, ka, va, qkv_b, gate_b, beta_b, alpha_b,
     qc, kc, vc, gate_c) = jnp.split(h, _split_points(), axis=-1)

    qa = _rotary(qa.reshape(bsz, L, 2 * H_A, HEAD_DIM), q_pos)
    ka = _rotary(ka.reshape(bsz, L, 2 * H_A, HEAD_DIM), q_pos)
    va = va.reshape(bsz, L, H_A, 2 * HEAD_DIM)
    k_all = jnp.concatenate([past_k.astype(F32), ka], axis=1)
    v_all = jnp.concatenate([past_v.astype(F32), va], axis=1)
    k_pos = jnp.arange(k_all.shape[1], dtype=jnp.int32)
    lam = (jnp.exp(jnp.sum(p['lam_q1'].astype(F32) * p['lam_k1'].astype(F32)))
           - jnp.exp(jnp.sum(p['lam_q2'].astype(F32) * p['lam_k2'].astype(F32))) + lam_init)
    oa = _diff_attention(qa, k_all, v_all, q_pos, k_pos, lam)
    oa = _rms(oa) * p['diff_norm_g'].astype(F32) * (1.0 - lam_init)

    cb, new_conv = _short_conv(qkv_b, conv_buf.astype(F32), p['conv_w'].astype(F32))
    qb_, kb_, vb_ = jnp.split(cb, 3, axis=-1)
    qb_ = _l2norm(qb_.reshape(bsz, L, H_B, HEAD_DIM)) * (HEAD_DIM ** -0.5)
    kb_ = _l2norm(kb_.reshape(bsz, L, H_B, HEAD_DIM))
    vb_ = vb_.reshape(bsz, L, H_B, HEAD_DIM)
    beta = jax.nn.sigmoid(beta_b)
    g = -jnp.exp(p['a_log'].astype(F32)) * jax.nn.softplus(alpha_b + p['dt_bias'].astype(F32))
    ob, new_delta = _gated_delta_chunked(qb_, kb_, vb_, g, beta, s_delta.astype(F32))
    ob = _rms(ob) * p['delta_norm_g'].astype(F32) * jax.nn.silu(gate_b.reshape(bsz, L, H_B, HEAD_DIM))

    qc = _rotary(qc.reshape(bsz, L, H_C, HEAD_DIM), q_pos)
    kc = _rotary(kc.reshape(bsz, L, H_C, HEAD_DIM), q_pos) * (HEAD_DIM ** -0.5)
    vc = vc.reshape(bsz, L, H_C, HEAD_DIM)
    oc, new_ret = _retention_chunked(qc, kc, vc, _retention_log_decay(), s_ret.astype(F32))
    oc = _rms(oc) * jax.nn.silu(gate_c.reshape(bsz, L, H_C, HEAD_DIM))

    mix = jnp.concatenate([oa.reshape(bsz, L, W_A), ob.reshape(bsz, L, W_B),
                           oc.reshape(bsz, L, W_C)], axis=-1).astype(dt)
    x = _layer_norm(ALPHA * x + jnp.matmul(mix, p['w_out']), p['ln1_g'], p['ln1_b'])
    hid = jnp.square(jax.nn.relu(jnp.matmul(x, p['w_up'])))
    x = _layer_norm(ALPHA * x + jnp.matmul(hid, p['w_down']), p['ln2_g'], p['ln2_b'])
    new_state = (ka.astype(dt), va.astype(dt), new_delta.astype(dt),
                 new_conv.astype(dt), new_ret.astype(dt))
    return x, new_state


def setup_inputs(seed: int = 0) -> dict:
    key = jax.random.key(seed)
    ks = jax.random.split(key, 28)

    def nrm(k, shape, scale):
        return jax.random.normal(k, shape, F32) * scale

    dt0 = jnp.exp(jax.random.uniform(ks[16], (DEPTH, H_B), F32, math.log(1e-3), math.log(1e-1)))
    return {
        'x_prompt': nrm(ks[0], (BATCH, SEQ, D_MODEL), 1.0),
        'x_sample': nrm(ks[1], (DEC_BATCH, DEC_SEQ, D_MODEL), 1.0),
        'cache_k': nrm(ks[2], (DEPTH, DEC_BATCH, PAST_LEN, 2 * H_A, HEAD_DIM), 1.0),
        'cache_v': nrm(ks[3], (DEPTH, DEC_BATCH, PAST_LEN, H_A, 2 * HEAD_DIM), 1.0),
        'state_delta': nrm(ks[4], (DEPTH, DEC_BATCH, H_B, HEAD_DIM, HEAD_DIM), 0.1),
        'state_conv': nrm(ks[5], (DEPTH, DEC_BATCH, CONV_W - 1, 3 * W_B), 1.0),
        'state_ret': nrm(ks[6], (DEPTH, DEC_BATCH, H_C, HEAD_DIM, HEAD_DIM), 0.5),
        'ln0_g': 1.0 + nrm(ks[7], (D_MODEL,), 0.02),
        'ln0_b': nrm(ks[8], (D_MODEL,), 0.02),
        'w_in': nrm(ks[9], (DEPTH, D_MODEL, N_IN), D_MODEL ** -0.5),
        'lam_q1': nrm(ks[10], (DEPTH, HEAD_DIM), 0.1),
        'lam_k1': nrm(ks[11], (DEPTH, HEAD_DIM), 0.1),
        'lam_q2': nrm(ks[12], (DEPTH, HEAD_DIM), 0.1),
        'lam_k2': nrm(ks[13], (DEPTH, HEAD_DIM), 0.1),
        'diff_norm_g': 1.0 + nrm(ks[14], (DEPTH, 2 * HEAD_DIM), 0.02),
        'conv_w': nrm(ks[15], (DEPTH, CONV_W, 3 * W_B), CONV_W ** -0.5),
        'a_log': jnp.log(jax.random.uniform(ks[17], (DEPTH, H_B), F32, 1.0, 16.0)),
        'dt_bias': dt0 + jnp.log(-jnp.expm1(-dt0)),
        'delta_norm_g': 1.0 + nrm(ks[18], (DEPTH, HEAD_DIM), 0.02),
        'w_out': nrm(ks[19], (DEPTH, MIX_WIDTH, D_MODEL), MIX_WIDTH ** -0.5 * DEEPNORM_BETA),
        'ln1_g': 1.0 + nrm(ks[20], (DEPTH, D_MODEL), 0.02),
        'ln1_b': nrm(ks[21], (DEPTH, D_MODEL), 0.02),
        'w_up': nrm(ks[22], (DEPTH, D_MODEL, D_FF), D_MODEL ** -0.5),
        'w_down': nrm(ks[23], (DEPTH, D_FF, D_MODEL), D_FF ** -0.5 * DEEPNORM_BETA),
        'ln2_g': 1.0 + nrm(ks[24], (DEPTH, D_MODEL), 0.02),
        'ln2_b': nrm(ks[25], (DEPTH, D_MODEL), 0.02),
    }


def reference(x_prompt, x_sample, cache_k, cache_v, state_delta, state_conv, state_ret,
              ln0_g, ln0_b, w_in, lam_q1, lam_k1, lam_q2, lam_k2, diff_norm_g, conv_w,
              a_log, dt_bias, delta_norm_g, w_out, ln1_g, ln1_b, w_up, w_down, ln2_g, ln2_b):
    dt = x_prompt.dtype
    bp = x_prompt.shape[0]
    xp = _layer_norm(x_prompt, ln0_g, ln0_b)
    xs = _layer_norm(x_sample, ln0_g, ln0_b)
    p_new = []
    s_new = []
    for l in range(DEPTH):
        p = {'w_in': w_in[l], 'lam_q1': lam_q1[l], 'lam_k1': lam_k1[l], 'lam_q2': lam_q2[l],
             'lam_k2': lam_k2[l], 'diff_norm_g': diff_norm_g[l], 'conv_w': conv_w[l],
             'a_log': a_log[l], 'dt_bias': dt_bias[l], 'delta_norm_g': delta_norm_g[l],
             'w_out': w_out[l], 'ln1_g': ln1_g[l], 'ln1_b': ln1_b[l], 'w_up': w_up[l],
             'w_down': w_down[l], 'ln2_g': ln2_g[l], 'ln2_b': ln2_b[l]}
        lam_init = 0.8 - 0.6 * math.exp(-0.3 * l)
        xp, ns_p = _layer(xp,
                          jnp.zeros((bp, 0, 2 * H_A, HEAD_DIM), dt),
                          jnp.zeros((bp, 0, H_A, 2 * HEAD_DIM), dt),
                          jnp.zeros((bp, H_B, HEAD_DIM, HEAD_DIM), dt),
                          jnp.zeros((bp, CONV_W - 1, 3 * W_B), dt),
                          jnp.zeros((bp, H_C, HEAD_DIM, HEAD_DIM), dt),
                          lam_init, p)
        xs, ns_s = _layer(xs, cache_k[l], cache_v[l], state_delta[l], state_conv[l],
                          state_ret[l], lam_init, p)
        p_new.append(ns_p)
        s_new.append(ns_s)
    prompt_k, prompt_v, prompt_delta, prompt_conv, prompt_ret = [jnp.stack(t) for t in zip(*p_new)]
    sample_k, sample_v, sample_delta, sample_conv, sample_ret = [jnp.stack(t) for t in zip(*s_new)]
    return (xp, xs, prompt_k, prompt_v, prompt_delta, prompt_conv, prompt_ret,
            sample_k, sample_v, sample_delta, sample_conv, sample_ret)
```

```python
import math
import numpy as np
import ml_dtypes
import concourse.bass as bass
import concourse.mybir as mybir
from concourse.bass_utils import run_bass_kernel_spmd

F32 = mybir.dt.float32
BF16 = mybir.dt.bfloat16
AF = mybir.ActivationFunctionType
ALU = mybir.AluOpType
AX = mybir.AxisListType

D = 1024
NIN = 3592
DFF = 4096
DEPTH = 2
ALPHA = (2 * DEPTH) ** 0.25
LN_EPS = 1e-5
NORM_EPS = 1e-6
NCORES = 8


class Tk:
    __slots__ = ("w", "r", "dsem", "dcnt", "name")

    def __init__(self, name=""):
        self.w = None
        self.r = {}
        self.dsem = None
        self.dcnt = 0
        self.name = name


class Ctx:
    def __init__(self, nc):
        self.nc = nc
        self.E = {"pe": nc.tensor, "dve": nc.vector, "act": nc.scalar, "pool": nc.gpsimd,
                  "sp": nc.sync}
        self.sem = {k: nc.alloc_semaphore("es_" + k) for k in self.E}
        self.cnt = {k: 0 for k in self.E}
        self.seen = {k: {} for k in self.E}
        self.dsems = []
        self.nd = 0

    def _wait(self, e, dep):
        key, sem, val = dep
        if self.seen[e].get(key, 0) >= val:
            return
        self.E[e].wait_ge(sem, val)
        self.seen[e][key] = val

    def _deps(self, e, reads, writes):
        for t in reads:
            if t.w is not None:
                self._wait(e, t.w)
        for t in writes:
            if t.w is not None:
                self._wait(e, t.w)
            for d in t.r.values():
                self._wait(e, d)

    def _mark(self, me, reads, writes):
        for t in reads:
            old = t.r.get(me[0])
            if old is None or old[2] < me[2]:
                t.r[me[0]] = me
        for t in writes:
            t.w = me
            t.r = {}

    def op(self, e, fn, reads=(), writes=()):
        self._deps(e, reads, writes)
        ins = fn(self.E[e])
        self.cnt[e] += 1
        ins.then_inc(self.sem[e], 1)
        me = (e, self.sem[e], self.cnt[e])
        if e == "pe":
            self.seen[e][e] = self.cnt[e]
        self._mark(me, reads, writes)
        return ins

    def dma(self, q, out, in_, owner, reads=(), writes=(), **kw):
        self._deps(q, reads, writes)
        if owner.dsem is None:
            owner.dsem = self.nc.alloc_semaphore("ds%d" % self.nd)
            self.nd += 1
            self.dsems.append(owner)
        ins = self.E[q].dma_start(out=out, in_=in_, **kw)
        owner.dcnt += 16
        ins.then_inc(owner.dsem, 16)
        me = ("d%d" % id(owner), owner.dsem, owner.dcnt)
        self._mark(me, reads, writes)
        return ins

    def barrier(self):
        for e in self.E:
            for f in self.E:
                if f != e and self.cnt[f] > 0:
                    self._wait(e, (f, self.sem[f], self.cnt[f]))
            for o in self.dsems:
                if o.dcnt > 0:
                    self._wait(e, ("d%d" % id(o), o.dsem, o.dcnt))

    def finish(self):
        self.barrier()


def bc(ap, shape):
    return ap.to_broadcast(list(shape))


import os
from contextlib import ExitStack
STOP = int(os.environ.get('KSTOP', '99'))


def build(Lp=4096, NS=2, Ls=32, PAST=2048):
    nc = bass.Bass("TRN2", target_bir_lowering=False)
    C = Ctx(nc)
    Ltot = Lp + NS * Ls
    LK = max(Lp, PAST + Ls)

    def din(name, shape, dt=F32):
        return nc.dram_tensor(name, list(shape), dt, kind="ExternalInput").ap()

    def dout(name, shape):
        return nc.dram_tensor(name, list(shape), F32, kind="ExternalOutput").ap()

    xp = din("xp", [Lp, D])
    xs = din("xs", [NS * Ls, D])
    ck = din("ck", [DEPTH, NS, PAST, 512])
    cv = din("cv", [DEPTH, NS, PAST, 512])
    sdl = din("sdl", [DEPTH, NS, 256, 64])
    scv = din("scv", [DEPTH, NS, 3, 768])
    srt = din("srt", [DEPTH, NS, 256, 64])
    ln0g = din("ln0g", [1, D]); ln0b = din("ln0b", [1, D])
    w_in = din("w_in", [DEPTH, D, NIN])
    lamq1 = din("lamq1", [DEPTH, 64]); lamk1 = din("lamk1", [DEPTH, 64])
    lamq2 = din("lamq2", [DEPTH, 64]); lamk2 = din("lamk2", [DEPTH, 64])
    dng = din("dng", [DEPTH, 128])
    convw = din("convw", [DEPTH, 4, 768])
    alog = din("alog", [DEPTH, 4]); dtb = din("dtb", [DEPTH, 4])
    dlng = din("dlng", [DEPTH, 64])
    w_out = din("w_out", [DEPTH, D, D])
    ln1g = din("ln1g", [DEPTH, D]); ln1b = din("ln1b", [DEPTH, D])
    w_up = din("w_up", [DEPTH, D, DFF])
    w_down = din("w_down", [DEPTH, DFF, D])
    ln2g = din("ln2g", [DEPTH, D]); ln2b = din("ln2b", [DEPTH, D])
    c_ident = din("c_ident", [128, 128])
    c_rotp = din("c_rotp", [Lp, 128])
    c_rots = din("c_rots", [Ls, 128])
    c_amask = din("c_amask", [128, 512])
    c_triu = din("c_triu", [64, 64])
    c_striu = din("c_striu", [64, 64])
    c_rm = din("c_rm", [64, 256])
    c_xi = din("c_xi", [64, 512])
    c_xis = din("c_xis", [64, 4 * Ls])
    c_zeta = din("c_zeta", [128, 4])
    c_zetas = din("c_zetas", [Ls, 4])
    c_bones = din("c_bones", [128, 128])

    yp = dout("yp", [Lp, D]); ys = dout("ys", [NS * Ls, D])
    pk = dout("pk", [DEPTH, Lp, 512]); pv = dout("pv", [DEPTH, Lp, 512])
    pdl = dout("pdl", [DEPTH, 256, 64]); pcv = dout("pcv", [DEPTH, 3, 768])
    prt = dout("prt", [DEPTH, 256, 64])
    sk = dout("sk", [DEPTH, NS * Ls, 512]); sv = dout("sv", [DEPTH, NS * Ls, 512])
    sdlo = dout("sdlo", [DEPTH, NS, 256, 64]); scvo = dout("scvo", [DEPTH, NS, 3, 768])
    srto = dout("srto", [DEPTH, NS, 256, 64])
    x1s = nc.dram_tensor("x1s", [Ltot, D], F32).ap()
    xcs = nc.dram_tensor("xcs", [Ltot, D], F32).ap()
    vsc = nc.dram_tensor("vsc", [LK, 512], BF16).ap()
    dram_tk = {}

    def dtk(name, r0):
        k = (name, r0)
        if k not in dram_tk:
            dram_tk[k] = Tk()
        return dram_tk[k]

    PS = [nc.alloc_psum_tensor("ps%d" % i, [128, 512], F32) for i in range(8)]
    PSK = [Tk("ps%d" % i) for i in range(8)]
    rot_state = {"pj": 0, "mi": 0, "s": 0}

    def bank(kind):
        base, n = {"pj": (0, 2), "s": (2, 2), "mi": (6, 2)}[kind]
        i = base + rot_state[kind] % n
        rot_state[kind] += 1
        return PS[i], PSK[i]

    OB, OBK = PS[4], PSK[4]
    LB, LBK = PS[5], PSK[5]

    def sb(name, shape, dt=F32):
        return nc.alloc_sbuf_tensor(name, list(shape), dt)

    identf = sb("identf", [128, 128]); identb = sb("identb", [128, 128], BF16)
    onesb = sb("onesb", [128, 128], BF16)
    onesf = sb("onesf", [128, 128])
    bonesf = sb("bonesf", [128, 128])
    triu = sb("triu", [64, 64]); striu = sb("striu", [64, 64])
    rmc = sb("rmc", [64, 256]); xic = sb("xic", [64, 512]); xisc = sb("xisc", [64, 4 * Ls])
    zetac = sb("zetac", [128, 4]); zetasc = sb("zetasc", [Ls, 4])
    amask = sb("amask", [128, 512], BF16)
    epsb = sb("epsb", [128, 2])
    constk = Tk("const")
    q = "sp"
    C.dma(q, identf[:], c_ident[:, :], constk, writes=[constk])
    C.dma(q, bonesf[:], c_bones[:, :], constk, writes=[constk])
    C.dma(q, triu[:], c_triu[:, :], constk, writes=[constk])
    C.dma(q, striu[:], c_striu[:, :], constk, writes=[constk])
    C.dma(q, rmc[:], c_rm[:, :], constk, writes=[constk])
    C.dma(q, xic[:], c_xi[:, :], constk, writes=[constk])
    C.dma(q, xisc[:], c_xis[:, :], constk, writes=[constk])
    C.dma(q, zetac[:], c_zeta[:, :], constk, writes=[constk])
    C.dma(q, zetasc[:], c_zetas[:, :], constk, writes=[constk])
    C.dma("pool", amask[:], c_amask[:, :], constk, writes=[constk])
    C.op("dve", lambda e: e.tensor_copy(out=identb[:], in_=identf[:]), reads=[constk], writes=[constk])
    C.op("dve", lambda e: e.memset(onesb[:], 1.0), writes=[constk])
    C.op("dve", lambda e: e.memset(onesf[:], 1.0), writes=[constk])
    C.op("dve", lambda e: e.memset(epsb[:, 0:1], LN_EPS), writes=[constk])
    C.op("dve", lambda e: e.memset(epsb[:, 1:2], NORM_EPS), writes=[constk])

    WA = sb("WA", [128, 65536], BF16)
    WAK = Tk("WA")

    lnst = sb("lnst", [128, 2, 6]); lnmv = sb("lnmv", [128, 2]); lnk = Tk("ln")

    def layer_norm(xap, n, gt, bt, gk, xk):
        tks = [xk]
        C.op("dve", lambda e: e.bn_stats(out=lnst[:n, 0, :], in_=xap[:, 0:512]), reads=tks, writes=[lnk])
        C.op("dve", lambda e: e.bn_stats(out=lnst[:n, 1, :], in_=xap[:, 512:1024]), reads=tks, writes=[lnk])
        C.op("dve", lambda e: e.bn_aggr(out=lnmv[:n, :], in_=lnst[:n, :, :].rearrange("p a b -> p (a b)")),
             reads=[lnk], writes=[lnk])
        C.op("act", lambda e: e.activation(out=lnmv[:n, 1:2], in_=lnmv[:n, 1:2], func=AF.Ln, bias=epsb[:n, 0:1]),
             reads=[lnk, constk], writes=[lnk])
        C.op("act", lambda e: e.activation(out=lnmv[:n, 1:2], in_=lnmv[:n, 1:2], func=AF.Exp, scale=-0.5),
             reads=[lnk], writes=[lnk])
        C.op("dve", lambda e: e.tensor_scalar(out=xap, in0=xap, scalar1=lnmv[:n, 0:1], scalar2=lnmv[:n, 1:2],
                                              op0=ALU.subtract, op1=ALU.mult),
             reads=[lnk] + tks, writes=tks)
        C.op("pool", lambda e: e.tensor_tensor(out=xap, in0=xap, in1=gt[:n, :], op=ALU.mult),
             reads=tks + [gk], writes=tks)
        C.op("pool", lambda e: e.tensor_tensor(out=xap, in0=xap, in1=bt[:n, :], op=ALU.add),
             reads=tks + [gk], writes=tks)

    seqs = [dict(name="p", L=Lp, past=0, TS=128, c=64, row0=0, b=0)]
    for b_ in range(NS):
        seqs.append(dict(name="s%d" % b_, L=Ls, past=PAST, TS=Ls, c=Ls, row0=Lp + b_ * Ls, b=b_))

    with ExitStack() as st0:
        g0 = st0.enter_context(nc.sbuf_tensor("g0", [128, D], F32))
        b0 = st0.enter_context(nc.sbuf_tensor("b0", [128, D], F32))
        xl = st0.enter_context(nc.sbuf_tensor("xl", [128, 2, D], F32))
        xlk = [Tk(), Tk()]
        pk0 = Tk()
        C.dma("sp", g0[:], ln0g[0:1, :].partition_broadcast(128), pk0, writes=[pk0])
        C.dma("sp", b0[:], ln0b[0:1, :].partition_broadcast(128), pk0, writes=[pk0])
        tiles0 = [(xp, r, r, 128) for r in range(0, Lp, 128)]
        for b_ in range(NS):
            tiles0.append((xs, b_ * Ls, Lp + b_ * Ls, Ls))
        for i0, (src0, sr0, dr0, n0) in enumerate(tiles0):
            sl0 = i0 % 2
            C.dma("sp", xl[:n0, sl0, :], src0[sr0:sr0 + n0, :], xlk[sl0], writes=[xlk[sl0]])
            layer_norm(xl[:n0, sl0, :], n0, g0, b0, pk0, xlk[sl0])
            C.dma("pool", xcs[dr0:dr0 + n0, :], xl[:n0, sl0, :], xlk[sl0], reads=[xlk[sl0]], writes=[dtk("xcs", dr0)])
    if STOP == 0:
        C.finish()
        return nc

    for l in range(DEPTH):
        lam_init = 0.8 - 0.6 * math.exp(-0.3 * l)
        C.barrier()
        WA_in = WA[:, 0:8 * NIN].rearrange("p (k n) -> p k n", k=8)
        WA_out = WA[:, 8 * NIN:8 * NIN + 8192].rearrange("p (k n) -> p k n", k=8)
        for k in range(8):
            for c0_ in range(0, NIN, 512):
                c1_ = min(NIN, c0_ + 512)
                C.dma("pool", WA_in[:, k, c0_:c1_], w_in[l, k * 128:(k + 1) * 128, c0_:c1_], WAK, writes=[WAK])
        for k in range(8):
            for c0_ in range(0, D, 512):
                C.dma("pool", WA_out[:, k, c0_:c0_ + 512], w_out[l, k * 128:(k + 1) * 128, c0_:c0_ + 512], WAK, writes=[WAK])
        with ExitStack() as st:
            def T(name, shape, dt=F32):
                return st.enter_context(nc.sbuf_tensor("%s_%d" % (name, l), list(shape), dt))
            wa_off = [8 * NIN + 8192]

            def WT(shape):
                n = 1
                for d_ in shape[1:]:
                    n *= d_
                ap = WA[:, wa_off[0]:wa_off[0] + n]
                wa_off[0] += n
                assert wa_off[0] <= 65536, wa_off[0]
                if len(shape) == 3:
                    ap = ap.rearrange("p (a b) -> p a b", a=shape[1])
                return ap
            KT = WT([128, 4, LK])
            ktk = [Tk("kt%d" % i) for i in range((LK + 127) // 128)]
            g1 = T("g1", [128, D]); b1 = T("b1", [128, D])
            pk_ = Tk("params")
            C.dma("sp", g1[:], ln1g[l:l + 1, :].partition_broadcast(128), pk_, writes=[pk_])
            C.dma("sp", b1[:], ln1b[l:l + 1, :].partition_broadcast(128), pk_, writes=[pk_])
            lamt = T("lamt", [128, 4, 64]); lamr = T("lamr", [128, 4]); neglam = T("neglam", [128, 1])
            for i, src in enumerate((lamq1, lamk1, lamq2, lamk2)):
                C.dma("sp", lamt[:, i, :], src[l:l + 1, :].partition_broadcast(128), pk_, writes=[pk_])
            C.op("dve", lambda e: e.tensor_tensor(out=lamt[:, 0, :], in0=lamt[:, 0, :], in1=lamt[:, 1, :], op=ALU.mult),
                 reads=[pk_], writes=[pk_])
            C.op("dve", lambda e: e.tensor_tensor(out=lamt[:, 2, :], in0=lamt[:, 2, :], in1=lamt[:, 3, :], op=ALU.mult),
                 reads=[pk_], writes=[pk_])
            C.op("dve", lambda e: e.tensor_reduce(out=lamr[:, 0:1], in_=lamt[:, 0, :], axis=AX.X, op=ALU.add),
                 reads=[pk_], writes=[pk_])
            C.op("dve", lambda e: e.tensor_reduce(out=lamr[:, 1:2], in_=lamt[:, 2, :], axis=AX.X, op=ALU.add),
                 reads=[pk_], writes=[pk_])
            C.op("act", lambda e: e.activation(out=lamr[:, 2:4], in_=lamr[:, 0:2], func=AF.Exp),
                 reads=[pk_], writes=[pk_])
            C.op("dve", lambda e: e.scalar_tensor_tensor(out=neglam[:], in0=lamr[:, 3:4], scalar=-lam_init,
                                                         in1=lamr[:, 2:3], op0=ALU.add, op1=ALU.subtract),
                 reads=[pk_], writes=[pk_])
            dngc = T("dngc", [128, 1]); dlngc = T("dlngc", [64, 1])
            C.dma("sp", dngc[:], dng[l:l + 1, :].rearrange("o e -> e o"), pk_, writes=[pk_])
            C.dma("sp", dlngc[:], dlng[l:l + 1, :].rearrange("o e -> e o"), pk_, writes=[pk_])
            cw = T("cw", [128, 6, 4])
            for ch in range(6):
                C.dma("sp", cw[:, ch, :], convw[l, :, ch * 128:(ch + 1) * 128].rearrange("j p -> p j"),
                      pk_, writes=[pk_], allow_slow_non_contiguous=True)
            nA = T("nA", [128, 4]); dtbb = T("dtbb", [128, 4])
            C.dma("sp", nA[:], alog[l:l + 1, :].partition_broadcast(128), pk_, writes=[pk_])
            C.dma("sp", dtbb[:], dtb[l:l + 1, :].partition_broadcast(128), pk_, writes=[pk_])
            C.op("act", lambda e: e.activation(out=nA[:], in_=nA[:], func=AF.Exp), reads=[pk_], writes=[pk_])
            C.op("dve", lambda e: e.tensor_scalar(out=nA[:], in0=nA[:], scalar1=-1.0, scalar2=None, op0=ALU.mult),
                 reads=[pk_], writes=[pk_])
            if STOP == 6:
                C.finish()
                return nc

            xres = T("xres", [128, D]); xk = Tk("xres")
            xbf = WT([128, D]); xbfk = Tk()
            kbf = xbf[:, 0:512]; kbfk = xbfk
            xT = WT([128, 8, 256]); xTk = Tk()
            mixT = xT; mixk = xTk
            QTa = WT([128, 4, 256]); QTb = WT([128, 4, 256]); QTk = Tk()
            C.op("pool", lambda e: e.memset(QTa[64:128, :, :], 0.0), writes=[QTk])
            C.op("pool", lambda e: e.memset(QTb[0:64, :, :], 0.0), writes=[QTk])
            rot = T("rot", [128, 128]); rotk = Tk()
            tmpA = T("tmpA", [128, 512]); tmpB = T("tmpB", [128, 512]); tmpk = Tk()
            kout = T("kout", [128, 512]); koutk = Tk()
            vout = T("vout", [128, 512]); voutk = Tk()
            vbf = WT([128, 512]); vbfk = Tk()
            uT = T("uT", [128, 6, 131]); uTk = Tk()
            cT = T("cT", [128, 6, 128]); cTk = Tk()
            cE = T("cE", [128, 6, 128]); cEk = Tk()
            nTh = WT([128, 8, 128]); nTk = Tk()
            qdT = WT([128, 4, 128]); qdk = Tk()
            g4 = T("g4", [128, 264]); g4e = T("g4e", [128, 264]); g4k = Tk()
            sgT = T("sgT", [64, 8, 128]); sgTk = Tk()
            qkc = T("qkc", [128, 512]); qkck = Tk()
            qkcT = WT([128, 8, 128]); qkcTk = Tk()
            sg6 = g4[:, 0:256]; sg6e = g4e[:, 0:256]; sg6k = g4k
            ktok = T("ktok", [64, 2, 512]); ktokk = Tk()
            kz = T("kz", [64, 2, 256], BF16); kzk = Tk()
            vcb = T("vcb", [64, 2, 256], BF16); kzc = T("kzc", [64, 2, 256], BF16); vck = Tk()
            gb = T("gb", [64, 2, 8]); gbk = Tk()
            sm = T("sm", [128, 64]); smk = Tk()
            GT = T("GT", [64, 4, 64]); GTk = Tk()
            Drow = T("Drow", [128, 4, 64]); EDrow = T("EDrow", [128, 4, 64]); Drk = Tk()
            LM = T("LM", [64, 4, 64]); LMs = T("LMs", [64, 4, 64]); LMi = T("LMi", [64, 4, 64]); LMk = Tk()
            Xf = T("Xf", [64, 4, 64]); Xk = Tk()
            Yb = T("Yb", [64, 2, 4, 64], BF16); YTb = T("YTb", [64, 2, 4, 64], BF16); Yk = Tk()
            Pf = T("Pf", [64, 4, 64]); Pb = T("Pb", [64, 4, 64], BF16); Pk = Tk()
            inT = T("inT", [64, 4, 64], BF16); inTk = Tk()
            inR = T("inR", [64, 4, 64], BF16); inRk = Tk()
            rt = T("rt", [64, 256]); rb = T("rb", [64, 256], BF16); rk = Tk()
            scr = T("scr", [64, 4, 64]); scrk = Tk()
            vnb = T("vnb", [64, 256], BF16); vnk = Tk()
            Sd = T("Sd", [64, 4, 64]); Sdb = T("Sdb", [64, 4, 64], BF16); Sdk = Tk()
            Sr = T("Sr", [64, 4, 64]); Srb = T("Srb", [64, 4, 64], BF16); Srk = Tk()
            ot = T("ot", [64, 4, 128]); ot2 = T("ot2", [64, 4, 128]); otb = T("otb", [64, 4, 128], BF16); otk = Tk()
            PT = [WT([128, 2, 256]), WT([128, 2, 256])]; PTk = [Tk(), Tk()]
            Vt = [WT([128, 128]), WT([128, 128]), WT([128, 128])]; Vtk = [Tk(), Tk(), Tk()]
            ckb = WT([128, 512]); ckbk = Tk()
            Rr = tmpA[:, :].rearrange("p (j q) -> p j q", j=2); Tt = tmpB[:, :].rearrange("p (j q) -> p j q", j=2)
            oaf = T("oaf", [128, 256])
            oab = T("oab", [128, 256], BF16); atk = tmpk
            pcst = cE[:3, :, :].rearrange("p a b -> p (a b)"); pcstk = cEk

            def proj(TS, col0, c0, n):
                pb, pbk = bank("pj")
                for k in range(8):
                    C.op("pe", lambda e: e.matmul(pb[:TS, 0:n], lhsT=xT[:, k, col0:col0 + TS],
                                                  rhs=WA_in[:, k, c0:c0 + n], start=(k == 0), stop=(k == 7)),
                         reads=[xTk, WAK], writes=[pbk])
                return pb, pbk

            def rotary(pb, pbk, TS, outs):
                pv_ = pb[:TS, :].rearrange("p (h d) -> p h d", h=8)
                C.op("dve", lambda e: e.tensor_tensor(out=tmpA[:TS, :].rearrange("p (h d) -> p h d", h=8), in0=pv_,
                                                      in1=bc(rot[:TS, 0:64].unsqueeze(1), [TS, 8, 64]), op=ALU.mult),
                     reads=[pbk, rotk], writes=[tmpk])
                tb = tmpB[:TS, :].rearrange("p (h d) -> p h d", h=8)
                C.op("dve", lambda e: e.tensor_tensor(out=tb[:, :, 0:32], in0=pv_[:, :, 32:64],
                                                      in1=bc(rot[:TS, 64:96].unsqueeze(1), [TS, 8, 32]), op=ALU.mult),
                     reads=[pbk, rotk], writes=[tmpk])
                C.op("dve", lambda e: e.tensor_tensor(out=tb[:, :, 32:64], in0=pv_[:, :, 0:32],
                                                      in1=bc(rot[:TS, 96:128].unsqueeze(1), [TS, 8, 32]), op=ALU.mult),
                     reads=[pbk, rotk], writes=[tmpk])
                for (eng, oap, otk_) in outs:
                    C.op(eng, lambda e: e.tensor_tensor(out=oap, in0=tmpA[:TS, :], in1=tmpB[:TS, :], op=ALU.add),
                         reads=[tmpk], writes=[otk_])

            def transposes_bf(src, srck, TS, n):
                pb, pbk = bank("mi")
                pbb = pb[:].bitcast(BF16)
                for i in range(n):
                    C.op("pe", lambda e: e.transpose(out=pbb[:, i * TS:(i + 1) * TS],
                                                     in_=src[:TS, i * 128:(i + 1) * 128],
                                                     identity=identb[:TS, :TS]),
                         reads=[srck, constk], writes=[pbk])
                return pbb[:, 0:n * TS].rearrange("p (k t) -> p k t", k=n), pbk

            def silu_from(pb_ap, xs_, es_, k_, pbk):
                C.op("act", lambda e: e.activation(out=es_, in_=pb_ap, func=AF.Exp, scale=-1.0), reads=[pbk], writes=[k_])
                C.op("act", lambda e: e.activation(out=xs_, in_=pb_ap, func=AF.Copy), reads=[pbk], writes=[k_])
                C.op("dve", lambda e: e.tensor_scalar(out=es_, in0=es_, scalar1=1.0, scalar2=None, op0=ALU.add),
                     reads=[k_], writes=[k_])
                C.op("dve", lambda e: e.reciprocal(out=es_, in_=es_), reads=[k_], writes=[k_])

            for sq_ in seqs:
                TS, c, L, past, row0, sb_ = sq_["TS"], sq_["c"], sq_["L"], sq_["past"], sq_["row0"], sq_["b"]
                isP = sq_["name"] == "p"
                CPT = TS // c
                TPB = 2 if isP else 1
                nblk = L // (TS * TPB)
                nq = TS * TPB
                nit = 5 if c == 64 else 4
                rsrc = c_rotp if isP else c_rots
                kdst = pk if isP else sk
                vdst = pv if isP else sv
                orow0 = 0 if isP else sb_ * Ls
                zt = zetac if isP else zetasc
                xit = xic if isP else xisc
                gch = [math.exp(math.log(1.0 - 2.0 ** (-5.0 - h)) * c) for h in range(4)]
                if isP:
                    C.op("dve", lambda e: e.memset(Sd[:], 0.0), writes=[Sdk])
                    C.op("dve", lambda e: e.memset(Sr[:], 0.0), writes=[Srk])
                    C.op("dve", lambda e: e.memset(uT[:, :, 0:3], 0.0), writes=[uTk])
                else:
                    C.dma("sp", Sd[:], sdl[l, sb_].rearrange("(h d) e -> d h e", d=64), Sdk, writes=[Sdk])
                    C.dma("sp", Sr[:], srt[l, sb_].rearrange("(h d) e -> d h e", d=64), Srk, writes=[Srk])
                    for ch in range(6):
                        C.dma("sp", uT[:, ch, 0:3], scv[l, sb_, :, ch * 128:(ch + 1) * 128].rearrange("j p -> p j"),
                              uTk, writes=[uTk], allow_slow_non_contiguous=True)
                    for kt in range(past // 128):
                        C.dma("pool", ckb[:], ck[l, sb_, kt * 128:(kt + 1) * 128, :], ckbk, writes=[ckbk])
                        pv4, pv4k = transposes_bf(ckb, ckbk, 128, 4)
                        C.op("dve", lambda e: e.tensor_copy(out=KT[:, :, kt * 128:(kt + 1) * 128], in_=pv4),
                             reads=[pv4k], writes=[ktk[kt]])
                C.op("pool", lambda e: e.tensor_copy(out=Sdb[:], in_=Sd[:]), reads=[Sdk], writes=[Sdk])
                C.op("pool", lambda e: e.tensor_copy(out=Srb[:], in_=Sr[:]), reads=[Srk], writes=[Srk])

                for blk in range(nblk):
                    for ti in range(TPB):
                        t = blk * TPB + ti
                        r0 = t * TS
                        col0 = ti * TS
                        kcol = past + r0
                        kt_own = kcol // 128
                        xa = xres[:TS, :]
                        C.dma("sp", xa, xcs[row0 + r0:row0 + r0 + TS, :], xk,
                              reads=[dtk("xcs", row0 + r0)], writes=[xk])
                        C.dma("sp", rot[:TS, :], rsrc[r0:r0 + TS, :], rotk, writes=[rotk])
                        C.op("act", lambda e: e.activation(out=xbf[:TS, :], in_=xa, func=AF.Copy), reads=[xk], writes=[xbfk])
                        pv8, pv8k = transposes_bf(xbf, xbfk, TS, 8)
                        C.op("dve", lambda e: e.tensor_copy(out=xT[:, :, col0:col0 + TS], in_=pv8), reads=[pv8k], writes=[xTk])
                        if STOP == 7:
                            C.finish()
                            return nc
                        pb, pbk = proj(TS, col0, 0, 512)
                        rotary(pb, pbk, TS, [("pool", kbf[:TS, :], kbfk)])
                        pv4, pv4k = transposes_bf(kbf, kbfk, TS, 4)
                        C.op("dve", lambda e: e.tensor_copy(out=QTa[0:64, :, col0:col0 + TS], in_=pv4[0:64, :, :]),
                             reads=[pv4k], writes=[QTk])
                        C.op("dve", lambda e: e.tensor_copy(out=QTb[64:128, :, col0:col0 + TS], in_=pv4[64:128, :, :]),
                             reads=[pv4k], writes=[QTk])
                        pb, pbk = proj(TS, col0, 512, 512)
                        rotary(pb, pbk, TS, [("pool", kout[:TS, :], koutk), ("pool", kbf[:TS, :], kbfk)])
                        C.dma("pool", kdst[l, orow0 + r0:orow0 + r0 + TS, :], kout[:TS, :], koutk, reads=[koutk])
                        pv4, pv4k = transposes_bf(kbf, kbfk, TS, 4)
                        C.op("dve", lambda e: e.tensor_copy(out=KT[:, :, kcol:kcol + TS], in_=pv4),
                             reads=[pv4k], writes=[ktk[kt_own]])
                        pb, pbk = proj(TS, col0, 1024, 512)
                        C.op("act", lambda e: e.activation(out=vout[:TS, :], in_=pb[:TS, :], func=AF.Copy), reads=[pbk], writes=[voutk])
                        C.op("pool", lambda e: e.tensor_copy(out=vbf[:TS, :], in_=vout[:TS, :]), reads=[voutk], writes=[vbfk])
                        C.dma("pool", vdst[l, orow0 + r0:orow0 + r0 + TS, :], vout[:TS, :], voutk, reads=[voutk])
                        C.dma("pool", vsc[kcol:kcol + TS, :], vbf[:TS, :], vbfk, reads=[vbfk], writes=[dtk("vsc", kt_own)])
                        if STOP == 5:
                            C.finish()
                            return nc
                        pb, pbk = proj(TS, col0, 2304, 264)
                        silu_from(pb[:TS, 0:260], g4[:TS, 0:260], g4e[:TS, 0:260], g4k, pbk)
                        C.op("dve", lambda e: e.tensor_tensor(out=g4[:TS, 0:256], in0=g4[:TS, 0:256], in1=g4e[:TS, 0:256],
                                                              op=ALU.mult), reads=[g4k], writes=[g4k])
                        C.op("dve", lambda e: e.tensor_tensor(out=g4[:TS, 260:264], in0=pb[:TS, 260:264], in1=dtbb[:TS, :],
                                                              op=ALU.add), reads=[pbk, pk_], writes=[g4k])
                        C.op("act", lambda e: e.activation(out=g4[:TS, 260:264], in_=g4[:TS, 260:264], func=AF.Exp),
                             reads=[g4k], writes=[g4k])
                        C.op("act", lambda e: e.activation(out=g4[:TS, 260:264], in_=g4[:TS, 260:264], func=AF.Ln, bias=1.0),
                             reads=[g4k], writes=[g4k])
                        C.op("dve", lambda e: e.tensor_tensor(out=g4[:TS, 260:264], in0=g4[:TS, 260:264], in1=nA[:TS, :],
                                                              op=ALU.mult), reads=[g4k, pk_], writes=[g4k])
                        for ck_ in range(CPT):
                            C.op("dve", lambda e: e.tensor_copy(out=gb[:c, ck_, 0:4], in_=g4[ck_ * c:(ck_ + 1) * c, 260:264]),
                                 reads=[g4k], writes=[gbk])
                            C.op("dve", lambda e: e.tensor_copy(out=gb[:c, ck_, 4:8], in_=g4e[ck_ * c:(ck_ + 1) * c, 256:260]),
                                 reads=[g4k], writes=[gbk])
                        pm, pmk = bank("mi")
                        for h in range(4):
                            C.op("pe", lambda e: e.transpose(out=pm[:64, h * TS:(h + 1) * TS], in_=g4[:TS, h * 64:(h + 1) * 64],
                                                             identity=identf[:TS, :TS]),
                                 reads=[g4k, constk], writes=[pmk])
                        C.op("act", lambda e: e.activation(out=sgT[:, 0:4, :TS],
                                                           in_=pm[:64, 0:4 * TS].rearrange("p (h t) -> p h t", h=4), func=AF.Copy),
                             reads=[pmk], writes=[sgTk])
                        pb, pbk = proj(TS, col0, 2568, 512)
                        rotary(pb, pbk, TS, [("pool", qkc[:TS, :], qkck)])
                        C.op("dve", lambda e: e.tensor_scalar(out=qkc[:TS, 256:512], in0=qkc[:TS, 256:512], scalar1=0.125,
                                                              scalar2=None, op0=ALU.mult), reads=[qkck], writes=[qkck])
                        for half in range(2):
                            pm, pmk = bank("mi")
                            for h in range(4):
                                C.op("pe", lambda e: e.transpose(
                                    out=pm[:64, h * TS:(h + 1) * TS],
                                    in_=qkc[:TS, half * 256 + h * 64:half * 256 + (h + 1) * 64],
                                    identity=identf[:TS, :TS]), reads=[qkck, constk], writes=[pmk])
                            C.op("act", lambda e: e.activation(
                                out=qkcT[0:64, half * 4:half * 4 + 4, :TS],
                                in_=pm[:64, 0:4 * TS].rearrange("p (h t) -> p h t", h=4), func=AF.Copy), reads=[pmk], writes=[qkcTk])
                        for ck_ in range(CPT):
                            C.op("dve", lambda e: e.tensor_tensor(
                                out=kzc[:c, ck_, :].rearrange("p (h d) -> p h d", h=4),
                                in0=qkc[ck_ * c:(ck_ + 1) * c, 256:512].rearrange("p (h d) -> p h d", h=4),
                                in1=bc(zt[ck_ * c:(ck_ + 1) * c, :].unsqueeze(2), [c, 4, 64]), op=ALU.mult),
                                 reads=[qkck, constk], writes=[vck])
                        pb, pbk = proj(TS, col0, 3080, 512)
                        for ck_ in range(CPT):
                            C.op("act", lambda e: e.activation(out=vcb[:c, ck_, :], in_=pb[ck_ * c:(ck_ + 1) * c, 0:256], func=AF.Copy),
                                 reads=[pbk], writes=[vck])
                        silu_from(pb[:TS, 256:512], sg6[:TS, :], sg6e[:TS, :], sg6k, pbk)
                        C.op("dve", lambda e: e.tensor_tensor(out=sg6[:TS, :], in0=sg6[:TS, :], in1=sg6e[:TS, :], op=ALU.mult),
                             reads=[sg6k], writes=[sg6k])
                        pm, pmk = bank("mi")
                        for h in range(4):
                            C.op("pe", lambda e: e.transpose(out=pm[:64, h * TS:(h + 1) * TS], in_=sg6[:TS, h * 64:(h + 1) * 64],
                                                             identity=identf[:TS, :TS]),
                                 reads=[sg6k, constk], writes=[pmk])
                        C.op("act", lambda e: e.activation(out=sgT[:, 4:8, :TS],
                                                           in_=pm[:64, 0:4 * TS].rearrange("p (h t) -> p h t", h=4), func=AF.Copy),
                             reads=[pmk], writes=[sgTk])
                        for rnd in range(2):
                            pb, pbk = bank("pj")
                            for j in range(3):
                                ch = rnd * 3 + j
                                for k in range(8):
                                    C.op("pe", lambda e: e.matmul(
                                        pb[:, j * TS:(j + 1) * TS],
                                        lhsT=WA_in[:, k, 1536 + ch * 128:1536 + (ch + 1) * 128],
                                        rhs=xT[:, k, col0:col0 + TS], start=(k == 0), stop=(k == 7)),
                                         reads=[xTk, WAK], writes=[pbk])
                            C.op("act", lambda e: e.activation(
                                out=uT[:, rnd * 3:rnd * 3 + 3, 3:3 + TS],
                                in_=pb[:, 0:3 * TS].rearrange("p (j t) -> p j t", j=3), func=AF.Copy), reads=[pbk], writes=[uTk])
                        for ch in range(6):
                            C.op("dve", lambda e: e.tensor_scalar(out=cT[:, ch, :TS], in0=uT[:, ch, 0:TS],
                                                                  scalar1=cw[:, ch, 0:1], scalar2=None, op0=ALU.mult),
                                 reads=[uTk, pk_], writes=[cTk])
                            for j in range(1, 4):
                                C.op("dve", lambda e: e.scalar_tensor_tensor(
                                    out=cT[:, ch, :TS], in0=uT[:, ch, j:j + TS], scalar=cw[:, ch, j:j + 1],
                                    in1=cT[:, ch, :TS], op0=ALU.mult, op1=ALU.add), reads=[uTk, pk_, cTk], writes=[cTk])
                        if t == L // TS - 1:
                            for (c0_, c1_) in ((0, 4), (4, 6)):
                                pm, pmk = bank("mi")
                                for ch in range(c0_, c1_):
                                    C.op("pe", lambda e: e.transpose(out=pm[:3, (ch - c0_) * 128:(ch - c0_ + 1) * 128],
                                                                     in_=uT[:, ch, TS:TS + 3], identity=identf[:, :]),
                                         reads=[uTk, constk], writes=[pmk])
                                C.op("act", lambda e: e.activation(out=pcst[:, c0_ * 128:c1_ * 128], in_=pm[:3, 0:(c1_ - c0_) * 128],
                                                                   func=AF.Copy), reads=[pmk], writes=[pcstk])
                            cdst = pcv[l] if isP else scvo[l, sb_]
                            C.dma("pool", cdst, pcst[:, :], pcstk, reads=[pcstk])
                        C.op("pool", lambda e: e.tensor_copy(out=uT[:, :, 0:3], in_=uT[:, :, TS:TS + 3]),
                             reads=[uTk], writes=[uTk])
                        C.op("act", lambda e: e.activation(out=cE[:, :, :TS], in_=cT[:, :, :TS], func=AF.Exp, scale=-1.0),
                             reads=[cTk], writes=[cEk])
                        C.op("dve", lambda e: e.tensor_scalar(out=cE[:, :, :TS], in0=cE[:, :, :TS], scalar1=1.0, scalar2=None,
                                                              op0=ALU.add), reads=[cEk], writes=[cEk])
                        C.op("dve", lambda e: e.reciprocal(out=cE[:, :, :TS], in_=cE[:, :, :TS]), reads=[cEk], writes=[cEk])
                        C.op("dve", lambda e: e.tensor_tensor(out=cT[:, :, :TS], in0=cT[:, :, :TS], in1=cE[:, :, :TS], op=ALU.mult),
                             reads=[cEk, cTk], writes=[cTk])
                        C.op("pool", lambda e: e.tensor_tensor(out=cE[:, 0:4, :TS], in0=cT[:, 0:4, :TS], in1=cT[:, 0:4, :TS],
                                                               op=ALU.mult), reads=[cTk, cEk], writes=[cEk])
                        pm, pmk = bank("mi")
                        for j in range(4):
                            C.op("pe", lambda e: e.matmul(pm[:, j * TS:(j + 1) * TS], lhsT=bonesf[:, :], rhs=cE[:, j, :TS],
                                                          start=True, stop=True), reads=[cEk, constk], writes=[pmk])
                        C.op("act", lambda e: e.activation(out=cE[:, 0:4, :TS],
                                                           in_=pm[:, 0:4 * TS].rearrange("p (j t) -> p j t", j=4),
                                                           func=AF.Ln, bias=epsb[:, 1:2]),
                             reads=[pmk, constk], writes=[cEk])
                        C.op("act", lambda e: e.activation(out=cE[:, 0:4, :TS], in_=cE[:, 0:4, :TS], func=AF.Exp, scale=-0.5),
                             reads=[cEk], writes=[cEk])
                        C.op("dve", lambda e: e.scalar_tensor_tensor(out=cT[:, 0:2, :TS], in0=cT[:, 0:2, :TS], scalar=0.125,
                                                                     in1=cE[:, 0:2, :TS], op0=ALU.mult, op1=ALU.mult),
                             reads=[cTk, cEk], writes=[cTk])
                        C.op("dve", lambda e: e.tensor_tensor(out=cT[:, 2:4, :TS], in0=cT[:, 2:4, :TS], in1=cE[:, 2:4, :TS],
                                                              op=ALU.mult), reads=[cTk, cEk], writes=[cTk])
                        for j in range(4):
                            for s_ in range(2):
                                hidx = (j // 2) * 4 + (j % 2) * 2 + s_
                                C.op("pool", lambda e: e.tensor_copy(out=nTh[0:64, hidx, :TS], in_=cT[s_ * 64:(s_ + 1) * 64, j, :TS]),
                                     reads=[cTk], writes=[nTk])
                        for ck_ in range(CPT):
                            pm, pmk = bank("mi")
                            for j in range(4):
                                C.op("pe", lambda e: e.transpose(
                                    out=pm[:c, j * 128:(j + 1) * 128], in_=cT[:, 2 + j, ck_ * c:(ck_ + 1) * c],
                                    identity=identf[:, :]), reads=[cTk, constk], writes=[pmk])
                            C.op("act", lambda e: e.activation(out=ktok[:c, ck_, :], in_=pm[:c, :], func=AF.Copy),
                                 reads=[pmk], writes=[ktokk])
                        if STOP == 4:
                            C.finish()
                            return nc

                        po, pok = bank("pj")
                        po2, po2k = bank("pj")
                        for ck_ in range(CPT):
                            cs = slice(ck_ * c, (ck_ + 1) * c)
                            pm, pmk = bank("mi")
                            C.op("pe", lambda e: e.matmul(pm[:c, 0:4], lhsT=triu[:c, :c], rhs=gb[:c, ck_, 0:4],
                                                          start=True, stop=True), reads=[gbk, constk], writes=[pmk])
                            C.op("dve", lambda e: e.tensor_tensor(out=GT[:c, :, :c],
                                                                  in0=bc(triu[:c, :c].unsqueeze(1), [c, 4, c]),
                                                                  in1=bc(gb[:c, ck_, 0:4].unsqueeze(2), [c, 4, c]),
                                                                  op=ALU.mult), reads=[gbk, constk], writes=[GTk])
                            pm2, pm2k = bank("mi")
                            for h in range(4):
                                C.op("pe", lambda e: e.matmul(pm2[:, h * c:(h + 1) * c], lhsT=onesf[:c, :],
                                                              rhs=GT[:c, h, :c], start=True, stop=True),
                                     reads=[GTk, constk], writes=[pm2k])
                            C.op("act", lambda e: e.activation(out=sm[:c, 0:4], in_=pm[:c, 0:4], func=AF.Copy), reads=[pmk], writes=[smk])
                            C.op("act", lambda e: e.activation(out=sm[:c, 4:8], in_=pm[:c, 0:4], func=AF.Exp), reads=[pmk], writes=[smk])
                            C.op("act", lambda e: e.activation(out=Drow[:, :, :c], in_=pm2[:, 0:4 * c].rearrange("p (h i) -> p h i", h=4),
                                                               func=AF.Copy), reads=[pm2k], writes=[Drk])
                            C.op("act", lambda e: e.activation(out=EDrow[:, :, :c],
                                                               in_=pm2[:, 0:4 * c].rearrange("p (h i) -> p h i", h=4), func=AF.Exp),
                                 reads=[pm2k], writes=[Drk])
                            C.op("dve", lambda e: e.tensor_tensor(out=sm[:c, 12:16], in0=Drow[:c, :, c - 1], in1=sm[:c, 0:4],
                                                                  op=ALU.subtract), reads=[Drk, smk], writes=[smk])
                            C.op("act", lambda e: e.activation(out=sm[:c, 8:12], in_=sm[:c, 12:16], func=AF.Exp), reads=[smk], writes=[smk])
                            C.op("dve", lambda e: e.tensor_tensor(out=LM[:c, :, :c], in0=Drow[:c, :, :c],
                                                                  in1=bc(sm[:c, 0:4].unsqueeze(2), [c, 4, c]), op=ALU.subtract),
                                 reads=[Drk, smk], writes=[LMk])
                            C.op("dve", lambda e: e.tensor_scalar(out=LM[:c, :, :c], in0=LM[:c, :, :c], scalar1=0.0, scalar2=None,
                                                                  op0=ALU.min), reads=[LMk], writes=[LMk])
                            C.op("act", lambda e: e.activation(out=LM[:c, :, :c], in_=LM[:c, :, :c], func=AF.Exp), reads=[LMk], writes=[LMk])
                            C.op("pool", lambda e: e.tensor_tensor(out=LMs[:c, :, :c], in0=LM[:c, :, :c],
                                                                   in1=bc(striu[:c, :c].unsqueeze(1), [c, 4, c]), op=ALU.mult),
                                 reads=[LMk, constk], writes=[LMk])
                            C.op("pool", lambda e: e.tensor_tensor(out=LMi[:c, :, :c], in0=LM[:c, :, :c],
                                                                   in1=bc(triu[:c, :c].unsqueeze(1), [c, 4, c]), op=ALU.mult),
                                 reads=[LMk, constk], writes=[LMk])
                            C.op("dve", lambda e: e.tensor_tensor(out=qdT[0:64, :, cs], in0=nTh[0:64, 0:4, cs],
                                                                  in1=EDrow[0:64, :, :c], op=ALU.mult), reads=[nTk, Drk], writes=[qdk])
                            C.op("dve", lambda e: e.tensor_tensor(out=kz[:c, ck_, :].rearrange("p (h d) -> p h d", h=4),
                                                                  in0=ktok[:c, ck_, 0:256].rearrange("p (h d) -> p h d", h=4),
                                                                  in1=bc(sm[:c, 8:12].unsqueeze(2), [c, 4, 64]), op=ALU.mult),
                                 reads=[ktokk, smk], writes=[kzk])
                            if STOP == 10:
                                C.finish()
                                return nc
                            pg, pgk = bank("mi")
                            for h in range(4):
                                C.op("pe", lambda e: e.matmul(pg[:c, h * c:(h + 1) * c], lhsT=nTh[0:64, 4 + h, cs], rhs=nTh[0:64, 4 + h, cs],
                                                              start=True, stop=True), reads=[nTk], writes=[pgk])
                            for h in range(4):
                                C.op("pe", lambda e: e.matmul(pg[:c, 256 + h * c:256 + (h + 1) * c], lhsT=nTh[0:64, 4 + h, cs],
                                                              rhs=nTh[0:64, h, cs], start=True, stop=True), reads=[nTk], writes=[pgk])
                            KKv = pg[:c, 0:4 * c].rearrange("p (h i) -> p h i", h=4)
                            KQv = pg[:c, 256:256 + 4 * c].rearrange("p (h i) -> p h i", h=4)
                            C.op("dve", lambda e: e.tensor_tensor(out=Xf[:c, :, :c], in0=KKv,
                                                                  in1=bc(gb[:c, ck_, 4:8].unsqueeze(2), [c, 4, c]), op=ALU.mult),
                                 reads=[pgk, gbk], writes=[Xk])
                            C.op("dve", lambda e: e.tensor_tensor(out=Xf[:c, :, :c], in0=Xf[:c, :, :c], in1=LMs[:c, :, :c], op=ALU.mult),
                                 reads=[Xk, LMk], writes=[Xk])
                            C.op("dve", lambda e: e.tensor_tensor(out=scr[:c, :, :c], in0=KQv, in1=LMi[:c, :, :c], op=ALU.mult),
                                 reads=[pgk, LMk], writes=[scrk])
                            C.op("pool", lambda e: e.tensor_copy(out=inT[:c, :, :c], in_=scr[:c, :, :c]), reads=[scrk], writes=[inTk])
                            if STOP == 11:
                                C.finish()
                                return nc
                            pm, pmk = bank("mi")
                            for h in range(4):
                                C.op("pe", lambda e: e.transpose(out=pm[:c, h * c:(h + 1) * c], in_=Xf[:c, h, :c],
                                                                 identity=identf[:c, :c]), reads=[Xk, constk], writes=[pmk])
                            C.op("act", lambda e: e.activation(out=YTb[:c, 0, :, :c], in_=pm[:c, 0:4 * c].rearrange("p (h i) -> p h i", h=4),
                                                               func=AF.Copy), reads=[pmk], writes=[Yk])
                            C.op("act", lambda e: e.activation(out=Yb[:c, 0, :, :c], in_=Xf[:c, :, :c], func=AF.Copy), reads=[Xk], writes=[Yk])
                            C.op("dve", lambda e: e.tensor_tensor(out=Pf[:c, :, :c], in0=bc(identf[:c, :c].unsqueeze(1), [c, 4, c]),
                                                                  in1=Xf[:c, :, :c], op=ALU.subtract), reads=[Xk, constk], writes=[Pk])
                            C.op("pool", lambda e: e.tensor_copy(out=Pb[:c, :, :c], in_=Pf[:c, :, :c]), reads=[Pk], writes=[Pk])
                            cur = 0
                            for it in range(nit):
                                nxt = 1 - cur
                                pm, pmk = bank("mi")
                                for h in range(4):
                                    C.op("pe", lambda e: e.matmul(pm[:c, h * c:(h + 1) * c], lhsT=YTb[:c, cur, h, :c],
                                                                  rhs=Yb[:c, cur, h, :c], start=True, stop=True),
                                         reads=[Yk], writes=[pmk])
                                for h in range(4):
                                    C.op("pe", lambda e: e.matmul(pm[:c, 256 + h * c:256 + (h + 1) * c],
                                                                  lhsT=Yb[:c, cur, h, :c], rhs=YTb[:c, cur, h, :c],
                                                                  start=True, stop=True), reads=[Yk], writes=[pmk])
                                C.op("act", lambda e: e.activation(out=Yb[:c, nxt, :, :c],
                                                                   in_=pm[:c, 0:4 * c].rearrange("p (h i) -> p h i", h=4), func=AF.Copy),
                                     reads=[pmk], writes=[Yk])
                                C.op("act", lambda e: e.activation(out=YTb[:c, nxt, :, :c],
                                                                   in_=pm[:c, 256:256 + 4 * c].rearrange("p (h i) -> p h i", h=4), func=AF.Copy),
                                     reads=[pmk], writes=[Yk])
                                pm, pmk = bank("mi")
                                for h in range(4):
                                    C.op("pe", lambda e: e.matmul(pm[:c, h * c:(h + 1) * c], lhsT=YTb[:c, nxt, h, :c],
                                                                  rhs=Pb[:c, h, :c], start=True, stop=True),
                                         reads=[Yk, Pk], writes=[pmk])
                                C.op("dve", lambda e: e.tensor_tensor(out=Pf[:c, :, :c], in0=Pf[:c, :, :c],
                                                                      in1=pm[:c, 0:4 * c].rearrange("p (h i) -> p h i", h=4), op=ALU.add),
                                     reads=[pmk, Pk], writes=[Pk])
                                C.op("pool", lambda e: e.tensor_copy(out=Pb[:c, :, :c], in_=Pf[:c, :, :c]), reads=[Pk], writes=[Pk])
                                cur = nxt
                            if STOP == 12:
                                C.finish()
                                return nc
                            pc, pck = bank("mi")
                            for h in range(4):
                                C.op("pe", lambda e: e.matmul(pc[:c, h * 64:(h + 1) * 64], lhsT=nTh[0:64, 4 + h, cs],
                                                              rhs=Sdb[:, h, :], start=True, stop=True),
                                     reads=[nTk, Sdk], writes=[pck])
                            C.op("dve", lambda e: e.tensor_tensor(out=rt[:c, :].rearrange("p (h d) -> p h d", h=4),
                                                                  in0=pc[:c, 0:256].rearrange("p (h d) -> p h d", h=4),
                                                                  in1=bc(sm[:c, 4:8].unsqueeze(2), [c, 4, 64]), op=ALU.mult),
                                 reads=[pck, smk], writes=[rk])
                            C.op("dve", lambda e: e.tensor_tensor(out=rb[:c, :], in0=ktok[:c, ck_, 256:512], in1=rt[:c, :], op=ALU.subtract),
                                 reads=[rk, ktokk], writes=[rk])
                            pc2, pc2k = bank("mi")
                            for h in range(4):
                                C.op("pe", lambda e: e.matmul(pc2[:c, h * 64:(h + 1) * 64], lhsT=Pb[:c, h, :c],
                                                              rhs=rb[:c, h * 64:(h + 1) * 64], start=True, stop=True),
                                     reads=[Pk, rk], writes=[pc2k])
                            C.op("dve", lambda e: e.tensor_tensor(out=rt[:c, :].rearrange("p (h d) -> p h d", h=4),
                                                                  in0=pc2[:c, 0:256].rearrange("p (h d) -> p h d", h=4),
                                                                  in1=bc(gb[:c, ck_, 4:8].unsqueeze(2), [c, 4, 64]), op=ALU.mult),
                                 reads=[pc2k, gbk], writes=[rk])
                            C.op("pool", lambda e: e.tensor_copy(out=vnb[:c, :], in_=rt[:c, :]), reads=[rk], writes=[vnk])
                            for h in range(4):
                                oc_ = slice(h * TS + ck_ * c, h * TS + (ck_ + 1) * c)
                                C.op("pe", lambda e: e.matmul(po[:64, oc_], lhsT=Sdb[:, h, :], rhs=qdT[0:64, h, cs],
                                                              start=True, stop=False), reads=[Sdk, qdk], writes=[pok])
                                C.op("pe", lambda e: e.matmul(po[:64, oc_], lhsT=vnb[:c, h * 64:(h + 1) * 64], rhs=inT[:c, h, :c],
                                                              start=False, stop=True), reads=[vnk, inTk], writes=[pok])
                            psu, psuk = bank("mi")
                            for h in range(4):
                                C.op("pe", lambda e: e.matmul(psu[:64, h * 64:(h + 1) * 64], lhsT=kz[:c, ck_, h * 64:(h + 1) * 64],
                                                              rhs=vnb[:c, h * 64:(h + 1) * 64], start=True, stop=True),
                                     reads=[kzk, vnk], writes=[psuk])
                            for h in range(4):
                                C.op("dve", lambda e: e.scalar_tensor_tensor(
                                    out=Sd[:, h, :], in0=Sd[:, h, :], scalar=EDrow[0:64, h, c - 1:c], in1=psu[:64, h * 64:(h + 1) * 64],
                                    op0=ALU.mult, op1=ALU.add), reads=[Drk, psuk, Sdk], writes=[Sdk])
                            C.op("pool", lambda e: e.tensor_copy(out=Sdb[:], in_=Sd[:]), reads=[Sdk], writes=[Sdk])
                            if STOP == 13:
                                C.finish()
                                return nc
                            pr, prk = bank("mi")
                            for h in range(4):
                                C.op("pe", lambda e: e.matmul(pr[:c, h * c:(h + 1) * c], lhsT=qkcT[0:64, 4 + h, cs], rhs=qkcT[0:64, h, cs],
                                                              start=True, stop=True), reads=[qkcTk], writes=[prk])
                            C.op("dve", lambda e: e.tensor_tensor(out=scr[:c, :, :c], in0=pr[:c, 0:4 * c].rearrange("p (h i) -> p h i", h=4),
                                                                  in1=rmc[:c, :].rearrange("p (h i) -> p h i", h=4)[:, :, :c], op=ALU.mult),
                                 reads=[prk, constk], writes=[scrk])
                            C.op("pool", lambda e: e.tensor_copy(out=inR[:c, :, :c], in_=scr[:c, :, :c]), reads=[scrk], writes=[inRk])
                            for h in range(4):
                                oc_ = slice(h * TS + ck_ * c, h * TS + (ck_ + 1) * c)
                                C.op("pe", lambda e: e.matmul(po2[:64, oc_], lhsT=Srb[:, h, :], rhs=qkcT[0:64, h, cs],
                                                              start=True, stop=False), reads=[Srk, qkcTk], writes=[po2k])
                                C.op("pe", lambda e: e.matmul(po2[:64, oc_], lhsT=vcb[:c, ck_, h * 64:(h + 1) * 64], rhs=inR[:c, h, :c],
                                                              start=False, stop=True), reads=[vck, inRk], writes=[po2k])
                            psu, psuk = bank("mi")
                            for h in range(4):
                                C.op("pe", lambda e: e.matmul(psu[:64, h * 64:(h + 1) * 64], lhsT=kzc[:c, ck_, h * 64:(h + 1) * 64],
                                                              rhs=vcb[:c, ck_, h * 64:(h + 1) * 64], start=True, stop=True),
                                     reads=[vck], writes=[psuk])
                            for h in range(4):
                                C.op("dve", lambda e: e.scalar_tensor_tensor(out=Sr[:, h, :], in0=Sr[:, h, :], scalar=float(gch[h]),
                                                                             in1=psu[:64, h * 64:(h + 1) * 64], op0=ALU.mult, op1=ALU.add),
                                     reads=[psuk, Srk], writes=[Srk])
                            C.op("pool", lambda e: e.tensor_copy(out=Srb[:], in_=Sr[:]), reads=[Srk], writes=[Srk])

                        for which in range(2):
                            pso, psok = (po, pok) if which == 0 else (po2, po2k)
                            pov = pso[:64, 0:4 * TS].rearrange("p (h t) -> p h t", h=4)
                            if which == 0:
                                C.op("act", lambda e: e.activation(out=ot[:, :, :TS], in_=pov, func=AF.Copy), reads=[psok], writes=[otk])
                            else:
                                C.op("dve", lambda e: e.tensor_tensor(out=ot[:, :, :TS], in0=pov,
                                                                      in1=xit[:, :].rearrange("p (h t) -> p h t", h=4)[:, :, :TS], op=ALU.mult),
                                     reads=[psok, constk], writes=[otk])
                            C.op("pool", lambda e: e.tensor_tensor(out=otb[:, :, :TS], in0=ot[:, :, :TS], in1=ot[:, :, :TS], op=ALU.mult),
                                 reads=[otk], writes=[otk])
                            pm, pmk = bank("mi")
                            for h in range(4):
                                C.op("pe", lambda e: e.matmul(pm[:64, h * TS:(h + 1) * TS], lhsT=onesb[:64, :64], rhs=otb[:, h, :TS],
                                                              start=True, stop=True), reads=[otk, constk], writes=[pmk])
                            C.op("act", lambda e: e.activation(out=ot2[:, :, :TS], in_=pm[:64, 0:4 * TS].rearrange("p (h t) -> p h t", h=4),
                                                               func=AF.Ln, scale=1.0 / 64, bias=epsb[:64, 1:2]),
                                 reads=[pmk, constk], writes=[otk])
                            C.op("act", lambda e: e.activation(out=ot2[:, :, :TS], in_=ot2[:, :, :TS], func=AF.Exp, scale=-0.5),
                                 reads=[otk], writes=[otk])
                            C.op("dve", lambda e: e.tensor_tensor(out=ot[:, :, :TS], in0=ot[:, :, :TS], in1=ot2[:, :, :TS], op=ALU.mult),
                                 reads=[otk], writes=[otk])
                            C.op("pool", lambda e: e.tensor_tensor(out=ot[:, :, :TS], in0=ot[:, :, :TS],
                                                                   in1=sgT[:, which * 4:which * 4 + 4, :TS], op=ALU.mult),
                                 reads=[otk, sgTk], writes=[otk])
                            for h in range(4):
                                p_, s_ = divmod(h, 2)
                                dst = mixT[s_ * 64:(s_ + 1) * 64, 4 + which * 2 + p_, col0:col0 + TS]
                                if which == 0:
                                    C.op("dve", lambda e: e.tensor_scalar(out=dst, in0=ot[:, h, :TS], scalar1=dlngc[:, 0:1],
                                                                          scalar2=None, op0=ALU.mult),
                                         reads=[otk, pk_], writes=[mixk])
                                else:
                                    C.op("pool", lambda e: e.tensor_copy(out=dst, in_=ot[:, h, :TS]), reads=[otk], writes=[mixk])

                    if STOP == 2:
                        C.finish()
                        return nc
                    if isP:
                        kts = [(kt, 128, None) for kt in range(2 * blk)] + [(2 * blk, 128, 0), (2 * blk + 1, 128, 1)]
                    else:
                        kts = [(kt, 128, None) for kt in range(past // 128)] + [(past // 128, TS, None)]
                    its = [(hd, kt, rows, msk) for hd in range(4) for (kt, rows, msk) in kts]

                    def load_v(ii):
                        hd, kt, rows, msk = its[ii]
                        vt, vtk = Vt[ii % 3], Vtk[ii % 3]
                        if kt < past // 128:
                            C.dma("pool", vt[:rows, :], cv[l, sb_, kt * 128:kt * 128 + rows, hd * 128:(hd + 1) * 128], vtk, writes=[vtk])
                        else:
                            C.dma("sp", vt[:rows, :], vsc[kt * 128:kt * 128 + rows, hd * 128:(hd + 1) * 128], vtk,
                                  reads=[dtk("vsc", kt)], writes=[vtk])
                    load_v(0)
                    if len(its) > 1:
                        load_v(1)
                    for ii, (hd, kt, rows, msk) in enumerate(its):
                        if ii + 2 < len(its):
                            load_v(ii + 2)
                        first = (ii % len(kts) == 0)
                        last = (ii % len(kts) == len(kts) - 1)
                        if first:
                            C.op("dve", lambda e: e.memset(OB[:, :], 0.0), writes=[OBK])
                            C.op("dve", lambda e: e.memset(LB[:, :], 0.0), writes=[LBK])
                        sbk_, sbkk = bank("s")
                        kc0 = kt * 128
                        C.op("pe", lambda e: e.matmul(sbk_[:rows, 0:nq], lhsT=KT[:, hd, kc0:kc0 + rows], rhs=QTa[:, hd, 0:nq],
                                                      start=True, stop=True), reads=[ktk[kt], QTk], writes=[sbkk])
                        C.op("pe", lambda e: e.matmul(sbk_[:rows, 256:256 + nq], lhsT=KT[:, hd, kc0:kc0 + rows],
                                                      rhs=QTb[:, hd, 0:nq], start=True, stop=True),
                             reads=[ktk[kt], QTk], writes=[sbkk])
                        pt, ptk = PT[ii % 2], PTk[ii % 2]
                        C.op("act", lambda e: e.activation(out=pt[:rows, :, 0:nq],
                                                           in_=sbk_[:rows, :].rearrange("p (j q) -> p j q", j=2)[:, :, 0:nq],
                                                           func=AF.Exp, scale=0.125), reads=[sbkk], writes=[ptk])
                        if msk is not None:
                            C.op("pool", lambda e: e.tensor_tensor(out=pt[:rows, :, 0:nq], in0=pt[:rows, :, 0:nq],
                                                                   in1=bc(amask[:rows, msk * 256:msk * 256 + nq].unsqueeze(1), [rows, 2, nq]),
                                                                   op=ALU.mult), reads=[ptk, constk], writes=[ptk])
                        vt, vtk = Vt[ii % 3], Vtk[ii % 3]
                        for j in range(2):
                            C.op("pe", lambda e: e.matmul(OB[:, j * 256:j * 256 + nq], lhsT=vt[:rows, :],
                                                          rhs=pt[:rows, j, 0:nq], start=False, stop=False, skip_group_check=True),
                                 reads=[vtk, ptk], writes=[OBK])
                            C.op("pe", lambda e: e.matmul(LB[:, j * 256:j * 256 + nq], lhsT=onesb[:rows, :], rhs=pt[:rows, j, 0:nq],
                                                          start=False, stop=False, skip_group_check=True),
                                 reads=[constk, ptk], writes=[LBK])
                        if not last:
                            continue
                        Lv = LB[:, :].rearrange("p (j q) -> p j q", j=2)[:, :, 0:nq]
                        Ov = OB[:, :].rearrange("p (j q) -> p j q", j=2)[:, :, 0:nq]
                        C.op("dve", lambda e: e.reciprocal(out=Rr[:, :, 0:nq], in_=Lv), reads=[LBK], writes=[atk])
                        C.op("dve", lambda e: e.tensor_tensor(out=Tt[:, :, 0:nq], in0=Ov, in1=Rr[:, :, 0:nq], op=ALU.mult),
                             reads=[OBK, atk], writes=[atk])
                        C.op("dve", lambda e: e.scalar_tensor_tensor(out=oaf[:, 0:nq], in0=Tt[:, 1, 0:nq], scalar=neglam[:, 0:1],
                                                                     in1=Tt[:, 0, 0:nq], op0=ALU.mult, op1=ALU.add),
                             reads=[atk, pk_], writes=[atk])
                        C.op("pool", lambda e: e.tensor_tensor(out=oab[:, 0:nq], in0=oaf[:, 0:nq], in1=oaf[:, 0:nq], op=ALU.mult),
                             reads=[atk], writes=[atk])
                        pm, pmk = bank("mi")
                        C.op("pe", lambda e: e.matmul(pm[:, 0:nq], lhsT=onesb[:, :], rhs=oab[:, 0:nq], start=True, stop=True),
                             reads=[atk, constk], writes=[pmk])
                        C.op("act", lambda e: e.activation(out=Rr[:, 0, 0:nq], in_=pm[:, 0:nq], func=AF.Ln, scale=1.0 / 128,
                                                           bias=epsb[:, 1:2]), reads=[pmk, constk], writes=[atk])
                        C.op("act", lambda e: e.activation(out=Rr[:, 0, 0:nq], in_=Rr[:, 0, 0:nq], func=AF.Exp, scale=-0.5),
                             reads=[atk], writes=[atk])
                        C.op("dve", lambda e: e.tensor_tensor(out=oaf[:, 0:nq], in0=oaf[:, 0:nq], in1=Rr[:, 0, 0:nq], op=ALU.mult),
                             reads=[atk], writes=[atk])
                        C.op("dve", lambda e: e.tensor_scalar(out=mixT[:, hd, 0:nq], in0=oaf[:, 0:nq], scalar1=dngc[:, 0:1],
                                                              scalar2=float(1.0 - lam_init), op0=ALU.mult, op1=ALU.mult),
                             reads=[atk, pk_], writes=[mixk])

                    if STOP == 3:
                        C.finish()
                        return nc
                    for ti in range(TPB):
                        t = blk * TPB + ti
                        r0 = t * TS
                        col0 = ti * TS
                        C.dma("sp", xres[:TS, :], xcs[row0 + r0:row0 + r0 + TS, :], xk,
                              reads=[dtk("xcs", row0 + r0)], writes=[xk])
                        for half in range(2):
                            pb, pbk = bank("pj")
                            for k in range(8):
                                C.op("pe", lambda e: e.matmul(pb[:TS, :], lhsT=mixT[:, k, col0:col0 + TS],
                                                              rhs=WA_out[:, k, half * 512:(half + 1) * 512], start=(k == 0), stop=(k == 7)),
                                     reads=[mixk, WAK], writes=[pbk])
                            C.op("dve", lambda e: e.scalar_tensor_tensor(out=xres[:TS, half * 512:(half + 1) * 512],
                                                                         in0=xres[:TS, half * 512:(half + 1) * 512], scalar=float(ALPHA),
                                                                         in1=pb[:TS, :], op0=ALU.mult, op1=ALU.add),
                                 reads=[pbk, xk], writes=[xk])
                        layer_norm(xres[:TS, :], TS, g1, b1, pk_, xk)
                        C.dma("pool", x1s[row0 + r0:row0 + r0 + TS, :], xres[:TS, :], xk, reads=[xk], writes=[dtk("x1s", row0 + r0)])

                if STOP == 1:
                    C.finish()
                    return nc
                ddst = pdl[l] if isP else sdlo[l, sb_]
                rdst = prt[l] if isP else srto[l, sb_]
                C.dma("pool", ddst.rearrange("(h d) e -> d h e", d=64), Sd[:], Sdk, reads=[Sdk])
                C.dma("pool", rdst.rearrange("(h d) e -> d h e", d=64), Sr[:], Srk, reads=[Srk])

        if STOP == 20 + l:
            C.finish()
            return nc
        C.barrier()
        WA_up = WA[:, 0:8 * DFF].rearrange("p (k n) -> p k n", k=8)
        WA_dn = WA[:, 8 * DFF:8 * DFF + 32 * D].rearrange("p (k n) -> p k n", k=32)
        for k in range(8):
            for c0_ in range(0, DFF, 512):
                C.dma("pool", WA_up[:, k, c0_:c0_ + 512], w_up[l, k * 128:(k + 1) * 128, c0_:c0_ + 512], WAK, writes=[WAK])
        for k in range(32):
            for c0_ in range(0, D, 512):
                C.dma("pool", WA_dn[:, k, c0_:c0_ + 512], w_down[l, k * 128:(k + 1) * 128, c0_:c0_ + 512], WAK, writes=[WAK])
        with ExitStack() as st:
            def T(name, shape, dt=F32):
                return st.enter_context(nc.sbuf_tensor("%s_f%d" % (name, l), list(shape), dt))
            g2 = T("g2", [128, D]); b2 = T("b2", [128, D]); pk2 = Tk()
            C.dma("sp", g2[:], ln2g[l:l + 1, :].partition_broadcast(128), pk2, writes=[pk2])
            C.dma("sp", b2[:], ln2b[l:l + 1, :].partition_broadcast(128), pk2, writes=[pk2])
            x1 = T("x1", [128, 2, D]); x1k = [Tk(), Tk()]
            x1b = T("x1b", [128, D], BF16); x1bk = Tk()
            x1T = T("x1T", [128, 8, 256], BF16); x1Tk = Tk()
            hidT = T("hidT", [128, 32, 256], BF16); hidk = Tk()
            rl = [T("rl0", [128, 256]), T("rl1", [128, 256])]; rlk = [Tk(), Tk()]
            blocks = []
            for b0_ in range(0, Lp, 256):
                blocks.append((b0_, 128, min(2, (Lp - b0_) // 128)))
            for b_ in range(NS):
                blocks.append((Lp + b_ * Ls, Ls, 1))
            for (rb0, TS, ntl) in blocks:
                nb = TS * ntl
                for ti in range(ntl):
                    r0 = rb0 + ti * TS
                    C.dma("sp", x1[:TS, ti, :], x1s[r0:r0 + TS, :], x1k[ti], reads=[dtk("x1s", r0)], writes=[x1k[ti]])
                    C.op("act", lambda e: e.activation(out=x1b[:TS, :], in_=x1[:TS, ti, :], func=AF.Copy), reads=[x1k[ti]], writes=[x1bk])
                    pb, pbk = bank("mi")
                    pbb = pb[:].bitcast(BF16)
                    for k in range(8):
                        C.op("pe", lambda e: e.transpose(out=pbb[:, k * TS:(k + 1) * TS], in_=x1b[:TS, k * 128:(k + 1) * 128],
                                                         identity=identb[:TS, :TS]), reads=[x1bk, constk], writes=[pbk])
                    C.op("dve", lambda e: e.tensor_copy(out=x1T[:, :, ti * TS:(ti + 1) * TS],
                                                        in_=pbb[:, 0:8 * TS].rearrange("p (k t) -> p k t", k=8)),
                         reads=[pbk], writes=[x1Tk])
                for f in range(32):
                    pb, pbk = bank("pj")
                    for k in range(8):
                        C.op("pe", lambda e: e.matmul(pb[:, 0:nb], lhsT=WA_up[:, k, f * 128:(f + 1) * 128], rhs=x1T[:, k, 0:nb],
                                                      start=(k == 0), stop=(k == 7)), reads=[x1Tk, WAK], writes=[pbk])
                    r_, rk_ = rl[f % 2], rlk[f % 2]
                    C.op("act", lambda e: e.activation(out=r_[:, 0:nb], in_=pb[:, 0:nb], func=AF.Relu), reads=[pbk], writes=[rk_])
                    eng = "pool" if f % 2 == 0 else "dve"
                    C.op(eng, lambda e: e.tensor_tensor(out=hidT[:, f, 0:nb], in0=r_[:, 0:nb], in1=r_[:, 0:nb], op=ALU.mult),
                         reads=[rk_], writes=[hidk])
                for ti in range(ntl):
                    r0 = rb0 + ti * TS
                    for half in range(2):
                        pb, pbk = bank("pj")
                        for f in range(32):
                            C.op("pe", lambda e: e.matmul(pb[:TS, :], lhsT=hidT[:, f, ti * TS:(ti + 1) * TS],
                                                          rhs=WA_dn[:, f, half * 512:(half + 1) * 512], start=(f == 0), stop=(f == 31)),
                                 reads=[hidk, WAK], writes=[pbk])
                        C.op("dve", lambda e: e.scalar_tensor_tensor(out=x1[:TS, ti, half * 512:(half + 1) * 512],
                                                                     in0=x1[:TS, ti, half * 512:(half + 1) * 512], scalar=float(ALPHA),
                                                                     in1=pb[:TS, :], op0=ALU.mult, op1=ALU.add),
                             reads=[pbk, x1k[ti]], writes=[x1k[ti]])
                    layer_norm(x1[:TS, ti, :], TS, g2, b2, pk2, x1k[ti])
                    if l < DEPTH - 1:
                        C.dma("pool", xcs[r0:r0 + TS, :], x1[:TS, ti, :], x1k[ti], reads=[x1k[ti]], writes=[dtk("xcs", r0)])
                    else:
                        if r0 < Lp:
                            C.dma("pool", yp[r0:r0 + TS, :], x1[:TS, ti, :], x1k[ti], reads=[x1k[ti]])
                        else:
                            C.dma("pool", ys[r0 - Lp:r0 - Lp + TS, :], x1[:TS, ti, :], x1k[ti], reads=[x1k[ti]])
    C.finish()
    return nc


def make_consts(Lp, Ls, PAST):
    c = {}
    c["c_ident"] = np.eye(128, dtype=np.float32)
    half = 32
    inv_freq = (10000.0 ** (-np.arange(half, dtype=np.float32) / half)).astype(np.float32)

    def rot(pos):
        ang = pos.astype(np.float32)[:, None] * inv_freq[None, :]
        cos = np.cos(ang).astype(np.float32)
        sin = np.sin(ang).astype(np.float32)
        return np.concatenate([cos, cos, -sin, sin], axis=1).astype(np.float32)

    c["c_rotp"] = rot(np.arange(Lp))
    c["c_rots"] = rot(PAST + np.arange(Ls))
    k = np.arange(128)[:, None]
    qq = np.arange(256)[None, :]
    m0 = ((0 + k // 64) <= (qq // 64)).astype(np.float32)
    m1 = ((2 + k // 64) <= (qq // 64)).astype(np.float32)
    c["c_amask"] = np.concatenate([m0, m1], axis=1)
    j = np.arange(64)[:, None]
    i = np.arange(64)[None, :]
    c["c_triu"] = (j <= i).astype(np.float32)
    c["c_striu"] = (j < i).astype(np.float32)
    lg = np.log(1.0 - 2.0 ** (-5.0 - np.arange(4, dtype=np.float64)))
    rm = np.zeros((64, 4, 64), np.float64)
    for h in range(4):
        rm[:, h, :] = np.exp(-lg[h] * (j + 1.0)) * (j <= i)
    c["c_rm"] = rm.reshape(64, 256).astype(np.float32)
    xi = np.zeros((64, 4, 128), np.float64)
    xis = np.zeros((64, 4, Ls), np.float64)
    for h in range(4):
        xi[:, h, :] = np.exp(lg[h] * ((np.arange(128) % 64) + 1.0))[None, :]
        xis[:, h, :] = np.exp(lg[h] * (np.arange(Ls) + 1.0))[None, :]
    c["c_xi"] = xi.reshape(64, 512).astype(np.float32)
    c["c_xis"] = xis.reshape(64, 4 * Ls).astype(np.float32)
    zeta = np.zeros((128, 4), np.float64)
    zetas = np.zeros((Ls, 4), np.float64)
    for h in range(4):
        zeta[:, h] = np.exp(lg[h] * (63.0 - (np.arange(128) % 64)))
        zetas[:, h] = np.exp(lg[h] * (Ls - 1.0 - np.arange(Ls)))
    c["c_zeta"] = zeta.astype(np.float32)
    c["c_zetas"] = zetas.astype(np.float32)
    bo = np.zeros((128, 128), np.float32)
    bo[:64, :64] = 1.0
    bo[64:, 64:] = 1.0
    c["c_bones"] = bo
    return c


_NC_CACHE = {}


def run(inputs, Lp, NS, Ls, PAST, ncores):
    key = (Lp, NS, Ls, PAST)
    if key not in _NC_CACHE:
        _NC_CACHE[key] = build(Lp, NS, Ls, PAST)
    nc = _NC_CACHE[key]
    f = lambda a: np.ascontiguousarray(np.asarray(a, dtype=np.float32))
    consts = make_consts(Lp, Ls, PAST)
    I = {k: f(v) for k, v in inputs.items()}
    in_maps = []
    for i in range(ncores):
        sl = slice(i * NS, (i + 1) * NS)
        m = dict(consts)
        m["xp"] = f(I["x_prompt"][i])
        m["xs"] = f(I["x_sample"][sl].reshape(NS * Ls, D))
        m["ck"] = f(I["cache_k"][:, sl].reshape(DEPTH, NS, PAST, 512))
        m["cv"] = f(I["cache_v"][:, sl].reshape(DEPTH, NS, PAST, 512))
        m["sdl"] = f(I["state_delta"][:, sl].reshape(DEPTH, NS, 256, 64))
        m["scv"] = f(I["state_conv"][:, sl])
        m["srt"] = f(I["state_ret"][:, sl].reshape(DEPTH, NS, 256, 64))
        m["ln0g"] = f(I["ln0_g"].reshape(1, D)); m["ln0b"] = f(I["ln0_b"].reshape(1, D))
        m["w_in"] = I["w_in"]
        m["lamq1"] = I["lam_q1"]; m["lamk1"] = I["lam_k1"]; m["lamq2"] = I["lam_q2"]; m["lamk2"] = I["lam_k2"]
        m["dng"] = I["diff_norm_g"]; m["convw"] = I["conv_w"]; m["alog"] = I["a_log"]; m["dtb"] = I["dt_bias"]
        m["dlng"] = I["delta_norm_g"]; m["w_out"] = I["w_out"]
        m["ln1g"] = I["ln1_g"]; m["ln1b"] = I["ln1_b"]; m["w_up"] = I["w_up"]; m["w_down"] = I["w_down"]
        m["ln2g"] = I["ln2_g"]; m["ln2b"] = I["ln2_b"]
        in_maps.append(m)
    res = run_bass_kernel_spmd(nc, in_maps, core_ids=list(range(ncores)))
    R = res.results
    B = ncores
    st = lambda name: np.stack([np.asarray(R[i][name]) for i in range(B)])
    y_p = st("yp")
    y_s = st("ys").reshape(B * NS, Ls, D)
    p_k = st("pk").transpose(1, 0, 2, 3).reshape(DEPTH, B, Lp, 8, 64)
    p_v = st("pv").transpose(1, 0, 2, 3).reshape(DEPTH, B, Lp, 4, 128)
    p_d = st("pdl").transpose(1, 0, 2, 3).reshape(DEPTH, B, 4, 64, 64)
    p_c = st("pcv").transpose(1, 0, 2, 3)
    p_r = st("prt").transpose(1, 0, 2, 3).reshape(DEPTH, B, 4, 64, 64)
    s_k = st("sk").transpose(1, 0, 2, 3).reshape(DEPTH, B * NS, Ls, 8, 64)
    s_v = st("sv").transpose(1, 0, 2, 3).reshape(DEPTH, B * NS, Ls, 4, 128)
    s_d = st("sdlo").transpose(1, 0, 2, 3, 4).reshape(DEPTH, B * NS, 4, 64, 64)
    s_c = st("scvo").transpose(1, 0, 2, 3, 4).reshape(DEPTH, B * NS, 3, 768)
    s_r = st("srto").transpose(1, 0, 2, 3, 4).reshape(DEPTH, B * NS, 4, 64, 64)
    outs = (y_p, y_s, p_k, p_v, p_d, p_c, p_r, s_k, s_v, s_d, s_c, s_r)
    return tuple(np.ascontiguousarray(o.astype(np.float32)) for o in outs)


def kernel(**inputs):
    return run(inputs, 4096, 2, 32, 2048, NCORES)
```

```python
import math
import numpy as np
import ml_dtypes
import concourse.bass as bass
import concourse.mybir as mybir
from concourse.bass_utils import run_bass_kernel_spmd

F32 = mybir.dt.float32
BF16 = mybir.dt.bfloat16
AF = mybir.ActivationFunctionType
ALU = mybir.AluOpType
AX = mybir.AxisListType

D = 1024
NIN = 3592
DFF = 4096
DEPTH = 2
ALPHA = (2 * DEPTH) ** 0.25
LN_EPS = 1e-5
NORM_EPS = 1e-6
NCORES = 8


class Tk:
    __slots__ = ("w", "r", "dsem", "dcnt", "name")

    def __init__(self, name=""):
        self.w = None
        self.r = {}
        self.dsem = None
        self.dcnt = 0
        self.name = name


class Ctx:
    def __init__(self, nc):
        self.nc = nc
        self.E = {"pe": nc.tensor, "dve": nc.vector, "act": nc.scalar, "pool": nc.gpsimd,
                  "sp": nc.sync}
        self.sem = {k: nc.alloc_semaphore("es_" + k) for k in self.E}
        self.cnt = {k: 0 for k in self.E}
        self.seen = {k: {} for k in self.E}
        self.dsems = []
        self.nd = 0

    def _wait(self, e, dep):
        key, sem, val = dep
        if self.seen[e].get(key, 0) >= val:
            return
        self.E[e].wait_ge(sem, val)
        self.seen[e][key] = val

    def _deps(self, e, reads, writes):
        for t in reads:
            if t.w is not None:
                self._wait(e, t.w)
        for t in writes:
            if t.w is not None:
                self._wait(e, t.w)
            for d in t.r.values():
                self._wait(e, d)

    def _mark(self, me, reads, writes):
        for t in reads:
            old = t.r.get(me[0])
            if old is None or old[2] < me[2]:
                t.r[me[0]] = me
        for t in writes:
            t.w = me
            t.r = {}

    def op(self, e, fn, reads=(), writes=()):
        self._deps(e, reads, writes)
        ins = fn(self.E[e])
        self.cnt[e] += 1
        ins.then_inc(self.sem[e], 1)
        me = (e, self.sem[e], self.cnt[e])
        if e == "pe":
            self.seen[e][e] = self.cnt[e]
        self._mark(me, reads, writes)
        return ins

    def dma(self, q, out, in_, owner, reads=(), writes=(), **kw):
        self._deps(q, reads, writes)
        if owner.dsem is None:
            owner.dsem = self.nc.alloc_semaphore("ds%d" % self.nd)
            self.nd += 1
            self.dsems.append(owner)
        ins = self.E[q].dma_start(out=out, in_=in_, **kw)
        owner.dcnt += 16
        ins.then_inc(owner.dsem, 16)
        me = ("d%d" % id(owner), owner.dsem, owner.dcnt)
        self._mark(me, reads, writes)
        return ins

    def barrier(self):
        for e in self.E:
            for f in self.E:
                if f != e and self.cnt[f] > 0:
                    self._wait(e, (f, self.sem[f], self.cnt[f]))
            for o in self.dsems:
                if o.dcnt > 0:
                    self._wait(e, ("d%d" % id(o), o.dsem, o.dcnt))

    def finish(self):
        self.barrier()


def bc(ap, shape):
    return ap.to_broadcast(list(shape))


import os
from contextlib import ExitStack
STOP = int(os.environ.get('KSTOP', '99'))


def build(Lp=4096, NS=2, Ls=32, PAST=2048):
    nc = bass.Bass("TRN2", target_bir_lowering=False)
    C = Ctx(nc)
    Ltot = Lp + NS * Ls
    LK = max(Lp, PAST + Ls)

    def din(name, shape, dt=F32):
        return nc.dram_tensor(name, list(shape), dt, kind="ExternalInput").ap()

    def dout(name, shape):
        return nc.dram_tensor(name, list(shape), F32, kind="ExternalOutput").ap()

    xp = din("xp", [Lp, D])
    xs = din("xs", [NS * Ls, D])
    ck = din("ck", [DEPTH, NS, PAST, 512])
    cv = din("cv", [DEPTH, NS, PAST, 512])
    sdl = din("sdl", [DEPTH, NS, 256, 64])
    scv = din("scv", [DEPTH, NS, 3, 768])
    srt = din("srt", [DEPTH, NS, 256, 64])
    ln0g = din("ln0g", [1, D]); ln0b = din("ln0b", [1, D])
    w_in = din("w_in", [DEPTH, D, NIN])
    lamq1 = din("lamq1", [DEPTH, 64]); lamk1 = din("lamk1", [DEPTH, 64])
    lamq2 = din("lamq2", [DEPTH, 64]); lamk2 = din("lamk2", [DEPTH, 64])
    dng = din("dng", [DEPTH, 128])
    convw = din("convw", [DEPTH, 4, 768])
    alog = din("alog", [DEPTH, 4]); dtb = din("dtb", [DEPTH, 4])
    dlng = din("dlng", [DEPTH, 64])
    w_out = din("w_out", [DEPTH, D, D])
    ln1g = din("ln1g", [DEPTH, D]); ln1b = din("ln1b", [DEPTH, D])
    w_up = din("w_up", [DEPTH, D, DFF])
    w_down = din("w_down", [DEPTH, DFF, D])
    ln2g = din("ln2g", [DEPTH, D]); ln2b = din("ln2b", [DEPTH, D])
    c_ident = din("c_ident", [128, 128])
    c_rotp = din("c_rotp", [Lp, 128])
    c_rots = din("c_rots", [Ls, 128])
    c_amask = din("c_amask", [128, 512])
    c_triu = din("c_triu", [64, 64])
    c_striu = din("c_striu", [64, 64])
    c_rm = din("c_rm", [64, 256])
    c_xi = din("c_xi", [64, 512])
    c_xis = din("c_xis", [64, 4 * Ls])
    c_zeta = din("c_zeta", [128, 4])
    c_zetas = din("c_zetas", [Ls, 4])
    c_bones = din("c_bones", [128, 128])

    yp = dout("yp", [Lp, D]); ys = dout("ys", [NS * Ls, D])
    pk = dout("pk", [DEPTH, Lp, 512]); pv = dout("pv", [DEPTH, Lp, 512])
    pdl = dout("pdl", [DEPTH, 256, 64]); pcv = dout("pcv", [DEPTH, 3, 768])
    prt = dout("prt", [DEPTH, 256, 64])
    sk = dout("sk", [DEPTH, NS * Ls, 512]); sv = dout("sv", [DEPTH, NS * Ls, 512])
    sdlo = dout("sdlo", [DEPTH, NS, 256, 64]); scvo = dout("scvo", [DEPTH, NS, 3, 768])
    srto = dout("srto", [DEPTH, NS, 256, 64])
    x1s = nc.dram_tensor("x1s", [Ltot, D], F32).ap()
    xcs = nc.dram_tensor("xcs", [Ltot, D], F32).ap()
    vsc = nc.dram_tensor("vsc", [LK, 512], BF16).ap()
    dram_tk = {}

    def dtk(name, r0):
        k = (name, r0)
        if k not in dram_tk:
            dram_tk[k] = Tk()
        return dram_tk[k]

    PS = [nc.alloc_psum_tensor("ps%d" % i, [128, 512], F32) for i in range(8)]
    PSK = [Tk("ps%d" % i) for i in range(8)]
    rot_state = {"pj": 0, "mi": 0, "s": 0}

    def bank(kind):
        base, n = {"pj": (0, 2), "s": (2, 2), "mi": (6, 2)}[kind]
        i = base + rot_state[kind] % n
        rot_state[kind] += 1
        return PS[i], PSK[i]

    OB, OBK = PS[4], PSK[4]
    LB, LBK = PS[5], PSK[5]

    def sb(name, shape, dt=F32):
        return nc.alloc_sbuf_tensor(name, list(shape), dt)

    identf = sb("identf", [128, 128]); identb = sb("identb", [128, 128], BF16)
    onesb = sb("onesb", [128, 128], BF16)
    onesf = sb("onesf", [128, 128])
    bonesf = sb("bonesf", [128, 128])
    triu = sb("triu", [64, 64]); striu = sb("striu", [64, 64])
    rmc = sb("rmc", [64, 256]); xic = sb("xic", [64, 512]); xisc = sb("xisc", [64, 4 * Ls])
    zetac = sb("zetac", [128, 4]); zetasc = sb("zetasc", [Ls, 4])
    amask = sb("amask", [128, 512], BF16)
    epsb = sb("epsb", [128, 2])
    constk = Tk("const")
    q = "sp"
    C.dma(q, identf[:], c_ident[:, :], constk, writes=[constk])
    C.dma(q, bonesf[:], c_bones[:, :], constk, writes=[constk])
    C.dma(q, triu[:], c_triu[:, :], constk, writes=[constk])
    C.dma(q, striu[:], c_striu[:, :], constk, writes=[constk])
    C.dma(q, rmc[:], c_rm[:, :], constk, writes=[constk])
    C.dma(q, xic[:], c_xi[:, :], constk, writes=[constk])
    C.dma(q, xisc[:], c_xis[:, :], constk, writes=[constk])
    C.dma(q, zetac[:], c_zeta[:, :], constk, writes=[constk])
    C.dma(q, zetasc[:], c_zetas[:, :], constk, writes=[constk])
    C.dma("pool", amask[:], c_amask[:, :], constk, writes=[constk])
    C.op("dve", lambda e: e.tensor_copy(out=identb[:], in_=identf[:]), reads=[constk], writes=[constk])
    C.op("dve", lambda e: e.memset(onesb[:], 1.0), writes=[constk])
    C.op("dve", lambda e: e.memset(onesf[:], 1.0), writes=[constk])
    C.op("dve", lambda e: e.memset(epsb[:, 0:1], LN_EPS), writes=[constk])
    C.op("dve", lambda e: e.memset(epsb[:, 1:2], NORM_EPS), writes=[constk])

    WA = sb("WA", [128, 65536], BF16)
    WAK = Tk("WA")

    lnst = sb("lnst", [128, 2, 6]); lnmv = sb("lnmv", [128, 2]); lnk = Tk("ln")

    def layer_norm(xap, n, gt, bt, gk, xk):
        tks = [xk]
        C.op("dve", lambda e: e.bn_stats(out=lnst[:n, 0, :], in_=xap[:, 0:512]), reads=tks, writes=[lnk])
        C.op("dve", lambda e: e.bn_stats(out=lnst[:n, 1, :], in_=xap[:, 512:1024]), reads=tks, writes=[lnk])
        C.op("dve", lambda e: e.bn_aggr(out=lnmv[:n, :], in_=lnst[:n, :, :].rearrange("p a b -> p (a b)")),
             reads=[lnk], writes=[lnk])
        C.op("act", lambda e: e.activation(out=lnmv[:n, 1:2], in_=lnmv[:n, 1:2], func=AF.Ln, bias=epsb[:n, 0:1]),
             reads=[lnk, constk], writes=[lnk])
        C.op("act", lambda e: e.activation(out=lnmv[:n, 1:2], in_=lnmv[:n, 1:2], func=AF.Exp, scale=-0.5),
             reads=[lnk], writes=[lnk])
        C.op("dve", lambda e: e.tensor_scalar(out=xap, in0=xap, scalar1=lnmv[:n, 0:1], scalar2=lnmv[:n, 1:2],
                                              op0=ALU.subtract, op1=ALU.mult),
             reads=[lnk] + tks, writes=tks)
        C.op("pool", lambda e: e.tensor_tensor(out=xap, in0=xap, in1=gt[:n, :], op=ALU.mult),
             reads=tks + [gk], writes=tks)
        C.op("pool", lambda e: e.tensor_tensor(out=xap, in0=xap, in1=bt[:n, :], op=ALU.add),
             reads=tks + [gk], writes=tks)

    seqs = [dict(name="p", L=Lp, past=0, TS=128, c=64, row0=0, b=0)]
    for b_ in range(NS):
        seqs.append(dict(name="s%d" % b_, L=Ls, past=PAST, TS=Ls, c=Ls, row0=Lp + b_ * Ls, b=b_))

    with ExitStack() as st0:
        g0 = st0.enter_context(nc.sbuf_tensor("g0", [128, D], F32))
        b0 = st0.enter_context(nc.sbuf_tensor("b0", [128, D], F32))
        xl = st0.enter_context(nc.sbuf_tensor("xl", [128, 2, D], F32))
        xlk = [Tk(), Tk()]
        pk0 = Tk()
        C.dma("sp", g0[:], ln0g[0:1, :].partition_broadcast(128), pk0, writes=[pk0])
        C.dma("sp", b0[:], ln0b[0:1, :].partition_broadcast(128), pk0, writes=[pk0])
        tiles0 = [(xp, r, r, 128) for r in range(0, Lp, 128)]
        for b_ in range(NS):
            tiles0.append((xs, b_ * Ls, Lp + b_ * Ls, Ls))
        for i0, (src0, sr0, dr0, n0) in enumerate(tiles0):
            sl0 = i0 % 2
            C.dma("sp", xl[:n0, sl0, :], src0[sr0:sr0 + n0, :], xlk[sl0], writes=[xlk[sl0]])
            layer_norm(xl[:n0, sl0, :], n0, g0, b0, pk0, xlk[sl0])
            C.dma("pool", xcs[dr0:dr0 + n0, :], xl[:n0, sl0, :], xlk[sl0], reads=[xlk[sl0]], writes=[dtk("xcs", dr0)])
    if STOP == 0:
        C.finish()
        return nc

    for l in range(DEPTH):
        lam_init = 0.8 - 0.6 * math.exp(-0.3 * l)
        C.barrier()
        WA_in = WA[:, 0:8 * NIN].rearrange("p (k n) -> p k n", k=8)
        WA_out = WA[:, 8 * NIN:8 * NIN + 8192].rearrange("p (k n) -> p k n", k=8)
        for k in range(8):
            for c0_ in range(0, NIN, 512):
                c1_ = min(NIN, c0_ + 512)
                C.dma("pool", WA_in[:, k, c0_:c1_], w_in[l, k * 128:(k + 1) * 128, c0_:c1_], WAK, writes=[WAK])
        for k in range(8):
            for c0_ in range(0, D, 512):
                C.dma("pool", WA_out[:, k, c0_:c0_ + 512], w_out[l, k * 128:(k + 1) * 128, c0_:c0_ + 512], WAK, writes=[WAK])
        with ExitStack() as st:
            def T(name, shape, dt=F32):
                return st.enter_context(nc.sbuf_tensor("%s_%d" % (name, l), list(shape), dt))
            wa_off = [8 * NIN + 8192]

            def WT(shape):
                n = 1
                for d_ in shape[1:]:
                    n *= d_
                ap = WA[:, wa_off[0]:wa_off[0] + n]
                wa_off[0] += n
                assert wa_off[0] <= 65536, wa_off[0]
                if len(shape) == 3:
                    ap = ap.rearrange("p (a b) -> p a b", a=shape[1])
                return ap
            KT = WT([128, 4, LK])
            ktk = [Tk("kt%d" % i) for i in range((LK + 127) // 128)]
            g1 = T("g1", [128, D]); b1 = T("b1", [128, D])
            pk_ = Tk("params")
            C.dma("sp", g1[:], ln1g[l:l + 1, :].partition_broadcast(128), pk_, writes=[pk_])
            C.dma("sp", b1[:], ln1b[l:l + 1, :].partition_broadcast(128), pk_, writes=[pk_])
            lamt = T("lamt", [128, 4, 64]); lamr = T("lamr", [128, 4]); neglam = T("neglam", [128, 1])
            for i, src in enumerate((lamq1, lamk1, lamq2, lamk2)):
                C.dma("sp", lamt[:, i, :], src[l:l + 1, :].partition_broadcast(128), pk_, writes=[pk_])
            C.op("dve", lambda e: e.tensor_tensor(out=lamt[:, 0, :], in0=lamt[:, 0, :], in1=lamt[:, 1, :], op=ALU.mult),
                 reads=[pk_], writes=[pk_])
            C.op("dve", lambda e: e.tensor_tensor(out=lamt[:, 2, :], in0=lamt[:, 2, :], in1=lamt[:, 3, :], op=ALU.mult),
                 reads=[pk_], writes=[pk_])
            C.op("dve", lambda e: e.tensor_reduce(out=lamr[:, 0:1], in_=lamt[:, 0, :], axis=AX.X, op=ALU.add),
                 reads=[pk_], writes=[pk_])
            C.op("dve", lambda e: e.tensor_reduce(out=lamr[:, 1:2], in_=lamt[:, 2, :], axis=AX.X, op=ALU.add),
                 reads=[pk_], writes=[pk_])
            C.op("act", lambda e: e.activation(out=lamr[:, 2:4], in_=lamr[:, 0:2], func=AF.Exp),
                 reads=[pk_], writes=[pk_])
            C.op("dve", lambda e: e.scalar_tensor_tensor(out=neglam[:], in0=lamr[:, 3:4], scalar=-lam_init,
                                                         in1=lamr[:, 2:3], op0=ALU.add, op1=ALU.subtract),
                 reads=[pk_], writes=[pk_])
            dngc = T("dngc", [128, 1]); dlngc = T("dlngc", [64, 1])
            C.dma("sp", dngc[:], dng[l:l + 1, :].rearrange("o e -> e o"), pk_, writes=[pk_])
            C.dma("sp", dlngc[:], dlng[l:l + 1, :].rearrange("o e -> e o"), pk_, writes=[pk_])
            cw = T("cw", [128, 6, 4])
            for ch in range(6):
                C.dma("sp", cw[:, ch, :], convw[l, :, ch * 128:(ch + 1) * 128].rearrange("j p -> p j"),
                      pk_, writes=[pk_], allow_slow_non_contiguous=True)
            nA = T("nA", [128, 4]); dtbb = T("dtbb", [128, 4])
            C.dma("sp", nA[:], alog[l:l + 1, :].partition_broadcast(128), pk_, writes=[pk_])
            C.dma("sp", dtbb[:], dtb[l:l + 1, :].partition_broadcast(128), pk_, writes=[pk_])
            C.op("act", lambda e: e.activation(out=nA[:], in_=nA[:], func=AF.Exp), reads=[pk_], writes=[pk_])
            C.op("dve", lambda e: e.tensor_scalar(out=nA[:], in0=nA[:], scalar1=-1.0, scalar2=None, op0=ALU.mult),
                 reads=[pk_], writes=[pk_])
            if STOP == 6:
                C.finish()
                return nc

            xres = T("xres", [128, D]); xk = Tk("xres")
            xbf = WT([128, D]); xbfk = Tk()
            kbf = xbf[:, 0:512]; kbfk = xbfk
            xT = WT([128, 8, 256]); xTk = Tk()
            mixT = xT; mixk = xTk
            QTa = WT([128, 4, 256]); QTb = WT([128, 4, 256]); QTk = Tk()
            C.op("pool", lambda e: e.memset(QTa[64:128, :, :], 0.0), writes=[QTk])
            C.op("pool", lambda e: e.memset(QTb[0:64, :, :], 0.0), writes=[QTk])
            rot = T("rot", [128, 128]); rotk = Tk()
            tmpA = T("tmpA", [128, 512]); tmpB = T("tmpB", [128, 512]); tmpk = Tk()
            kout = T("kout", [128, 512]); koutk = Tk()
            vout = T("vout", [128, 512]); voutk = Tk()
            vbf = WT([128, 512]); vbfk = Tk()
            uT = T("uT", [128, 6, 131]); uTk = Tk()
            cT = T("cT", [128, 6, 128]); cTk = Tk()
            cE = T("cE", [128, 6, 128]); cEk = Tk()
            nTh = WT([128, 8, 128]); nTk = Tk()
            qdT = WT([128, 4, 128]); qdk = Tk()
            g4 = T("g4", [128, 264]); g4e = T("g4e", [128, 264]); g4k = Tk()
            sgT = T("sgT", [64, 8, 128]); sgTk = Tk()
            qkc = T("qkc", [128, 512]); qkck = Tk()
            qkcT = WT([128, 8, 128]); qkcTk = Tk()
            sg6 = g4[:, 0:256]; sg6e = g4e[:, 0:256]; sg6k = g4k
            ktok = T("ktok", [64, 2, 512]); ktokk = Tk()
            kz = T("kz", [64, 2, 256], BF16); kzk = Tk()
            vcb = T("vcb", [64, 2, 256], BF16); kzc = T("kzc", [64, 2, 256], BF16); vck = Tk()
            gb = T("gb", [64, 2, 8]); gbk = Tk()
            sm = T("sm", [128, 64]); smk = Tk()
            GT = T("GT", [64, 4, 64]); GTk = Tk()
            Drow = T("Drow", [128, 4, 64]); EDrow = T("EDrow", [128, 4, 64]); Drk = Tk()
            LM = T("LM", [64, 4, 64]); LMs = T("LMs", [64, 4, 64]); LMi = T("LMi", [64, 4, 64]); LMk = Tk()
            Xf = T("Xf", [64, 4, 64]); Xk = Tk()
            Yb = T("Yb", [64, 2, 4, 64], BF16); YTb = T("YTb", [64, 2, 4, 64], BF16); Yk = Tk()
            Pf = T("Pf", [64, 4, 64]); Pb = T("Pb", [64, 4, 64], BF16); Pk = Tk()
            inT = T("inT", [64, 4, 64], BF16); inTk = Tk()
            inR = T("inR", [64, 4, 64], BF16); inRk = Tk()
            rt = T("rt", [64, 256]); rb = T("rb", [64, 256], BF16); rk = Tk()
            scr = T("scr", [64, 4, 64]); scrk = Tk()
            vnb = T("vnb", [64, 256], BF16); vnk = Tk()
            Sd = T("Sd", [64, 4, 64]); Sdb = T("Sdb", [64, 4, 64], BF16); Sdk = Tk()
            Sr = T("Sr", [64, 4, 64]); Srb = T("Srb", [64, 4, 64], BF16); Srk = Tk()
            ot = T("ot", [64, 4, 128]); ot2 = T("ot2", [64, 4, 128]); otb = T("otb", [64, 4, 128], BF16); otk = Tk()
            PT = [WT([128, 2, 256]), WT([128, 2, 256])]; PTk = [Tk(), Tk()]
            Vt = [WT([128, 128]), WT([128, 128]), WT([128, 128])]; Vtk = [Tk(), Tk(), Tk()]
            ckb = WT([128, 512]); ckbk = Tk()
            Rr = tmpA[:, :].rearrange("p (j q) -> p j q", j=2); Tt = tmpB[:, :].rearrange("p (j q) -> p j q", j=2)
            oaf = T("oaf", [128, 256])
            oab = T("oab", [128, 256], BF16); atk = tmpk
            pcst = cE[:3, :, :].rearrange("p a b -> p (a b)"); pcstk = cEk

            def proj(TS, col0, c0, n, kind="pj"):
                pb, pbk = bank(kind)
                for k in range(8):
                    C.op("pe", lambda e: e.matmul(pb[:TS, 0:n], lhsT=xT[:, k, col0:col0 + TS],
                                                  rhs=WA_in[:, k, c0:c0 + n], start=(k == 0), stop=(k == 7)),
                         reads=[xTk, WAK], writes=[pbk])
                return pb, pbk

            def rotary(pb, pbk, TS, outs):
                pv_ = pb[:TS, :].rearrange("p (h d) -> p h d", h=8)
                C.op("dve", lambda e: e.tensor_tensor(out=tmpA[:TS, :].rearrange("p (h d) -> p h d", h=8), in0=pv_,
                                                      in1=bc(rot[:TS, 0:64].unsqueeze(1), [TS, 8, 64]), op=ALU.mult),
                     reads=[pbk, rotk], writes=[tmpk])
                tb = tmpB[:TS, :].rearrange("p (h d) -> p h d", h=8)
                C.op("dve", lambda e: e.tensor_tensor(out=tb[:, :, 0:32], in0=pv_[:, :, 32:64],
                                                      in1=bc(rot[:TS, 64:96].unsqueeze(1), [TS, 8, 32]), op=ALU.mult),
                     reads=[pbk, rotk], writes=[tmpk])
                C.op("dve", lambda e: e.tensor_tensor(out=tb[:, :, 32:64], in0=pv_[:, :, 0:32],
                                                      in1=bc(rot[:TS, 96:128].unsqueeze(1), [TS, 8, 32]), op=ALU.mult),
                     reads=[pbk, rotk], writes=[tmpk])
                for (eng, oap, otk_) in outs:
                    C.op(eng, lambda e: e.tensor_tensor(out=oap, in0=tmpA[:TS, :], in1=tmpB[:TS, :], op=ALU.add),
                         reads=[tmpk], writes=[otk_])

            def transposes_bf(src, srck, TS, n):
                pb, pbk = bank("mi")
                pbb = pb[:].bitcast(BF16)
                for i in range(n):
                    C.op("pe", lambda e: e.transpose(out=pbb[:, i * TS:(i + 1) * TS],
                                                     in_=src[:TS, i * 128:(i + 1) * 128],
                                                     identity=identb[:TS, :TS]),
                         reads=[srck, constk], writes=[pbk])
                return pbb[:, 0:n * TS].rearrange("p (k t) -> p k t", k=n), pbk

            def silu_from(pb_ap, xs_, es_, k_, pbk):
                C.op("act", lambda e: e.activation(out=es_, in_=pb_ap, func=AF.Exp, scale=-1.0), reads=[pbk], writes=[k_])
                C.op("act", lambda e: e.activation(out=xs_, in_=pb_ap, func=AF.Copy), reads=[pbk], writes=[k_])
                C.op("dve", lambda e: e.tensor_scalar(out=es_, in0=es_, scalar1=1.0, scalar2=None, op0=ALU.add),
                     reads=[k_], writes=[k_])
                C.op("dve", lambda e: e.reciprocal(out=es_, in_=es_), reads=[k_], writes=[k_])

            for sq_ in seqs:
                TS, c, L, past, row0, sb_ = sq_["TS"], sq_["c"], sq_["L"], sq_["past"], sq_["row0"], sq_["b"]
                isP = sq_["name"] == "p"
                CPT = TS // c
                TPB = 2 if isP else 1
                nblk = L // (TS * TPB)
                nq = TS * TPB
                nit = 5 if c == 64 else 4
                rsrc = c_rotp if isP else c_rots
                kdst = pk if isP else sk
                vdst = pv if isP else sv
                orow0 = 0 if isP else sb_ * Ls
                zt = zetac if isP else zetasc
                xit = xic if isP else xisc
                gch = [math.exp(math.log(1.0 - 2.0 ** (-5.0 - h)) * c) for h in range(4)]
                if isP:
                    C.op("dve", lambda e: e.memset(Sd[:], 0.0), writes=[Sdk])
                    C.op("dve", lambda e: e.memset(Sr[:], 0.0), writes=[Srk])
                    C.op("dve", lambda e: e.memset(uT[:, :, 0:3], 0.0), writes=[uTk])
                else:
                    C.dma("sp", Sd[:], sdl[l, sb_].rearrange("(h d) e -> d h e", d=64), Sdk, writes=[Sdk])
                    C.dma("sp", Sr[:], srt[l, sb_].rearrange("(h d) e -> d h e", d=64), Srk, writes=[Srk])
                    for ch in range(6):
                        C.dma("sp", uT[:, ch, 0:3], scv[l, sb_, :, ch * 128:(ch + 1) * 128].rearrange("j p -> p j"),
                              uTk, writes=[uTk], allow_slow_non_contiguous=True)
                    for kt in range(past // 128):
                        C.dma("pool", ckb[:], ck[l, sb_, kt * 128:(kt + 1) * 128, :], ckbk, writes=[ckbk])
                        pv4, pv4k = transposes_bf(ckb, ckbk, 128, 4)
                        C.op("dve", lambda e: e.tensor_copy(out=KT[:, :, kt * 128:(kt + 1) * 128], in_=pv4),
                             reads=[pv4k], writes=[ktk[kt]])
                C.op("pool", lambda e: e.tensor_copy(out=Sdb[:], in_=Sd[:]), reads=[Sdk], writes=[Sdk])
                C.op("pool", lambda e: e.tensor_copy(out=Srb[:], in_=Sr[:]), reads=[Srk], writes=[Srk])

                def gen_frontA(ti, t, pjk):
                    r0 = t * TS
                    col0 = ti * TS
                    kcol = past + r0
                    kt_own = kcol // 128
                    xa = xres[:TS, :]
                    C.dma("sp", xa, xcs[row0 + r0:row0 + r0 + TS, :], xk,
                          reads=[dtk("xcs", row0 + r0)], writes=[xk])
                    C.dma("sp", rot[:TS, :], rsrc[r0:r0 + TS, :], rotk, writes=[rotk])
                    C.op("act", lambda e: e.activation(out=xbf[:TS, :], in_=xa, func=AF.Copy), reads=[xk], writes=[xbfk])
                    pv8, pv8k = transposes_bf(xbf, xbfk, TS, 8)
                    C.op("dve", lambda e: e.tensor_copy(out=xT[:, :, col0:col0 + TS], in_=pv8), reads=[pv8k], writes=[xTk])
                    yield
                    pb, pbk = proj(TS, col0, 0, 512, pjk)
                    rotary(pb, pbk, TS, [("pool", kbf[:TS, :], kbfk)])
                    pv4, pv4k = transposes_bf(kbf, kbfk, TS, 4)
                    C.op("dve", lambda e: e.tensor_copy(out=QTa[0:64, :, col0:col0 + TS], in_=pv4[0:64, :, :]),
                         reads=[pv4k], writes=[QTk])
                    C.op("dve", lambda e: e.tensor_copy(out=QTb[64:128, :, col0:col0 + TS], in_=pv4[64:128, :, :]),
                         reads=[pv4k], writes=[QTk])
                    yield
                    pb, pbk = proj(TS, col0, 512, 512, pjk)
                    rotary(pb, pbk, TS, [("pool", kout[:TS, :], koutk), ("pool", kbf[:TS, :], kbfk)])
                    C.dma("pool", kdst[l, orow0 + r0:orow0 + r0 + TS, :], kout[:TS, :], koutk, reads=[koutk])
                    pv4, pv4k = transposes_bf(kbf, kbfk, TS, 4)
                    C.op("dve", lambda e: e.tensor_copy(out=KT[:, :, kcol:kcol + TS], in_=pv4),
                         reads=[pv4k], writes=[ktk[kt_own]])
                    yield
                    pb, pbk = proj(TS, col0, 1024, 512, pjk)
                    C.op("act", lambda e: e.activation(out=vout[:TS, :], in_=pb[:TS, :], func=AF.Copy), reads=[pbk], writes=[voutk])
                    C.op("pool", lambda e: e.tensor_copy(out=vbf[:TS, :], in_=vout[:TS, :]), reads=[voutk], writes=[vbfk])
                    C.dma("pool", vdst[l, orow0 + r0:orow0 + r0 + TS, :], vout[:TS, :], voutk, reads=[voutk])
                    C.dma("pool", vsc[kcol:kcol + TS, :], vbf[:TS, :], vbfk, reads=[vbfk], writes=[dtk("vsc", kt_own)])
                    yield

                def frontB(t, col0):
                    pb, pbk = proj(TS, col0, 2304, 264)
                    silu_from(pb[:TS, 0:260], g4[:TS, 0:260], g4e[:TS, 0:260], g4k, pbk)
                    C.op("dve", lambda e: e.tensor_tensor(out=g4[:TS, 0:256], in0=g4[:TS, 0:256], in1=g4e[:TS, 0:256],
                                                          op=ALU.mult), reads=[g4k], writes=[g4k])
                    C.op("dve", lambda e: e.tensor_tensor(out=g4[:TS, 260:264], in0=pb[:TS, 260:264], in1=dtbb[:TS, :],
                                                          op=ALU.add), reads=[pbk, pk_], writes=[g4k])
                    C.op("act", lambda e: e.activation(out=g4[:TS, 260:264], in_=g4[:TS, 260:264], func=AF.Exp),
                         reads=[g4k], writes=[g4k])
                    C.op("act", lambda e: e.activation(out=g4[:TS, 260:264], in_=g4[:TS, 260:264], func=AF.Ln, bias=1.0),
                         reads=[g4k], writes=[g4k])
                    C.op("dve", lambda e: e.tensor_tensor(out=g4[:TS, 260:264], in0=g4[:TS, 260:264], in1=nA[:TS, :],
                                                          op=ALU.mult), reads=[g4k, pk_], writes=[g4k])
                    for ck_ in range(CPT):
                        C.op("dve", lambda e: e.tensor_copy(out=gb[:c, ck_, 0:4], in_=g4[ck_ * c:(ck_ + 1) * c, 260:264]),
                             reads=[g4k], writes=[gbk])
                        C.op("dve", lambda e: e.tensor_copy(out=gb[:c, ck_, 4:8], in_=g4e[ck_ * c:(ck_ + 1) * c, 256:260]),
                             reads=[g4k], writes=[gbk])
                    pm, pmk = bank("mi")
                    for h in range(4):
                        C.op("pe", lambda e: e.transpose(out=pm[:64, h * TS:(h + 1) * TS], in_=g4[:TS, h * 64:(h + 1) * 64],
                                                         identity=identf[:TS, :TS]),
                             reads=[g4k, constk], writes=[pmk])
                    C.op("act", lambda e: e.activation(out=sgT[:, 0:4, :TS],
                                                       in_=pm[:64, 0:4 * TS].rearrange("p (h t) -> p h t", h=4), func=AF.Copy),
                         reads=[pmk], writes=[sgTk])
                    pb, pbk = proj(TS, col0, 2568, 512)
                    rotary(pb, pbk, TS, [("pool", qkc[:TS, :], qkck)])
                    C.op("dve", lambda e: e.tensor_scalar(out=qkc[:TS, 256:512], in0=qkc[:TS, 256:512], scalar1=0.125,
                                                          scalar2=None, op0=ALU.mult), reads=[qkck], writes=[qkck])
                    for half in range(2):
                        pm, pmk = bank("mi")
                        for h in range(4):
                            C.op("pe", lambda e: e.transpose(
                                out=pm[:64, h * TS:(h + 1) * TS],
                                in_=qkc[:TS, half * 256 + h * 64:half * 256 + (h + 1) * 64],
                                identity=identf[:TS, :TS]), reads=[qkck, constk], writes=[pmk])
                        C.op("act", lambda e: e.activation(
                            out=qkcT[0:64, half * 4:half * 4 + 4, :TS],
                            in_=pm[:64, 0:4 * TS].rearrange("p (h t) -> p h t", h=4), func=AF.Copy), reads=[pmk], writes=[qkcTk])
                    for ck_ in range(CPT):
                        C.op("dve", lambda e: e.tensor_tensor(
                            out=kzc[:c, ck_, :].rearrange("p (h d) -> p h d", h=4),
                            in0=qkc[ck_ * c:(ck_ + 1) * c, 256:512].rearrange("p (h d) -> p h d", h=4),
                            in1=bc(zt[ck_ * c:(ck_ + 1) * c, :].unsqueeze(2), [c, 4, 64]), op=ALU.mult),
                             reads=[qkck, constk], writes=[vck])
                    pb, pbk = proj(TS, col0, 3080, 512)
                    for ck_ in range(CPT):
                        C.op("act", lambda e: e.activation(out=vcb[:c, ck_, :], in_=pb[ck_ * c:(ck_ + 1) * c, 0:256], func=AF.Copy),
                             reads=[pbk], writes=[vck])
                    silu_from(pb[:TS, 256:512], sg6[:TS, :], sg6e[:TS, :], sg6k, pbk)
                    C.op("dve", lambda e: e.tensor_tensor(out=sg6[:TS, :], in0=sg6[:TS, :], in1=sg6e[:TS, :], op=ALU.mult),
                         reads=[sg6k], writes=[sg6k])
                    pm, pmk = bank("mi")
                    for h in range(4):
                        C.op("pe", lambda e: e.transpose(out=pm[:64, h * TS:(h + 1) * TS], in_=sg6[:TS, h * 64:(h + 1) * 64],
                                                         identity=identf[:TS, :TS]),
                             reads=[sg6k, constk], writes=[pmk])
                    C.op("act", lambda e: e.activation(out=sgT[:, 4:8, :TS],
                                                       in_=pm[:64, 0:4 * TS].rearrange("p (h t) -> p h t", h=4), func=AF.Copy),
                         reads=[pmk], writes=[sgTk])
                    for rnd in range(2):
                        pb, pbk = bank("pj")
                        for j in range(3):
                            ch = rnd * 3 + j
                            for k in range(8):
                                C.op("pe", lambda e: e.matmul(
                                    pb[:, j * TS:(j + 1) * TS],
                                    lhsT=WA_in[:, k, 1536 + ch * 128:1536 + (ch + 1) * 128],
                                    rhs=xT[:, k, col0:col0 + TS], start=(k == 0), stop=(k == 7)),
                                     reads=[xTk, WAK], writes=[pbk])
                        C.op("act", lambda e: e.activation(
                            out=uT[:, rnd * 3:rnd * 3 + 3, 3:3 + TS],
                            in_=pb[:, 0:3 * TS].rearrange("p (j t) -> p j t", j=3), func=AF.Copy), reads=[pbk], writes=[uTk])
                    for ch in range(6):
                        C.op("dve", lambda e: e.tensor_scalar(out=cT[:, ch, :TS], in0=uT[:, ch, 0:TS],
                                                              scalar1=cw[:, ch, 0:1], scalar2=None, op0=ALU.mult),
                             reads=[uTk, pk_], writes=[cTk])
                        for j in range(1, 4):
                            C.op("dve", lambda e: e.scalar_tensor_tensor(
                                out=cT[:, ch, :TS], in0=uT[:, ch, j:j + TS], scalar=cw[:, ch, j:j + 1],
                                in1=cT[:, ch, :TS], op0=ALU.mult, op1=ALU.add), reads=[uTk, pk_, cTk], writes=[cTk])
                    if t == L // TS - 1:
                        for (c0_, c1_) in ((0, 4), (4, 6)):
                            pm, pmk = bank("mi")
                            for ch in range(c0_, c1_):
                                C.op("pe", lambda e: e.transpose(out=pm[:3, (ch - c0_) * 128:(ch - c0_ + 1) * 128],
                                                                 in_=uT[:, ch, TS:TS + 3], identity=identf[:, :]),
                                     reads=[uTk, constk], writes=[pmk])
                            C.op("act", lambda e: e.activation(out=pcst[:, c0_ * 128:c1_ * 128], in_=pm[:3, 0:(c1_ - c0_) * 128],
                                                               func=AF.Copy), reads=[pmk], writes=[pcstk])
                        cdst = pcv[l] if isP else scvo[l, sb_]
                        C.dma("pool", cdst, pcst[:, :], pcstk, reads=[pcstk])
                    C.op("pool", lambda e: e.tensor_copy(out=uT[:, :, 0:3], in_=uT[:, :, TS:TS + 3]),
                         reads=[uTk], writes=[uTk])
                    C.op("act", lambda e: e.activation(out=cE[:, :, :TS], in_=cT[:, :, :TS], func=AF.Exp, scale=-1.0),
                         reads=[cTk], writes=[cEk])
                    C.op("dve", lambda e: e.tensor_scalar(out=cE[:, :, :TS], in0=cE[:, :, :TS], scalar1=1.0, scalar2=None,
                                                          op0=ALU.add), reads=[cEk], writes=[cEk])
                    C.op("dve", lambda e: e.reciprocal(out=cE[:, :, :TS], in_=cE[:, :, :TS]), reads=[cEk], writes=[cEk])
                    C.op("dve", lambda e: e.tensor_tensor(out=cT[:, :, :TS], in0=cT[:, :, :TS], in1=cE[:, :, :TS], op=ALU.mult),
                         reads=[cEk, cTk], writes=[cTk])
                    C.op("pool", lambda e: e.tensor_tensor(out=cE[:, 0:4, :TS], in0=cT[:, 0:4, :TS], in1=cT[:, 0:4, :TS],
                                                           op=ALU.mult), reads=[cTk, cEk], writes=[cEk])
                    pm, pmk = bank("mi")
                    for j in range(4):
                        C.op("pe", lambda e: e.matmul(pm[:, j * TS:(j + 1) * TS], lhsT=bonesf[:, :], rhs=cE[:, j, :TS],
                                                      start=True, stop=True), reads=[cEk, constk], writes=[pmk])
                    C.op("act", lambda e: e.activation(out=cE[:, 0:4, :TS],
                                                       in_=pm[:, 0:4 * TS].rearrange("p (j t) -> p j t", j=4),
                                                       func=AF.Ln, bias=epsb[:, 1:2]),
                         reads=[pmk, constk], writes=[cEk])
                    C.op("act", lambda e: e.activation(out=cE[:, 0:4, :TS], in_=cE[:, 0:4, :TS], func=AF.Exp, scale=-0.5),
                         reads=[cEk], writes=[cEk])
                    C.op("dve", lambda e: e.scalar_tensor_tensor(out=cT[:, 0:2, :TS], in0=cT[:, 0:2, :TS], scalar=0.125,
                                                                 in1=cE[:, 0:2, :TS], op0=ALU.mult, op1=ALU.mult),
                         reads=[cTk, cEk], writes=[cTk])
                    C.op("dve", lambda e: e.tensor_tensor(out=cT[:, 2:4, :TS], in0=cT[:, 2:4, :TS], in1=cE[:, 2:4, :TS],
                                                          op=ALU.mult), reads=[cTk, cEk], writes=[cTk])
                    for j in range(4):
                        for s_ in range(2):
                            hidx = (j // 2) * 4 + (j % 2) * 2 + s_
                            C.op("pool", lambda e: e.tensor_copy(out=nTh[0:64, hidx, :TS], in_=cT[s_ * 64:(s_ + 1) * 64, j, :TS]),
                                 reads=[cTk], writes=[nTk])
                    for ck_ in range(CPT):
                        pm, pmk = bank("mi")
                        for j in range(4):
                            C.op("pe", lambda e: e.transpose(
                                out=pm[:c, j * 128:(j + 1) * 128], in_=cT[:, 2 + j, ck_ * c:(ck_ + 1) * c],
                                identity=identf[:, :]), reads=[cTk, constk], writes=[pmk])
                        C.op("act", lambda e: e.activation(out=ktok[:c, ck_, :], in_=pm[:c, :], func=AF.Copy),
                             reads=[pmk], writes=[ktokk])


                def gen_chunks(col0):
                    po, pok = bank("pj")
                    po2, po2k = bank("pj")
                    for ck_ in range(CPT):
                        cs = slice(ck_ * c, (ck_ + 1) * c)
                        pm, pmk = bank("mi")
                        C.op("pe", lambda e: e.matmul(pm[:c, 0:4], lhsT=triu[:c, :c], rhs=gb[:c, ck_, 0:4],
                                                      start=True, stop=True), reads=[gbk, constk], writes=[pmk])
                        C.op("dve", lambda e: e.tensor_tensor(out=GT[:c, :, :c],
                                                              in0=bc(triu[:c, :c].unsqueeze(1), [c, 4, c]),
                                                              in1=bc(gb[:c, ck_, 0:4].unsqueeze(2), [c, 4, c]),
                                                              op=ALU.mult), reads=[gbk, constk], writes=[GTk])
                        pm2, pm2k = bank("mi")
                        for h in range(4):
                            C.op("pe", lambda e: e.matmul(pm2[:, h * c:(h + 1) * c], lhsT=onesf[:c, :],
                                                          rhs=GT[:c, h, :c], start=True, stop=True),
                                 reads=[GTk, constk], writes=[pm2k])
                        C.op("act", lambda e: e.activation(out=sm[:c, 0:4], in_=pm[:c, 0:4], func=AF.Copy), reads=[pmk], writes=[smk])
                        C.op("act", lambda e: e.activation(out=sm[:c, 4:8], in_=pm[:c, 0:4], func=AF.Exp), reads=[pmk], writes=[smk])
                        C.op("act", lambda e: e.activation(out=Drow[:, :, :c], in_=pm2[:, 0:4 * c].rearrange("p (h i) -> p h i", h=4),
                                                           func=AF.Copy), reads=[pm2k], writes=[Drk])
                        C.op("act", lambda e: e.activation(out=EDrow[:, :, :c],
                                                           in_=pm2[:, 0:4 * c].rearrange("p (h i) -> p h i", h=4), func=AF.Exp),
                             reads=[pm2k], writes=[Drk])
                        C.op("dve", lambda e: e.tensor_tensor(out=sm[:c, 12:16], in0=Drow[:c, :, c - 1], in1=sm[:c, 0:4],
                                                              op=ALU.subtract), reads=[Drk, smk], writes=[smk])
                        C.op("act", lambda e: e.activation(out=sm[:c, 8:12], in_=sm[:c, 12:16], func=AF.Exp), reads=[smk], writes=[smk])
                        yield
                        C.op("dve", lambda e: e.tensor_tensor(out=LM[:c, :, :c], in0=Drow[:c, :, :c],
                                                              in1=bc(sm[:c, 0:4].unsqueeze(2), [c, 4, c]), op=ALU.subtract),
                             reads=[Drk, smk], writes=[LMk])
                        C.op("dve", lambda e: e.tensor_scalar(out=LM[:c, :, :c], in0=LM[:c, :, :c], scalar1=0.0, scalar2=None,
                                                              op0=ALU.min), reads=[LMk], writes=[LMk])
                        C.op("act", lambda e: e.activation(out=LM[:c, :, :c], in_=LM[:c, :, :c], func=AF.Exp), reads=[LMk], writes=[LMk])
                        C.op("pool", lambda e: e.tensor_tensor(out=LMs[:c, :, :c], in0=LM[:c, :, :c],
                                                               in1=bc(striu[:c, :c].unsqueeze(1), [c, 4, c]), op=ALU.mult),
                             reads=[LMk, constk], writes=[LMk])
                        C.op("pool", lambda e: e.tensor_tensor(out=LMi[:c, :, :c], in0=LM[:c, :, :c],
                                                               in1=bc(triu[:c, :c].unsqueeze(1), [c, 4, c]), op=ALU.mult),
                             reads=[LMk, constk], writes=[LMk])
                        C.op("dve", lambda e: e.tensor_tensor(out=qdT[0:64, :, cs], in0=nTh[0:64, 0:4, cs],
                                                              in1=EDrow[0:64, :, :c], op=ALU.mult), reads=[nTk, Drk], writes=[qdk])
                        C.op("dve", lambda e: e.tensor_tensor(out=kz[:c, ck_, :].rearrange("p (h d) -> p h d", h=4),
                                                              in0=ktok[:c, ck_, 0:256].rearrange("p (h d) -> p h d", h=4),
                                                              in1=bc(sm[:c, 8:12].unsqueeze(2), [c, 4, 64]), op=ALU.mult),
                             reads=[ktokk, smk], writes=[kzk])
                        yield
                        pg, pgk = bank("mi")
                        for h in range(4):
                            C.op("pe", lambda e: e.matmul(pg[:c, h * c:(h + 1) * c], lhsT=nTh[0:64, 4 + h, cs], rhs=nTh[0:64, 4 + h, cs],
                                                          start=True, stop=True), reads=[nTk], writes=[pgk])
                        for h in range(4):
                            C.op("pe", lambda e: e.matmul(pg[:c, 256 + h * c:256 + (h + 1) * c], lhsT=nTh[0:64, 4 + h, cs],
                                                          rhs=nTh[0:64, h, cs], start=True, stop=True), reads=[nTk], writes=[pgk])
                        KKv = pg[:c, 0:4 * c].rearrange("p (h i) -> p h i", h=4)
                        KQv = pg[:c, 256:256 + 4 * c].rearrange("p (h i) -> p h i", h=4)
                        C.op("dve", lambda e: e.tensor_tensor(out=Xf[:c, :, :c], in0=KKv,
                                                              in1=bc(gb[:c, ck_, 4:8].unsqueeze(2), [c, 4, c]), op=ALU.mult),
                             reads=[pgk, gbk], writes=[Xk])
                        C.op("dve", lambda e: e.tensor_tensor(out=Xf[:c, :, :c], in0=Xf[:c, :, :c], in1=LMs[:c, :, :c], op=ALU.mult),
                             reads=[Xk, LMk], writes=[Xk])
                        C.op("dve", lambda e: e.tensor_tensor(out=scr[:c, :, :c], in0=KQv, in1=LMi[:c, :, :c], op=ALU.mult),
                             reads=[pgk, LMk], writes=[scrk])
                        C.op("pool", lambda e: e.tensor_copy(out=inT[:c, :, :c], in_=scr[:c, :, :c]), reads=[scrk], writes=[inTk])
                        yield
                        pm, pmk = bank("mi")
                        for h in range(4):
                            C.op("pe", lambda e: e.transpose(out=pm[:c, h * c:(h + 1) * c], in_=Xf[:c, h, :c],
                                                             identity=identf[:c, :c]), reads=[Xk, constk], writes=[pmk])
                        C.op("act", lambda e: e.activation(out=YTb[:c, 0, :, :c], in_=pm[:c, 0:4 * c].rearrange("p (h i) -> p h i", h=4),
                                                           func=AF.Copy), reads=[pmk], writes=[Yk])
                        C.op("act", lambda e: e.activation(out=Yb[:c, 0, :, :c], in_=Xf[:c, :, :c], func=AF.Copy), reads=[Xk], writes=[Yk])
                        C.op("dve", lambda e: e.tensor_tensor(out=Pf[:c, :, :c], in0=bc(identf[:c, :c].unsqueeze(1), [c, 4, c]),
                                                              in1=Xf[:c, :, :c], op=ALU.subtract), reads=[Xk, constk], writes=[Pk])
                        C.op("pool", lambda e: e.tensor_copy(out=Pb[:c, :, :c], in_=Pf[:c, :, :c]), reads=[Pk], writes=[Pk])
                        yield
                        cur = 0
                        for it in range(nit):
                            nxt = 1 - cur
                            pm, pmk = bank("mi")
                            for h in range(4):
                                C.op("pe", lambda e: e.matmul(pm[:c, h * c:(h + 1) * c], lhsT=YTb[:c, cur, h, :c],
                                                              rhs=Yb[:c, cur, h, :c], start=True, stop=True),
                                     reads=[Yk], writes=[pmk])
                            for h in range(4):
                                C.op("pe", lambda e: e.matmul(pm[:c, 256 + h * c:256 + (h + 1) * c],
                                                              lhsT=Yb[:c, cur, h, :c], rhs=YTb[:c, cur, h, :c],
                                                              start=True, stop=True), reads=[Yk], writes=[pmk])
                            C.op("act", lambda e: e.activation(out=Yb[:c, nxt, :, :c],
                                                               in_=pm[:c, 0:4 * c].rearrange("p (h i) -> p h i", h=4), func=AF.Copy),
                                 reads=[pmk], writes=[Yk])
                            C.op("act", lambda e: e.activation(out=YTb[:c, nxt, :, :c],
                                                               in_=pm[:c, 256:256 + 4 * c].rearrange("p (h i) -> p h i", h=4), func=AF.Copy),
                                 reads=[pmk], writes=[Yk])
                            yield
                            pm, pmk = bank("mi")
                            for h in range(4):
                                C.op("pe", lambda e: e.matmul(pm[:c, h * c:(h + 1) * c], lhsT=YTb[:c, nxt, h, :c],
                                                              rhs=Pb[:c, h, :c], start=True, stop=True),
                                     reads=[Yk, Pk], writes=[pmk])
                            C.op("dve", lambda e: e.tensor_tensor(out=Pf[:c, :, :c], in0=Pf[:c, :, :c],
                                                                  in1=pm[:c, 0:4 * c].rearrange("p (h i) -> p h i", h=4), op=ALU.add),
                                 reads=[pmk, Pk], writes=[Pk])
                            C.op("pool", lambda e: e.tensor_copy(out=Pb[:c, :, :c], in_=Pf[:c, :, :c]), reads=[Pk], writes=[Pk])
                            cur = nxt
                            yield
                        pc, pck = bank("mi")
                        for h in range(4):
                            C.op("pe", lambda e: e.matmul(pc[:c, h * 64:(h + 1) * 64], lhsT=nTh[0:64, 4 + h, cs],
                                                          rhs=Sdb[:, h, :], start=True, stop=True),
                                 reads=[nTk, Sdk], writes=[pck])
                        C.op("dve", lambda e: e.tensor_tensor(out=rt[:c, :].rearrange("p (h d) -> p h d", h=4),
                                                              in0=pc[:c, 0:256].rearrange("p (h d) -> p h d", h=4),
                                                              in1=bc(sm[:c, 4:8].unsqueeze(2), [c, 4, 64]), op=ALU.mult),
                             reads=[pck, smk], writes=[rk])
                        C.op("dve", lambda e: e.tensor_tensor(out=rb[:c, :], in0=ktok[:c, ck_, 256:512], in1=rt[:c, :], op=ALU.subtract),
                             reads=[rk, ktokk], writes=[rk])
                        yield
                        pc2, pc2k = bank("mi")
                        for h in range(4):
                            C.op("pe", lambda e: e.matmul(pc2[:c, h * 64:(h + 1) * 64], lhsT=Pb[:c, h, :c],
                                                          rhs=rb[:c, h * 64:(h + 1) * 64], start=True, stop=True),
                                 reads=[Pk, rk], writes=[pc2k])
                        C.op("dve", lambda e: e.tensor_tensor(out=rt[:c, :].rearrange("p (h d) -> p h d", h=4),
                                                              in0=pc2[:c, 0:256].rearrange("p (h d) -> p h d", h=4),
                                                              in1=bc(gb[:c, ck_, 4:8].unsqueeze(2), [c, 4, 64]), op=ALU.mult),
                             reads=[pc2k, gbk], writes=[rk])
                        C.op("pool", lambda e: e.tensor_copy(out=vnb[:c, :], in_=rt[:c, :]), reads=[rk], writes=[vnk])
                        yield
                        for h in range(4):
                            oc_ = slice(h * TS + ck_ * c, h * TS + (ck_ + 1) * c)
                            C.op("pe", lambda e: e.matmul(po[:64, oc_], lhsT=Sdb[:, h, :], rhs=qdT[0:64, h, cs],
                                                          start=True, stop=False), reads=[Sdk, qdk], writes=[pok])
                            C.op("pe", lambda e: e.matmul(po[:64, oc_], lhsT=vnb[:c, h * 64:(h + 1) * 64], rhs=inT[:c, h, :c],
                                                          start=False, stop=True), reads=[vnk, inTk], writes=[pok])
                        psu, psuk = bank("mi")
                        for h in range(4):
                            C.op("pe", lambda e: e.matmul(psu[:64, h * 64:(h + 1) * 64], lhsT=kz[:c, ck_, h * 64:(h + 1) * 64],
                                                          rhs=vnb[:c, h * 64:(h + 1) * 64], start=True, stop=True),
                                 reads=[kzk, vnk], writes=[psuk])
                        for h in range(4):
                            C.op("dve", lambda e: e.scalar_tensor_tensor(
                                out=Sd[:, h, :], in0=Sd[:, h, :], scalar=EDrow[0:64, h, c - 1:c], in1=psu[:64, h * 64:(h + 1) * 64],
                                op0=ALU.mult, op1=ALU.add), reads=[Drk, psuk, Sdk], writes=[Sdk])
                        C.op("pool", lambda e: e.tensor_copy(out=Sdb[:], in_=Sd[:]), reads=[Sdk], writes=[Sdk])
                        yield
                        pr, prk = bank("mi")
                        for h in range(4):
                            C.op("pe", lambda e: e.matmul(pr[:c, h * c:(h + 1) * c], lhsT=qkcT[0:64, 4 + h, cs], rhs=qkcT[0:64, h, cs],
                                                          start=True, stop=True), reads=[qkcTk], writes=[prk])
                        C.op("dve", lambda e: e.tensor_tensor(out=scr[:c, :, :c], in0=pr[:c, 0:4 * c].rearrange("p (h i) -> p h i", h=4),
                                                              in1=rmc[:c, :].rearrange("p (h i) -> p h i", h=4)[:, :, :c], op=ALU.mult),
                             reads=[prk, constk], writes=[scrk])
                        C.op("pool", lambda e: e.tensor_copy(out=inR[:c, :, :c], in_=scr[:c, :, :c]), reads=[scrk], writes=[inRk])
                        yield
                        for h in range(4):
                            oc_ = slice(h * TS + ck_ * c, h * TS + (ck_ + 1) * c)
                            C.op("pe", lambda e: e.matmul(po2[:64, oc_], lhsT=Srb[:, h, :], rhs=qkcT[0:64, h, cs],
                                                          start=True, stop=False), reads=[Srk, qkcTk], writes=[po2k])
                            C.op("pe", lambda e: e.matmul(po2[:64, oc_], lhsT=vcb[:c, ck_, h * 64:(h + 1) * 64], rhs=inR[:c, h, :c],
                                                          start=False, stop=True), reads=[vck, inRk], writes=[po2k])
                        psu, psuk = bank("mi")
                        for h in range(4):
                            C.op("pe", lambda e: e.matmul(psu[:64, h * 64:(h + 1) * 64], lhsT=kzc[:c, ck_, h * 64:(h + 1) * 64],
                                                          rhs=vcb[:c, ck_, h * 64:(h + 1) * 64], start=True, stop=True),
                                 reads=[vck], writes=[psuk])
                        for h in range(4):
                            C.op("dve", lambda e: e.scalar_tensor_tensor(out=Sr[:, h, :], in0=Sr[:, h, :], scalar=float(gch[h]),
                                                                         in1=psu[:64, h * 64:(h + 1) * 64], op0=ALU.mult, op1=ALU.add),
                                 reads=[psuk, Srk], writes=[Srk])
                        C.op("pool", lambda e: e.tensor_copy(out=Srb[:], in_=Sr[:]), reads=[Srk], writes=[Srk])

                    yield
                    for which in range(2):
                        pso, psok = (po, pok) if which == 0 else (po2, po2k)
                        pov = pso[:64, 0:4 * TS].rearrange("p (h t) -> p h t", h=4)
                        if which == 0:
                            C.op("act", lambda e: e.activation(out=ot[:, :, :TS], in_=pov, func=AF.Copy), reads=[psok], writes=[otk])
                        else:
                            C.op("dve", lambda e: e.tensor_tensor(out=ot[:, :, :TS], in0=pov,
                                                                  in1=xit[:, :].rearrange("p (h t) -> p h t", h=4)[:, :, :TS], op=ALU.mult),
                                 reads=[psok, constk], writes=[otk])
                        C.op("pool", lambda e: e.tensor_tensor(out=otb[:, :, :TS], in0=ot[:, :, :TS], in1=ot[:, :, :TS], op=ALU.mult),
                             reads=[otk], writes=[otk])
                        yield
                        pm, pmk = bank("mi")
                        for h in range(4):
                            C.op("pe", lambda e: e.matmul(pm[:64, h * TS:(h + 1) * TS], lhsT=onesb[:64, :64], rhs=otb[:, h, :TS],
                                                          start=True, stop=True), reads=[otk, constk], writes=[pmk])
                        C.op("act", lambda e: e.activation(out=ot2[:, :, :TS], in_=pm[:64, 0:4 * TS].rearrange("p (h t) -> p h t", h=4),
                                                           func=AF.Ln, scale=1.0 / 64, bias=epsb[:64, 1:2]),
                             reads=[pmk, constk], writes=[otk])
                        C.op("act", lambda e: e.activation(out=ot2[:, :, :TS], in_=ot2[:, :, :TS], func=AF.Exp, scale=-0.5),
                             reads=[otk], writes=[otk])
                        yield
                        C.op("dve", lambda e: e.tensor_tensor(out=ot[:, :, :TS], in0=ot[:, :, :TS], in1=ot2[:, :, :TS], op=ALU.mult),
                             reads=[otk], writes=[otk])
                        C.op("pool", lambda e: e.tensor_tensor(out=ot[:, :, :TS], in0=ot[:, :, :TS],
                                                               in1=sgT[:, which * 4:which * 4 + 4, :TS], op=ALU.mult),
                             reads=[otk, sgTk], writes=[otk])
                        for h in range(4):
                            p_, s_ = divmod(h, 2)
                            dst = mixT[s_ * 64:(s_ + 1) * 64, 4 + which * 2 + p_, col0:col0 + TS]
                            if which == 0:
                                C.op("dve", lambda e: e.tensor_scalar(out=dst, in0=ot[:, h, :TS], scalar1=dlngc[:, 0:1],
                                                                      scalar2=None, op0=ALU.mult),
                                     reads=[otk, pk_], writes=[mixk])
                            else:
                                C.op("pool", lambda e: e.tensor_copy(out=dst, in_=ot[:, h, :TS]), reads=[otk], writes=[mixk])

                    yield

                def back_tile(ti, t):
                    r0 = t * TS
                    col0 = ti * TS
                    C.dma("sp", xres[:TS, :], xcs[row0 + r0:row0 + r0 + TS, :], xk,
                          reads=[dtk("xcs", row0 + r0)], writes=[xk])
                    for half in range(2):
                        pb, pbk = bank("pj")
                        for k in range(8):
                            C.op("pe", lambda e: e.matmul(pb[:TS, :], lhsT=mixT[:, k, col0:col0 + TS],
                                                          rhs=WA_out[:, k, half * 512:(half + 1) * 512], start=(k == 0), stop=(k == 7)),
                                 reads=[mixk, WAK], writes=[pbk])
                        C.op("dve", lambda e: e.scalar_tensor_tensor(out=xres[:TS, half * 512:(half + 1) * 512],
                                                                     in0=xres[:TS, half * 512:(half + 1) * 512], scalar=float(ALPHA),
                                                                     in1=pb[:TS, :], op0=ALU.mult, op1=ALU.add),
                             reads=[pbk, xk], writes=[xk])
                    layer_norm(xres[:TS, :], TS, g1, b1, pk_, xk)
                    C.dma("pool", x1s[row0 + r0:row0 + r0 + TS, :], xres[:TS, :], xk, reads=[xk], writes=[dtk("x1s", row0 + r0)])


                def gen_attention(blk):
                    if isP:
                        kts = [(kt, 128, None) for kt in range(2 * blk)] + [(2 * blk, 128, 0), (2 * blk + 1, 128, 1)]
                    else:
                        kts = [(kt, 128, None) for kt in range(past // 128)] + [(past // 128, TS, None)]
                    its = [(hd, kt, rows, msk) for hd in range(4) for (kt, rows, msk) in kts]
                    nk = len(kts)

                    def load_v(ii):
                        hd, kt, rows, msk = its[ii]
                        vt, vtk = Vt[ii % 3], Vtk[ii % 3]
                        if kt < past // 128:
                            C.dma("pool", vt[:rows, :], cv[l, sb_, kt * 128:kt * 128 + rows, hd * 128:(hd + 1) * 128], vtk, writes=[vtk])
                        else:
                            C.dma("sp", vt[:rows, :], vsc[kt * 128:kt * 128 + rows, hd * 128:(hd + 1) * 128], vtk,
                                  reads=[dtk("vsc", kt)], writes=[vtk])

                    def emit_s(ii):
                        hd, kt, rows, msk = its[ii]
                        sbk_, sbkk = PS[2 + ii % 2], PSK[2 + ii % 2]
                        kc0 = kt * 128
                        C.op("pe", lambda e: e.matmul(sbk_[:rows, 0:nq], lhsT=KT[:, hd, kc0:kc0 + rows], rhs=QTa[:, hd, 0:nq],
                                                      start=True, stop=True), reads=[ktk[kt], QTk], writes=[sbkk])
                        C.op("pe", lambda e: e.matmul(sbk_[:rows, 256:256 + nq], lhsT=KT[:, hd, kc0:kc0 + rows],
                                                      rhs=QTb[:, hd, 0:nq], start=True, stop=True),
                             reads=[ktk[kt], QTk], writes=[sbkk])
                    load_v(0)
                    if len(its) > 1:
                        load_v(1)
                    emit_s(0)
                    for ii, (hd, kt, rows, msk) in enumerate(its):
                        if ii + 2 < len(its):
                            load_v(ii + 2)
                        if ii + 1 < len(its):
                            emit_s(ii + 1)
                        first = (ii % nk == 0)
                        last = (ii % nk == nk - 1)
                        sbk_, sbkk = PS[2 + ii % 2], PSK[2 + ii % 2]
                        pt, ptk = PT[ii % 2], PTk[ii % 2]
                        C.op("act", lambda e: e.activation(out=pt[:rows, :, 0:nq],
                                                           in_=sbk_[:rows, :].rearrange("p (j q) -> p j q", j=2)[:, :, 0:nq],
                                                           func=AF.Exp, scale=0.125), reads=[sbkk], writes=[ptk])
                        if msk is not None:
                            C.op("pool", lambda e: e.tensor_tensor(out=pt[:rows, :, 0:nq], in0=pt[:rows, :, 0:nq],
                                                                   in1=bc(amask[:rows, msk * 256:msk * 256 + nq].unsqueeze(1), [rows, 2, nq]),
                                                                   op=ALU.mult), reads=[ptk, constk], writes=[ptk])
                        if first:
                            C.op("dve", lambda e: e.memset(OB[:, :], 0.0), writes=[OBK])
                            C.op("dve", lambda e: e.memset(LB[:, :], 0.0), writes=[LBK])
                        vt, vtk = Vt[ii % 3], Vtk[ii % 3]
                        for j in range(2):
                            C.op("pe", lambda e: e.matmul(OB[:, j * 256:j * 256 + nq], lhsT=vt[:rows, :],
                                                          rhs=pt[:rows, j, 0:nq], start=False, stop=False, skip_group_check=True),
                                 reads=[vtk, ptk], writes=[OBK])
                            C.op("pe", lambda e: e.matmul(LB[:, j * 256:j * 256 + nq], lhsT=onesb[:rows, :], rhs=pt[:rows, j, 0:nq],
                                                          start=False, stop=False, skip_group_check=True),
                                 reads=[constk, ptk], writes=[LBK])
                        if last:
                            Lv = LB[:, :].rearrange("p (j q) -> p j q", j=2)[:, :, 0:nq]
                            Ov = OB[:, :].rearrange("p (j q) -> p j q", j=2)[:, :, 0:nq]
                            C.op("dve", lambda e: e.reciprocal(out=Rr[:, :, 0:nq], in_=Lv), reads=[LBK], writes=[atk])
                            C.op("dve", lambda e: e.tensor_tensor(out=Tt[:, :, 0:nq], in0=Ov, in1=Rr[:, :, 0:nq], op=ALU.mult),
                                 reads=[OBK, atk], writes=[atk])
                            C.op("dve", lambda e: e.scalar_tensor_tensor(out=oaf[:, 0:nq], in0=Tt[:, 1, 0:nq], scalar=neglam[:, 0:1],
                                                                         in1=Tt[:, 0, 0:nq], op0=ALU.mult, op1=ALU.add),
                                 reads=[atk, pk_], writes=[atk])
                            C.op("pool", lambda e: e.tensor_tensor(out=oab[:, 0:nq], in0=oaf[:, 0:nq], in1=oaf[:, 0:nq], op=ALU.mult),
                                 reads=[atk], writes=[atk])
                            yield
                            pm, pmk = bank("mi")
                            C.op("pe", lambda e: e.matmul(pm[:, 0:nq], lhsT=onesb[:, :], rhs=oab[:, 0:nq], start=True, stop=True),
                                 reads=[atk, constk], writes=[pmk])
                            C.op("act", lambda e: e.activation(out=Rr[:, 0, 0:nq], in_=pm[:, 0:nq], func=AF.Ln, scale=1.0 / 128,
                                                               bias=epsb[:, 1:2]), reads=[pmk, constk], writes=[atk])
                            C.op("act", lambda e: e.activation(out=Rr[:, 0, 0:nq], in_=Rr[:, 0, 0:nq], func=AF.Exp, scale=-0.5),
                                 reads=[atk], writes=[atk])
                            C.op("dve", lambda e: e.tensor_tensor(out=oaf[:, 0:nq], in0=oaf[:, 0:nq], in1=Rr[:, 0, 0:nq], op=ALU.mult),
                                 reads=[atk], writes=[atk])
                            C.op("dve", lambda e: e.tensor_scalar(out=mixT[:, hd, 0:nq], in0=oaf[:, 0:nq], scalar1=dngc[:, 0:1],
                                                                  scalar2=float(1.0 - lam_init), op0=ALU.mult, op1=ALU.mult),
                                 reads=[atk, pk_], writes=[mixk])
                        yield

                def run_gens(*gens):
                    live = list(gens)
                    while live:
                        for g in list(live):
                            try:
                                next(g)
                            except StopIteration:
                                live.remove(g)

                for blk in range(nblk):
                    t0_ = blk * TPB
                    run_gens(gen_frontA(0, t0_, "pj"))
                    frontB(t0_, 0)
                    if TPB == 2:
                        run_gens(gen_chunks(0), gen_frontA(1, t0_ + 1, "s"))
                        frontB(t0_ + 1, TS)
                        run_gens(gen_chunks(TS), gen_attention(blk))
                    else:
                        run_gens(gen_chunks(0), gen_attention(blk))
                    for ti in range(TPB):
                        back_tile(ti, t0_ + ti)

                if STOP == 1:
                    C.finish()
                    return nc
                ddst = pdl[l] if isP else sdlo[l, sb_]
                rdst = prt[l] if isP else srto[l, sb_]
                C.dma("pool", ddst.rearrange("(h d) e -> d h e", d=64), Sd[:], Sdk, reads=[Sdk])
                C.dma("pool", rdst.rearrange("(h d) e -> d h e", d=64), Sr[:], Srk, reads=[Srk])

        if STOP == 20 + l:
            C.finish()
            return nc
        C.barrier()
        WA_up = WA[:, 0:8 * DFF].rearrange("p (k n) -> p k n", k=8)
        WA_dn = WA[:, 8 * DFF:8 * DFF + 32 * D].rearrange("p (k n) -> p k n", k=32)
        for k in range(8):
            for c0_ in range(0, DFF, 512):
                C.dma("pool", WA_up[:, k, c0_:c0_ + 512], w_up[l, k * 128:(k + 1) * 128, c0_:c0_ + 512], WAK, writes=[WAK])
        for k in range(32):
            for c0_ in range(0, D, 512):
                C.dma("pool", WA_dn[:, k, c0_:c0_ + 512], w_down[l, k * 128:(k + 1) * 128, c0_:c0_ + 512], WAK, writes=[WAK])
        with ExitStack() as st:
            def T(name, shape, dt=F32):
                return st.enter_context(nc.sbuf_tensor("%s_f%d" % (name, l), list(shape), dt))
            g2 = T("g2", [128, D]); b2 = T("b2", [128, D]); pk2 = Tk()
            C.dma("sp", g2[:], ln2g[l:l + 1, :].partition_broadcast(128), pk2, writes=[pk2])
            C.dma("sp", b2[:], ln2b[l:l + 1, :].partition_broadcast(128), pk2, writes=[pk2])
            x1 = T("x1", [128, 2, D]); x1k = [Tk(), Tk()]
            x1b = T("x1b", [128, D], BF16); x1bk = Tk()
            x1T = T("x1T", [128, 8, 256], BF16); x1Tk = Tk()
            hidT = T("hidT", [128, 32, 256], BF16); hidk = Tk()
            rl = [T("rl0", [128, 256]), T("rl1", [128, 256])]; rlk = [Tk(), Tk()]
            blocks = []
            for b0_ in range(0, Lp, 256):
                blocks.append((b0_, 128, min(2, (Lp - b0_) // 128)))
            for b_ in range(NS):
                blocks.append((Lp + b_ * Ls, Ls, 1))
            for (rb0, TS, ntl) in blocks:
                nb = TS * ntl
                for ti in range(ntl):
                    r0 = rb0 + ti * TS
                    C.dma("sp", x1[:TS, ti, :], x1s[r0:r0 + TS, :], x1k[ti], reads=[dtk("x1s", r0)], writes=[x1k[ti]])
                    C.op("act", lambda e: e.activation(out=x1b[:TS, :], in_=x1[:TS, ti, :], func=AF.Copy), reads=[x1k[ti]], writes=[x1bk])
                    pb, pbk = bank("mi")
                    pbb = pb[:].bitcast(BF16)
                    for k in range(8):
                        C.op("pe", lambda e: e.transpose(out=pbb[:, k * TS:(k + 1) * TS], in_=x1b[:TS, k * 128:(k + 1) * 128],
                                                         identity=identb[:TS, :TS]), reads=[x1bk, constk], writes=[pbk])
                    C.op("dve", lambda e: e.tensor_copy(out=x1T[:, :, ti * TS:(ti + 1) * TS],
                                                        in_=pbb[:, 0:8 * TS].rearrange("p (k t) -> p k t", k=8)),
                         reads=[pbk], writes=[x1Tk])
                for f in range(32):
                    pb, pbk = bank("pj")
                    for k in range(8):
                        C.op("pe", lambda e: e.matmul(pb[:, 0:nb], lhsT=WA_up[:, k, f * 128:(f + 1) * 128], rhs=x1T[:, k, 0:nb],
                                                      start=(k == 0), stop=(k == 7)), reads=[x1Tk, WAK], writes=[pbk])
                    r_, rk_ = rl[f % 2], rlk[f % 2]
                    C.op("act", lambda e: e.activation(out=r_[:, 0:nb], in_=pb[:, 0:nb], func=AF.Relu), reads=[pbk], writes=[rk_])
                    eng = "pool" if f % 2 == 0 else "dve"
                    C.op(eng, lambda e: e.tensor_tensor(out=hidT[:, f, 0:nb], in0=r_[:, 0:nb], in1=r_[:, 0:nb], op=ALU.mult),
                         reads=[rk_], writes=[hidk])
                for ti in range(ntl):
                    r0 = rb0 + ti * TS
                    for half in range(2):
                        pb, pbk = bank("pj")
                        for f in range(32):
                            C.op("pe", lambda e: e.matmul(pb[:TS, :], lhsT=hidT[:, f, ti * TS:(ti + 1) * TS],
                                                          rhs=WA_dn[:, f, half * 512:(half + 1) * 512], start=(f == 0), stop=(f == 31)),
                                 reads=[hidk, WAK], writes=[pbk])
                        C.op("dve", lambda e: e.scalar_tensor_tensor(out=x1[:TS, ti, half * 512:(half + 1) * 512],
                                                                     in0=x1[:TS, ti, half * 512:(half + 1) * 512], scalar=float(ALPHA),
                                                                     in1=pb[:TS, :], op0=ALU.mult, op1=ALU.add),
                             reads=[pbk, x1k[ti]], writes=[x1k[ti]])
                    layer_norm(x1[:TS, ti, :], TS, g2, b2, pk2, x1k[ti])
                    if l < DEPTH - 1:
                        C.dma("pool", xcs[r0:r0 + TS, :], x1[:TS, ti, :], x1k[ti], reads=[x1k[ti]], writes=[dtk("xcs", r0)])
                    else:
                        if r0 < Lp:
                            C.dma("pool", yp[r0:r0 + TS, :], x1[:TS, ti, :], x1k[ti], reads=[x1k[ti]])
                        else:
                            C.dma("pool", ys[r0 - Lp:r0 - Lp + TS, :], x1[:TS, ti, :], x1k[ti], reads=[x1k[ti]])
    C.finish()
    return nc


def make_consts(Lp, Ls, PAST):
    c = {}
    c["c_ident"] = np.eye(128, dtype=np.float32)
    half = 32
    inv_freq = (10000.0 ** (-np.arange(half, dtype=np.float32) / half)).astype(np.float32)

    def rot(pos):
        ang = pos.astype(np.float32)[:, None] * inv_freq[None, :]
        cos = np.cos(ang).astype(np.float32)
        sin = np.sin(ang).astype(np.float32)
        return np.concatenate([cos, cos, -sin, sin], axis=1).astype(np.float32)

    c["c_rotp"] = rot(np.arange(Lp))
    c["c_rots"] = rot(PAST + np.arange(Ls))
    k = np.arange(128)[:, None]
    qq = np.arange(256)[None, :]
    m0 = ((0 + k // 64) <= (qq // 64)).astype(np.float32)
    m1 = ((2 + k // 64) <= (qq // 64)).astype(np.float32)
    c["c_amask"] = np.concatenate([m0, m1], axis=1)
    j = np.arange(64)[:, None]
    i = np.arange(64)[None, :]
    c["c_triu"] = (j <= i).astype(np.float32)
    c["c_striu"] = (j < i).astype(np.float32)
    lg = np.log(1.0 - 2.0 ** (-5.0 - np.arange(4, dtype=np.float64)))
    rm = np.zeros((64, 4, 64), np.float64)
    for h in range(4):
        rm[:, h, :] = np.exp(-lg[h] * (j + 1.0)) * (j <= i)
    c["c_rm"] = rm.reshape(64, 256).astype(np.float32)
    xi = np.zeros((64, 4, 128), np.float64)
    xis = np.zeros((64, 4, Ls), np.float64)
    for h in range(4):
        xi[:, h, :] = np.exp(lg[h] * ((np.arange(128) % 64) + 1.0))[None, :]
        xis[:, h, :] = np.exp(lg[h] * (np.arange(Ls) + 1.0))[None, :]
    c["c_xi"] = xi.reshape(64, 512).astype(np.float32)
    c["c_xis"] = xis.reshape(64, 4 * Ls).astype(np.float32)
    zeta = np.zeros((128, 4), np.float64)
    zetas = np.zeros((Ls, 4), np.float64)
    for h in range(4):
        zeta[:, h] = np.exp(lg[h] * (63.0 - (np.arange(128) % 64)))
        zetas[:, h] = np.exp(lg[h] * (Ls - 1.0 - np.arange(Ls)))
    c["c_zeta"] = zeta.astype(np.float32)
    c["c_zetas"] = zetas.astype(np.float32)
    bo = np.zeros((128, 128), np.float32)
    bo[:64, :64] = 1.0
    bo[64:, 64:] = 1.0
    c["c_bones"] = bo
    return c


_NC_CACHE = {}


def run(inputs, Lp, NS, Ls, PAST, ncores):
    key = (Lp, NS, Ls, PAST)
    if key not in _NC_CACHE:
        _NC_CACHE[key] = build(Lp, NS, Ls, PAST)
    nc = _NC_CACHE[key]
    f = lambda a: np.ascontiguousarray(np.asarray(a, dtype=np.float32))
    consts = make_consts(Lp, Ls, PAST)
    I = {k: f(v) for k, v in inputs.items()}
    in_maps = []
    for i in range(ncores):
        sl = slice(i * NS, (i + 1) * NS)
        m = dict(consts)
        m["xp"] = f(I["x_prompt"][i])
        m["xs"] = f(I["x_sample"][sl].reshape(NS * Ls, D))
        m["ck"] = f(I["cache_k"][:, sl].reshape(DEPTH, NS, PAST, 512))
        m["cv"] = f(I["cache_v"][:, sl].reshape(DEPTH, NS, PAST, 512))
        m["sdl"] = f(I["state_delta"][:, sl].reshape(DEPTH, NS, 256, 64))
        m["scv"] = f(I["state_conv"][:, sl])
        m["srt"] = f(I["state_ret"][:, sl].reshape(DEPTH, NS, 256, 64))
        m["ln0g"] = f(I["ln0_g"].reshape(1, D)); m["ln0b"] = f(I["ln0_b"].reshape(1, D))
        m["w_in"] = I["w_in"]
        m["lamq1"] = I["lam_q1"]; m["lamk1"] = I["lam_k1"]; m["lamq2"] = I["lam_q2"]; m["lamk2"] = I["lam_k2"]
        m["dng"] = I["diff_norm_g"]; m["convw"] = I["conv_w"]; m["alog"] = I["a_log"]; m["dtb"] = I["dt_bias"]
        m["dlng"] = I["delta_norm_g"]; m["w_out"] = I["w_out"]
        m["ln1g"] = I["ln1_g"]; m["ln1b"] = I["ln1_b"]; m["w_up"] = I["w_up"]; m["w_down"] = I["w_down"]
        m["ln2g"] = I["ln2_g"]; m["ln2b"] = I["ln2_b"]
        in_maps.append(m)
    res = run_bass_kernel_spmd(nc, in_maps, core_ids=list(range(ncores)))
    R = res.results
    B = ncores
    st = lambda name: np.stack([np.asarray(R[i][name]) for i in range(B)])
    y_p = st("yp")
    y_s = st("ys").reshape(B * NS, Ls, D)
    p_k = st("pk").transpose(1, 0, 2, 3).reshape(DEPTH, B, Lp, 8, 64)
    p_v = st("pv").transpose(1, 0, 2, 3).reshape(DEPTH, B, Lp, 4, 128)
    p_d = st("pdl").transpose(1, 0, 2, 3).reshape(DEPTH, B, 4, 64, 64)
    p_c = st("pcv").transpose(1, 0, 2, 3)
    p_r = st("prt").transpose(1, 0, 2, 3).reshape(DEPTH, B, 4, 64, 64)
    s_k = st("sk").transpose(1, 0, 2, 3).reshape(DEPTH, B * NS, Ls, 8, 64)
    s_v = st("sv").transpose(1, 0, 2, 3).reshape(DEPTH, B * NS, Ls, 4, 128)
    s_d = st("sdlo").transpose(1, 0, 2, 3, 4).reshape(DEPTH, B * NS, 4, 64, 64)
    s_c = st("scvo").transpose(1, 0, 2, 3, 4).reshape(DEPTH, B * NS, 3, 768)
    s_r = st("srto").transpose(1, 0, 2, 3, 4).reshape(DEPTH, B * NS, 4, 64, 64)
    outs = (y_p, y_s, p_k, p_v, p_d, p_c, p_r, s_k, s_v, s_d, s_c, s_r)
    return tuple(np.ascontiguousarray(o.astype(np.float32)) for o in outs)


def kernel(**inputs):
    return run(inputs, 4096, 2, 32, 2048, NCORES)
```

```python
import math
import numpy as np
import ml_dtypes
import concourse.bass as bass
import concourse.mybir as mybir
from concourse.bass_utils import run_bass_kernel_spmd

F32 = mybir.dt.float32
BF16 = mybir.dt.bfloat16
AF = mybir.ActivationFunctionType
ALU = mybir.AluOpType
AX = mybir.AxisListType

D = 1024
NIN = 3592
DFF = 4096
DEPTH = 2
ALPHA = (2 * DEPTH) ** 0.25
LN_EPS = 1e-5
NORM_EPS = 1e-6
NCORES = 8


class Tk:
    __slots__ = ("w", "r", "dsem", "dcnt", "name")

    def __init__(self, name=""):
        self.w = None
        self.r = {}
        self.dsem = None
        self.dcnt = 0
        self.name = name


class Ctx:
    def __init__(self, nc):
        self.nc = nc
        self.E = {"pe": nc.tensor, "dve": nc.vector, "act": nc.scalar, "pool": nc.gpsimd,
                  "sp": nc.sync}
        self.sem = {k: nc.alloc_semaphore("es_" + k) for k in self.E}
        self.cnt = {k: 0 for k in self.E}
        self.seen = {k: {} for k in self.E}
        self.dsems = []
        self.nd = 0

    def _wait(self, e, dep):
        key, sem, val = dep
        if self.seen[e].get(key, 0) >= val:
            return
        self.E[e].wait_ge(sem, val)
        self.seen[e][key] = val

    def _deps(self, e, reads, writes):
        for t in reads:
            if t.w is not None:
                self._wait(e, t.w)
        for t in writes:
            if t.w is not None:
                self._wait(e, t.w)
            for d in t.r.values():
                self._wait(e, d)

    def _mark(self, me, reads, writes):
        for t in reads:
            old = t.r.get(me[0])
            if old is None or old[2] < me[2]:
                t.r[me[0]] = me
        for t in writes:
            t.w = me
            t.r = {}

    def op(self, e, fn, reads=(), writes=()):
        self._deps(e, reads, writes)
        ins = fn(self.E[e])
        self.cnt[e] += 1
        ins.then_inc(self.sem[e], 1)
        me = (e, self.sem[e], self.cnt[e])
        if e == "pe":
            self.seen[e][e] = self.cnt[e]
        self._mark(me, reads, writes)
        return ins

    def dma(self, q, out, in_, owner, reads=(), writes=(), **kw):
        self._deps(q, reads, writes)
        if owner.dsem is None:
            owner.dsem = self.nc.alloc_semaphore("ds%d" % self.nd)
            self.nd += 1
            self.dsems.append(owner)
        ins = self.E[q].dma_start(out=out, in_=in_, **kw)
        owner.dcnt += 16
        ins.then_inc(owner.dsem, 16)
        me = ("d%d" % id(owner), owner.dsem, owner.dcnt)
        self._mark(me, reads, writes)
        return ins

    def barrier(self):
        for e in self.E:
            for f in self.E:
                if f != e and self.cnt[f] > 0:
                    self._wait(e, (f, self.sem[f], self.cnt[f]))
            for o in self.dsems:
                if o.dcnt > 0:
                    self._wait(e, ("d%d" % id(o), o.dsem, o.dcnt))

    def finish(self):
        self.barrier()


def bc(ap, shape):
    return ap.to_broadcast(list(shape))


import os
from contextlib import ExitStack
STOP = int(os.environ.get('KSTOP', '99'))


def build(Lp=4096, NS=2, Ls=32, PAST=2048):
    nc = bass.Bass("TRN2", target_bir_lowering=False)
    C = Ctx(nc)
    Ltot = Lp + NS * Ls
    LK = max(Lp, PAST + Ls)

    def din(name, shape, dt=F32):
        return nc.dram_tensor(name, list(shape), dt, kind="ExternalInput").ap()

    def dout(name, shape):
        return nc.dram_tensor(name, list(shape), F32, kind="ExternalOutput").ap()

    xp = din("xp", [Lp, D])
    xs = din("xs", [NS * Ls, D])
    ck = din("ck", [DEPTH, NS, PAST, 512])
    cv = din("cv", [DEPTH, NS, PAST, 512])
    sdl = din("sdl", [DEPTH, NS, 256, 64])
    scv = din("scv", [DEPTH, NS, 3, 768])
    srt = din("srt", [DEPTH, NS, 256, 64])
    ln0g = din("ln0g", [1, D]); ln0b = din("ln0b", [1, D])
    w_in = din("w_in", [DEPTH, D, NIN])
    lamq1 = din("lamq1", [DEPTH, 64]); lamk1 = din("lamk1", [DEPTH, 64])
    lamq2 = din("lamq2", [DEPTH, 64]); lamk2 = din("lamk2", [DEPTH, 64])
    dng = din("dng", [DEPTH, 128])
    convw = din("convw", [DEPTH, 4, 768])
    alog = din("alog", [DEPTH, 4]); dtb = din("dtb", [DEPTH, 4])
    dlng = din("dlng", [DEPTH, 64])
    w_out = din("w_out", [DEPTH, D, D])
    ln1g = din("ln1g", [DEPTH, D]); ln1b = din("ln1b", [DEPTH, D])
    w_up = din("w_up", [DEPTH, D, DFF])
    w_down = din("w_down", [DEPTH, DFF, D])
    ln2g = din("ln2g", [DEPTH, D]); ln2b = din("ln2b", [DEPTH, D])
    c_ident = din("c_ident", [128, 128])
    c_rotp = din("c_rotp", [Lp, 128])
    c_rots = din("c_rots", [Ls, 128])
    c_amask = din("c_amask", [128, 512])
    c_triu = din("c_triu", [64, 64])
    c_striu = din("c_striu", [64, 64])
    c_rm = din("c_rm", [64, 256])
    c_xi = din("c_xi", [64, 512])
    c_xis = din("c_xis", [64, 4 * Ls])
    c_zeta = din("c_zeta", [128, 4])
    c_zetas = din("c_zetas", [Ls, 4])
    c_bones = din("c_bones", [128, 128])

    yp = dout("yp", [Lp, D]); ys = dout("ys", [NS * Ls, D])
    pk = dout("pk", [DEPTH, Lp, 512]); pv = dout("pv", [DEPTH, Lp, 512])
    pdl = dout("pdl", [DEPTH, 256, 64]); pcv = dout("pcv", [DEPTH, 3, 768])
    prt = dout("prt", [DEPTH, 256, 64])
    sk = dout("sk", [DEPTH, NS * Ls, 512]); sv = dout("sv", [DEPTH, NS * Ls, 512])
    sdlo = dout("sdlo", [DEPTH, NS, 256, 64]); scvo = dout("scvo", [DEPTH, NS, 3, 768])
    srto = dout("srto", [DEPTH, NS, 256, 64])
    x1s = nc.dram_tensor("x1s", [Ltot, D], F32).ap()
    xcs = nc.dram_tensor("xcs", [Ltot, D], F32).ap()
    vsc = nc.dram_tensor("vsc", [LK, 512], BF16).ap()
    dram_tk = {}

    def dtk(name, r0):
        k = (name, r0)
        if k not in dram_tk:
            dram_tk[k] = Tk()
        return dram_tk[k]

    PS = [nc.alloc_psum_tensor("ps%d" % i, [128, 512], F32) for i in range(8)]
    PSK = [Tk("ps%d" % i) for i in range(8)]
    rot_state = {"pj": 0, "mi": 0, "s": 0}

    def bank(kind):
        base, n = {"pj": (0, 2), "s": (2, 2), "mi": (6, 2)}[kind]
        i = base + rot_state[kind] % n
        rot_state[kind] += 1
        return PS[i], PSK[i]

    OB, OBK = PS[4], PSK[4]
    LB, LBK = PS[5], PSK[5]

    def sb(name, shape, dt=F32):
        return nc.alloc_sbuf_tensor(name, list(shape), dt)

    identf = sb("identf", [128, 128]); identb = sb("identb", [128, 128], BF16)
    onesb = sb("onesb", [128, 128], BF16)
    onesf = sb("onesf", [128, 128])
    bonesf = sb("bonesf", [128, 128])
    triu = sb("triu", [64, 64]); striu = sb("striu", [64, 64])
    rmc = sb("rmc", [64, 256]); xic = sb("xic", [64, 512]); xisc = sb("xisc", [64, 4 * Ls])
    zetac = sb("zetac", [128, 4]); zetasc = sb("zetasc", [Ls, 4])
    amask = sb("amask", [128, 512], BF16)
    epsb = sb("epsb", [128, 2])
    constk = Tk("const")
    q = "sp"
    C.dma(q, identf[:], c_ident[:, :], constk, writes=[constk])
    C.dma(q, bonesf[:], c_bones[:, :], constk, writes=[constk])
    C.dma(q, triu[:], c_triu[:, :], constk, writes=[constk])
    C.dma(q, striu[:], c_striu[:, :], constk, writes=[constk])
    C.dma(q, rmc[:], c_rm[:, :], constk, writes=[constk])
    C.dma(q, xic[:], c_xi[:, :], constk, writes=[constk])
    C.dma(q, xisc[:], c_xis[:, :], constk, writes=[constk])
    C.dma(q, zetac[:], c_zeta[:, :], constk, writes=[constk])
    C.dma(q, zetasc[:], c_zetas[:, :], constk, writes=[constk])
    C.dma("pool", amask[:], c_amask[:, :], constk, writes=[constk])
    C.op("dve", lambda e: e.tensor_copy(out=identb[:], in_=identf[:]), reads=[constk], writes=[constk])
    C.op("dve", lambda e: e.memset(onesb[:], 1.0), writes=[constk])
    C.op("dve", lambda e: e.memset(onesf[:], 1.0), writes=[constk])
    C.op("dve", lambda e: e.memset(epsb[:, 0:1], LN_EPS), writes=[constk])
    C.op("dve", lambda e: e.memset(epsb[:, 1:2], NORM_EPS), writes=[constk])

    WA = sb("WA", [128, 65536], BF16)
    WAK = Tk("WA")

    lnst = sb("lnst", [128, 2, 6]); lnmv = sb("lnmv", [128, 2]); lnk = Tk("ln")

    def layer_norm(xap, n, gt, bt, gk, xk):
        tks = [xk]
        C.op("dve", lambda e: e.bn_stats(out=lnst[:n, 0, :], in_=xap[:, 0:512]), reads=tks, writes=[lnk])
        C.op("dve", lambda e: e.bn_stats(out=lnst[:n, 1, :], in_=xap[:, 512:1024]), reads=tks, writes=[lnk])
        C.op("dve", lambda e: e.bn_aggr(out=lnmv[:n, :], in_=lnst[:n, :, :].rearrange("p a b -> p (a b)")),
             reads=[lnk], writes=[lnk])
        C.op("act", lambda e: e.activation(out=lnmv[:n, 1:2], in_=lnmv[:n, 1:2], func=AF.Ln, bias=epsb[:n, 0:1]),
             reads=[lnk, constk], writes=[lnk])
        C.op("act", lambda e: e.activation(out=lnmv[:n, 1:2], in_=lnmv[:n, 1:2], func=AF.Exp, scale=-0.5),
             reads=[lnk], writes=[lnk])
        C.op("dve", lambda e: e.tensor_scalar(out=xap, in0=xap, scalar1=lnmv[:n, 0:1], scalar2=lnmv[:n, 1:2],
                                              op0=ALU.subtract, op1=ALU.mult),
             reads=[lnk] + tks, writes=tks)
        C.op("pool", lambda e: e.tensor_tensor(out=xap, in0=xap, in1=gt[:n, :], op=ALU.mult),
             reads=tks + [gk], writes=tks)
        C.op("pool", lambda e: e.tensor_tensor(out=xap, in0=xap, in1=bt[:n, :], op=ALU.add),
             reads=tks + [gk], writes=tks)

    seqs = [dict(name="p", L=Lp, past=0, TS=128, c=64, row0=0, b=0)]
    for b_ in range(NS):
        seqs.append(dict(name="s%d" % b_, L=Ls, past=PAST, TS=Ls, c=Ls, row0=Lp + b_ * Ls, b=b_))

    with ExitStack() as st0:
        g0 = st0.enter_context(nc.sbuf_tensor("g0", [128, D], F32))
        b0 = st0.enter_context(nc.sbuf_tensor("b0", [128, D], F32))
        xl = st0.enter_context(nc.sbuf_tensor("xl", [128, 2, D], F32))
        xlk = [Tk(), Tk()]
        pk0 = Tk()
        C.dma("sp", g0[:], ln0g[0:1, :].partition_broadcast(128), pk0, writes=[pk0])
        C.dma("sp", b0[:], ln0b[0:1, :].partition_broadcast(128), pk0, writes=[pk0])
        tiles0 = [(xp, r, r, 128) for r in range(0, Lp, 128)]
        for b_ in range(NS):
            tiles0.append((xs, b_ * Ls, Lp + b_ * Ls, Ls))
        for i0, (src0, sr0, dr0, n0) in enumerate(tiles0):
            sl0 = i0 % 2
            C.dma("sp", xl[:n0, sl0, :], src0[sr0:sr0 + n0, :], xlk[sl0], writes=[xlk[sl0]])
            layer_norm(xl[:n0, sl0, :], n0, g0, b0, pk0, xlk[sl0])
            C.dma("pool", xcs[dr0:dr0 + n0, :], xl[:n0, sl0, :], xlk[sl0], reads=[xlk[sl0]], writes=[dtk("xcs", dr0)])
    if STOP == 0:
        C.finish()
        return nc

    for l in range(DEPTH):
        lam_init = 0.8 - 0.6 * math.exp(-0.3 * l)
        C.barrier()
        WA_in = WA[:, 0:8 * NIN].rearrange("p (k n) -> p k n", k=8)
        WA_out = WA[:, 8 * NIN:8 * NIN + 8192].rearrange("p (k n) -> p k n", k=8)
        for k in range(8):
            for c0_ in range(0, NIN, 512):
                c1_ = min(NIN, c0_ + 512)
                C.dma("pool", WA_in[:, k, c0_:c1_], w_in[l, k * 128:(k + 1) * 128, c0_:c1_], WAK, writes=[WAK])
        for k in range(8):
            for c0_ in range(0, D, 512):
                C.dma("pool", WA_out[:, k, c0_:c0_ + 512], w_out[l, k * 128:(k + 1) * 128, c0_:c0_ + 512], WAK, writes=[WAK])
        with ExitStack() as st:
            def T(name, shape, dt=F32):
                return st.enter_context(nc.sbuf_tensor("%s_%d" % (name, l), list(shape), dt))
            wa_off = [8 * NIN + 8192]

            def WT(shape):
                n = 1
                for d_ in shape[1:]:
                    n *= d_
                ap = WA[:, wa_off[0]:wa_off[0] + n]
                wa_off[0] += n
                assert wa_off[0] <= 65536, wa_off[0]
                if len(shape) == 3:
                    ap = ap.rearrange("p (a b) -> p a b", a=shape[1])
                return ap
            KT = WT([128, 4, LK])
            ktk = [Tk("kt%d" % i) for i in range((LK + 127) // 128)]
            g1 = T("g1", [128, D]); b1 = T("b1", [128, D])
            pk_ = Tk("params")
            C.dma("sp", g1[:], ln1g[l:l + 1, :].partition_broadcast(128), pk_, writes=[pk_])
            C.dma("sp", b1[:], ln1b[l:l + 1, :].partition_broadcast(128), pk_, writes=[pk_])
            lamt = T("lamt", [128, 4, 64]); lamr = T("lamr", [128, 4]); neglam = T("neglam", [128, 1])
            for i, src in enumerate((lamq1, lamk1, lamq2, lamk2)):
                C.dma("sp", lamt[:, i, :], src[l:l + 1, :].partition_broadcast(128), pk_, writes=[pk_])
            C.op("dve", lambda e: e.tensor_tensor(out=lamt[:, 0, :], in0=lamt[:, 0, :], in1=lamt[:, 1, :], op=ALU.mult),
                 reads=[pk_], writes=[pk_])
            C.op("dve", lambda e: e.tensor_tensor(out=lamt[:, 2, :], in0=lamt[:, 2, :], in1=lamt[:, 3, :], op=ALU.mult),
                 reads=[pk_], writes=[pk_])
            C.op("dve", lambda e: e.tensor_reduce(out=lamr[:, 0:1], in_=lamt[:, 0, :], axis=AX.X, op=ALU.add),
                 reads=[pk_], writes=[pk_])
            C.op("dve", lambda e: e.tensor_reduce(out=lamr[:, 1:2], in_=lamt[:, 2, :], axis=AX.X, op=ALU.add),
                 reads=[pk_], writes=[pk_])
            C.op("act", lambda e: e.activation(out=lamr[:, 2:4], in_=lamr[:, 0:2], func=AF.Exp),
                 reads=[pk_], writes=[pk_])
            C.op("dve", lambda e: e.scalar_tensor_tensor(out=neglam[:], in0=lamr[:, 3:4], scalar=-lam_init,
                                                         in1=lamr[:, 2:3], op0=ALU.add, op1=ALU.subtract),
                 reads=[pk_], writes=[pk_])
            dngc = T("dngc", [128, 1]); dlngc = T("dlngc", [64, 1])
            C.dma("sp", dngc[:], dng[l:l + 1, :].rearrange("o e -> e o"), pk_, writes=[pk_])
            C.dma("sp", dlngc[:], dlng[l:l + 1, :].rearrange("o e -> e o"), pk_, writes=[pk_])
            cw = T("cw", [128, 6, 4])
            for ch in range(6):
                C.dma("sp", cw[:, ch, :], convw[l, :, ch * 128:(ch + 1) * 128].rearrange("j p -> p j"),
                      pk_, writes=[pk_], allow_slow_non_contiguous=True)
            nA = T("nA", [128, 4]); dtbb = T("dtbb", [128, 4])
            C.dma("sp", nA[:], alog[l:l + 1, :].partition_broadcast(128), pk_, writes=[pk_])
            C.dma("sp", dtbb[:], dtb[l:l + 1, :].partition_broadcast(128), pk_, writes=[pk_])
            C.op("act", lambda e: e.activation(out=nA[:], in_=nA[:], func=AF.Exp), reads=[pk_], writes=[pk_])
            C.op("dve", lambda e: e.tensor_scalar(out=nA[:], in0=nA[:], scalar1=-1.0, scalar2=None, op0=ALU.mult),
                 reads=[pk_], writes=[pk_])
            if STOP == 6:
                C.finish()
                return nc

            xres = T("xres", [128, D]); xk = Tk("xres")
            xbf = WT([128, D]); xbfk = Tk()
            kbf = xbf[:, 0:512]; kbfk = xbfk
            xT = WT([128, 8, 256]); xTk = Tk()
            mixT = xT; mixk = xTk
            mixA = WT([128, 4, 256]); mixAk = Tk()
            QTa = WT([128, 4, 256]); QTb = WT([128, 4, 256]); QTk = Tk()
            C.op("pool", lambda e: e.memset(QTa[64:128, :, :], 0.0), writes=[QTk])
            C.op("pool", lambda e: e.memset(QTb[0:64, :, :], 0.0), writes=[QTk])
            rot = T("rot", [128, 128]); rotk = Tk()
            tmpA = T("tmpA", [128, 512]); tmpB = T("tmpB", [128, 512]); tmpk = Tk()
            kout = T("kout", [128, 512]); koutk = Tk()
            vout = T("vout", [128, 512]); voutk = Tk()
            vbf = WT([128, 512]); vbfk = Tk()
            uT = T("uT", [128, 6, 131]); uTk = Tk()
            cT = T("cT", [128, 6, 128]); cTk = Tk()
            cE = T("cE", [128, 6, 128]); cEk = Tk()
            nTh = WT([128, 8, 128]); nTk = Tk()
            qdT = WT([128, 4, 128]); qdk = Tk()
            g4 = T("g4", [128, 264]); g4e = T("g4e", [128, 264]); g4k = Tk()
            sgT = T("sgT", [64, 8, 128]); sgTk = Tk()
            qkc = T("qkc", [128, 512]); qkck = Tk()
            qkcT = WT([128, 8, 128]); qkcTk = Tk()
            sg6 = g4[:, 0:256]; sg6e = g4e[:, 0:256]; sg6k = g4k
            ktok = T("ktok", [64, 2, 512]); ktokk = Tk()
            kz = T("kz", [64, 2, 256], BF16); kzk = Tk()
            vcb = T("vcb", [64, 2, 256], BF16); kzc = T("kzc", [64, 2, 256], BF16); vck = Tk()
            gb = T("gb", [64, 2, 8]); gbk = Tk()
            sm = T("sm", [128, 64]); smk = Tk()
            GT = T("GT", [64, 4, 64]); GTk = Tk()
            Drow = T("Drow", [128, 4, 64]); EDrow = T("EDrow", [128, 4, 64]); Drk = Tk()
            LM = T("LM", [64, 4, 64]); LMs = T("LMs", [64, 4, 64]); LMi = T("LMi", [64, 4, 64]); LMk = Tk()
            Xf = T("Xf", [64, 4, 64]); Xk = Tk()
            Yb = T("Yb", [64, 2, 4, 64], BF16); YTb = T("YTb", [64, 2, 4, 64], BF16); Yk = Tk()
            Pf = T("Pf", [64, 4, 64]); Pb = T("Pb", [64, 4, 64], BF16); Pk = Tk()
            inT = T("inT", [64, 4, 64], BF16); inTk = Tk()
            inR = T("inR", [64, 4, 64], BF16); inRk = Tk()
            rt = T("rt", [64, 256]); rb = T("rb", [64, 256], BF16); rk = Tk()
            scr = T("scr", [64, 4, 64]); scrk = Tk()
            vnb = T("vnb", [64, 256], BF16); vnk = Tk()
            Sd = T("Sd", [64, 4, 64]); Sdb = T("Sdb", [64, 4, 64], BF16); Sdk = Tk()
            Sr = T("Sr", [64, 4, 64]); Srb = T("Srb", [64, 4, 64], BF16); Srk = Tk()
            ot = T("ot", [64, 4, 128]); ot2 = T("ot2", [64, 4, 128]); otb = T("otb", [64, 4, 128], BF16); otk = Tk()
            PT = [WT([128, 2, 256]), WT([128, 2, 256])]; PTk = [Tk(), Tk()]
            Vt = [WT([128, 128]), WT([128, 128]), WT([128, 128])]; Vtk = [Tk(), Tk(), Tk()]
            ckb = WT([128, 512]); ckbk = Tk()
            Rr = tmpA[:, :].rearrange("p (j q) -> p j q", j=2); Tt = tmpB[:, :].rearrange("p (j q) -> p j q", j=2)
            oaf = T("oaf", [128, 256])
            oab = T("oab", [128, 256], BF16); atk = tmpk
            pcst = cE[:3, :, :].rearrange("p a b -> p (a b)"); pcstk = cEk

            def proj(TS, col0, c0, n, kind="pj"):
                pb, pbk = bank(kind)
                for k in range(8):
                    C.op("pe", lambda e: e.matmul(pb[:TS, 0:n], lhsT=xT[:, k, col0:col0 + TS],
                                                  rhs=WA_in[:, k, c0:c0 + n], start=(k == 0), stop=(k == 7)),
                         reads=[xTk, WAK], writes=[pbk])
                return pb, pbk

            def rotary(pb, pbk, TS, outs):
                pv_ = pb[:TS, :].rearrange("p (h d) -> p h d", h=8)
                C.op("dve", lambda e: e.tensor_tensor(out=tmpA[:TS, :].rearrange("p (h d) -> p h d", h=8), in0=pv_,
                                                      in1=bc(rot[:TS, 0:64].unsqueeze(1), [TS, 8, 64]), op=ALU.mult),
                     reads=[pbk, rotk], writes=[tmpk])
                tb = tmpB[:TS, :].rearrange("p (h d) -> p h d", h=8)
                C.op("dve", lambda e: e.tensor_tensor(out=tb[:, :, 0:32], in0=pv_[:, :, 32:64],
                                                      in1=bc(rot[:TS, 64:96].unsqueeze(1), [TS, 8, 32]), op=ALU.mult),
                     reads=[pbk, rotk], writes=[tmpk])
                C.op("dve", lambda e: e.tensor_tensor(out=tb[:, :, 32:64], in0=pv_[:, :, 0:32],
                                                      in1=bc(rot[:TS, 96:128].unsqueeze(1), [TS, 8, 32]), op=ALU.mult),
                     reads=[pbk, rotk], writes=[tmpk])
                for (eng, oap, otk_) in outs:
                    C.op(eng, lambda e: e.tensor_tensor(out=oap, in0=tmpA[:TS, :], in1=tmpB[:TS, :], op=ALU.add),
                         reads=[tmpk], writes=[otk_])

            def transposes_bf(src, srck, TS, n):
                pb, pbk = bank("mi")
                pbb = pb[:].bitcast(BF16)
                for i in range(n):
                    C.op("pe", lambda e: e.transpose(out=pbb[:, i * TS:(i + 1) * TS],
                                                     in_=src[:TS, i * 128:(i + 1) * 128],
                                                     identity=identb[:TS, :TS]),
                         reads=[srck, constk], writes=[pbk])
                return pbb[:, 0:n * TS].rearrange("p (k t) -> p k t", k=n), pbk

            def silu_from(pb_ap, xs_, es_, k_, pbk):
                C.op("act", lambda e: e.activation(out=es_, in_=pb_ap, func=AF.Exp, scale=-1.0), reads=[pbk], writes=[k_])
                C.op("act", lambda e: e.activation(out=xs_, in_=pb_ap, func=AF.Copy), reads=[pbk], writes=[k_])
                C.op("dve", lambda e: e.tensor_scalar(out=es_, in0=es_, scalar1=1.0, scalar2=None, op0=ALU.add),
                     reads=[k_], writes=[k_])
                C.op("dve", lambda e: e.reciprocal(out=es_, in_=es_), reads=[k_], writes=[k_])

            for sq_ in seqs:
                TS, c, L, past, row0, sb_ = sq_["TS"], sq_["c"], sq_["L"], sq_["past"], sq_["row0"], sq_["b"]
                isP = sq_["name"] == "p"
                CPT = TS // c
                TPB = 2 if isP else 1
                nblk = L // (TS * TPB)
                nq = TS * TPB
                nit = 5 if c == 64 else 4
                rsrc = c_rotp if isP else c_rots
                kdst = pk if isP else sk
                vdst = pv if isP else sv
                orow0 = 0 if isP else sb_ * Ls
                zt = zetac if isP else zetasc
                xit = xic if isP else xisc
                gch = [math.exp(math.log(1.0 - 2.0 ** (-5.0 - h)) * c) for h in range(4)]
                if isP:
                    C.op("dve", lambda e: e.memset(Sd[:], 0.0), writes=[Sdk])
                    C.op("dve", lambda e: e.memset(Sr[:], 0.0), writes=[Srk])
                    C.op("dve", lambda e: e.memset(uT[:, :, 0:3], 0.0), writes=[uTk])
                else:
                    C.dma("sp", Sd[:], sdl[l, sb_].rearrange("(h d) e -> d h e", d=64), Sdk, writes=[Sdk])
                    C.dma("sp", Sr[:], srt[l, sb_].rearrange("(h d) e -> d h e", d=64), Srk, writes=[Srk])
                    for ch in range(6):
                        C.dma("sp", uT[:, ch, 0:3], scv[l, sb_, :, ch * 128:(ch + 1) * 128].rearrange("j p -> p j"),
                              uTk, writes=[uTk], allow_slow_non_contiguous=True)
                    for kt in range(past // 128):
                        C.dma("pool", ckb[:], ck[l, sb_, kt * 128:(kt + 1) * 128, :], ckbk, writes=[ckbk])
                        pv4, pv4k = transposes_bf(ckb, ckbk, 128, 4)
                        C.op("dve", lambda e: e.tensor_copy(out=KT[:, :, kt * 128:(kt + 1) * 128], in_=pv4),
                             reads=[pv4k], writes=[ktk[kt]])
                C.op("pool", lambda e: e.tensor_copy(out=Sdb[:], in_=Sd[:]), reads=[Sdk], writes=[Sdk])
                C.op("pool", lambda e: e.tensor_copy(out=Srb[:], in_=Sr[:]), reads=[Srk], writes=[Srk])

                def gen_frontA(ti, t, pjk):
                    r0 = t * TS
                    col0 = ti * TS
                    kcol = past + r0
                    kt_own = kcol // 128
                    xa = xres[:TS, :]
                    C.dma("sp", xa, xcs[row0 + r0:row0 + r0 + TS, :], xk,
                          reads=[dtk("xcs", row0 + r0)], writes=[xk])
                    C.dma("sp", rot[:TS, :], rsrc[r0:r0 + TS, :], rotk, writes=[rotk])
                    C.op("act", lambda e: e.activation(out=xbf[:TS, :], in_=xa, func=AF.Copy), reads=[xk], writes=[xbfk])
                    pv8, pv8k = transposes_bf(xbf, xbfk, TS, 8)
                    C.op("dve", lambda e: e.tensor_copy(out=xT[:, :, col0:col0 + TS], in_=pv8), reads=[pv8k], writes=[xTk])
                    yield
                    pb, pbk = proj(TS, col0, 0, 512, pjk)
                    rotary(pb, pbk, TS, [("pool", kbf[:TS, :], kbfk)])
                    pv4, pv4k = transposes_bf(kbf, kbfk, TS, 4)
                    C.op("dve", lambda e: e.tensor_copy(out=QTa[0:64, :, col0:col0 + TS], in_=pv4[0:64, :, :]),
                         reads=[pv4k], writes=[QTk])
                    C.op("dve", lambda e: e.tensor_copy(out=QTb[64:128, :, col0:col0 + TS], in_=pv4[64:128, :, :]),
                         reads=[pv4k], writes=[QTk])
                    yield
                    pb, pbk = proj(TS, col0, 512, 512, pjk)
                    rotary(pb, pbk, TS, [("pool", kout[:TS, :], koutk), ("pool", kbf[:TS, :], kbfk)])
                    C.dma("pool", kdst[l, orow0 + r0:orow0 + r0 + TS, :], kout[:TS, :], koutk, reads=[koutk])
                    pv4, pv4k = transposes_bf(kbf, kbfk, TS, 4)
                    C.op("dve", lambda e: e.tensor_copy(out=KT[:, :, kcol:kcol + TS], in_=pv4),
                         reads=[pv4k], writes=[ktk[kt_own]])
                    yield
                    pb, pbk = proj(TS, col0, 1024, 512, pjk)
                    C.op("act", lambda e: e.activation(out=vout[:TS, :], in_=pb[:TS, :], func=AF.Copy), reads=[pbk], writes=[voutk])
                    C.op("pool", lambda e: e.tensor_copy(out=vbf[:TS, :], in_=vout[:TS, :]), reads=[voutk], writes=[vbfk])
                    C.dma("pool", vdst[l, orow0 + r0:orow0 + r0 + TS, :], vout[:TS, :], voutk, reads=[voutk])
                    C.dma("pool", vsc[kcol:kcol + TS, :], vbf[:TS, :], vbfk, reads=[vbfk], writes=[dtk("vsc", kt_own)])
                    yield

                def gen_frontB(t, col0):
                    pb, pbk = proj(TS, col0, 2304, 264)
                    silu_from(pb[:TS, 0:260], g4[:TS, 0:260], g4e[:TS, 0:260], g4k, pbk)
                    C.op("dve", lambda e: e.tensor_tensor(out=g4[:TS, 0:256], in0=g4[:TS, 0:256], in1=g4e[:TS, 0:256],
                                                          op=ALU.mult), reads=[g4k], writes=[g4k])
                    C.op("dve", lambda e: e.tensor_tensor(out=g4[:TS, 260:264], in0=pb[:TS, 260:264], in1=dtbb[:TS, :],
                                                          op=ALU.add), reads=[pbk, pk_], writes=[g4k])
                    C.op("act", lambda e: e.activation(out=g4[:TS, 260:264], in_=g4[:TS, 260:264], func=AF.Exp),
                         reads=[g4k], writes=[g4k])
                    C.op("act", lambda e: e.activation(out=g4[:TS, 260:264], in_=g4[:TS, 260:264], func=AF.Ln, bias=1.0),
                         reads=[g4k], writes=[g4k])
                    C.op("dve", lambda e: e.tensor_tensor(out=g4[:TS, 260:264], in0=g4[:TS, 260:264], in1=nA[:TS, :],
                                                          op=ALU.mult), reads=[g4k, pk_], writes=[g4k])
                    for ck_ in range(CPT):
                        C.op("dve", lambda e: e.tensor_copy(out=gb[:c, ck_, 0:4], in_=g4[ck_ * c:(ck_ + 1) * c, 260:264]),
                             reads=[g4k], writes=[gbk])
                        C.op("dve", lambda e: e.tensor_copy(out=gb[:c, ck_, 4:8], in_=g4e[ck_ * c:(ck_ + 1) * c, 256:260]),
                             reads=[g4k], writes=[gbk])
                    pm, pmk = bank("mi")
                    for h in range(4):
                        C.op("pe", lambda e: e.transpose(out=pm[:64, h * TS:(h + 1) * TS], in_=g4[:TS, h * 64:(h + 1) * 64],
                                                         identity=identf[:TS, :TS]),
                             reads=[g4k, constk], writes=[pmk])
                    C.op("act", lambda e: e.activation(out=sgT[:, 0:4, :TS],
                                                       in_=pm[:64, 0:4 * TS].rearrange("p (h t) -> p h t", h=4), func=AF.Copy),
                         reads=[pmk], writes=[sgTk])
                    yield
                    pb, pbk = proj(TS, col0, 2568, 512)
                    rotary(pb, pbk, TS, [("pool", qkc[:TS, :], qkck)])
                    C.op("dve", lambda e: e.tensor_scalar(out=qkc[:TS, 256:512], in0=qkc[:TS, 256:512], scalar1=0.125,
                                                          scalar2=None, op0=ALU.mult), reads=[qkck], writes=[qkck])
                    for half in range(2):
                        pm, pmk = bank("mi")
                        for h in range(4):
                            C.op("pe", lambda e: e.transpose(
                                out=pm[:64, h * TS:(h + 1) * TS],
                                in_=qkc[:TS, half * 256 + h * 64:half * 256 + (h + 1) * 64],
                                identity=identf[:TS, :TS]), reads=[qkck, constk], writes=[pmk])
                        C.op("act", lambda e: e.activation(
                            out=qkcT[0:64, half * 4:half * 4 + 4, :TS],
                            in_=pm[:64, 0:4 * TS].rearrange("p (h t) -> p h t", h=4), func=AF.Copy), reads=[pmk], writes=[qkcTk])
                    for ck_ in range(CPT):
                        C.op("dve", lambda e: e.tensor_tensor(
                            out=kzc[:c, ck_, :].rearrange("p (h d) -> p h d", h=4),
                            in0=qkc[ck_ * c:(ck_ + 1) * c, 256:512].rearrange("p (h d) -> p h d", h=4),
                            in1=bc(zt[ck_ * c:(ck_ + 1) * c, :].unsqueeze(2), [c, 4, 64]), op=ALU.mult),
                             reads=[qkck, constk], writes=[vck])
                    yield
                    pb, pbk = proj(TS, col0, 3080, 512)
                    for ck_ in range(CPT):
                        C.op("act", lambda e: e.activation(out=vcb[:c, ck_, :], in_=pb[ck_ * c:(ck_ + 1) * c, 0:256], func=AF.Copy),
                             reads=[pbk], writes=[vck])
                    silu_from(pb[:TS, 256:512], sg6[:TS, :], sg6e[:TS, :], sg6k, pbk)
                    C.op("dve", lambda e: e.tensor_tensor(out=sg6[:TS, :], in0=sg6[:TS, :], in1=sg6e[:TS, :], op=ALU.mult),
                         reads=[sg6k], writes=[sg6k])
                    pm, pmk = bank("mi")
                    for h in range(4):
                        C.op("pe", lambda e: e.transpose(out=pm[:64, h * TS:(h + 1) * TS], in_=sg6[:TS, h * 64:(h + 1) * 64],
                                                         identity=identf[:TS, :TS]),
                             reads=[sg6k, constk], writes=[pmk])
                    C.op("act", lambda e: e.activation(out=sgT[:, 4:8, :TS],
                                                       in_=pm[:64, 0:4 * TS].rearrange("p (h t) -> p h t", h=4), func=AF.Copy),
                         reads=[pmk], writes=[sgTk])
                    yield
                    for rnd in range(2):
                        pb, pbk = bank("pj")
                        for j in range(3):
                            ch = rnd * 3 + j
                            for k in range(8):
                                C.op("pe", lambda e: e.matmul(
                                    pb[:, j * TS:(j + 1) * TS],
                                    lhsT=WA_in[:, k, 1536 + ch * 128:1536 + (ch + 1) * 128],
                                    rhs=xT[:, k, col0:col0 + TS], start=(k == 0), stop=(k == 7)),
                                     reads=[xTk, WAK], writes=[pbk])
                        C.op("act", lambda e: e.activation(
                            out=uT[:, rnd * 3:rnd * 3 + 3, 3:3 + TS],
                            in_=pb[:, 0:3 * TS].rearrange("p (j t) -> p j t", j=3), func=AF.Copy), reads=[pbk], writes=[uTk])
                        yield
                    yield
                    for ch in range(6):
                        C.op("dve", lambda e: e.tensor_scalar(out=cT[:, ch, :TS], in0=uT[:, ch, 0:TS],
                                                              scalar1=cw[:, ch, 0:1], scalar2=None, op0=ALU.mult),
                             reads=[uTk, pk_], writes=[cTk])
                        for j in range(1, 4):
                            C.op("dve", lambda e: e.scalar_tensor_tensor(
                                out=cT[:, ch, :TS], in0=uT[:, ch, j:j + TS], scalar=cw[:, ch, j:j + 1],
                                in1=cT[:, ch, :TS], op0=ALU.mult, op1=ALU.add), reads=[uTk, pk_, cTk], writes=[cTk])
                    yield
                    if t == L // TS - 1:
                        for (c0_, c1_) in ((0, 4), (4, 6)):
                            pm, pmk = bank("mi")
                            for ch in range(c0_, c1_):
                                C.op("pe", lambda e: e.transpose(out=pm[:3, (ch - c0_) * 128:(ch - c0_ + 1) * 128],
                                                                 in_=uT[:, ch, TS:TS + 3], identity=identf[:, :]),
                                     reads=[uTk, constk], writes=[pmk])
                            C.op("act", lambda e: e.activation(out=pcst[:, c0_ * 128:c1_ * 128], in_=pm[:3, 0:(c1_ - c0_) * 128],
                                                               func=AF.Copy), reads=[pmk], writes=[pcstk])
                        cdst = pcv[l] if isP else scvo[l, sb_]
                        C.dma("pool", cdst, pcst[:, :], pcstk, reads=[pcstk])
                    C.op("pool", lambda e: e.tensor_copy(out=uT[:, :, 0:3], in_=uT[:, :, TS:TS + 3]),
                         reads=[uTk], writes=[uTk])
                    yield
                    C.op("act", lambda e: e.activation(out=cE[:, :, :TS], in_=cT[:, :, :TS], func=AF.Exp, scale=-1.0),
                         reads=[cTk], writes=[cEk])
                    C.op("dve", lambda e: e.tensor_scalar(out=cE[:, :, :TS], in0=cE[:, :, :TS], scalar1=1.0, scalar2=None,
                                                          op0=ALU.add), reads=[cEk], writes=[cEk])
                    C.op("dve", lambda e: e.reciprocal(out=cE[:, :, :TS], in_=cE[:, :, :TS]), reads=[cEk], writes=[cEk])
                    C.op("dve", lambda e: e.tensor_tensor(out=cT[:, :, :TS], in0=cT[:, :, :TS], in1=cE[:, :, :TS], op=ALU.mult),
                         reads=[cEk, cTk], writes=[cTk])
                    yield
                    C.op("pool", lambda e: e.tensor_tensor(out=cE[:, 0:4, :TS], in0=cT[:, 0:4, :TS], in1=cT[:, 0:4, :TS],
                                                           op=ALU.mult), reads=[cTk, cEk], writes=[cEk])
                    pm, pmk = bank("mi")
                    for j in range(4):
                        C.op("pe", lambda e: e.matmul(pm[:, j * TS:(j + 1) * TS], lhsT=bonesf[:, :], rhs=cE[:, j, :TS],
                                                      start=True, stop=True), reads=[cEk, constk], writes=[pmk])
                    C.op("act", lambda e: e.activation(out=cE[:, 0:4, :TS],
                                                       in_=pm[:, 0:4 * TS].rearrange("p (j t) -> p j t", j=4),
                                                       func=AF.Ln, bias=epsb[:, 1:2]),
                         reads=[pmk, constk], writes=[cEk])
                    C.op("act", lambda e: e.activation(out=cE[:, 0:4, :TS], in_=cE[:, 0:4, :TS], func=AF.Exp, scale=-0.5),
                         reads=[cEk], writes=[cEk])
                    yield
                    C.op("dve", lambda e: e.scalar_tensor_tensor(out=cT[:, 0:2, :TS], in0=cT[:, 0:2, :TS], scalar=0.125,
                                                                 in1=cE[:, 0:2, :TS], op0=ALU.mult, op1=ALU.mult),
                         reads=[cTk, cEk], writes=[cTk])
                    C.op("dve", lambda e: e.tensor_tensor(out=cT[:, 2:4, :TS], in0=cT[:, 2:4, :TS], in1=cE[:, 2:4, :TS],
                                                          op=ALU.mult), reads=[cTk, cEk], writes=[cTk])
                    for j in range(4):
                        for s_ in range(2):
                            hidx = (j // 2) * 4 + (j % 2) * 2 + s_
                            C.op("pool", lambda e: e.tensor_copy(out=nTh[0:64, hidx, :TS], in_=cT[s_ * 64:(s_ + 1) * 64, j, :TS]),
                                 reads=[cTk], writes=[nTk])
                    yield
                    for ck_ in range(CPT):
                        pm, pmk = bank("mi")
                        for j in range(4):
                            C.op("pe", lambda e: e.transpose(
                                out=pm[:c, j * 128:(j + 1) * 128], in_=cT[:, 2 + j, ck_ * c:(ck_ + 1) * c],
                                identity=identf[:, :]), reads=[cTk, constk], writes=[pmk])
                        C.op("act", lambda e: e.activation(out=ktok[:c, ck_, :], in_=pm[:c, :], func=AF.Copy),
                             reads=[pmk], writes=[ktokk])
                    yield

                def gen_chunks(col0):
                    po, pok = bank("pj")
                    po2, po2k = bank("pj")
                    for ck_ in range(CPT):
                        cs = slice(ck_ * c, (ck_ + 1) * c)
                        pm, pmk = bank("mi")
                        C.op("pe", lambda e: e.matmul(pm[:c, 0:4], lhsT=triu[:c, :c], rhs=gb[:c, ck_, 0:4],
                                                      start=True, stop=True), reads=[gbk, constk], writes=[pmk])
                        C.op("dve", lambda e: e.tensor_tensor(out=GT[:c, :, :c],
                                                              in0=bc(triu[:c, :c].unsqueeze(1), [c, 4, c]),
                                                              in1=bc(gb[:c, ck_, 0:4].unsqueeze(2), [c, 4, c]),
                                                              op=ALU.mult), reads=[gbk, constk], writes=[GTk])
                        pm2, pm2k = bank("mi")
                        for h in range(4):
                            C.op("pe", lambda e: e.matmul(pm2[:, h * c:(h + 1) * c], lhsT=onesf[:c, :],
                                                          rhs=GT[:c, h, :c], start=True, stop=True),
                                 reads=[GTk, constk], writes=[pm2k])
                        C.op("act", lambda e: e.activation(out=sm[:c, 0:4], in_=pm[:c, 0:4], func=AF.Copy), reads=[pmk], writes=[smk])
                        C.op("act", lambda e: e.activation(out=sm[:c, 4:8], in_=pm[:c, 0:4], func=AF.Exp), reads=[pmk], writes=[smk])
                        C.op("act", lambda e: e.activation(out=Drow[:, :, :c], in_=pm2[:, 0:4 * c].rearrange("p (h i) -> p h i", h=4),
                                                           func=AF.Copy), reads=[pm2k], writes=[Drk])
                        C.op("act", lambda e: e.activation(out=EDrow[:, :, :c],
                                                           in_=pm2[:, 0:4 * c].rearrange("p (h i) -> p h i", h=4), func=AF.Exp),
                             reads=[pm2k], writes=[Drk])
                        C.op("dve", lambda e: e.tensor_tensor(out=sm[:c, 12:16], in0=Drow[:c, :, c - 1], in1=sm[:c, 0:4],
                                                              op=ALU.subtract), reads=[Drk, smk], writes=[smk])
                        C.op("act", lambda e: e.activation(out=sm[:c, 8:12], in_=sm[:c, 12:16], func=AF.Exp), reads=[smk], writes=[smk])
                        yield
                        C.op("dve", lambda e: e.tensor_tensor(out=LM[:c, :, :c], in0=Drow[:c, :, :c],
                                                              in1=bc(sm[:c, 0:4].unsqueeze(2), [c, 4, c]), op=ALU.subtract),
                             reads=[Drk, smk], writes=[LMk])
                        C.op("dve", lambda e: e.tensor_scalar(out=LM[:c, :, :c], in0=LM[:c, :, :c], scalar1=0.0, scalar2=None,
                                                              op0=ALU.min), reads=[LMk], writes=[LMk])
                        C.op("act", lambda e: e.activation(out=LM[:c, :, :c], in_=LM[:c, :, :c], func=AF.Exp), reads=[LMk], writes=[LMk])
                        C.op("pool", lambda e: e.tensor_tensor(out=LMs[:c, :, :c], in0=LM[:c, :, :c],
                                                               in1=bc(striu[:c, :c].unsqueeze(1), [c, 4, c]), op=ALU.mult),
                             reads=[LMk, constk], writes=[LMk])
                        C.op("pool", lambda e: e.tensor_tensor(out=LMi[:c, :, :c], in0=LM[:c, :, :c],
                                                               in1=bc(triu[:c, :c].unsqueeze(1), [c, 4, c]), op=ALU.mult),
                             reads=[LMk, constk], writes=[LMk])
                        C.op("dve", lambda e: e.tensor_tensor(out=qdT[0:64, :, cs], in0=nTh[0:64, 0:4, cs],
                                                              in1=EDrow[0:64, :, :c], op=ALU.mult), reads=[nTk, Drk], writes=[qdk])
                        C.op("dve", lambda e: e.tensor_tensor(out=kz[:c, ck_, :].rearrange("p (h d) -> p h d", h=4),
                                                              in0=ktok[:c, ck_, 0:256].rearrange("p (h d) -> p h d", h=4),
                                                              in1=bc(sm[:c, 8:12].unsqueeze(2), [c, 4, 64]), op=ALU.mult),
                             reads=[ktokk, smk], writes=[kzk])
                        yield
                        pg, pgk = bank("mi")
                        for h in range(4):
                            C.op("pe", lambda e: e.matmul(pg[:c, h * c:(h + 1) * c], lhsT=nTh[0:64, 4 + h, cs], rhs=nTh[0:64, 4 + h, cs],
                                                          start=True, stop=True), reads=[nTk], writes=[pgk])
                        for h in range(4):
                            C.op("pe", lambda e: e.matmul(pg[:c, 256 + h * c:256 + (h + 1) * c], lhsT=nTh[0:64, 4 + h, cs],
                                                          rhs=nTh[0:64, h, cs], start=True, stop=True), reads=[nTk], writes=[pgk])
                        KKv = pg[:c, 0:4 * c].rearrange("p (h i) -> p h i", h=4)
                        KQv = pg[:c, 256:256 + 4 * c].rearrange("p (h i) -> p h i", h=4)
                        C.op("dve", lambda e: e.tensor_tensor(out=Xf[:c, :, :c], in0=KKv,
                                                              in1=bc(gb[:c, ck_, 4:8].unsqueeze(2), [c, 4, c]), op=ALU.mult),
                             reads=[pgk, gbk], writes=[Xk])
                        C.op("dve", lambda e: e.tensor_tensor(out=Xf[:c, :, :c], in0=Xf[:c, :, :c], in1=LMs[:c, :, :c], op=ALU.mult),
                             reads=[Xk, LMk], writes=[Xk])
                        C.op("dve", lambda e: e.tensor_tensor(out=scr[:c, :, :c], in0=KQv, in1=LMi[:c, :, :c], op=ALU.mult),
                             reads=[pgk, LMk], writes=[scrk])
                        C.op("pool", lambda e: e.tensor_copy(out=inT[:c, :, :c], in_=scr[:c, :, :c]), reads=[scrk], writes=[inTk])
                        yield
                        pm, pmk = bank("mi")
                        for h in range(4):
                            C.op("pe", lambda e: e.transpose(out=pm[:c, h * c:(h + 1) * c], in_=Xf[:c, h, :c],
                                                             identity=identf[:c, :c]), reads=[Xk, constk], writes=[pmk])
                        C.op("act", lambda e: e.activation(out=YTb[:c, 0, :, :c], in_=pm[:c, 0:4 * c].rearrange("p (h i) -> p h i", h=4),
                                                           func=AF.Copy), reads=[pmk], writes=[Yk])
                        C.op("act", lambda e: e.activation(out=Yb[:c, 0, :, :c], in_=Xf[:c, :, :c], func=AF.Copy), reads=[Xk], writes=[Yk])
                        C.op("dve", lambda e: e.tensor_tensor(out=Pf[:c, :, :c], in0=bc(identf[:c, :c].unsqueeze(1), [c, 4, c]),
                                                              in1=Xf[:c, :, :c], op=ALU.subtract), reads=[Xk, constk], writes=[Pk])
                        C.op("pool", lambda e: e.tensor_copy(out=Pb[:c, :, :c], in_=Pf[:c, :, :c]), reads=[Pk], writes=[Pk])
                        yield
                        cur = 0
                        for it in range(nit):
                            nxt = 1 - cur
                            pm, pmk = bank("mi")
                            for h in range(4):
                                C.op("pe", lambda e: e.matmul(pm[:c, h * c:(h + 1) * c], lhsT=YTb[:c, cur, h, :c],
                                                              rhs=Yb[:c, cur, h, :c], start=True, stop=True),
                                     reads=[Yk], writes=[pmk])
                            for h in range(4):
                                C.op("pe", lambda e: e.matmul(pm[:c, 256 + h * c:256 + (h + 1) * c],
                                                              lhsT=Yb[:c, cur, h, :c], rhs=YTb[:c, cur, h, :c],
                                                              start=True, stop=True), reads=[Yk], writes=[pmk])
                            C.op("act", lambda e: e.activation(out=Yb[:c, nxt, :, :c],
                                                               in_=pm[:c, 0:4 * c].rearrange("p (h i) -> p h i", h=4), func=AF.Copy),
                                 reads=[pmk], writes=[Yk])
                            C.op("act", lambda e: e.activation(out=YTb[:c, nxt, :, :c],
                                                               in_=pm[:c, 256:256 + 4 * c].rearrange("p (h i) -> p h i", h=4), func=AF.Copy),
                                 reads=[pmk], writes=[Yk])
                            yield
                            pm, pmk = bank("mi")
                            for h in range(4):
                                C.op("pe", lambda e: e.matmul(pm[:c, h * c:(h + 1) * c], lhsT=YTb[:c, nxt, h, :c],
                                                              rhs=Pb[:c, h, :c], start=True, stop=True),
                                     reads=[Yk, Pk], writes=[pmk])
                            C.op("dve", lambda e: e.tensor_tensor(out=Pf[:c, :, :c], in0=Pf[:c, :, :c],
                                                                  in1=pm[:c, 0:4 * c].rearrange("p (h i) -> p h i", h=4), op=ALU.add),
                                 reads=[pmk, Pk], writes=[Pk])
                            C.op("pool", lambda e: e.tensor_copy(out=Pb[:c, :, :c], in_=Pf[:c, :, :c]), reads=[Pk], writes=[Pk])
                            cur = nxt
                            yield
                        pc, pck = bank("mi")
                        for h in range(4):
                            C.op("pe", lambda e: e.matmul(pc[:c, h * 64:(h + 1) * 64], lhsT=nTh[0:64, 4 + h, cs],
                                                          rhs=Sdb[:, h, :], start=True, stop=True),
                                 reads=[nTk, Sdk], writes=[pck])
                        C.op("dve", lambda e: e.tensor_tensor(out=rt[:c, :].rearrange("p (h d) -> p h d", h=4),
                                                              in0=pc[:c, 0:256].rearrange("p (h d) -> p h d", h=4),
                                                              in1=bc(sm[:c, 4:8].unsqueeze(2), [c, 4, 64]), op=ALU.mult),
                             reads=[pck, smk], writes=[rk])
                        C.op("dve", lambda e: e.tensor_tensor(out=rb[:c, :], in0=ktok[:c, ck_, 256:512], in1=rt[:c, :], op=ALU.subtract),
                             reads=[rk, ktokk], writes=[rk])
                        yield
                        pc2, pc2k = bank("mi")
                        for h in range(4):
                            C.op("pe", lambda e: e.matmul(pc2[:c, h * 64:(h + 1) * 64], lhsT=Pb[:c, h, :c],
                                                          rhs=rb[:c, h * 64:(h + 1) * 64], start=True, stop=True),
                                 reads=[Pk, rk], writes=[pc2k])
                        C.op("dve", lambda e: e.tensor_tensor(out=rt[:c, :].rearrange("p (h d) -> p h d", h=4),
                                                              in0=pc2[:c, 0:256].rearrange("p (h d) -> p h d", h=4),
                                                              in1=bc(gb[:c, ck_, 4:8].unsqueeze(2), [c, 4, 64]), op=ALU.mult),
                             reads=[pc2k, gbk], writes=[rk])
                        C.op("pool", lambda e: e.tensor_copy(out=vnb[:c, :], in_=rt[:c, :]), reads=[rk], writes=[vnk])
                        yield
                        for h in range(4):
                            oc_ = slice(h * TS + ck_ * c, h * TS + (ck_ + 1) * c)
                            C.op("pe", lambda e: e.matmul(po[:64, oc_], lhsT=Sdb[:, h, :], rhs=qdT[0:64, h, cs],
                                                          start=True, stop=False), reads=[Sdk, qdk], writes=[pok])
                            C.op("pe", lambda e: e.matmul(po[:64, oc_], lhsT=vnb[:c, h * 64:(h + 1) * 64], rhs=inT[:c, h, :c],
                                                          start=False, stop=True), reads=[vnk, inTk], writes=[pok])
                        psu, psuk = bank("mi")
                        for h in range(4):
                            C.op("pe", lambda e: e.matmul(psu[:64, h * 64:(h + 1) * 64], lhsT=kz[:c, ck_, h * 64:(h + 1) * 64],
                                                          rhs=vnb[:c, h * 64:(h + 1) * 64], start=True, stop=True),
                                 reads=[kzk, vnk], writes=[psuk])
                        for h in range(4):
                            C.op("dve", lambda e: e.scalar_tensor_tensor(
                                out=Sd[:, h, :], in0=Sd[:, h, :], scalar=EDrow[0:64, h, c - 1:c], in1=psu[:64, h * 64:(h + 1) * 64],
                                op0=ALU.mult, op1=ALU.add), reads=[Drk, psuk, Sdk], writes=[Sdk])
                        C.op("pool", lambda e: e.tensor_copy(out=Sdb[:], in_=Sd[:]), reads=[Sdk], writes=[Sdk])
                        yield
                        pr, prk = bank("mi")
                        for h in range(4):
                            C.op("pe", lambda e: e.matmul(pr[:c, h * c:(h + 1) * c], lhsT=qkcT[0:64, 4 + h, cs], rhs=qkcT[0:64, h, cs],
                                                          start=True, stop=True), reads=[qkcTk], writes=[prk])
                        C.op("dve", lambda e: e.tensor_tensor(out=scr[:c, :, :c], in0=pr[:c, 0:4 * c].rearrange("p (h i) -> p h i", h=4),
                                                              in1=rmc[:c, :].rearrange("p (h i) -> p h i", h=4)[:, :, :c], op=ALU.mult),
                             reads=[prk, constk], writes=[scrk])
                        C.op("pool", lambda e: e.tensor_copy(out=inR[:c, :, :c], in_=scr[:c, :, :c]), reads=[scrk], writes=[inRk])
                        yield
                        for h in range(4):
                            oc_ = slice(h * TS + ck_ * c, h * TS + (ck_ + 1) * c)
                            C.op("pe", lambda e: e.matmul(po2[:64, oc_], lhsT=Srb[:, h, :], rhs=qkcT[0:64, h, cs],
                                                          start=True, stop=False), reads=[Srk, qkcTk], writes=[po2k])
                            C.op("pe", lambda e: e.matmul(po2[:64, oc_], lhsT=vcb[:c, ck_, h * 64:(h + 1) * 64], rhs=inR[:c, h, :c],
                                                          start=False, stop=True), reads=[vck, inRk], writes=[po2k])
                        psu, psuk = bank("mi")
                        for h in range(4):
                            C.op("pe", lambda e: e.matmul(psu[:64, h * 64:(h + 1) * 64], lhsT=kzc[:c, ck_, h * 64:(h + 1) * 64],
                                                          rhs=vcb[:c, ck_, h * 64:(h + 1) * 64], start=True, stop=True),
                                 reads=[vck], writes=[psuk])
                        for h in range(4):
                            C.op("dve", lambda e: e.scalar_tensor_tensor(out=Sr[:, h, :], in0=Sr[:, h, :], scalar=float(gch[h]),
                                                                         in1=psu[:64, h * 64:(h + 1) * 64], op0=ALU.mult, op1=ALU.add),
                                 reads=[psuk, Srk], writes=[Srk])
                        C.op("pool", lambda e: e.tensor_copy(out=Srb[:], in_=Sr[:]), reads=[Srk], writes=[Srk])

                    yield
                    for which in range(2):
                        pso, psok = (po, pok) if which == 0 else (po2, po2k)
                        pov = pso[:64, 0:4 * TS].rearrange("p (h t) -> p h t", h=4)
                        if which == 0:
                            C.op("act", lambda e: e.activation(out=ot[:, :, :TS], in_=pov, func=AF.Copy), reads=[psok], writes=[otk])
                        else:
                            C.op("dve", lambda e: e.tensor_tensor(out=ot[:, :, :TS], in0=pov,
                                                                  in1=xit[:, :].rearrange("p (h t) -> p h t", h=4)[:, :, :TS], op=ALU.mult),
                                 reads=[psok, constk], writes=[otk])
                        C.op("pool", lambda e: e.tensor_tensor(out=otb[:, :, :TS], in0=ot[:, :, :TS], in1=ot[:, :, :TS], op=ALU.mult),
                             reads=[otk], writes=[otk])
                        yield
                        pm, pmk = bank("mi")
                        for h in range(4):
                            C.op("pe", lambda e: e.matmul(pm[:64, h * TS:(h + 1) * TS], lhsT=onesb[:64, :64], rhs=otb[:, h, :TS],
                                                          start=True, stop=True), reads=[otk, constk], writes=[pmk])
                        C.op("act", lambda e: e.activation(out=ot2[:, :, :TS], in_=pm[:64, 0:4 * TS].rearrange("p (h t) -> p h t", h=4),
                                                           func=AF.Ln, scale=1.0 / 64, bias=epsb[:64, 1:2]),
                             reads=[pmk, constk], writes=[otk])
                        C.op("act", lambda e: e.activation(out=ot2[:, :, :TS], in_=ot2[:, :, :TS], func=AF.Exp, scale=-0.5),
                             reads=[otk], writes=[otk])
                        yield
                        C.op("dve", lambda e: e.tensor_tensor(out=ot[:, :, :TS], in0=ot[:, :, :TS], in1=ot2[:, :, :TS], op=ALU.mult),
                             reads=[otk], writes=[otk])
                        C.op("pool", lambda e: e.tensor_tensor(out=ot[:, :, :TS], in0=ot[:, :, :TS],
                                                               in1=sgT[:, which * 4:which * 4 + 4, :TS], op=ALU.mult),
                             reads=[otk, sgTk], writes=[otk])
                        for h in range(4):
                            p_, s_ = divmod(h, 2)
                            dst = mixT[s_ * 64:(s_ + 1) * 64, 4 + which * 2 + p_, col0:col0 + TS]
                            if which == 0:
                                C.op("dve", lambda e: e.tensor_scalar(out=dst, in0=ot[:, h, :TS], scalar1=dlngc[:, 0:1],
                                                                      scalar2=None, op0=ALU.mult),
                                     reads=[otk, pk_], writes=[mixk])
                            else:
                                C.op("pool", lambda e: e.tensor_copy(out=dst, in_=ot[:, h, :TS]), reads=[otk], writes=[mixk])

                    yield

                def back_tile(ti, t):
                    r0 = t * TS
                    col0 = ti * TS
                    C.dma("sp", xres[:TS, :], xcs[row0 + r0:row0 + r0 + TS, :], xk,
                          reads=[dtk("xcs", row0 + r0)], writes=[xk])
                    for half in range(2):
                        pb, pbk = bank("pj")
                        for k in range(8):
                            C.op("pe", lambda e: e.matmul(pb[:TS, :], lhsT=(mixA if k < 4 else mixT)[:, k, col0:col0 + TS],
                                                          rhs=WA_out[:, k, half * 512:(half + 1) * 512], start=(k == 0), stop=(k == 7)),
                                 reads=[mixk, mixAk, WAK], writes=[pbk])
                        C.op("dve", lambda e: e.scalar_tensor_tensor(out=xres[:TS, half * 512:(half + 1) * 512],
                                                                     in0=xres[:TS, half * 512:(half + 1) * 512], scalar=float(ALPHA),
                                                                     in1=pb[:TS, :], op0=ALU.mult, op1=ALU.add),
                             reads=[pbk, xk], writes=[xk])
                    layer_norm(xres[:TS, :], TS, g1, b1, pk_, xk)
                    C.dma("pool", x1s[row0 + r0:row0 + r0 + TS, :], xres[:TS, :], xk, reads=[xk], writes=[dtk("x1s", row0 + r0)])


                def gen_attention(blk):
                    if isP:
                        kts = [(kt, 128, None) for kt in range(2 * blk)] + [(2 * blk, 128, 0), (2 * blk + 1, 128, 1)]
                    else:
                        kts = [(kt, 128, None) for kt in range(past // 128)] + [(past // 128, TS, None)]
                    its = [(hd, kt, rows, msk) for hd in range(4) for (kt, rows, msk) in kts]
                    nk = len(kts)

                    def load_v(ii):
                        hd, kt, rows, msk = its[ii]
                        vt, vtk = Vt[ii % 3], Vtk[ii % 3]
                        if kt < past // 128:
                            C.dma("pool", vt[:rows, :], cv[l, sb_, kt * 128:kt * 128 + rows, hd * 128:(hd + 1) * 128], vtk, writes=[vtk])
                        else:
                            C.dma("sp", vt[:rows, :], vsc[kt * 128:kt * 128 + rows, hd * 128:(hd + 1) * 128], vtk,
                                  reads=[dtk("vsc", kt)], writes=[vtk])

                    def emit_s(ii):
                        hd, kt, rows, msk = its[ii]
                        sbk_, sbkk = PS[2 + ii % 2], PSK[2 + ii % 2]
                        kc0 = kt * 128
                        C.op("pe", lambda e: e.matmul(sbk_[:rows, 0:nq], lhsT=KT[:, hd, kc0:kc0 + rows], rhs=QTa[:, hd, 0:nq],
                                                      start=True, stop=True), reads=[ktk[kt], QTk], writes=[sbkk])
                        C.op("pe", lambda e: e.matmul(sbk_[:rows, 256:256 + nq], lhsT=KT[:, hd, kc0:kc0 + rows],
                                                      rhs=QTb[:, hd, 0:nq], start=True, stop=True),
                             reads=[ktk[kt], QTk], writes=[sbkk])
                    load_v(0)
                    if len(its) > 1:
                        load_v(1)
                    emit_s(0)
                    for ii, (hd, kt, rows, msk) in enumerate(its):
                        if ii + 2 < len(its):
                            load_v(ii + 2)
                        if ii + 1 < len(its):
                            emit_s(ii + 1)
                        first = (ii % nk == 0)
                        last = (ii % nk == nk - 1)
                        sbk_, sbkk = PS[2 + ii % 2], PSK[2 + ii % 2]
                        pt, ptk = PT[ii % 2], PTk[ii % 2]
                        C.op("act", lambda e: e.activation(out=pt[:rows, :, 0:nq],
                                                           in_=sbk_[:rows, :].rearrange("p (j q) -> p j q", j=2)[:, :, 0:nq],
                                                           func=AF.Exp, scale=0.125), reads=[sbkk], writes=[ptk])
                        if msk is not None:
                            C.op("pool", lambda e: e.tensor_tensor(out=pt[:rows, :, 0:nq], in0=pt[:rows, :, 0:nq],
                                                                   in1=bc(amask[:rows, msk * 256:msk * 256 + nq].unsqueeze(1), [rows, 2, nq]),
                                                                   op=ALU.mult), reads=[ptk, constk], writes=[ptk])
                        if first:
                            C.op("dve", lambda e: e.memset(OB[:, :], 0.0), writes=[OBK])
                            C.op("dve", lambda e: e.memset(LB[:, :], 0.0), writes=[LBK])
                        vt, vtk = Vt[ii % 3], Vtk[ii % 3]
                        for j in range(2):
                            C.op("pe", lambda e: e.matmul(OB[:, j * 256:j * 256 + nq], lhsT=vt[:rows, :],
                                                          rhs=pt[:rows, j, 0:nq], start=False, stop=False, skip_group_check=True),
                                 reads=[vtk, ptk], writes=[OBK])
                            C.op("pe", lambda e: e.matmul(LB[:, j * 256:j * 256 + nq], lhsT=onesb[:rows, :], rhs=pt[:rows, j, 0:nq],
                                                          start=False, stop=False, skip_group_check=True),
                                 reads=[constk, ptk], writes=[LBK])
                        if last:
                            Lv = LB[:, :].rearrange("p (j q) -> p j q", j=2)[:, :, 0:nq]
                            Ov = OB[:, :].rearrange("p (j q) -> p j q", j=2)[:, :, 0:nq]
                            C.op("dve", lambda e: e.reciprocal(out=Rr[:, :, 0:nq], in_=Lv), reads=[LBK], writes=[atk])
                            C.op("dve", lambda e: e.tensor_tensor(out=Tt[:, :, 0:nq], in0=Ov, in1=Rr[:, :, 0:nq], op=ALU.mult),
                                 reads=[OBK, atk], writes=[atk])
                            C.op("dve", lambda e: e.scalar_tensor_tensor(out=oaf[:, 0:nq], in0=Tt[:, 1, 0:nq], scalar=neglam[:, 0:1],
                                                                         in1=Tt[:, 0, 0:nq], op0=ALU.mult, op1=ALU.add),
                                 reads=[atk, pk_], writes=[atk])
                            C.op("pool", lambda e: e.tensor_tensor(out=oab[:, 0:nq], in0=oaf[:, 0:nq], in1=oaf[:, 0:nq], op=ALU.mult),
                                 reads=[atk], writes=[atk])
                            yield
                            pm, pmk = bank("mi")
                            C.op("pe", lambda e: e.matmul(pm[:, 0:nq], lhsT=onesb[:, :], rhs=oab[:, 0:nq], start=True, stop=True),
                                 reads=[atk, constk], writes=[pmk])
                            C.op("act", lambda e: e.activation(out=Rr[:, 0, 0:nq], in_=pm[:, 0:nq], func=AF.Ln, scale=1.0 / 128,
                                                               bias=epsb[:, 1:2]), reads=[pmk, constk], writes=[atk])
                            C.op("act", lambda e: e.activation(out=Rr[:, 0, 0:nq], in_=Rr[:, 0, 0:nq], func=AF.Exp, scale=-0.5),
                                 reads=[atk], writes=[atk])
                            C.op("dve", lambda e: e.tensor_tensor(out=oaf[:, 0:nq], in0=oaf[:, 0:nq], in1=Rr[:, 0, 0:nq], op=ALU.mult),
                                 reads=[atk], writes=[atk])
                            C.op("dve", lambda e: e.tensor_scalar(out=mixA[:, hd, 0:nq], in0=oaf[:, 0:nq], scalar1=dngc[:, 0:1],
                                                                  scalar2=float(1.0 - lam_init), op0=ALU.mult, op1=ALU.mult),
                                 reads=[atk, pk_], writes=[mixAk])
                        yield

                def run_gens(*gens):
                    live = list(gens)
                    while live:
                        for g in list(live):
                            try:
                                next(g)
                            except StopIteration:
                                live.remove(g)

                def chain(*gs):
                    for g_ in gs:
                        yield from g_

                for blk in range(nblk):
                    t0_ = blk * TPB
                    run_gens(gen_frontA(0, t0_, "pj"), gen_frontB(t0_, 0))
                    if TPB == 2:
                        run_gens(gen_chunks(0), gen_frontA(1, t0_ + 1, "s"))
                        run_gens(gen_attention(blk), chain(gen_frontB(t0_ + 1, TS), gen_chunks(TS)))
                    else:
                        run_gens(gen_attention(blk), gen_chunks(0))
                    for ti in range(TPB):
                        back_tile(ti, t0_ + ti)

                if STOP == 1:
                    C.finish()
                    return nc
                ddst = pdl[l] if isP else sdlo[l, sb_]
                rdst = prt[l] if isP else srto[l, sb_]
                C.dma("pool", ddst.rearrange("(h d) e -> d h e", d=64), Sd[:], Sdk, reads=[Sdk])
                C.dma("pool", rdst.rearrange("(h d) e -> d h e", d=64), Sr[:], Srk, reads=[Srk])

        if STOP == 20 + l:
            C.finish()
            return nc
        C.barrier()
        WA_up = WA[:, 0:8 * DFF].rearrange("p (k n) -> p k n", k=8)
        WA_dn = WA[:, 8 * DFF:8 * DFF + 32 * D].rearrange("p (k n) -> p k n", k=32)
        for k in range(8):
            for c0_ in range(0, DFF, 512):
                C.dma("pool", WA_up[:, k, c0_:c0_ + 512], w_up[l, k * 128:(k + 1) * 128, c0_:c0_ + 512], WAK, writes=[WAK])
        for k in range(32):
            for c0_ in range(0, D, 512):
                C.dma("pool", WA_dn[:, k, c0_:c0_ + 512], w_down[l, k * 128:(k + 1) * 128, c0_:c0_ + 512], WAK, writes=[WAK])
        with ExitStack() as st:
            def T(name, shape, dt=F32):
                return st.enter_context(nc.sbuf_tensor("%s_f%d" % (name, l), list(shape), dt))
            g2 = T("g2", [128, D]); b2 = T("b2", [128, D]); pk2 = Tk()
            C.dma("sp", g2[:], ln2g[l:l + 1, :].partition_broadcast(128), pk2, writes=[pk2])
            C.dma("sp", b2[:], ln2b[l:l + 1, :].partition_broadcast(128), pk2, writes=[pk2])
            x1 = T("x1", [128, 2, D]); x1k = [Tk(), Tk()]
            x1b = T("x1b", [128, D], BF16); x1bk = Tk()
            x1T = T("x1T", [128, 8, 256], BF16); x1Tk = Tk()
            hidT = T("hidT", [128, 32, 256], BF16); hidk = Tk()
            rl = [T("rl0", [128, 256]), T("rl1", [128, 256])]; rlk = [Tk(), Tk()]
            blocks = []
            for b0_ in range(0, Lp, 256):
                blocks.append((b0_, 128, min(2, (Lp - b0_) // 128)))
            for b_ in range(NS):
                blocks.append((Lp + b_ * Ls, Ls, 1))
            for (rb0, TS, ntl) in blocks:
                nb = TS * ntl
                for ti in range(ntl):
                    r0 = rb0 + ti * TS
                    C.dma("sp", x1[:TS, ti, :], x1s[r0:r0 + TS, :], x1k[ti], reads=[dtk("x1s", r0)], writes=[x1k[ti]])
                    C.op("act", lambda e: e.activation(out=x1b[:TS, :], in_=x1[:TS, ti, :], func=AF.Copy), reads=[x1k[ti]], writes=[x1bk])
                    pb, pbk = bank("mi")
                    pbb = pb[:].bitcast(BF16)
                    for k in range(8):
                        C.op("pe", lambda e: e.transpose(out=pbb[:, k * TS:(k + 1) * TS], in_=x1b[:TS, k * 128:(k + 1) * 128],
                                                         identity=identb[:TS, :TS]), reads=[x1bk, constk], writes=[pbk])
                    C.op("dve", lambda e: e.tensor_copy(out=x1T[:, :, ti * TS:(ti + 1) * TS],
                                                        in_=pbb[:, 0:8 * TS].rearrange("p (k t) -> p k t", k=8)),
                         reads=[pbk], writes=[x1Tk])
                for f in range(32):
                    pb, pbk = bank("pj")
                    for k in range(8):
                        C.op("pe", lambda e: e.matmul(pb[:, 0:nb], lhsT=WA_up[:, k, f * 128:(f + 1) * 128], rhs=x1T[:, k, 0:nb],
                                                      start=(k == 0), stop=(k == 7)), reads=[x1Tk, WAK], writes=[pbk])
                    r_, rk_ = rl[f % 2], rlk[f % 2]
                    C.op("act", lambda e: e.activation(out=r_[:, 0:nb], in_=pb[:, 0:nb], func=AF.Relu), reads=[pbk], writes=[rk_])
                    eng = "pool" if f % 2 == 0 else "dve"
                    C.op(eng, lambda e: e.tensor_tensor(out=hidT[:, f, 0:nb], in0=r_[:, 0:nb], in1=r_[:, 0:nb], op=ALU.mult),
                         reads=[rk_], writes=[hidk])
                for ti in range(ntl):
                    r0 = rb0 + ti * TS
                    for half in range(2):
                        pb, pbk = bank("pj")
                        for f in range(32):
                            C.op("pe", lambda e: e.matmul(pb[:TS, :], lhsT=hidT[:, f, ti * TS:(ti + 1) * TS],
                                                          rhs=WA_dn[:, f, half * 512:(half + 1) * 512], start=(f == 0), stop=(f == 31)),
                                 reads=[hidk, WAK], writes=[pbk])
                        C.op("dve", lambda e: e.scalar_tensor_tensor(out=x1[:TS, ti, half * 512:(half + 1) * 512],
                                                                     in0=x1[:TS, ti, half * 512:(half + 1) * 512], scalar=float(ALPHA),
                                                                     in1=pb[:TS, :], op0=ALU.mult, op1=ALU.add),
                             reads=[pbk, x1k[ti]], writes=[x1k[ti]])
                    layer_norm(x1[:TS, ti, :], TS, g2, b2, pk2, x1k[ti])
                    if l < DEPTH - 1:
                        C.dma("pool", xcs[r0:r0 + TS, :], x1[:TS, ti, :], x1k[ti], reads=[x1k[ti]], writes=[dtk("xcs", r0)])
                    else:
                        if r0 < Lp:
                            C.dma("pool", yp[r0:r0 + TS, :], x1[:TS, ti, :], x1k[ti], reads=[x1k[ti]])
                        else:
                            C.dma("pool", ys[r0 - Lp:r0 - Lp + TS, :], x1[:TS, ti, :], x1k[ti], reads=[x1k[ti]])
    C.finish()
    return nc


def make_consts(Lp, Ls, PAST):
    c = {}
    c["c_ident"] = np.eye(128, dtype=np.float32)
    half = 32
    inv_freq = (10000.0 ** (-np.arange(half, dtype=np.float32) / half)).astype(np.float32)

    def rot(pos):
        ang = pos.astype(np.float32)[:, None] * inv_freq[None, :]
        cos = np.cos(ang).astype(np.float32)
        sin = np.sin(ang).astype(np.float32)
        return np.concatenate([cos, cos, -sin, sin], axis=1).astype(np.float32)

    c["c_rotp"] = rot(np.arange(Lp))
    c["c_rots"] = rot(PAST + np.arange(Ls))
    k = np.arange(128)[:, None]
    qq = np.arange(256)[None, :]
    m0 = ((0 + k // 64) <= (qq // 64)).astype(np.float32)
    m1 = ((2 + k // 64) <= (qq // 64)).astype(np.float32)
    c["c_amask"] = np.concatenate([m0, m1], axis=1)
    j = np.arange(64)[:, None]
    i = np.arange(64)[None, :]
    c["c_triu"] = (j <= i).astype(np.float32)
    c["c_striu"] = (j < i).astype(np.float32)
    lg = np.log(1.0 - 2.0 ** (-5.0 - np.arange(4, dtype=np.float64)))
    rm = np.zeros((64, 4, 64), np.float64)
    for h in range(4):
        rm[:, h, :] = np.exp(-lg[h] * (j + 1.0)) * (j <= i)
    c["c_rm"] = rm.reshape(64, 256).astype(np.float32)
    xi = np.zeros((64, 4, 128), np.float64)
    xis = np.zeros((64, 4, Ls), np.float64)
    for h in range(4):
        xi[:, h, :] = np.exp(lg[h] * ((np.arange(128) % 64) + 1.0))[None, :]
        xis[:, h, :] = np.exp(lg[h] * (np.arange(Ls) + 1.0))[None, :]
    c["c_xi"] = xi.reshape(64, 512).astype(np.float32)
    c["c_xis"] = xis.reshape(64, 4 * Ls).astype(np.float32)
    zeta = np.zeros((128, 4), np.float64)
    zetas = np.zeros((Ls, 4), np.float64)
    for h in range(4):
        zeta[:, h] = np.exp(lg[h] * (63.0 - (np.arange(128) % 64)))
        zetas[:, h] = np.exp(lg[h] * (Ls - 1.0 - np.arange(Ls)))
    c["c_zeta"] = zeta.astype(np.float32)
    c["c_zetas"] = zetas.astype(np.float32)
    bo = np.zeros((128, 128), np.float32)
    bo[:64, :64] = 1.0
    bo[64:, 64:] = 1.0
    c["c_bones"] = bo
    return c


_NC_CACHE = {}


def run(inputs, Lp, NS, Ls, PAST, ncores):
    key = (Lp, NS, Ls, PAST)
    if key not in _NC_CACHE:
        _NC_CACHE[key] = build(Lp, NS, Ls, PAST)
    nc = _NC_CACHE[key]
    f = lambda a: np.ascontiguousarray(np.asarray(a, dtype=np.float32))
    consts = make_consts(Lp, Ls, PAST)
    I = {k: f(v) for k, v in inputs.items()}
    in_maps = []
    for i in range(ncores):
        sl = slice(i * NS, (i + 1) * NS)
        m = dict(consts)
        m["xp"] = f(I["x_prompt"][i])
        m["xs"] = f(I["x_sample"][sl].reshape(NS * Ls, D))
        m["ck"] = f(I["cache_k"][:, sl].reshape(DEPTH, NS, PAST, 512))
        m["cv"] = f(I["cache_v"][:, sl].reshape(DEPTH, NS, PAST, 512))
        m["sdl"] = f(I["state_delta"][:, sl].reshape(DEPTH, NS, 256, 64))
        m["scv"] = f(I["state_conv"][:, sl])
        m["srt"] = f(I["state_ret"][:, sl].reshape(DEPTH, NS, 256, 64))
        m["ln0g"] = f(I["ln0_g"].reshape(1, D)); m["ln0b"] = f(I["ln0_b"].reshape(1, D))
        m["w_in"] = I["w_in"]
        m["lamq1"] = I["lam_q1"]; m["lamk1"] = I["lam_k1"]; m["lamq2"] = I["lam_q2"]; m["lamk2"] = I["lam_k2"]
        m["dng"] = I["diff_norm_g"]; m["convw"] = I["conv_w"]; m["alog"] = I["a_log"]; m["dtb"] = I["dt_bias"]
        m["dlng"] = I["delta_norm_g"]; m["w_out"] = I["w_out"]
        m["ln1g"] = I["ln1_g"]; m["ln1b"] = I["ln1_b"]; m["w_up"] = I["w_up"]; m["w_down"] = I["w_down"]
        m["ln2g"] = I["ln2_g"]; m["ln2b"] = I["ln2_b"]
        in_maps.append(m)
    res = run_bass_kernel_spmd(nc, in_maps, core_ids=list(range(ncores)))
    R = res.results
    B = ncores
    st = lambda name: np.stack([np.asarray(R[i][name]) for i in range(B)])
    y_p = st("yp")
    y_s = st("ys").reshape(B * NS, Ls, D)
    p_k = st("pk").transpose(1, 0, 2, 3).reshape(DEPTH, B, Lp, 8, 64)
    p_v = st("pv").transpose(1, 0, 2, 3).reshape(DEPTH, B, Lp, 4, 128)
    p_d = st("pdl").transpose(1, 0, 2, 3).reshape(DEPTH, B, 4, 64, 64)
    p_c = st("pcv").transpose(1, 0, 2, 3)
    p_r = st("prt").transpose(1, 0, 2, 3).reshape(DEPTH, B, 4, 64, 64)
    s_k = st("sk").transpose(1, 0, 2, 3).reshape(DEPTH, B * NS, Ls, 8, 64)
    s_v = st("sv").transpose(1, 0, 2, 3).reshape(DEPTH, B * NS, Ls, 4, 128)
    s_d = st("sdlo").transpose(1, 0, 2, 3, 4).reshape(DEPTH, B * NS, 4, 64, 64)
    s_c = st("scvo").transpose(1, 0, 2, 3, 4).reshape(DEPTH, B * NS, 3, 768)
    s_r = st("srto").transpose(1, 0, 2, 3, 4).reshape(DEPTH, B * NS, 4, 64, 64)
    outs = (y_p, y_s, p_k, p_v, p_d, p_c, p_r, s_k, s_v, s_d, s_c, s_r)
    return tuple(np.ascontiguousarray(o.astype(np.float32)) for o in outs)


def kernel(**inputs):
    return run(inputs, 4096, 2, 32, 2048, NCORES)
```

```python
import math
import numpy as np
import ml_dtypes
import concourse.bass as bass
import concourse.mybir as mybir
from concourse.bass_utils import run_bass_kernel_spmd

F32 = mybir.dt.float32
BF16 = mybir.dt.bfloat16
AF = mybir.ActivationFunctionType
ALU = mybir.AluOpType
AX = mybir.AxisListType

D = 1024
NIN = 3592
DFF = 4096
DEPTH = 2
ALPHA = (2 * DEPTH) ** 0.25
LN_EPS = 1e-5
NORM_EPS = 1e-6
NCORES = 8


class Tk:
    __slots__ = ("w", "r", "dsem", "dcnt", "name")

    def __init__(self, name=""):
        self.w = None
        self.r = {}
        self.dsem = None
        self.dcnt = 0
        self.name = name


class Ctx:
    def __init__(self, nc):
        self.nc = nc
        self.E = {"pe": nc.tensor, "dve": nc.vector, "act": nc.scalar, "pool": nc.gpsimd,
                  "sp": nc.sync}
        self.sem = {k: nc.alloc_semaphore("es_" + k) for k in self.E}
        self.cnt = {k: 0 for k in self.E}
        self.seen = {k: {} for k in self.E}
        self.dsems = []
        self.nd = 0

    def _wait(self, e, dep):
        key, sem, val = dep
        if self.seen[e].get(key, 0) >= val:
            return
        self.E[e].wait_ge(sem, val)
        self.seen[e][key] = val

    def _deps(self, e, reads, writes):
        for t in reads:
            if t.w is not None:
                self._wait(e, t.w)
        for t in writes:
            if t.w is not None:
                self._wait(e, t.w)
            for d in t.r.values():
                self._wait(e, d)

    def _mark(self, me, reads, writes):
        for t in reads:
            old = t.r.get(me[0])
            if old is None or old[2] < me[2]:
                t.r[me[0]] = me
        for t in writes:
            t.w = me
            t.r = {}

    def op(self, e, fn, reads=(), writes=()):
        self._deps(e, reads, writes)
        ins = fn(self.E[e])
        self.cnt[e] += 1
        ins.then_inc(self.sem[e], 1)
        me = (e, self.sem[e], self.cnt[e])
        if e == "pe":
            self.seen[e][e] = self.cnt[e]
        self._mark(me, reads, writes)
        return ins

    def dma(self, q, out, in_, owner, reads=(), writes=(), **kw):
        self._deps(q, reads, writes)
        if owner.dsem is None:
            owner.dsem = self.nc.alloc_semaphore("ds%d" % self.nd)
            self.nd += 1
            self.dsems.append(owner)
        ins = self.E[q].dma_start(out=out, in_=in_, **kw)
        owner.dcnt += 16
        ins.then_inc(owner.dsem, 16)
        me = ("d%d" % id(owner), owner.dsem, owner.dcnt)
        self._mark(me, reads, writes)
        return ins

    def barrier(self):
        for e in self.E:
            for f in self.E:
                if f != e and self.cnt[f] > 0:
                    self._wait(e, (f, self.sem[f], self.cnt[f]))
            for o in self.dsems:
                if o.dcnt > 0:
                    self._wait(e, ("d%d" % id(o), o.dsem, o.dcnt))

    def finish(self):
        self.barrier()


def bc(ap, shape):
    return ap.to_broadcast(list(shape))


import os
from contextlib import ExitStack
STOP = int(os.environ.get('KSTOP', '99'))


def build(Lp=4096, NS=2, Ls=32, PAST=2048):
    nc = bass.Bass("TRN2", target_bir_lowering=False)
    C = Ctx(nc)
    Ltot = Lp + NS * Ls
    LK = max(Lp, PAST + Ls)

    def din(name, shape, dt=F32):
        return nc.dram_tensor(name, list(shape), dt, kind="ExternalInput").ap()

    def dout(name, shape):
        return nc.dram_tensor(name, list(shape), F32, kind="ExternalOutput").ap()

    xp = din("xp", [Lp, D])
    xs = din("xs", [NS * Ls, D])
    ck = din("ck", [DEPTH, NS, PAST, 512])
    cv = din("cv", [DEPTH, NS, PAST, 512])
    sdl = din("sdl", [DEPTH, NS, 256, 64])
    scv = din("scv", [DEPTH, NS, 3, 768])
    srt = din("srt", [DEPTH, NS, 256, 64])
    ln0g = din("ln0g", [1, D]); ln0b = din("ln0b", [1, D])
    w_in = din("w_in", [DEPTH, D, NIN])
    lamq1 = din("lamq1", [DEPTH, 64]); lamk1 = din("lamk1", [DEPTH, 64])
    lamq2 = din("lamq2", [DEPTH, 64]); lamk2 = din("lamk2", [DEPTH, 64])
    dng = din("dng", [DEPTH, 128])
    convw = din("convw", [DEPTH, 4, 768])
    alog = din("alog", [DEPTH, 4]); dtb = din("dtb", [DEPTH, 4])
    dlng = din("dlng", [DEPTH, 64])
    w_out = din("w_out", [DEPTH, D, D])
    ln1g = din("ln1g", [DEPTH, D]); ln1b = din("ln1b", [DEPTH, D])
    w_up = din("w_up", [DEPTH, D, DFF])
    w_down = din("w_down", [DEPTH, DFF, D])
    ln2g = din("ln2g", [DEPTH, D]); ln2b = din("ln2b", [DEPTH, D])
    c_ident = din("c_ident", [128, 128])
    c_rotp = din("c_rotp", [Lp, 128])
    c_rots = din("c_rots", [Ls, 128])
    c_amask = din("c_amask", [128, 512])
    c_triu = din("c_triu", [64, 64])
    c_striu = din("c_striu", [64, 64])
    c_rm = din("c_rm", [64, 256])
    c_xi = din("c_xi", [64, 512])
    c_xis = din("c_xis", [64, 4 * Ls])
    c_zeta = din("c_zeta", [128, 4])
    c_zetas = din("c_zetas", [Ls, 4])
    c_bones = din("c_bones", [128, 128])

    yp = dout("yp", [Lp, D]); ys = dout("ys", [NS * Ls, D])
    pk = dout("pk", [DEPTH, Lp, 512]); pv = dout("pv", [DEPTH, Lp, 512])
    pdl = dout("pdl", [DEPTH, 256, 64]); pcv = dout("pcv", [DEPTH, 3, 768])
    prt = dout("prt", [DEPTH, 256, 64])
    sk = dout("sk", [DEPTH, NS * Ls, 512]); sv = dout("sv", [DEPTH, NS * Ls, 512])
    sdlo = dout("sdlo", [DEPTH, NS, 256, 64]); scvo = dout("scvo", [DEPTH, NS, 3, 768])
    srto = dout("srto", [DEPTH, NS, 256, 64])
    x1s = nc.dram_tensor("x1s", [Ltot, D], F32).ap()
    xcs = nc.dram_tensor("xcs", [Ltot, D], F32).ap()
    vsc = nc.dram_tensor("vsc", [LK, 512], BF16).ap()
    dram_tk = {}

    def dtk(name, r0):
        k = (name, r0)
        if k not in dram_tk:
            dram_tk[k] = Tk()
        return dram_tk[k]

    PS = [nc.alloc_psum_tensor("ps%d" % i, [128, 512], F32) for i in range(8)]
    PSK = [Tk("ps%d" % i) for i in range(8)]
    rot_state = {"pj": 0, "mi": 0, "s": 0}

    def bank(kind):
        base, n = {"pj": (0, 2), "s": (2, 2), "mi": (6, 2)}[kind]
        i = base + rot_state[kind] % n
        rot_state[kind] += 1
        return PS[i], PSK[i]

    OB, OBK = PS[4], PSK[4]
    LB, LBK = PS[5], PSK[5]

    def sb(name, shape, dt=F32):
        return nc.alloc_sbuf_tensor(name, list(shape), dt)

    identf = sb("identf", [128, 128]); identb = sb("identb", [128, 128], BF16)
    onesb = sb("onesb", [128, 128], BF16)
    onesf = sb("onesf", [128, 128])
    bonesf = sb("bonesf", [128, 128])
    triu = sb("triu", [64, 64]); striu = sb("striu", [64, 64])
    rmc = sb("rmc", [64, 256]); xic = sb("xic", [64, 512]); xisc = sb("xisc", [64, 4 * Ls])
    zetac = sb("zetac", [128, 4]); zetasc = sb("zetasc", [Ls, 4])
    amask = sb("amask", [128, 512], BF16)
    epsb = sb("epsb", [128, 2])
    constk = Tk("const")
    q = "sp"
    C.dma(q, identf[:], c_ident[:, :], constk, writes=[constk])
    C.dma(q, bonesf[:], c_bones[:, :], constk, writes=[constk])
    C.dma(q, triu[:], c_triu[:, :], constk, writes=[constk])
    C.dma(q, striu[:], c_striu[:, :], constk, writes=[constk])
    C.dma(q, rmc[:], c_rm[:, :], constk, writes=[constk])
    C.dma(q, xic[:], c_xi[:, :], constk, writes=[constk])
    C.dma(q, xisc[:], c_xis[:, :], constk, writes=[constk])
    C.dma(q, zetac[:], c_zeta[:, :], constk, writes=[constk])
    C.dma(q, zetasc[:], c_zetas[:, :], constk, writes=[constk])
    C.dma("pool", amask[:], c_amask[:, :], constk, writes=[constk])
    C.op("dve", lambda e: e.tensor_copy(out=identb[:], in_=identf[:]), reads=[constk], writes=[constk])
    C.op("dve", lambda e: e.memset(onesb[:], 1.0), writes=[constk])
    C.op("dve", lambda e: e.memset(onesf[:], 1.0), writes=[constk])
    C.op("dve", lambda e: e.memset(epsb[:, 0:1], LN_EPS), writes=[constk])
    C.op("dve", lambda e: e.memset(epsb[:, 1:2], NORM_EPS), writes=[constk])

    WA = sb("WA", [128, 65536], BF16)
    WAK = Tk("WA")

    lnst = sb("lnst", [128, 2, 6]); lnmv = sb("lnmv", [128, 2]); lnk = Tk("ln")

    def layer_norm(xap, n, gt, bt, gk, xk):
        tks = [xk]
        C.op("dve", lambda e: e.bn_stats(out=lnst[:n, 0, :], in_=xap[:, 0:512]), reads=tks, writes=[lnk])
        C.op("dve", lambda e: e.bn_stats(out=lnst[:n, 1, :], in_=xap[:, 512:1024]), reads=tks, writes=[lnk])
        C.op("dve", lambda e: e.bn_aggr(out=lnmv[:n, :], in_=lnst[:n, :, :].rearrange("p a b -> p (a b)")),
             reads=[lnk], writes=[lnk])
        C.op("act", lambda e: e.activation(out=lnmv[:n, 1:2], in_=lnmv[:n, 1:2], func=AF.Ln, bias=epsb[:n, 0:1]),
             reads=[lnk, constk], writes=[lnk])
        C.op("act", lambda e: e.activation(out=lnmv[:n, 1:2], in_=lnmv[:n, 1:2], func=AF.Exp, scale=-0.5),
             reads=[lnk], writes=[lnk])
        C.op("dve", lambda e: e.tensor_scalar(out=xap, in0=xap, scalar1=lnmv[:n, 0:1], scalar2=lnmv[:n, 1:2],
                                              op0=ALU.subtract, op1=ALU.mult),
             reads=[lnk] + tks, writes=tks)
        C.op("pool", lambda e: e.tensor_tensor(out=xap, in0=xap, in1=gt[:n, :], op=ALU.mult),
             reads=tks + [gk], writes=tks)
        C.op("pool", lambda e: e.tensor_tensor(out=xap, in0=xap, in1=bt[:n, :], op=ALU.add),
             reads=tks + [gk], writes=tks)

    seqs = [dict(name="p", L=Lp, past=0, TS=128, c=64, row0=0, b=0)]
    for b_ in range(NS):
        seqs.append(dict(name="s%d" % b_, L=Ls, past=PAST, TS=Ls, c=Ls, row0=Lp + b_ * Ls, b=b_))

    with ExitStack() as st0:
        g0 = st0.enter_context(nc.sbuf_tensor("g0", [128, D], F32))
        b0 = st0.enter_context(nc.sbuf_tensor("b0", [128, D], F32))
        xl = st0.enter_context(nc.sbuf_tensor("xl", [128, 2, D], F32))
        xlk = [Tk(), Tk()]
        pk0 = Tk()
        C.dma("sp", g0[:], ln0g[0:1, :].partition_broadcast(128), pk0, writes=[pk0])
        C.dma("sp", b0[:], ln0b[0:1, :].partition_broadcast(128), pk0, writes=[pk0])
        tiles0 = [(xp, r, r, 128) for r in range(0, Lp, 128)]
        for b_ in range(NS):
            tiles0.append((xs, b_ * Ls, Lp + b_ * Ls, Ls))
        for i0, (src0, sr0, dr0, n0) in enumerate(tiles0):
            sl0 = i0 % 2
            C.dma("sp", xl[:n0, sl0, :], src0[sr0:sr0 + n0, :], xlk[sl0], writes=[xlk[sl0]])
            layer_norm(xl[:n0, sl0, :], n0, g0, b0, pk0, xlk[sl0])
            C.dma("pool", xcs[dr0:dr0 + n0, :], xl[:n0, sl0, :], xlk[sl0], reads=[xlk[sl0]], writes=[dtk("xcs", dr0)])
    if STOP == 0:
        C.finish()
        return nc

    for l in range(DEPTH):
        lam_init = 0.8 - 0.6 * math.exp(-0.3 * l)
        C.barrier()
        WA_in = WA[:, 0:8 * NIN].rearrange("p (k n) -> p k n", k=8)
        WA_out = WA[:, 8 * NIN:8 * NIN + 8192].rearrange("p (k n) -> p k n", k=8)
        WAK2 = Tk("WA2")
        for k in range(8):
            for c0_ in range(0, NIN, 1024):
                c1_ = min(NIN, c0_ + 1024)
                C.dma("pool", WA_in[:, k, c0_:c1_], w_in[l, k * 128:(k + 1) * 128, c0_:c1_], WAK, writes=[WAK])
        for k in range(8):
            C.dma("pool", WA_out[:, k, :], w_out[l, k * 128:(k + 1) * 128, :], WAK2, writes=[WAK2])
        with ExitStack() as st:
            def T(name, shape, dt=F32):
                return st.enter_context(nc.sbuf_tensor("%s_%d" % (name, l), list(shape), dt))
            wa_off = [8 * NIN + 8192]

            def WT(shape):
                n = 1
                for d_ in shape[1:]:
                    n *= d_
                ap = WA[:, wa_off[0]:wa_off[0] + n]
                wa_off[0] += n
                assert wa_off[0] <= 65536, wa_off[0]
                if len(shape) == 3:
                    ap = ap.rearrange("p (a b) -> p a b", a=shape[1])
                return ap
            KT = WT([128, 4, LK])
            ktk = [Tk("kt%d" % i) for i in range((LK + 127) // 128)]
            g1 = T("g1", [128, D]); b1 = T("b1", [128, D])
            pk_ = Tk("params")
            C.dma("sp", g1[:], ln1g[l:l + 1, :].partition_broadcast(128), pk_, writes=[pk_])
            C.dma("sp", b1[:], ln1b[l:l + 1, :].partition_broadcast(128), pk_, writes=[pk_])
            lamt = T("lamt", [128, 4, 64]); lamr = T("lamr", [128, 4]); neglam = T("neglam", [128, 1])
            for i, src in enumerate((lamq1, lamk1, lamq2, lamk2)):
                C.dma("sp", lamt[:, i, :], src[l:l + 1, :].partition_broadcast(128), pk_, writes=[pk_])
            C.op("dve", lambda e: e.tensor_tensor(out=lamt[:, 0, :], in0=lamt[:, 0, :], in1=lamt[:, 1, :], op=ALU.mult),
                 reads=[pk_], writes=[pk_])
            C.op("dve", lambda e: e.tensor_tensor(out=lamt[:, 2, :], in0=lamt[:, 2, :], in1=lamt[:, 3, :], op=ALU.mult),
                 reads=[pk_], writes=[pk_])
            C.op("dve", lambda e: e.tensor_reduce(out=lamr[:, 0:1], in_=lamt[:, 0, :], axis=AX.X, op=ALU.add),
                 reads=[pk_], writes=[pk_])
            C.op("dve", lambda e: e.tensor_reduce(out=lamr[:, 1:2], in_=lamt[:, 2, :], axis=AX.X, op=ALU.add),
                 reads=[pk_], writes=[pk_])
            C.op("act", lambda e: e.activation(out=lamr[:, 2:4], in_=lamr[:, 0:2], func=AF.Exp),
                 reads=[pk_], writes=[pk_])
            C.op("dve", lambda e: e.scalar_tensor_tensor(out=neglam[:], in0=lamr[:, 3:4], scalar=-lam_init,
                                                         in1=lamr[:, 2:3], op0=ALU.add, op1=ALU.subtract),
                 reads=[pk_], writes=[pk_])
            dngc = T("dngc", [128, 1]); dlngc = T("dlngc", [64, 1])
            C.dma("sp", dngc[:], dng[l:l + 1, :].rearrange("o e -> e o"), pk_, writes=[pk_])
            C.dma("sp", dlngc[:], dlng[l:l + 1, :].rearrange("o e -> e o"), pk_, writes=[pk_])
            cw = T("cw", [128, 6, 4])
            for ch in range(6):
                C.dma("sp", cw[:, ch, :], convw[l, :, ch * 128:(ch + 1) * 128].rearrange("j p -> p j"),
                      pk_, writes=[pk_], allow_slow_non_contiguous=True)
            nA = T("nA", [128, 4]); dtbb = T("dtbb", [128, 4])
            C.dma("sp", nA[:], alog[l:l + 1, :].partition_broadcast(128), pk_, writes=[pk_])
            C.dma("sp", dtbb[:], dtb[l:l + 1, :].partition_broadcast(128), pk_, writes=[pk_])
            C.op("act", lambda e: e.activation(out=nA[:], in_=nA[:], func=AF.Exp), reads=[pk_], writes=[pk_])
            C.op("dve", lambda e: e.tensor_scalar(out=nA[:], in0=nA[:], scalar1=-1.0, scalar2=None, op0=ALU.mult),
                 reads=[pk_], writes=[pk_])
            if STOP == 6:
                C.finish()
                return nc

            xres = T("xres", [128, D]); xk = Tk("xres")
            xbf = WT([128, D]); xbfk = Tk()
            kbf = xbf[:, 0:512]; kbfk = xbfk
            xT = WT([128, 8, 256]); xTk = Tk()
            mixT = xT; mixk = xTk
            mixA = WT([128, 4, 256]); mixAk = Tk()
            QTa = WT([128, 4, 256]); QTb = WT([128, 4, 256]); QTk = Tk()
            C.op("pool", lambda e: e.memset(QTa[64:128, :, :], 0.0), writes=[QTk])
            C.op("pool", lambda e: e.memset(QTb[0:64, :, :], 0.0), writes=[QTk])
            rot = T("rot", [128, 128]); rotk = Tk()
            tmpA = T("tmpA", [128, 512]); tmpB = T("tmpB", [128, 512]); tmpk = Tk()
            kout = T("kout", [128, 512]); koutk = Tk()
            vout = T("vout", [128, 512]); voutk = Tk()
            vbf = WT([128, 512]); vbfk = Tk()
            uT = T("uT", [128, 6, 131]); uTk = Tk()
            cT = T("cT", [128, 6, 128]); cTk = Tk()
            cE = T("cE", [128, 6, 128]); cEk = Tk()
            nTh = WT([128, 8, 128]); nTk = Tk()
            qdT = WT([128, 4, 128]); qdk = Tk()
            g4 = T("g4", [128, 264]); g4e = T("g4e", [128, 264]); g4k = Tk()
            sgT = T("sgT", [64, 8, 128]); sgTk = Tk()
            qkc = T("qkc", [128, 512]); qkck = Tk()
            qkcT = WT([128, 8, 128]); qkcTk = Tk()
            sg6 = g4[:, 0:256]; sg6e = g4e[:, 0:256]; sg6k = g4k
            ktok = T("ktok", [64, 2, 512]); ktokk = Tk()
            kz = T("kz", [64, 2, 256], BF16); kzk = Tk()
            vcb = T("vcb", [64, 2, 256], BF16); kzc = T("kzc", [64, 2, 256], BF16); vck = Tk()
            gb = T("gb", [64, 2, 8]); gbk = Tk()
            sm = T("sm", [128, 64]); smk = Tk()
            GT = T("GT", [64, 4, 64]); GTk = Tk()
            Drow = T("Drow", [128, 4, 64]); EDrow = T("EDrow", [128, 4, 64]); Drk = Tk()
            LM = T("LM", [64, 4, 64]); LMs = T("LMs", [64, 4, 64]); LMi = T("LMi", [64, 4, 64]); LMk = Tk()
            Xf = T("Xf", [64, 4, 64]); Xk = Tk()
            Yb = T("Yb", [64, 2, 4, 64], BF16); YTb = T("YTb", [64, 6, 4, 64], BF16); Yk = [Tk() for _ in range(8)]; YTk = [Tk() for _ in range(8)]
            Pf = T("Pf", [64, 4, 64]); Pb = T("Pb", [64, 4, 64], BF16); Pk = Tk()
            inT = T("inT", [64, 4, 64], BF16); inTk = Tk()
            inR = T("inR", [64, 4, 64], BF16); inRk = Tk()
            rt = T("rt", [64, 256]); rb = T("rb", [64, 256], BF16); rk = Tk()
            scr = T("scr", [64, 4, 64]); scrk = Tk()
            vnb = T("vnb", [64, 256], BF16); vnk = Tk()
            Sd = T("Sd", [64, 4, 64]); Sdb = T("Sdb", [64, 4, 64], BF16); Sdk = Tk()
            Sr = T("Sr", [64, 4, 64]); Srb = T("Srb", [64, 4, 64], BF16); Srk = Tk()
            ot = T("ot", [64, 4, 128]); ot2 = T("ot2", [64, 4, 128]); otb = T("otb", [64, 4, 128], BF16); otk = Tk()
            PT = [WT([128, 2, 256]), WT([128, 2, 256])]; PTk = [Tk(), Tk()]
            Vt = [WT([128, 128]), WT([128, 128]), WT([128, 128])]; Vtk = [Tk(), Tk(), Tk()]
            ckb = WT([128, 512]); ckbk = Tk()
            Rr = tmpA[:, :].rearrange("p (j q) -> p j q", j=2); Tt = tmpB[:, :].rearrange("p (j q) -> p j q", j=2)
            oaf = T("oaf", [128, 256])
            oab = T("oab", [128, 256], BF16); atk = tmpk
            pcst = cE[:3, :, :].rearrange("p a b -> p (a b)"); pcstk = cEk

            def proj(TS, col0, c0, n, kind="pj"):
                pb, pbk = bank(kind)
                for k in range(8):
                    C.op("pe", lambda e: e.matmul(pb[:TS, 0:n], lhsT=xT[:, k, col0:col0 + TS],
                                                  rhs=WA_in[:, k, c0:c0 + n], start=(k == 0), stop=(k == 7)),
                         reads=[xTk, WAK], writes=[pbk])
                return pb, pbk

            def rotary(pb, pbk, TS, outs):
                pv_ = pb[:TS, :].rearrange("p (h d) -> p h d", h=8)
                C.op("dve", lambda e: e.tensor_tensor(out=tmpA[:TS, :].rearrange("p (h d) -> p h d", h=8), in0=pv_,
                                                      in1=bc(rot[:TS, 0:64].unsqueeze(1), [TS, 8, 64]), op=ALU.mult),
                     reads=[pbk, rotk], writes=[tmpk])
                tb = tmpB[:TS, :].rearrange("p (h d) -> p h d", h=8)
                C.op("dve", lambda e: e.tensor_tensor(out=tb[:, :, 0:32], in0=pv_[:, :, 32:64],
                                                      in1=bc(rot[:TS, 64:96].unsqueeze(1), [TS, 8, 32]), op=ALU.mult),
                     reads=[pbk, rotk], writes=[tmpk])
                C.op("dve", lambda e: e.tensor_tensor(out=tb[:, :, 32:64], in0=pv_[:, :, 0:32],
                                                      in1=bc(rot[:TS, 96:128].unsqueeze(1), [TS, 8, 32]), op=ALU.mult),
                     reads=[pbk, rotk], writes=[tmpk])
                for (eng, oap, otk_) in outs:
                    C.op(eng, lambda e: e.tensor_tensor(out=oap, in0=tmpA[:TS, :], in1=tmpB[:TS, :], op=ALU.add),
                         reads=[tmpk], writes=[otk_])

            def transposes_bf(src, srck, TS, n):
                pb, pbk = bank("mi")
                pbb = pb[:].bitcast(BF16)
                for i in range(n):
                    C.op("pe", lambda e: e.transpose(out=pbb[:, i * TS:(i + 1) * TS],
                                                     in_=src[:TS, i * 128:(i + 1) * 128],
                                                     identity=identb[:TS, :TS]),
                         reads=[srck, constk], writes=[pbk])
                return pbb[:, 0:n * TS].rearrange("p (k t) -> p k t", k=n), pbk

            def silu_from(pb_ap, xs_, es_, k_, pbk):
                C.op("act", lambda e: e.activation(out=es_, in_=pb_ap, func=AF.Exp, scale=-1.0), reads=[pbk], writes=[k_])
                C.op("act", lambda e: e.activation(out=xs_, in_=pb_ap, func=AF.Copy), reads=[pbk], writes=[k_])
                C.op("dve", lambda e: e.tensor_scalar(out=es_, in0=es_, scalar1=1.0, scalar2=None, op0=ALU.add),
                     reads=[k_], writes=[k_])
                C.op("dve", lambda e: e.reciprocal(out=es_, in_=es_), reads=[k_], writes=[k_])

            for sq_ in seqs:
                TS, c, L, past, row0, sb_ = sq_["TS"], sq_["c"], sq_["L"], sq_["past"], sq_["row0"], sq_["b"]
                isP = sq_["name"] == "p"
                CPT = TS // c
                TPB = 2 if isP else 1
                nblk = L // (TS * TPB)
                nq = TS * TPB
                nit = 5 if c == 64 else 4
                rsrc = c_rotp if isP else c_rots
                kdst = pk if isP else sk
                vdst = pv if isP else sv
                orow0 = 0 if isP else sb_ * Ls
                zt = zetac if isP else zetasc
                xit = xic if isP else xisc
                gch = [math.exp(math.log(1.0 - 2.0 ** (-5.0 - h)) * c) for h in range(4)]
                if isP:
                    C.op("dve", lambda e: e.memset(Sd[:], 0.0), writes=[Sdk])
                    C.op("dve", lambda e: e.memset(Sr[:], 0.0), writes=[Srk])
                    C.op("dve", lambda e: e.memset(uT[:, :, 0:3], 0.0), writes=[uTk])
                else:
                    C.dma("sp", Sd[:], sdl[l, sb_].rearrange("(h d) e -> d h e", d=64), Sdk, writes=[Sdk])
                    C.dma("sp", Sr[:], srt[l, sb_].rearrange("(h d) e -> d h e", d=64), Srk, writes=[Srk])
                    for ch in range(6):
                        C.dma("sp", uT[:, ch, 0:3], scv[l, sb_, :, ch * 128:(ch + 1) * 128].rearrange("j p -> p j"),
                              uTk, writes=[uTk], allow_slow_non_contiguous=True)
                    for kt in range(past // 128):
                        C.dma("pool", ckb[:], ck[l, sb_, kt * 128:(kt + 1) * 128, :], ckbk, writes=[ckbk])
                        pv4, pv4k = transposes_bf(ckb, ckbk, 128, 4)
                        C.op("dve", lambda e: e.tensor_copy(out=KT[:, :, kt * 128:(kt + 1) * 128], in_=pv4),
                             reads=[pv4k], writes=[ktk[kt]])
                C.op("pool", lambda e: e.tensor_copy(out=Sdb[:], in_=Sd[:]), reads=[Sdk], writes=[Sdk])
                C.op("pool", lambda e: e.tensor_copy(out=Srb[:], in_=Sr[:]), reads=[Srk], writes=[Srk])

                def gen_frontA(ti, t, pjk):
                    r0 = t * TS
                    col0 = ti * TS
                    kcol = past + r0
                    kt_own = kcol // 128
                    xa = xres[:TS, :]
                    C.dma("sp", xa, xcs[row0 + r0:row0 + r0 + TS, :], xk,
                          reads=[dtk("xcs", row0 + r0)], writes=[xk])
                    C.dma("sp", rot[:TS, :], rsrc[r0:r0 + TS, :], rotk, writes=[rotk])
                    C.op("act", lambda e: e.activation(out=xbf[:TS, :], in_=xa, func=AF.Copy), reads=[xk], writes=[xbfk])
                    pv8, pv8k = transposes_bf(xbf, xbfk, TS, 8)
                    C.op("dve", lambda e: e.tensor_copy(out=xT[:, :, col0:col0 + TS], in_=pv8), reads=[pv8k], writes=[xTk])
                    yield
                    pb, pbk = proj(TS, col0, 0, 512, pjk)
                    rotary(pb, pbk, TS, [("pool", kbf[:TS, :], kbfk)])
                    pv4, pv4k = transposes_bf(kbf, kbfk, TS, 4)
                    C.op("dve", lambda e: e.tensor_copy(out=QTa[0:64, :, col0:col0 + TS], in_=pv4[0:64, :, :]),
                         reads=[pv4k], writes=[QTk])
                    C.op("dve", lambda e: e.tensor_copy(out=QTb[64:128, :, col0:col0 + TS], in_=pv4[64:128, :, :]),
                         reads=[pv4k], writes=[QTk])
                    yield
                    pb, pbk = proj(TS, col0, 512, 512, pjk)
                    rotary(pb, pbk, TS, [("pool", kout[:TS, :], koutk), ("pool", kbf[:TS, :], kbfk)])
                    C.dma("pool", kdst[l, orow0 + r0:orow0 + r0 + TS, :], kout[:TS, :], koutk, reads=[koutk])
                    pv4, pv4k = transposes_bf(kbf, kbfk, TS, 4)
                    C.op("dve", lambda e: e.tensor_copy(out=KT[:, :, kcol:kcol + TS], in_=pv4),
                         reads=[pv4k], writes=[ktk[kt_own]])
                    yield
                    pb, pbk = proj(TS, col0, 1024, 512, pjk)
                    C.op("act", lambda e: e.activation(out=vout[:TS, :], in_=pb[:TS, :], func=AF.Copy), reads=[pbk], writes=[voutk])
                    C.op("pool", lambda e: e.tensor_copy(out=vbf[:TS, :], in_=vout[:TS, :]), reads=[voutk], writes=[vbfk])
                    C.dma("pool", vdst[l, orow0 + r0:orow0 + r0 + TS, :], vout[:TS, :], voutk, reads=[voutk])
                    C.dma("pool", vsc[kcol:kcol + TS, :], vbf[:TS, :], vbfk, reads=[vbfk], writes=[dtk("vsc", kt_own)])
                    yield

                def gen_frontB(t, col0):
                    pb, pbk = proj(TS, col0, 2304, 264)
                    silu_from(pb[:TS, 0:260], g4[:TS, 0:260], g4e[:TS, 0:260], g4k, pbk)
                    C.op("dve", lambda e: e.tensor_tensor(out=g4[:TS, 0:256], in0=g4[:TS, 0:256], in1=g4e[:TS, 0:256],
                                                          op=ALU.mult), reads=[g4k], writes=[g4k])
                    C.op("dve", lambda e: e.tensor_tensor(out=g4[:TS, 260:264], in0=pb[:TS, 260:264], in1=dtbb[:TS, :],
                                                          op=ALU.add), reads=[pbk, pk_], writes=[g4k])
                    C.op("act", lambda e: e.activation(out=g4[:TS, 260:264], in_=g4[:TS, 260:264], func=AF.Exp),
                         reads=[g4k], writes=[g4k])
                    C.op("act", lambda e: e.activation(out=g4[:TS, 260:264], in_=g4[:TS, 260:264], func=AF.Ln, bias=1.0),
                         reads=[g4k], writes=[g4k])
                    C.op("dve", lambda e: e.tensor_tensor(out=g4[:TS, 260:264], in0=g4[:TS, 260:264], in1=nA[:TS, :],
                                                          op=ALU.mult), reads=[g4k, pk_], writes=[g4k])
                    for ck_ in range(CPT):
                        C.op("dve", lambda e: e.tensor_copy(out=gb[:c, ck_, 0:4], in_=g4[ck_ * c:(ck_ + 1) * c, 260:264]),
                             reads=[g4k], writes=[gbk])
                        C.op("dve", lambda e: e.tensor_copy(out=gb[:c, ck_, 4:8], in_=g4e[ck_ * c:(ck_ + 1) * c, 256:260]),
                             reads=[g4k], writes=[gbk])
                    pm, pmk = bank("mi")
                    for h in range(4):
                        C.op("pe", lambda e: e.transpose(out=pm[:64, h * TS:(h + 1) * TS], in_=g4[:TS, h * 64:(h + 1) * 64],
                                                         identity=identf[:TS, :TS]),
                             reads=[g4k, constk], writes=[pmk])
                    C.op("act", lambda e: e.activation(out=sgT[:, 0:4, :TS],
                                                       in_=pm[:64, 0:4 * TS].rearrange("p (h t) -> p h t", h=4), func=AF.Copy),
                         reads=[pmk], writes=[sgTk])
                    yield
                    pb, pbk = proj(TS, col0, 2568, 512)
                    rotary(pb, pbk, TS, [("pool", qkc[:TS, :], qkck)])
                    C.op("dve", lambda e: e.tensor_scalar(out=qkc[:TS, 256:512], in0=qkc[:TS, 256:512], scalar1=0.125,
                                                          scalar2=None, op0=ALU.mult), reads=[qkck], writes=[qkck])
                    for half in range(2):
                        pm, pmk = bank("mi")
                        for h in range(4):
                            C.op("pe", lambda e: e.transpose(
                                out=pm[:64, h * TS:(h + 1) * TS],
                                in_=qkc[:TS, half * 256 + h * 64:half * 256 + (h + 1) * 64],
                                identity=identf[:TS, :TS]), reads=[qkck, constk], writes=[pmk])
                        C.op("act", lambda e: e.activation(
                            out=qkcT[0:64, half * 4:half * 4 + 4, :TS],
                            in_=pm[:64, 0:4 * TS].rearrange("p (h t) -> p h t", h=4), func=AF.Copy), reads=[pmk], writes=[qkcTk])
                    for ck_ in range(CPT):
                        C.op("dve", lambda e: e.tensor_tensor(
                            out=kzc[:c, ck_, :].rearrange("p (h d) -> p h d", h=4),
                            in0=qkc[ck_ * c:(ck_ + 1) * c, 256:512].rearrange("p (h d) -> p h d", h=4),
                            in1=bc(zt[ck_ * c:(ck_ + 1) * c, :].unsqueeze(2), [c, 4, 64]), op=ALU.mult),
                             reads=[qkck, constk], writes=[vck])
                    yield
                    pb, pbk = proj(TS, col0, 3080, 512)
                    for ck_ in range(CPT):
                        C.op("act", lambda e: e.activation(out=vcb[:c, ck_, :], in_=pb[ck_ * c:(ck_ + 1) * c, 0:256], func=AF.Copy),
                             reads=[pbk], writes=[vck])
                    silu_from(pb[:TS, 256:512], sg6[:TS, :], sg6e[:TS, :], sg6k, pbk)
                    C.op("dve", lambda e: e.tensor_tensor(out=sg6[:TS, :], in0=sg6[:TS, :], in1=sg6e[:TS, :], op=ALU.mult),
                         reads=[sg6k], writes=[sg6k])
                    pm, pmk = bank("mi")
                    for h in range(4):
                        C.op("pe", lambda e: e.transpose(out=pm[:64, h * TS:(h + 1) * TS], in_=sg6[:TS, h * 64:(h + 1) * 64],
                                                         identity=identf[:TS, :TS]),
                             reads=[sg6k, constk], writes=[pmk])
                    C.op("act", lambda e: e.activation(out=sgT[:, 4:8, :TS],
                                                       in_=pm[:64, 0:4 * TS].rearrange("p (h t) -> p h t", h=4), func=AF.Copy),
                         reads=[pmk], writes=[sgTk])
                    yield
                    for rnd in range(2):
                        pb, pbk = bank("pj")
                        for j in range(3):
                            ch = rnd * 3 + j
                            for k in range(8):
                                C.op("pe", lambda e: e.matmul(
                                    pb[:, j * TS:(j + 1) * TS],
                                    lhsT=WA_in[:, k, 1536 + ch * 128:1536 + (ch + 1) * 128],
                                    rhs=xT[:, k, col0:col0 + TS], start=(k == 0), stop=(k == 7)),
                                     reads=[xTk, WAK], writes=[pbk])
                        C.op("act", lambda e: e.activation(
                            out=uT[:, rnd * 3:rnd * 3 + 3, 3:3 + TS],
                            in_=pb[:, 0:3 * TS].rearrange("p (j t) -> p j t", j=3), func=AF.Copy), reads=[pbk], writes=[uTk])
                        yield
                    yield
                    for ch in range(6):
                        C.op("dve", lambda e: e.tensor_scalar(out=cT[:, ch, :TS], in0=uT[:, ch, 0:TS],
                                                              scalar1=cw[:, ch, 0:1], scalar2=None, op0=ALU.mult),
                             reads=[uTk, pk_], writes=[cTk])
                        for j in range(1, 4):
                            C.op("dve", lambda e: e.scalar_tensor_tensor(
                                out=cT[:, ch, :TS], in0=uT[:, ch, j:j + TS], scalar=cw[:, ch, j:j + 1],
                                in1=cT[:, ch, :TS], op0=ALU.mult, op1=ALU.add), reads=[uTk, pk_, cTk], writes=[cTk])
                    yield
                    if t == L // TS - 1:
                        for (c0_, c1_) in ((0, 4), (4, 6)):
                            pm, pmk = bank("mi")
                            for ch in range(c0_, c1_):
                                C.op("pe", lambda e: e.transpose(out=pm[:3, (ch - c0_) * 128:(ch - c0_ + 1) * 128],
                                                                 in_=uT[:, ch, TS:TS + 3], identity=identf[:, :]),
                                     reads=[uTk, constk], writes=[pmk])
                            C.op("act", lambda e: e.activation(out=pcst[:, c0_ * 128:c1_ * 128], in_=pm[:3, 0:(c1_ - c0_) * 128],
                                                               func=AF.Copy), reads=[pmk], writes=[pcstk])
                        cdst = pcv[l] if isP else scvo[l, sb_]
                        C.dma("pool", cdst, pcst[:, :], pcstk, reads=[pcstk])
                    C.op("pool", lambda e: e.tensor_copy(out=uT[:, :, 0:3], in_=uT[:, :, TS:TS + 3]),
                         reads=[uTk], writes=[uTk])
                    yield
                    C.op("act", lambda e: e.activation(out=cE[:, :, :TS], in_=cT[:, :, :TS], func=AF.Exp, scale=-1.0),
                         reads=[cTk], writes=[cEk])
                    C.op("dve", lambda e: e.tensor_scalar(out=cE[:, :, :TS], in0=cE[:, :, :TS], scalar1=1.0, scalar2=None,
                                                          op0=ALU.add), reads=[cEk], writes=[cEk])
                    C.op("dve", lambda e: e.reciprocal(out=cE[:, :, :TS], in_=cE[:, :, :TS]), reads=[cEk], writes=[cEk])
                    C.op("dve", lambda e: e.tensor_tensor(out=cT[:, :, :TS], in0=cT[:, :, :TS], in1=cE[:, :, :TS], op=ALU.mult),
                         reads=[cEk, cTk], writes=[cTk])
                    yield
                    C.op("pool", lambda e: e.tensor_tensor(out=cE[:, 0:4, :TS], in0=cT[:, 0:4, :TS], in1=cT[:, 0:4, :TS],
                                                           op=ALU.mult), reads=[cTk, cEk], writes=[cEk])
                    pm, pmk = bank("mi")
                    for j in range(4):
                        C.op("pe", lambda e: e.matmul(pm[:, j * TS:(j + 1) * TS], lhsT=bonesf[:, :], rhs=cE[:, j, :TS],
                                                      start=True, stop=True), reads=[cEk, constk], writes=[pmk])
                    C.op("act", lambda e: e.activation(out=cE[:, 0:4, :TS],
                                                       in_=pm[:, 0:4 * TS].rearrange("p (j t) -> p j t", j=4),
                                                       func=AF.Ln, bias=epsb[:, 1:2]),
                         reads=[pmk, constk], writes=[cEk])
                    C.op("act", lambda e: e.activation(out=cE[:, 0:4, :TS], in_=cE[:, 0:4, :TS], func=AF.Exp, scale=-0.5),
                         reads=[cEk], writes=[cEk])
                    yield
                    C.op("dve", lambda e: e.scalar_tensor_tensor(out=cT[:, 0:2, :TS], in0=cT[:, 0:2, :TS], scalar=0.125,
                                                                 in1=cE[:, 0:2, :TS], op0=ALU.mult, op1=ALU.mult),
                         reads=[cTk, cEk], writes=[cTk])
                    C.op("dve", lambda e: e.tensor_tensor(out=cT[:, 2:4, :TS], in0=cT[:, 2:4, :TS], in1=cE[:, 2:4, :TS],
                                                          op=ALU.mult), reads=[cTk, cEk], writes=[cTk])
                    for j in range(4):
                        for s_ in range(2):
                            hidx = (j // 2) * 4 + (j % 2) * 2 + s_
                            C.op("pool", lambda e: e.tensor_copy(out=nTh[0:64, hidx, :TS], in_=cT[s_ * 64:(s_ + 1) * 64, j, :TS]),
                                 reads=[cTk], writes=[nTk])
                    yield
                    for ck_ in range(CPT):
                        pm, pmk = bank("mi")
                        for j in range(4):
                            C.op("pe", lambda e: e.transpose(
                                out=pm[:c, j * 128:(j + 1) * 128], in_=cT[:, 2 + j, ck_ * c:(ck_ + 1) * c],
                                identity=identf[:, :]), reads=[cTk, constk], writes=[pmk])
                        C.op("act", lambda e: e.activation(out=ktok[:c, ck_, :], in_=pm[:c, :], func=AF.Copy),
                             reads=[pmk], writes=[ktokk])
                    yield

                def gen_chunks(col0):
                    po, pok = bank("pj")
                    po2, po2k = bank("pj")
                    for ck_ in range(CPT):
                        cs = slice(ck_ * c, (ck_ + 1) * c)
                        pm, pmk = bank("mi")
                        C.op("pe", lambda e: e.matmul(pm[:c, 0:4], lhsT=triu[:c, :c], rhs=gb[:c, ck_, 0:4],
                                                      start=True, stop=True), reads=[gbk, constk], writes=[pmk])
                        C.op("dve", lambda e: e.tensor_tensor(out=GT[:c, :, :c],
                                                              in0=bc(triu[:c, :c].unsqueeze(1), [c, 4, c]),
                                                              in1=bc(gb[:c, ck_, 0:4].unsqueeze(2), [c, 4, c]),
                                                              op=ALU.mult), reads=[gbk, constk], writes=[GTk])
                        pm2, pm2k = bank("mi")
                        for h in range(4):
                            C.op("pe", lambda e: e.matmul(pm2[:, h * c:(h + 1) * c], lhsT=onesf[:c, :],
                                                          rhs=GT[:c, h, :c], start=True, stop=True),
                                 reads=[GTk, constk], writes=[pm2k])
                        C.op("act", lambda e: e.activation(out=sm[:c, 0:4], in_=pm[:c, 0:4], func=AF.Copy), reads=[pmk], writes=[smk])
                        C.op("act", lambda e: e.activation(out=sm[:c, 4:8], in_=pm[:c, 0:4], func=AF.Exp), reads=[pmk], writes=[smk])
                        C.op("act", lambda e: e.activation(out=Drow[:, :, :c], in_=pm2[:, 0:4 * c].rearrange("p (h i) -> p h i", h=4),
                                                           func=AF.Copy), reads=[pm2k], writes=[Drk])
                        C.op("act", lambda e: e.activation(out=EDrow[:, :, :c],
                                                           in_=pm2[:, 0:4 * c].rearrange("p (h i) -> p h i", h=4), func=AF.Exp),
                             reads=[pm2k], writes=[Drk])
                        C.op("dve", lambda e: e.tensor_tensor(out=sm[:c, 12:16], in0=Drow[:c, :, c - 1], in1=sm[:c, 0:4],
                                                              op=ALU.subtract), reads=[Drk, smk], writes=[smk])
                        C.op("act", lambda e: e.activation(out=sm[:c, 8:12], in_=sm[:c, 12:16], func=AF.Exp), reads=[smk], writes=[smk])
                        yield
                        C.op("dve", lambda e: e.tensor_tensor(out=LM[:c, :, :c], in0=Drow[:c, :, :c],
                                                              in1=bc(sm[:c, 0:4].unsqueeze(2), [c, 4, c]), op=ALU.subtract),
                             reads=[Drk, smk], writes=[LMk])
                        C.op("dve", lambda e: e.tensor_scalar(out=LM[:c, :, :c], in0=LM[:c, :, :c], scalar1=0.0, scalar2=None,
                                                              op0=ALU.min), reads=[LMk], writes=[LMk])
                        C.op("act", lambda e: e.activation(out=LM[:c, :, :c], in_=LM[:c, :, :c], func=AF.Exp), reads=[LMk], writes=[LMk])
                        C.op("pool", lambda e: e.tensor_tensor(out=LMs[:c, :, :c], in0=LM[:c, :, :c],
                                                               in1=bc(striu[:c, :c].unsqueeze(1), [c, 4, c]), op=ALU.mult),
                             reads=[LMk, constk], writes=[LMk])
                        C.op("pool", lambda e: e.tensor_tensor(out=LMi[:c, :, :c], in0=LM[:c, :, :c],
                                                               in1=bc(triu[:c, :c].unsqueeze(1), [c, 4, c]), op=ALU.mult),
                             reads=[LMk, constk], writes=[LMk])
                        C.op("dve", lambda e: e.tensor_tensor(out=qdT[0:64, :, cs], in0=nTh[0:64, 0:4, cs],
                                                              in1=EDrow[0:64, :, :c], op=ALU.mult), reads=[nTk, Drk], writes=[qdk])
                        C.op("dve", lambda e: e.tensor_tensor(out=kz[:c, ck_, :].rearrange("p (h d) -> p h d", h=4),
                                                              in0=ktok[:c, ck_, 0:256].rearrange("p (h d) -> p h d", h=4),
                                                              in1=bc(sm[:c, 8:12].unsqueeze(2), [c, 4, 64]), op=ALU.mult),
                             reads=[ktokk, smk], writes=[kzk])
                        yield
                        pg, pgk = bank("mi")
                        for h in range(4):
                            C.op("pe", lambda e: e.matmul(pg[:c, h * c:(h + 1) * c], lhsT=nTh[0:64, 4 + h, cs], rhs=nTh[0:64, 4 + h, cs],
                                                          start=True, stop=True), reads=[nTk], writes=[pgk])
                        for h in range(4):
                            C.op("pe", lambda e: e.matmul(pg[:c, 256 + h * c:256 + (h + 1) * c], lhsT=nTh[0:64, 4 + h, cs],
                                                          rhs=nTh[0:64, h, cs], start=True, stop=True), reads=[nTk], writes=[pgk])
                        KKv = pg[:c, 0:4 * c].rearrange("p (h i) -> p h i", h=4)
                        KQv = pg[:c, 256:256 + 4 * c].rearrange("p (h i) -> p h i", h=4)
                        C.op("dve", lambda e: e.tensor_tensor(out=Xf[:c, :, :c], in0=KKv,
                                                              in1=bc(gb[:c, ck_, 4:8].unsqueeze(2), [c, 4, c]), op=ALU.mult),
                             reads=[pgk, gbk], writes=[Xk])
                        C.op("dve", lambda e: e.tensor_tensor(out=Xf[:c, :, :c], in0=Xf[:c, :, :c], in1=LMs[:c, :, :c], op=ALU.mult),
                             reads=[Xk, LMk], writes=[Xk])
                        C.op("dve", lambda e: e.tensor_tensor(out=scr[:c, :, :c], in0=KQv, in1=LMi[:c, :, :c], op=ALU.mult),
                             reads=[pgk, LMk], writes=[scrk])
                        C.op("pool", lambda e: e.tensor_copy(out=inT[:c, :, :c], in_=scr[:c, :, :c]), reads=[scrk], writes=[inTk])
                        yield
                        pm, pmk = bank("mi")
                        for h in range(4):
                            C.op("pe", lambda e: e.transpose(out=pm[:c, h * c:(h + 1) * c], in_=Xf[:c, h, :c],
                                                             identity=identf[:c, :c]), reads=[Xk, constk], writes=[pmk])
                        C.op("act", lambda e: e.activation(out=YTb[:c, 0, :, :c], in_=pm[:c, 0:4 * c].rearrange("p (h i) -> p h i", h=4),
                                                           func=AF.Copy), reads=[pmk], writes=[YTk[0]])
                        C.op("act", lambda e: e.activation(out=Yb[:c, 0, :, :c], in_=Xf[:c, :, :c], func=AF.Copy), reads=[Xk], writes=[Yk[0]])
                        C.op("dve", lambda e: e.tensor_tensor(out=Pf[:c, :, :c], in0=bc(identf[:c, :c].unsqueeze(1), [c, 4, c]),
                                                              in1=Xf[:c, :, :c], op=ALU.subtract), reads=[Xk, constk], writes=[Pk])
                        C.op("pool", lambda e: e.tensor_copy(out=Pb[:c, :, :c], in_=Pf[:c, :, :c]), reads=[Pk], writes=[Pk])
                        yield

                        def emit_prod(k):
                            pmq, pmqk = bank("mi")
                            for h in range(4):
                                C.op("pe", lambda e: e.matmul(pmq[:c, h * c:(h + 1) * c], lhsT=YTb[:c, k, h, :c],
                                                              rhs=Pb[:c, h, :c], start=True, stop=True),
                                     reads=[YTk[k], Pk], writes=[pmqk])
                            C.op("dve", lambda e: e.tensor_tensor(out=Pf[:c, :, :c], in0=Pf[:c, :, :c],
                                                                  in1=pmq[:c, 0:4 * c].rearrange("p (h i) -> p h i", h=4), op=ALU.add),
                                 reads=[pmqk, Pk], writes=[Pk])
                            C.op("pool", lambda e: e.tensor_copy(out=Pb[:c, :, :c], in_=Pf[:c, :, :c]), reads=[Pk], writes=[Pk])
                        for k in range(1, nit + 1):
                            ys, yd = (k - 1) % 2, k % 2
                            pm, pmk = bank("mi")
                            if k < nit:
                                for h in range(4):
                                    C.op("pe", lambda e: e.matmul(pm[:c, h * c:(h + 1) * c], lhsT=YTb[:c, k - 1, h, :c],
                                                                  rhs=Yb[:c, ys, h, :c], start=True, stop=True),
                                         reads=[Yk[ys], YTk[k - 1]], writes=[pmk])
                            for h in range(4):
                                C.op("pe", lambda e: e.matmul(pm[:c, 256 + h * c:256 + (h + 1) * c],
                                                              lhsT=Yb[:c, ys, h, :c], rhs=YTb[:c, k - 1, h, :c],
                                                              start=True, stop=True), reads=[Yk[ys], YTk[k - 1]], writes=[pmk])
                            if k < nit:
                                C.op("act", lambda e: e.activation(out=Yb[:c, yd, :, :c],
                                                                   in_=pm[:c, 0:4 * c].rearrange("p (h i) -> p h i", h=4), func=AF.Copy),
                                     reads=[pmk], writes=[Yk[yd]])
                            C.op("act", lambda e: e.activation(out=YTb[:c, k, :, :c],
                                                               in_=pm[:c, 256:256 + 4 * c].rearrange("p (h i) -> p h i", h=4), func=AF.Copy),
                                 reads=[pmk], writes=[YTk[k]])
                            yield
                            if k >= 2:
                                emit_prod(k - 1)
                                yield
                        emit_prod(nit)
                        yield
                        pc, pck = bank("mi")
                        for h in range(4):
                            C.op("pe", lambda e: e.matmul(pc[:c, h * 64:(h + 1) * 64], lhsT=nTh[0:64, 4 + h, cs],
                                                          rhs=Sdb[:, h, :], start=True, stop=True),
                                 reads=[nTk, Sdk], writes=[pck])
                        C.op("dve", lambda e: e.tensor_tensor(out=rt[:c, :].rearrange("p (h d) -> p h d", h=4),
                                                              in0=pc[:c, 0:256].rearrange("p (h d) -> p h d", h=4),
                                                              in1=bc(sm[:c, 4:8].unsqueeze(2), [c, 4, 64]), op=ALU.mult),
                             reads=[pck, smk], writes=[rk])
                        C.op("dve", lambda e: e.tensor_tensor(out=rb[:c, :], in0=ktok[:c, ck_, 256:512], in1=rt[:c, :], op=ALU.subtract),
                             reads=[rk, ktokk], writes=[rk])
                        yield
                        pc2, pc2k = bank("mi")
                        for h in range(4):
                            C.op("pe", lambda e: e.matmul(pc2[:c, h * 64:(h + 1) * 64], lhsT=Pb[:c, h, :c],
                                                          rhs=rb[:c, h * 64:(h + 1) * 64], start=True, stop=True),
                                 reads=[Pk, rk], writes=[pc2k])
                        C.op("dve", lambda e: e.tensor_tensor(out=rt[:c, :].rearrange("p (h d) -> p h d", h=4),
                                                              in0=pc2[:c, 0:256].rearrange("p (h d) -> p h d", h=4),
                                                              in1=bc(gb[:c, ck_, 4:8].unsqueeze(2), [c, 4, 64]), op=ALU.mult),
                             reads=[pc2k, gbk], writes=[rk])
                        C.op("pool", lambda e: e.tensor_copy(out=vnb[:c, :], in_=rt[:c, :]), reads=[rk], writes=[vnk])
                        yield
                        for h in range(4):
                            oc_ = slice(h * TS + ck_ * c, h * TS + (ck_ + 1) * c)
                            C.op("pe", lambda e: e.matmul(po[:64, oc_], lhsT=Sdb[:, h, :], rhs=qdT[0:64, h, cs],
                                                          start=True, stop=False), reads=[Sdk, qdk], writes=[pok])
                            C.op("pe", lambda e: e.matmul(po[:64, oc_], lhsT=vnb[:c, h * 64:(h + 1) * 64], rhs=inT[:c, h, :c],
                                                          start=False, stop=True), reads=[vnk, inTk], writes=[pok])
                        psu, psuk = bank("mi")
                        for h in range(4):
                            C.op("pe", lambda e: e.matmul(psu[:64, h * 64:(h + 1) * 64], lhsT=kz[:c, ck_, h * 64:(h + 1) * 64],
                                                          rhs=vnb[:c, h * 64:(h + 1) * 64], start=True, stop=True),
                                 reads=[kzk, vnk], writes=[psuk])
                        for h in range(4):
                            C.op("dve", lambda e: e.scalar_tensor_tensor(
                                out=Sd[:, h, :], in0=Sd[:, h, :], scalar=EDrow[0:64, h, c - 1:c], in1=psu[:64, h * 64:(h + 1) * 64],
                                op0=ALU.mult, op1=ALU.add), reads=[Drk, psuk, Sdk], writes=[Sdk])
                        C.op("pool", lambda e: e.tensor_copy(out=Sdb[:], in_=Sd[:]), reads=[Sdk], writes=[Sdk])
                        yield
                        pr, prk = bank("mi")
                        for h in range(4):
                            C.op("pe", lambda e: e.matmul(pr[:c, h * c:(h + 1) * c], lhsT=qkcT[0:64, 4 + h, cs], rhs=qkcT[0:64, h, cs],
                                                          start=True, stop=True), reads=[qkcTk], writes=[prk])
                        C.op("dve", lambda e: e.tensor_tensor(out=scr[:c, :, :c], in0=pr[:c, 0:4 * c].rearrange("p (h i) -> p h i", h=4),
                                                              in1=rmc[:c, :].rearrange("p (h i) -> p h i", h=4)[:, :, :c], op=ALU.mult),
                             reads=[prk, constk], writes=[scrk])
                        C.op("pool", lambda e: e.tensor_copy(out=inR[:c, :, :c], in_=scr[:c, :, :c]), reads=[scrk], writes=[inRk])
                        yield
                        for h in range(4):
                            oc_ = slice(h * TS + ck_ * c, h * TS + (ck_ + 1) * c)
                            C.op("pe", lambda e: e.matmul(po2[:64, oc_], lhsT=Srb[:, h, :], rhs=qkcT[0:64, h, cs],
                                                          start=True, stop=False), reads=[Srk, qkcTk], writes=[po2k])
                            C.op("pe", lambda e: e.matmul(po2[:64, oc_], lhsT=vcb[:c, ck_, h * 64:(h + 1) * 64], rhs=inR[:c, h, :c],
                                                          start=False, stop=True), reads=[vck, inRk], writes=[po2k])
                        psu, psuk = bank("mi")
                        for h in range(4):
                            C.op("pe", lambda e: e.matmul(psu[:64, h * 64:(h + 1) * 64], lhsT=kzc[:c, ck_, h * 64:(h + 1) * 64],
                                                          rhs=vcb[:c, ck_, h * 64:(h + 1) * 64], start=True, stop=True),
                                 reads=[vck], writes=[psuk])
                        for h in range(4):
                            C.op("dve", lambda e: e.scalar_tensor_tensor(out=Sr[:, h, :], in0=Sr[:, h, :], scalar=float(gch[h]),
                                                                         in1=psu[:64, h * 64:(h + 1) * 64], op0=ALU.mult, op1=ALU.add),
                                 reads=[psuk, Srk], writes=[Srk])
                        C.op("pool", lambda e: e.tensor_copy(out=Srb[:], in_=Sr[:]), reads=[Srk], writes=[Srk])

                    yield
                    for which in range(2):
                        pso, psok = (po, pok) if which == 0 else (po2, po2k)
                        pov = pso[:64, 0:4 * TS].rearrange("p (h t) -> p h t", h=4)
                        if which == 0:
                            C.op("act", lambda e: e.activation(out=ot[:, :, :TS], in_=pov, func=AF.Copy), reads=[psok], writes=[otk])
                        else:
                            C.op("dve", lambda e: e.tensor_tensor(out=ot[:, :, :TS], in0=pov,
                                                                  in1=xit[:, :].rearrange("p (h t) -> p h t", h=4)[:, :, :TS], op=ALU.mult),
                                 reads=[psok, constk], writes=[otk])
                        C.op("pool", lambda e: e.tensor_tensor(out=otb[:, :, :TS], in0=ot[:, :, :TS], in1=ot[:, :, :TS], op=ALU.mult),
                             reads=[otk], writes=[otk])
                        yield
                        pm, pmk = bank("mi")
                        for h in range(4):
                            C.op("pe", lambda e: e.matmul(pm[:64, h * TS:(h + 1) * TS], lhsT=onesb[:64, :64], rhs=otb[:, h, :TS],
                                                          start=True, stop=True), reads=[otk, constk], writes=[pmk])
                        C.op("act", lambda e: e.activation(out=ot2[:, :, :TS], in_=pm[:64, 0:4 * TS].rearrange("p (h t) -> p h t", h=4),
                                                           func=AF.Ln, scale=1.0 / 64, bias=epsb[:64, 1:2]),
                             reads=[pmk, constk], writes=[otk])
                        C.op("act", lambda e: e.activation(out=ot2[:, :, :TS], in_=ot2[:, :, :TS], func=AF.Exp, scale=-0.5),
                             reads=[otk], writes=[otk])
                        yield
                        C.op("dve", lambda e: e.tensor_tensor(out=ot[:, :, :TS], in0=ot[:, :, :TS], in1=ot2[:, :, :TS], op=ALU.mult),
                             reads=[otk], writes=[otk])
                        C.op("pool", lambda e: e.tensor_tensor(out=ot[:, :, :TS], in0=ot[:, :, :TS],
                                                               in1=sgT[:, which * 4:which * 4 + 4, :TS], op=ALU.mult),
                             reads=[otk, sgTk], writes=[otk])
                        for h in range(4):
                            p_, s_ = divmod(h, 2)
                            dst = mixT[s_ * 64:(s_ + 1) * 64, 4 + which * 2 + p_, col0:col0 + TS]
                            if which == 0:
                                C.op("dve", lambda e: e.tensor_scalar(out=dst, in0=ot[:, h, :TS], scalar1=dlngc[:, 0:1],
                                                                      scalar2=None, op0=ALU.mult),
                                     reads=[otk, pk_], writes=[mixk])
                            else:
                                C.op("pool", lambda e: e.tensor_copy(out=dst, in_=ot[:, h, :TS]), reads=[otk], writes=[mixk])

                    yield

                def back_tile(ti, t):
                    r0 = t * TS
                    col0 = ti * TS
                    C.dma("sp", xres[:TS, :], xcs[row0 + r0:row0 + r0 + TS, :], xk,
                          reads=[dtk("xcs", row0 + r0)], writes=[xk])
                    for half in range(2):
                        pb, pbk = bank("pj")
                        for k in range(8):
                            C.op("pe", lambda e: e.matmul(pb[:TS, :], lhsT=(mixA if k < 4 else mixT)[:, k, col0:col0 + TS],
                                                          rhs=WA_out[:, k, half * 512:(half + 1) * 512], start=(k == 0), stop=(k == 7)),
                                 reads=[mixk, mixAk, WAK2], writes=[pbk])
                        C.op("dve", lambda e: e.scalar_tensor_tensor(out=xres[:TS, half * 512:(half + 1) * 512],
                                                                     in0=xres[:TS, half * 512:(half + 1) * 512], scalar=float(ALPHA),
                                                                     in1=pb[:TS, :], op0=ALU.mult, op1=ALU.add),
                             reads=[pbk, xk], writes=[xk])
                    layer_norm(xres[:TS, :], TS, g1, b1, pk_, xk)
                    C.dma("pool", x1s[row0 + r0:row0 + r0 + TS, :], xres[:TS, :], xk, reads=[xk], writes=[dtk("x1s", row0 + r0)])


                def gen_attention(blk):
                    if isP:
                        kts = [(kt, 128, None) for kt in range(2 * blk)] + [(2 * blk, 128, 0), (2 * blk + 1, 128, 1)]
                    else:
                        kts = [(kt, 128, None) for kt in range(past // 128)] + [(past // 128, TS, None)]
                    its = [(hd, kt, rows, msk) for hd in range(4) for (kt, rows, msk) in kts]
                    nk = len(kts)

                    def load_v(ii):
                        hd, kt, rows, msk = its[ii]
                        vt, vtk = Vt[ii % 3], Vtk[ii % 3]
                        if kt < past // 128:
                            C.dma("pool", vt[:rows, :], cv[l, sb_, kt * 128:kt * 128 + rows, hd * 128:(hd + 1) * 128], vtk, writes=[vtk])
                        else:
                            C.dma("sp", vt[:rows, :], vsc[kt * 128:kt * 128 + rows, hd * 128:(hd + 1) * 128], vtk,
                                  reads=[dtk("vsc", kt)], writes=[vtk])

                    def emit_s(ii):
                        hd, kt, rows, msk = its[ii]
                        sbk_, sbkk = PS[2 + ii % 2], PSK[2 + ii % 2]
                        kc0 = kt * 128
                        C.op("pe", lambda e: e.matmul(sbk_[:rows, 0:nq], lhsT=KT[:, hd, kc0:kc0 + rows], rhs=QTa[:, hd, 0:nq],
                                                      start=True, stop=True), reads=[ktk[kt], QTk], writes=[sbkk])
                        C.op("pe", lambda e: e.matmul(sbk_[:rows, 256:256 + nq], lhsT=KT[:, hd, kc0:kc0 + rows],
                                                      rhs=QTb[:, hd, 0:nq], start=True, stop=True),
                             reads=[ktk[kt], QTk], writes=[sbkk])
                    load_v(0)
                    if len(its) > 1:
                        load_v(1)
                    emit_s(0)
                    for ii, (hd, kt, rows, msk) in enumerate(its):
                        if ii + 2 < len(its):
                            load_v(ii + 2)
                        if ii + 1 < len(its):
                            emit_s(ii + 1)
                        first = (ii % nk == 0)
                        last = (ii % nk == nk - 1)
                        sbk_, sbkk = PS[2 + ii % 2], PSK[2 + ii % 2]
                        pt, ptk = PT[ii % 2], PTk[ii % 2]
                        C.op("act", lambda e: e.activation(out=pt[:rows, :, 0:nq],
                                                           in_=sbk_[:rows, :].rearrange("p (j q) -> p j q", j=2)[:, :, 0:nq],
                                                           func=AF.Exp, scale=0.125), reads=[sbkk], writes=[ptk])
                        if msk is not None:
                            C.op("pool", lambda e: e.tensor_tensor(out=pt[:rows, :, 0:nq], in0=pt[:rows, :, 0:nq],
                                                                   in1=bc(amask[:rows, msk * 256:msk * 256 + nq].unsqueeze(1), [rows, 2, nq]),
                                                                   op=ALU.mult), reads=[ptk, constk], writes=[ptk])
                        if first:
                            C.op("dve", lambda e: e.memset(OB[:, :], 0.0), writes=[OBK])
                            C.op("dve", lambda e: e.memset(LB[:, :], 0.0), writes=[LBK])
                        vt, vtk = Vt[ii % 3], Vtk[ii % 3]
                        for j in range(2):
                            C.op("pe", lambda e: e.matmul(OB[:, j * 256:j * 256 + nq], lhsT=vt[:rows, :],
                                                          rhs=pt[:rows, j, 0:nq], start=False, stop=False, skip_group_check=True),
                                 reads=[vtk, ptk], writes=[OBK])
                            C.op("pe", lambda e: e.matmul(LB[:, j * 256:j * 256 + nq], lhsT=onesb[:rows, :], rhs=pt[:rows, j, 0:nq],
                                                          start=False, stop=False, skip_group_check=True),
                                 reads=[constk, ptk], writes=[LBK])
                        if last:
                            Lv = LB[:, :].rearrange("p (j q) -> p j q", j=2)[:, :, 0:nq]
                            Ov = OB[:, :].rearrange("p (j q) -> p j q", j=2)[:, :, 0:nq]
                            C.op("dve", lambda e: e.reciprocal(out=Rr[:, :, 0:nq], in_=Lv), reads=[LBK], writes=[atk])
                            C.op("dve", lambda e: e.tensor_tensor(out=Tt[:, :, 0:nq], in0=Ov, in1=Rr[:, :, 0:nq], op=ALU.mult),
                                 reads=[OBK, atk], writes=[atk])
                            C.op("dve", lambda e: e.scalar_tensor_tensor(out=oaf[:, 0:nq], in0=Tt[:, 1, 0:nq], scalar=neglam[:, 0:1],
                                                                         in1=Tt[:, 0, 0:nq], op0=ALU.mult, op1=ALU.add),
                                 reads=[atk, pk_], writes=[atk])
                            C.op("pool", lambda e: e.tensor_tensor(out=oab[:, 0:nq], in0=oaf[:, 0:nq], in1=oaf[:, 0:nq], op=ALU.mult),
                                 reads=[atk], writes=[atk])
                            yield
                            pm, pmk = bank("mi")
                            C.op("pe", lambda e: e.matmul(pm[:, 0:nq], lhsT=onesb[:, :], rhs=oab[:, 0:nq], start=True, stop=True),
                                 reads=[atk, constk], writes=[pmk])
                            C.op("act", lambda e: e.activation(out=Rr[:, 0, 0:nq], in_=pm[:, 0:nq], func=AF.Ln, scale=1.0 / 128,
                                                               bias=epsb[:, 1:2]), reads=[pmk, constk], writes=[atk])
                            C.op("act", lambda e: e.activation(out=Rr[:, 0, 0:nq], in_=Rr[:, 0, 0:nq], func=AF.Exp, scale=-0.5),
                                 reads=[atk], writes=[atk])
                            C.op("dve", lambda e: e.tensor_tensor(out=oaf[:, 0:nq], in0=oaf[:, 0:nq], in1=Rr[:, 0, 0:nq], op=ALU.mult),
                                 reads=[atk], writes=[atk])
                            C.op("dve", lambda e: e.tensor_scalar(out=mixA[:, hd, 0:nq], in0=oaf[:, 0:nq], scalar1=dngc[:, 0:1],
                                                                  scalar2=float(1.0 - lam_init), op0=ALU.mult, op1=ALU.mult),
                                 reads=[atk, pk_], writes=[mixAk])
                        yield

                def run_gens(*gens):
                    live = list(gens)
                    while live:
                        for g in list(live):
                            try:
                                next(g)
                            except StopIteration:
                                live.remove(g)

                def chain(*gs):
                    for g_ in gs:
                        yield from g_

                for blk in range(nblk):
                    t0_ = blk * TPB
                    run_gens(gen_frontA(0, t0_, "pj"), gen_frontB(t0_, 0))
                    if TPB == 2:
                        run_gens(gen_chunks(0), gen_frontA(1, t0_ + 1, "s"))
                        run_gens(gen_attention(blk), chain(gen_frontB(t0_ + 1, TS), gen_chunks(TS)))
                    else:
                        run_gens(gen_attention(blk), gen_chunks(0))
                    for ti in range(TPB):
                        back_tile(ti, t0_ + ti)

                if STOP == 1:
                    C.finish()
                    return nc
                ddst = pdl[l] if isP else sdlo[l, sb_]
                rdst = prt[l] if isP else srto[l, sb_]
                C.dma("pool", ddst.rearrange("(h d) e -> d h e", d=64), Sd[:], Sdk, reads=[Sdk])
                C.dma("pool", rdst.rearrange("(h d) e -> d h e", d=64), Sr[:], Srk, reads=[Srk])

        if STOP == 20 + l:
            C.finish()
            return nc
        C.barrier()
        WA_up = WA[:, 0:8 * DFF].rearrange("p (k n) -> p k n", k=8)
        WA_dn = WA[:, 8 * DFF:8 * DFF + 32 * D].rearrange("p (k n) -> p k n", k=32)
        WAK2 = Tk("WA2f")
        for k in range(8):
            for c0_ in range(0, DFF, 1024):
                C.dma("pool", WA_up[:, k, c0_:c0_ + 1024], w_up[l, k * 128:(k + 1) * 128, c0_:c0_ + 1024], WAK, writes=[WAK])
        for k in range(32):
            C.dma("pool", WA_dn[:, k, :], w_down[l, k * 128:(k + 1) * 128, :], WAK2, writes=[WAK2])
        with ExitStack() as st:
            def T(name, shape, dt=F32):
                return st.enter_context(nc.sbuf_tensor("%s_f%d" % (name, l), list(shape), dt))
            g2 = T("g2", [128, D]); b2 = T("b2", [128, D]); pk2 = Tk()
            C.dma("sp", g2[:], ln2g[l:l + 1, :].partition_broadcast(128), pk2, writes=[pk2])
            C.dma("sp", b2[:], ln2b[l:l + 1, :].partition_broadcast(128), pk2, writes=[pk2])
            x1 = T("x1", [128, 2, D]); x1k = [Tk(), Tk()]
            x1b = T("x1b", [128, D], BF16); x1bk = Tk()
            x1T = T("x1T", [128, 8, 256], BF16); x1Tk = Tk()
            hidT = T("hidT", [128, 32, 256], BF16); hidk = Tk()
            rl = [T("rl0", [128, 256]), T("rl1", [128, 256])]; rlk = [Tk(), Tk()]
            blocks = []
            for b0_ in range(0, Lp, 256):
                blocks.append((b0_, 128, min(2, (Lp - b0_) // 128)))
            for b_ in range(NS):
                blocks.append((Lp + b_ * Ls, Ls, 1))
            for (rb0, TS, ntl) in blocks:
                nb = TS * ntl
                for ti in range(ntl):
                    r0 = rb0 + ti * TS
                    C.dma("sp", x1[:TS, ti, :], x1s[r0:r0 + TS, :], x1k[ti], reads=[dtk("x1s", r0)], writes=[x1k[ti]])
                    C.op("act", lambda e: e.activation(out=x1b[:TS, :], in_=x1[:TS, ti, :], func=AF.Copy), reads=[x1k[ti]], writes=[x1bk])
                    pb, pbk = bank("mi")
                    pbb = pb[:].bitcast(BF16)
                    for k in range(8):
                        C.op("pe", lambda e: e.transpose(out=pbb[:, k * TS:(k + 1) * TS], in_=x1b[:TS, k * 128:(k + 1) * 128],
                                                         identity=identb[:TS, :TS]), reads=[x1bk, constk], writes=[pbk])
                    C.op("dve", lambda e: e.tensor_copy(out=x1T[:, :, ti * TS:(ti + 1) * TS],
                                                        in_=pbb[:, 0:8 * TS].rearrange("p (k t) -> p k t", k=8)),
                         reads=[pbk], writes=[x1Tk])
                for f in range(32):
                    pb, pbk = bank("pj")
                    for k in range(8):
                        C.op("pe", lambda e: e.matmul(pb[:, 0:nb], lhsT=WA_up[:, k, f * 128:(f + 1) * 128], rhs=x1T[:, k, 0:nb],
                                                      start=(k == 0), stop=(k == 7)), reads=[x1Tk, WAK], writes=[pbk])
                    r_, rk_ = rl[f % 2], rlk[f % 2]
                    C.op("act", lambda e: e.activation(out=r_[:, 0:nb], in_=pb[:, 0:nb], func=AF.Relu), reads=[pbk], writes=[rk_])
                    eng = "pool" if f % 2 == 0 else "dve"
                    C.op(eng, lambda e: e.tensor_tensor(out=hidT[:, f, 0:nb], in0=r_[:, 0:nb], in1=r_[:, 0:nb], op=ALU.mult),
                         reads=[rk_], writes=[hidk])
                for ti in range(ntl):
                    r0 = rb0 + ti * TS
                    for half in range(2):
                        pb, pbk = bank("pj")
                        for f in range(32):
                            C.op("pe", lambda e: e.matmul(pb[:TS, :], lhsT=hidT[:, f, ti * TS:(ti + 1) * TS],
                                                          rhs=WA_dn[:, f, half * 512:(half + 1) * 512], start=(f == 0), stop=(f == 31)),
                                 reads=[hidk, WAK2], writes=[pbk])
                        C.op("dve", lambda e: e.scalar_tensor_tensor(out=x1[:TS, ti, half * 512:(half + 1) * 512],
                                                                     in0=x1[:TS, ti, half * 512:(half + 1) * 512], scalar=float(ALPHA),
                                                                     in1=pb[:TS, :], op0=ALU.mult, op1=ALU.add),
                             reads=[pbk, x1k[ti]], writes=[x1k[ti]])
                    layer_norm(x1[:TS, ti, :], TS, g2, b2, pk2, x1k[ti])
                    if l < DEPTH - 1:
                        C.dma("pool", xcs[r0:r0 + TS, :], x1[:TS, ti, :], x1k[ti], reads=[x1k[ti]], writes=[dtk("xcs", r0)])
                    else:
                        if r0 < Lp:
                            C.dma("pool", yp[r0:r0 + TS, :], x1[:TS, ti, :], x1k[ti], reads=[x1k[ti]])
                        else:
                            C.dma("pool", ys[r0 - Lp:r0 - Lp + TS, :], x1[:TS, ti, :], x1k[ti], reads=[x1k[ti]])
    C.finish()
    return nc


def make_consts(Lp, Ls, PAST):
    c = {}
    c["c_ident"] = np.eye(128, dtype=np.float32)
    half = 32
    inv_freq = (10000.0 ** (-np.arange(half, dtype=np.float32) / half)).astype(np.float32)

    def rot(pos):
        ang = pos.astype(np.float32)[:, None] * inv_freq[None, :]
        cos = np.cos(ang).astype(np.float32)
        sin = np.sin(ang).astype(np.float32)
        return np.concatenate([cos, cos, -sin, sin], axis=1).astype(np.float32)

    c["c_rotp"] = rot(np.arange(Lp))
    c["c_rots"] = rot(PAST + np.arange(Ls))
    k = np.arange(128)[:, None]
    qq = np.arange(256)[None, :]
    m0 = ((0 + k // 64) <= (qq // 64)).astype(np.float32)
    m1 = ((2 + k // 64) <= (qq // 64)).astype(np.float32)
    c["c_amask"] = np.concatenate([m0, m1], axis=1)
    j = np.arange(64)[:, None]
    i = np.arange(64)[None, :]
    c["c_triu"] = (j <= i).astype(np.float32)
    c["c_striu"] = (j < i).astype(np.float32)
    lg = np.log(1.0 - 2.0 ** (-5.0 - np.arange(4, dtype=np.float64)))
    rm = np.zeros((64, 4, 64), np.float64)
    for h in range(4):
        rm[:, h, :] = np.exp(-lg[h] * (j + 1.0)) * (j <= i)
    c["c_rm"] = rm.reshape(64, 256).astype(np.float32)
    xi = np.zeros((64, 4, 128), np.float64)
    xis = np.zeros((64, 4, Ls), np.float64)
    for h in range(4):
        xi[:, h, :] = np.exp(lg[h] * ((np.arange(128) % 64) + 1.0))[None, :]
        xis[:, h, :] = np.exp(lg[h] * (np.arange(Ls) + 1.0))[None, :]
    c["c_xi"] = xi.reshape(64, 512).astype(np.float32)
    c["c_xis"] = xis.reshape(64, 4 * Ls).astype(np.float32)
    zeta = np.zeros((128, 4), np.float64)
    zetas = np.zeros((Ls, 4), np.float64)
    for h in range(4):
        zeta[:, h] = np.exp(lg[h] * (63.0 - (np.arange(128) % 64)))
        zetas[:, h] = np.exp(lg[h] * (Ls - 1.0 - np.arange(Ls)))
    c["c_zeta"] = zeta.astype(np.float32)
    c["c_zetas"] = zetas.astype(np.float32)
    bo = np.zeros((128, 128), np.float32)
    bo[:64, :64] = 1.0
    bo[64:, 64:] = 1.0
    c["c_bones"] = bo
    return c


_NC_CACHE = {}


def run(inputs, Lp, NS, Ls, PAST, ncores):
    key = (Lp, NS, Ls, PAST)
    if key not in _NC_CACHE:
        _NC_CACHE[key] = build(Lp, NS, Ls, PAST)
    nc = _NC_CACHE[key]
    f = lambda a: np.ascontiguousarray(np.asarray(a, dtype=np.float32))
    consts = make_consts(Lp, Ls, PAST)
    I = {k: f(v) for k, v in inputs.items()}
    in_maps = []
    for i in range(ncores):
        sl = slice(i * NS, (i + 1) * NS)
        m = dict(consts)
        m["xp"] = f(I["x_prompt"][i])
        m["xs"] = f(I["x_sample"][sl].reshape(NS * Ls, D))
        m["ck"] = f(I["cache_k"][:, sl].reshape(DEPTH, NS, PAST, 512))
        m["cv"] = f(I["cache_v"][:, sl].reshape(DEPTH, NS, PAST, 512))
        m["sdl"] = f(I["state_delta"][:, sl].reshape(DEPTH, NS, 256, 64))
        m["scv"] = f(I["state_conv"][:, sl])
        m["srt"] = f(I["state_ret"][:, sl].reshape(DEPTH, NS, 256, 64))
        m["ln0g"] = f(I["ln0_g"].reshape(1, D)); m["ln0b"] = f(I["ln0_b"].reshape(1, D))
        m["w_in"] = I["w_in"]
        m["lamq1"] = I["lam_q1"]; m["lamk1"] = I["lam_k1"]; m["lamq2"] = I["lam_q2"]; m["lamk2"] = I["lam_k2"]
        m["dng"] = I["diff_norm_g"]; m["convw"] = I["conv_w"]; m["alog"] = I["a_log"]; m["dtb"] = I["dt_bias"]
        m["dlng"] = I["delta_norm_g"]; m["w_out"] = I["w_out"]
        m["ln1g"] = I["ln1_g"]; m["ln1b"] = I["ln1_b"]; m["w_up"] = I["w_up"]; m["w_down"] = I["w_down"]
        m["ln2g"] = I["ln2_g"]; m["ln2b"] = I["ln2_b"]
        in_maps.append(m)
    res = run_bass_kernel_spmd(nc, in_maps, core_ids=list(range(ncores)))
    R = res.results
    B = ncores
    st = lambda name: np.stack([np.asarray(R[i][name]) for i in range(B)])
    y_p = st("yp")
    y_s = st("ys").reshape(B * NS, Ls, D)
    p_k = st("pk").transpose(1, 0, 2, 3).reshape(DEPTH, B, Lp, 8, 64)
    p_v = st("pv").transpose(1, 0, 2, 3).reshape(DEPTH, B, Lp, 4, 128)
    p_d = st("pdl").transpose(1, 0, 2, 3).reshape(DEPTH, B, 4, 64, 64)
    p_c = st("pcv").transpose(1, 0, 2, 3)
    p_r = st("prt").transpose(1, 0, 2, 3).reshape(DEPTH, B, 4, 64, 64)
    s_k = st("sk").transpose(1, 0, 2, 3).reshape(DEPTH, B * NS, Ls, 8, 64)
    s_v = st("sv").transpose(1, 0, 2, 3).reshape(DEPTH, B * NS, Ls, 4, 128)
    s_d = st("sdlo").transpose(1, 0, 2, 3, 4).reshape(DEPTH, B * NS, 4, 64, 64)
    s_c = st("scvo").transpose(1, 0, 2, 3, 4).reshape(DEPTH, B * NS, 3, 768)
    s_r = st("srto").transpose(1, 0, 2, 3, 4).reshape(DEPTH, B * NS, 4, 64, 64)
    outs = (y_p, y_s, p_k, p_v, p_d, p_c, p_r, s_k, s_v, s_d, s_c, s_r)
    return tuple(np.ascontiguousarray(o.astype(np.float32)) for o in outs)


def kernel(**inputs):
    return run(inputs, 4096, 2, 32, 2048, NCORES)
```

```python
import math
import numpy as np
import ml_dtypes
import concourse.bass as bass
import concourse.mybir as mybir
from concourse.bass_utils import run_bass_kernel_spmd

F32 = mybir.dt.float32
BF16 = mybir.dt.bfloat16
AF = mybir.ActivationFunctionType
ALU = mybir.AluOpType
AX = mybir.AxisListType

D = 1024
NIN = 3592
DFF = 4096
DEPTH = 2
ALPHA = (2 * DEPTH) ** 0.25
LN_EPS = 1e-5
NORM_EPS = 1e-6
NCORES = 8


class Tk:
    __slots__ = ("w", "r", "dsem", "dcnt", "name")

    def __init__(self, name=""):
        self.w = None
        self.r = {}
        self.dsem = None
        self.dcnt = 0
        self.name = name


class Ctx:
    def __init__(self, nc):
        self.nc = nc
        self.E = {"pe": nc.tensor, "dve": nc.vector, "act": nc.scalar, "pool": nc.gpsimd,
                  "sp": nc.sync}
        self.sem = {k: nc.alloc_semaphore("es_" + k) for k in self.E}
        self.cnt = {k: 0 for k in self.E}
        self.seen = {k: {} for k in self.E}
        self.dsems = []
        self.nd = 0

    def _wait(self, e, dep):
        key, sem, val = dep
        if self.seen[e].get(key, 0) >= val:
            return
        self.E[e].wait_ge(sem, val)
        self.seen[e][key] = val

    def _deps(self, e, reads, writes):
        for t in reads:
            if t.w is not None:
                self._wait(e, t.w)
        for t in writes:
            if t.w is not None:
                self._wait(e, t.w)
            for d in t.r.values():
                self._wait(e, d)

    def _mark(self, me, reads, writes):
        for t in reads:
            old = t.r.get(me[0])
            if old is None or old[2] < me[2]:
                t.r[me[0]] = me
        for t in writes:
            t.w = me
            t.r = {}

    def op(self, e, fn, reads=(), writes=()):
        self._deps(e, reads, writes)
        ins = fn(self.E[e])
        self.cnt[e] += 1
        ins.then_inc(self.sem[e], 1)
        me = (e, self.sem[e], self.cnt[e])
        if e == "pe":
            self.seen[e][e] = self.cnt[e]
        self._mark(me, reads, writes)
        return ins

    def dma(self, q, out, in_, owner, reads=(), writes=(), **kw):
        self._deps(q, reads, writes)
        if owner.dsem is None:
            owner.dsem = self.nc.alloc_semaphore("ds%d" % self.nd)
            self.nd += 1
            self.dsems.append(owner)
        ins = self.E[q].dma_start(out=out, in_=in_, **kw)
        owner.dcnt += 16
        ins.then_inc(owner.dsem, 16)
        me = ("d%d" % id(owner), owner.dsem, owner.dcnt)
        self._mark(me, reads, writes)
        return ins

    def barrier(self):
        for e in self.E:
            for f in self.E:
                if f != e and self.cnt[f] > 0:
                    self._wait(e, (f, self.sem[f], self.cnt[f]))
            for o in self.dsems:
                if o.dcnt > 0:
                    self._wait(e, ("d%d" % id(o), o.dsem, o.dcnt))

    def finish(self):
        self.barrier()


def bc(ap, shape):
    return ap.to_broadcast(list(shape))


import os
from contextlib import ExitStack
STOP = int(os.environ.get('KSTOP', '99'))


def build(Lp=4096, NS=2, Ls=32, PAST=2048):
    nc = bass.Bass("TRN2", target_bir_lowering=False)
    C = Ctx(nc)
    Ltot = Lp + NS * Ls
    LK = max(Lp, PAST + Ls)

    def din(name, shape, dt=F32):
        return nc.dram_tensor(name, list(shape), dt, kind="ExternalInput").ap()

    def dout(name, shape):
        return nc.dram_tensor(name, list(shape), F32, kind="ExternalOutput").ap()

    xp = din("xp", [Lp, D])
    xs = din("xs", [NS * Ls, D])
    ck = din("ck", [DEPTH, NS, PAST, 512])
    cv = din("cv", [DEPTH, NS, PAST, 512])
    sdl = din("sdl", [DEPTH, NS, 256, 64])
    scv = din("scv", [DEPTH, NS, 3, 768])
    srt = din("srt", [DEPTH, NS, 256, 64])
    ln0g = din("ln0g", [1, D]); ln0b = din("ln0b", [1, D])
    w_in = din("w_in", [DEPTH, D, NIN])
    lamq1 = din("lamq1", [DEPTH, 64]); lamk1 = din("lamk1", [DEPTH, 64])
    lamq2 = din("lamq2", [DEPTH, 64]); lamk2 = din("lamk2", [DEPTH, 64])
    dng = din("dng", [DEPTH, 128])
    convw = din("convw", [DEPTH, 4, 768])
    alog = din("alog", [DEPTH, 4]); dtb = din("dtb", [DEPTH, 4])
    dlng = din("dlng", [DEPTH, 64])
    w_out = din("w_out", [DEPTH, D, D])
    ln1g = din("ln1g", [DEPTH, D]); ln1b = din("ln1b", [DEPTH, D])
    w_up = din("w_up", [DEPTH, D, DFF])
    w_down = din("w_down", [DEPTH, DFF, D])
    ln2g = din("ln2g", [DEPTH, D]); ln2b = din("ln2b", [DEPTH, D])
    c_ident = din("c_ident", [128, 128])
    c_rotp = din("c_rotp", [Lp, 128])
    c_rots = din("c_rots", [Ls, 128])
    c_amask = din("c_amask", [128, 512])
    c_triu = din("c_triu", [64, 64])
    c_striu = din("c_striu", [64, 64])
    c_rm = din("c_rm", [64, 256])
    c_xi = din("c_xi", [64, 512])
    c_xis = din("c_xis", [64, 4 * Ls])
    c_zeta = din("c_zeta", [128, 4])
    c_zetas = din("c_zetas", [Ls, 4])
    c_bones = din("c_bones", [128, 128])

    yp = dout("yp", [Lp, D]); ys = dout("ys", [NS * Ls, D])
    pk = dout("pk", [DEPTH, Lp, 512]); pv = dout("pv", [DEPTH, Lp, 512])
    pdl = dout("pdl", [DEPTH, 256, 64]); pcv = dout("pcv", [DEPTH, 3, 768])
    prt = dout("prt", [DEPTH, 256, 64])
    sk = dout("sk", [DEPTH, NS * Ls, 512]); sv = dout("sv", [DEPTH, NS * Ls, 512])
    sdlo = dout("sdlo", [DEPTH, NS, 256, 64]); scvo = dout("scvo", [DEPTH, NS, 3, 768])
    srto = dout("srto", [DEPTH, NS, 256, 64])
    x1s = nc.dram_tensor("x1s", [Ltot, D], F32).ap()
    xcs = nc.dram_tensor("xcs", [Ltot, D], F32).ap()
    vsc = nc.dram_tensor("vsc", [LK, 512], BF16).ap()
    dram_tk = {}

    def dtk(name, r0):
        k = (name, r0)
        if k not in dram_tk:
            dram_tk[k] = Tk()
        return dram_tk[k]

    PS = [nc.alloc_psum_tensor("ps%d" % i, [128, 512], F32) for i in range(8)]
    PSK = [Tk("ps%d" % i) for i in range(8)]
    rot_state = {"pj": 0, "mi": 0, "s": 0}

    def bank(kind):
        base, n = {"pj": (0, 2), "s": (2, 2), "mi": (6, 2)}[kind]
        i = base + rot_state[kind] % n
        rot_state[kind] += 1
        return PS[i], PSK[i]

    OB, OBK = PS[4], PSK[4]
    LB, LBK = PS[5], PSK[5]

    def sb(name, shape, dt=F32):
        return nc.alloc_sbuf_tensor(name, list(shape), dt)

    identf = sb("identf", [128, 128]); identb = sb("identb", [128, 128], BF16)
    onesb = sb("onesb", [128, 128], BF16)
    onesf = sb("onesf", [128, 128])
    bonesf = sb("bonesf", [128, 128])
    triu = sb("triu", [64, 64]); striu = sb("striu", [64, 64])
    rmc = sb("rmc", [64, 256]); xic = sb("xic", [64, 512]); xisc = sb("xisc", [64, 4 * Ls])
    zetac = sb("zetac", [128, 4]); zetasc = sb("zetasc", [Ls, 4])
    amask = sb("amask", [128, 512], BF16)
    epsb = sb("epsb", [128, 2])
    constk = Tk("const")
    q = "sp"
    C.dma(q, identf[:], c_ident[:, :], constk, writes=[constk])
    C.dma(q, bonesf[:], c_bones[:, :], constk, writes=[constk])
    C.dma(q, triu[:], c_triu[:, :], constk, writes=[constk])
    C.dma(q, striu[:], c_striu[:, :], constk, writes=[constk])
    C.dma(q, rmc[:], c_rm[:, :], constk, writes=[constk])
    C.dma(q, xic[:], c_xi[:, :], constk, writes=[constk])
    C.dma(q, xisc[:], c_xis[:, :], constk, writes=[constk])
    C.dma(q, zetac[:], c_zeta[:, :], constk, writes=[constk])
    C.dma(q, zetasc[:], c_zetas[:, :], constk, writes=[constk])
    C.dma("pool", amask[:], c_amask[:, :], constk, writes=[constk])
    C.op("dve", lambda e: e.tensor_copy(out=identb[:], in_=identf[:]), reads=[constk], writes=[constk])
    C.op("dve", lambda e: e.memset(onesb[:], 1.0), writes=[constk])
    C.op("dve", lambda e: e.memset(onesf[:], 1.0), writes=[constk])
    C.op("dve", lambda e: e.memset(epsb[:, 0:1], LN_EPS), writes=[constk])
    C.op("dve", lambda e: e.memset(epsb[:, 1:2], NORM_EPS), writes=[constk])

    WA = sb("WA", [128, 65536], BF16)
    WAK = Tk("WA")

    lnst = sb("lnst", [128, 2, 6]); lnmv = sb("lnmv", [128, 2]); lnk = Tk("ln")

    def layer_norm(xap, n, gt, bt, gk, xk):
        tks = [xk]
        C.op("dve", lambda e: e.bn_stats(out=lnst[:n, 0, :], in_=xap[:, 0:512]), reads=tks, writes=[lnk])
        C.op("dve", lambda e: e.bn_stats(out=lnst[:n, 1, :], in_=xap[:, 512:1024]), reads=tks, writes=[lnk])
        C.op("dve", lambda e: e.bn_aggr(out=lnmv[:n, :], in_=lnst[:n, :, :].rearrange("p a b -> p (a b)")),
             reads=[lnk], writes=[lnk])
        C.op("act", lambda e: e.activation(out=lnmv[:n, 1:2], in_=lnmv[:n, 1:2], func=AF.Ln, bias=epsb[:n, 0:1]),
             reads=[lnk, constk], writes=[lnk])
        C.op("act", lambda e: e.activation(out=lnmv[:n, 1:2], in_=lnmv[:n, 1:2], func=AF.Exp, scale=-0.5),
             reads=[lnk], writes=[lnk])
        C.op("dve", lambda e: e.tensor_scalar(out=xap, in0=xap, scalar1=lnmv[:n, 0:1], scalar2=lnmv[:n, 1:2],
                                              op0=ALU.subtract, op1=ALU.mult),
             reads=[lnk] + tks, writes=tks)
        C.op("dve", lambda e: e.tensor_tensor(out=xap, in0=xap, in1=gt[:n, :], op=ALU.mult),
             reads=tks + [gk], writes=tks)
        C.op("dve", lambda e: e.tensor_tensor(out=xap, in0=xap, in1=bt[:n, :], op=ALU.add),
             reads=tks + [gk], writes=tks)

    seqs = [dict(name="p", L=Lp, past=0, TS=128, c=64, row0=0, b=0)]
    for b_ in range(NS):
        seqs.append(dict(name="s%d" % b_, L=Ls, past=PAST, TS=Ls, c=Ls, row0=Lp + b_ * Ls, b=b_))

    with ExitStack() as st0:
        g0 = st0.enter_context(nc.sbuf_tensor("g0", [128, D], F32))
        b0 = st0.enter_context(nc.sbuf_tensor("b0", [128, D], F32))
        xl = st0.enter_context(nc.sbuf_tensor("xl", [128, 2, D], F32))
        xlk = [Tk(), Tk()]
        pk0 = Tk()
        C.dma("sp", g0[:], ln0g[0:1, :].partition_broadcast(128), pk0, writes=[pk0])
        C.dma("sp", b0[:], ln0b[0:1, :].partition_broadcast(128), pk0, writes=[pk0])
        tiles0 = [(xp, r, r, 128) for r in range(0, Lp, 128)]
        for b_ in range(NS):
            tiles0.append((xs, b_ * Ls, Lp + b_ * Ls, Ls))
        for i0, (src0, sr0, dr0, n0) in enumerate(tiles0):
            sl0 = i0 % 2
            C.dma("sp", xl[:n0, sl0, :], src0[sr0:sr0 + n0, :], xlk[sl0], writes=[xlk[sl0]])
            layer_norm(xl[:n0, sl0, :], n0, g0, b0, pk0, xlk[sl0])
            C.dma("pool", xcs[dr0:dr0 + n0, :], xl[:n0, sl0, :], xlk[sl0], reads=[xlk[sl0]], writes=[dtk("xcs", dr0)])
    if STOP == 0:
        C.finish()
        return nc

    for l in range(DEPTH):
        lam_init = 0.8 - 0.6 * math.exp(-0.3 * l)
        C.barrier()
        WA_in = WA[:, 0:8 * NIN].rearrange("p (k n) -> p k n", k=8)
        WA_out = WA[:, 8 * NIN:8 * NIN + 8192].rearrange("p (k n) -> p k n", k=8)
        WAK2 = Tk("WA2")
        for k in range(8):
            for c0_ in range(0, NIN, 1024):
                c1_ = min(NIN, c0_ + 1024)
                C.dma("pool", WA_in[:, k, c0_:c1_], w_in[l, k * 128:(k + 1) * 128, c0_:c1_], WAK, writes=[WAK])
        for k in range(8):
            C.dma("pool", WA_out[:, k, :], w_out[l, k * 128:(k + 1) * 128, :], WAK2, writes=[WAK2])
        with ExitStack() as st:
            def T(name, shape, dt=F32):
                return st.enter_context(nc.sbuf_tensor("%s_%d" % (name, l), list(shape), dt))
            wa_off = [8 * NIN + 8192]

            def WT(shape):
                n = 1
                for d_ in shape[1:]:
                    n *= d_
                ap = WA[:, wa_off[0]:wa_off[0] + n]
                wa_off[0] += n
                assert wa_off[0] <= 65536, wa_off[0]
                if len(shape) == 3:
                    ap = ap.rearrange("p (a b) -> p a b", a=shape[1])
                return ap
            KT = WT([128, 4, LK])
            ktk = [Tk("kt%d" % i) for i in range((LK + 127) // 128)]
            g1 = T("g1", [128, D]); b1 = T("b1", [128, D])
            pk_ = Tk("params")
            C.dma("sp", g1[:], ln1g[l:l + 1, :].partition_broadcast(128), pk_, writes=[pk_])
            C.dma("sp", b1[:], ln1b[l:l + 1, :].partition_broadcast(128), pk_, writes=[pk_])
            lamt = T("lamt", [128, 4, 64]); lamr = T("lamr", [128, 4]); neglam = T("neglam", [128, 1])
            for i, src in enumerate((lamq1, lamk1, lamq2, lamk2)):
                C.dma("sp", lamt[:, i, :], src[l:l + 1, :].partition_broadcast(128), pk_, writes=[pk_])
            C.op("dve", lambda e: e.tensor_tensor(out=lamt[:, 0, :], in0=lamt[:, 0, :], in1=lamt[:, 1, :], op=ALU.mult),
                 reads=[pk_], writes=[pk_])
            C.op("dve", lambda e: e.tensor_tensor(out=lamt[:, 2, :], in0=lamt[:, 2, :], in1=lamt[:, 3, :], op=ALU.mult),
                 reads=[pk_], writes=[pk_])
            C.op("dve", lambda e: e.tensor_reduce(out=lamr[:, 0:1], in_=lamt[:, 0, :], axis=AX.X, op=ALU.add),
                 reads=[pk_], writes=[pk_])
            C.op("dve", lambda e: e.tensor_reduce(out=lamr[:, 1:2], in_=lamt[:, 2, :], axis=AX.X, op=ALU.add),
                 reads=[pk_], writes=[pk_])
            C.op("act", lambda e: e.activation(out=lamr[:, 2:4], in_=lamr[:, 0:2], func=AF.Exp),
                 reads=[pk_], writes=[pk_])
            C.op("dve", lambda e: e.scalar_tensor_tensor(out=neglam[:], in0=lamr[:, 3:4], scalar=-lam_init,
                                                         in1=lamr[:, 2:3], op0=ALU.add, op1=ALU.subtract),
                 reads=[pk_], writes=[pk_])
            dngc = T("dngc", [128, 1]); dlngc = T("dlngc", [64, 1])
            C.dma("sp", dngc[:], dng[l:l + 1, :].rearrange("o e -> e o"), pk_, writes=[pk_])
            C.dma("sp", dlngc[:], dlng[l:l + 1, :].rearrange("o e -> e o"), pk_, writes=[pk_])
            cw = T("cw", [128, 6, 4])
            for ch in range(6):
                C.dma("sp", cw[:, ch, :], convw[l, :, ch * 128:(ch + 1) * 128].rearrange("j p -> p j"),
                      pk_, writes=[pk_], allow_slow_non_contiguous=True)
            nA = T("nA", [128, 4]); dtbb = T("dtbb", [128, 4])
            C.dma("sp", nA[:], alog[l:l + 1, :].partition_broadcast(128), pk_, writes=[pk_])
            C.dma("sp", dtbb[:], dtb[l:l + 1, :].partition_broadcast(128), pk_, writes=[pk_])
            C.op("act", lambda e: e.activation(out=nA[:], in_=nA[:], func=AF.Exp), reads=[pk_], writes=[pk_])
            C.op("dve", lambda e: e.tensor_scalar(out=nA[:], in0=nA[:], scalar1=-1.0, scalar2=None, op0=ALU.mult),
                 reads=[pk_], writes=[pk_])
            if STOP == 6:
                C.finish()
                return nc

            xres = T("xres", [128, D]); xk = Tk("xres")
            xbf = WT([128, D]); xbfk = Tk()
            kbf = xbf[:, 0:512]; kbfk = xbfk
            xT = WT([128, 8, 256]); xTk = Tk()
            mixT = xT; mixk = xTk
            mixA = WT([128, 4, 256]); mixAk = Tk()
            QTa = WT([128, 4, 256]); QTb = WT([128, 4, 256]); QTk = Tk()
            C.op("pool", lambda e: e.memset(QTa[64:128, :, :], 0.0), writes=[QTk])
            C.op("pool", lambda e: e.memset(QTb[0:64, :, :], 0.0), writes=[QTk])
            rot = T("rot", [128, 128]); rotk = Tk()
            tmpA = T("tmpA", [128, 512]); tmpB = T("tmpB", [128, 512]); tmpk = Tk()
            kout = T("kout", [128, 512]); koutk = Tk()
            vout = T("vout", [128, 512]); voutk = Tk()
            vbf = WT([128, 512]); vbfk = Tk()
            uT = T("uT", [128, 6, 131]); uTk = Tk()
            cT = T("cT", [128, 6, 128]); cTk = Tk()
            cE = T("cE", [128, 6, 128]); cEk = Tk()
            nTh = WT([128, 8, 128]); nTk = Tk()
            qdT = WT([128, 4, 128]); qdk = Tk()
            g4 = T("g4", [128, 264]); g4e = T("g4e", [128, 264]); g4k = Tk()
            sgT = T("sgT", [64, 8, 128]); sgTk = Tk()
            qkc = T("qkc", [128, 512]); qkck = Tk()
            qkcT = WT([128, 8, 128]); qkcTk = Tk()
            sg6 = g4[:, 0:256]; sg6e = g4e[:, 0:256]; sg6k = g4k
            ktok = T("ktok", [64, 2, 512]); ktokk = Tk()
            kz = T("kz", [64, 2, 256], BF16); kzk = Tk()
            vcb = T("vcb", [64, 2, 256], BF16); kzc = T("kzc", [64, 2, 256], BF16); vck = Tk()
            gb = T("gb", [64, 2, 8]); gbk = Tk()
            sm = T("sm", [128, 64]); smk = Tk()
            GT = T("GT", [64, 4, 64]); GTk = Tk()
            Drow = T("Drow", [128, 4, 64]); EDrow = T("EDrow", [128, 4, 64]); Drk = Tk()
            LM = T("LM", [64, 4, 64]); LMs = T("LMs", [64, 4, 64]); LMi = T("LMi", [64, 4, 64]); LMk = Tk()
            Xf = T("Xf", [64, 4, 64]); Xk = Tk()
            Yb = T("Yb", [64, 2, 4, 64], BF16); YTb = T("YTb", [64, 6, 4, 64], BF16); Yk = [Tk() for _ in range(8)]; YTk = [Tk() for _ in range(8)]
            Pf = T("Pf", [64, 4, 64]); Pb = T("Pb", [64, 4, 64], BF16); Pk = Tk()
            inT = T("inT", [64, 4, 64], BF16); inTk = Tk()
            inR = T("inR", [64, 4, 64], BF16); inRk = Tk()
            rt = T("rt", [64, 256]); rb = T("rb", [64, 256], BF16); rk = Tk()
            scr = T("scr", [64, 4, 64]); scrk = Tk()
            vnb = T("vnb", [64, 256], BF16); vnk = Tk()
            Sd = T("Sd", [64, 4, 64]); Sdb = T("Sdb", [64, 4, 64], BF16); Sdk = Tk()
            Sr = T("Sr", [64, 4, 64]); Srb = T("Srb", [64, 4, 64], BF16); Srk = Tk()
            ot = T("ot", [64, 4, 128]); ot2 = T("ot2", [64, 4, 128]); otb = T("otb", [64, 4, 128], BF16); otk = Tk()
            PT = [WT([128, 2, 256]), WT([128, 2, 256])]; PTk = [Tk(), Tk()]
            Vt = [WT([128, 128]), WT([128, 128]), WT([128, 128])]; Vtk = [Tk(), Tk(), Tk()]
            ckb = WT([128, 512]); ckbk = Tk()
            Rr = tmpA[:, :].rearrange("p (j q) -> p j q", j=2); Tt = tmpB[:, :].rearrange("p (j q) -> p j q", j=2)
            oaf = T("oaf", [128, 256])
            oab = T("oab", [128, 256], BF16); atk = tmpk
            pcst = cE[:3, :, :].rearrange("p a b -> p (a b)"); pcstk = cEk

            def proj(TS, col0, c0, n, kind="pj"):
                pb, pbk = bank(kind)
                for k in range(8):
                    C.op("pe", lambda e: e.matmul(pb[:TS, 0:n], lhsT=xT[:, k, col0:col0 + TS],
                                                  rhs=WA_in[:, k, c0:c0 + n], start=(k == 0), stop=(k == 7)),
                         reads=[xTk, WAK], writes=[pbk])
                return pb, pbk

            def rotary(pb, pbk, TS, outs):
                pv_ = pb[:TS, :].rearrange("p (h d) -> p h d", h=8)
                C.op("dve", lambda e: e.tensor_tensor(out=tmpA[:TS, :].rearrange("p (h d) -> p h d", h=8), in0=pv_,
                                                      in1=bc(rot[:TS, 0:64].unsqueeze(1), [TS, 8, 64]), op=ALU.mult),
                     reads=[pbk, rotk], writes=[tmpk])
                tb = tmpB[:TS, :].rearrange("p (h d) -> p h d", h=8)
                C.op("dve", lambda e: e.tensor_tensor(out=tb[:, :, 0:32], in0=pv_[:, :, 32:64],
                                                      in1=bc(rot[:TS, 64:96].unsqueeze(1), [TS, 8, 32]), op=ALU.mult),
                     reads=[pbk, rotk], writes=[tmpk])
                C.op("dve", lambda e: e.tensor_tensor(out=tb[:, :, 32:64], in0=pv_[:, :, 0:32],
                                                      in1=bc(rot[:TS, 96:128].unsqueeze(1), [TS, 8, 32]), op=ALU.mult),
                     reads=[pbk, rotk], writes=[tmpk])
                for (eng, oap, otk_) in outs:
                    C.op(eng, lambda e: e.tensor_tensor(out=oap, in0=tmpA[:TS, :], in1=tmpB[:TS, :], op=ALU.add),
                         reads=[tmpk], writes=[otk_])

            def transposes_bf(src, srck, TS, n):
                pb, pbk = bank("mi")
                pbb = pb[:].bitcast(BF16)
                for i in range(n):
                    C.op("pe", lambda e: e.transpose(out=pbb[:, i * TS:(i + 1) * TS],
                                                     in_=src[:TS, i * 128:(i + 1) * 128],
                                                     identity=identb[:TS, :TS]),
                         reads=[srck, constk], writes=[pbk])
                return pbb[:, 0:n * TS].rearrange("p (k t) -> p k t", k=n), pbk

            def silu_from(pb_ap, xs_, es_, k_, pbk):
                C.op("act", lambda e: e.activation(out=es_, in_=pb_ap, func=AF.Exp, scale=-1.0), reads=[pbk], writes=[k_])
                C.op("act", lambda e: e.activation(out=xs_, in_=pb_ap, func=AF.Copy), reads=[pbk], writes=[k_])
                C.op("dve", lambda e: e.tensor_scalar(out=es_, in0=es_, scalar1=1.0, scalar2=None, op0=ALU.add),
                     reads=[k_], writes=[k_])
                C.op("dve", lambda e: e.reciprocal(out=es_, in_=es_), reads=[k_], writes=[k_])

            for sq_ in seqs:
                TS, c, L, past, row0, sb_ = sq_["TS"], sq_["c"], sq_["L"], sq_["past"], sq_["row0"], sq_["b"]
                isP = sq_["name"] == "p"
                CPT = TS // c
                TPB = 2 if isP else 1
                nblk = L // (TS * TPB)
                nq = TS * TPB
                nit = 5 if c == 64 else 4
                rsrc = c_rotp if isP else c_rots
                kdst = pk if isP else sk
                vdst = pv if isP else sv
                orow0 = 0 if isP else sb_ * Ls
                zt = zetac if isP else zetasc
                xit = xic if isP else xisc
                gch = [math.exp(math.log(1.0 - 2.0 ** (-5.0 - h)) * c) for h in range(4)]
                if isP:
                    C.op("dve", lambda e: e.memset(Sd[:], 0.0), writes=[Sdk])
                    C.op("dve", lambda e: e.memset(Sr[:], 0.0), writes=[Srk])
                    C.op("dve", lambda e: e.memset(uT[:, :, 0:3], 0.0), writes=[uTk])
                else:
                    C.dma("sp", Sd[:], sdl[l, sb_].rearrange("(h d) e -> d h e", d=64), Sdk, writes=[Sdk])
                    C.dma("sp", Sr[:], srt[l, sb_].rearrange("(h d) e -> d h e", d=64), Srk, writes=[Srk])
                    for ch in range(6):
                        C.dma("sp", uT[:, ch, 0:3], scv[l, sb_, :, ch * 128:(ch + 1) * 128].rearrange("j p -> p j"),
                              uTk, writes=[uTk], allow_slow_non_contiguous=True)
                    for kt in range(past // 128):
                        C.dma("pool", ckb[:], ck[l, sb_, kt * 128:(kt + 1) * 128, :], ckbk, writes=[ckbk])
                        pv4, pv4k = transposes_bf(ckb, ckbk, 128, 4)
                        C.op("dve", lambda e: e.tensor_copy(out=KT[:, :, kt * 128:(kt + 1) * 128], in_=pv4),
                             reads=[pv4k], writes=[ktk[kt]])
                C.op("act", lambda e: e.activation(func=AF.Copy, out=Sdb[:], in_=Sd[:]), reads=[Sdk], writes=[Sdk])
                C.op("act", lambda e: e.activation(func=AF.Copy, out=Srb[:], in_=Sr[:]), reads=[Srk], writes=[Srk])

                def gen_frontA(ti, t, pjk):
                    r0 = t * TS
                    col0 = ti * TS
                    kcol = past + r0
                    kt_own = kcol // 128
                    xa = xres[:TS, :]
                    C.dma("sp", xa, xcs[row0 + r0:row0 + r0 + TS, :], xk,
                          reads=[dtk("xcs", row0 + r0)], writes=[xk])
                    C.dma("sp", rot[:TS, :], rsrc[r0:r0 + TS, :], rotk, writes=[rotk])
                    C.op("act", lambda e: e.activation(out=xbf[:TS, :], in_=xa, func=AF.Copy), reads=[xk], writes=[xbfk])
                    pv8, pv8k = transposes_bf(xbf, xbfk, TS, 8)
                    C.op("dve", lambda e: e.tensor_copy(out=xT[:, :, col0:col0 + TS], in_=pv8), reads=[pv8k], writes=[xTk])
                    yield
                    pb, pbk = proj(TS, col0, 0, 512, pjk)
                    rotary(pb, pbk, TS, [("dve", kbf[:TS, :], kbfk)])
                    pv4, pv4k = transposes_bf(kbf, kbfk, TS, 4)
                    C.op("dve", lambda e: e.tensor_copy(out=QTa[0:64, :, col0:col0 + TS], in_=pv4[0:64, :, :]),
                         reads=[pv4k], writes=[QTk])
                    C.op("dve", lambda e: e.tensor_copy(out=QTb[64:128, :, col0:col0 + TS], in_=pv4[64:128, :, :]),
                         reads=[pv4k], writes=[QTk])
                    yield
                    pb, pbk = proj(TS, col0, 512, 512, pjk)
                    rotary(pb, pbk, TS, [("dve", kout[:TS, :], koutk), ("dve", kbf[:TS, :], kbfk)])
                    C.dma("pool", kdst[l, orow0 + r0:orow0 + r0 + TS, :], kout[:TS, :], koutk, reads=[koutk])
                    pv4, pv4k = transposes_bf(kbf, kbfk, TS, 4)
                    C.op("dve", lambda e: e.tensor_copy(out=KT[:, :, kcol:kcol + TS], in_=pv4),
                         reads=[pv4k], writes=[ktk[kt_own]])
                    yield
                    pb, pbk = proj(TS, col0, 1024, 512, pjk)
                    C.op("act", lambda e: e.activation(out=vout[:TS, :], in_=pb[:TS, :], func=AF.Copy), reads=[pbk], writes=[voutk])
                    C.op("dve", lambda e: e.tensor_copy(out=vbf[:TS, :], in_=vout[:TS, :]), reads=[voutk], writes=[vbfk])
                    C.dma("pool", vdst[l, orow0 + r0:orow0 + r0 + TS, :], vout[:TS, :], voutk, reads=[voutk])
                    C.dma("pool", vsc[kcol:kcol + TS, :], vbf[:TS, :], vbfk, reads=[vbfk], writes=[dtk("vsc", kt_own)])
                    yield

                def gen_frontB(t, col0):
                    pb, pbk = proj(TS, col0, 2304, 264)
                    silu_from(pb[:TS, 0:260], g4[:TS, 0:260], g4e[:TS, 0:260], g4k, pbk)
                    C.op("dve", lambda e: e.tensor_tensor(out=g4[:TS, 0:256], in0=g4[:TS, 0:256], in1=g4e[:TS, 0:256],
                                                          op=ALU.mult), reads=[g4k], writes=[g4k])
                    C.op("dve", lambda e: e.tensor_tensor(out=g4[:TS, 260:264], in0=pb[:TS, 260:264], in1=dtbb[:TS, :],
                                                          op=ALU.add), reads=[pbk, pk_], writes=[g4k])
                    C.op("act", lambda e: e.activation(out=g4[:TS, 260:264], in_=g4[:TS, 260:264], func=AF.Exp),
                         reads=[g4k], writes=[g4k])
                    C.op("act", lambda e: e.activation(out=g4[:TS, 260:264], in_=g4[:TS, 260:264], func=AF.Ln, bias=1.0),
                         reads=[g4k], writes=[g4k])
                    C.op("dve", lambda e: e.tensor_tensor(out=g4[:TS, 260:264], in0=g4[:TS, 260:264], in1=nA[:TS, :],
                                                          op=ALU.mult), reads=[g4k, pk_], writes=[g4k])
                    for ck_ in range(CPT):
                        C.op("dve", lambda e: e.tensor_copy(out=gb[:c, ck_, 0:4], in_=g4[ck_ * c:(ck_ + 1) * c, 260:264]),
                             reads=[g4k], writes=[gbk])
                        C.op("dve", lambda e: e.tensor_copy(out=gb[:c, ck_, 4:8], in_=g4e[ck_ * c:(ck_ + 1) * c, 256:260]),
                             reads=[g4k], writes=[gbk])
                    pm, pmk = bank("mi")
                    for h in range(4):
                        C.op("pe", lambda e: e.transpose(out=pm[:64, h * TS:(h + 1) * TS], in_=g4[:TS, h * 64:(h + 1) * 64],
                                                         identity=identf[:TS, :TS]),
                             reads=[g4k, constk], writes=[pmk])
                    C.op("act", lambda e: e.activation(out=sgT[:, 0:4, :TS],
                                                       in_=pm[:64, 0:4 * TS].rearrange("p (h t) -> p h t", h=4), func=AF.Copy),
                         reads=[pmk], writes=[sgTk])
                    yield
                    pb, pbk = proj(TS, col0, 2568, 512)
                    rotary(pb, pbk, TS, [("dve", qkc[:TS, :], qkck)])
                    C.op("dve", lambda e: e.tensor_scalar(out=qkc[:TS, 256:512], in0=qkc[:TS, 256:512], scalar1=0.125,
                                                          scalar2=None, op0=ALU.mult), reads=[qkck], writes=[qkck])
                    for half in range(2):
                        pm, pmk = bank("mi")
                        for h in range(4):
                            C.op("pe", lambda e: e.transpose(
                                out=pm[:64, h * TS:(h + 1) * TS],
                                in_=qkc[:TS, half * 256 + h * 64:half * 256 + (h + 1) * 64],
                                identity=identf[:TS, :TS]), reads=[qkck, constk], writes=[pmk])
                        C.op("act", lambda e: e.activation(
                            out=qkcT[0:64, half * 4:half * 4 + 4, :TS],
                            in_=pm[:64, 0:4 * TS].rearrange("p (h t) -> p h t", h=4), func=AF.Copy), reads=[pmk], writes=[qkcTk])
                    for ck_ in range(CPT):
                        C.op("dve", lambda e: e.tensor_tensor(
                            out=kzc[:c, ck_, :].rearrange("p (h d) -> p h d", h=4),
                            in0=qkc[ck_ * c:(ck_ + 1) * c, 256:512].rearrange("p (h d) -> p h d", h=4),
                            in1=bc(zt[ck_ * c:(ck_ + 1) * c, :].unsqueeze(2), [c, 4, 64]), op=ALU.mult),
                             reads=[qkck, constk], writes=[vck])
                    yield
                    pb, pbk = proj(TS, col0, 3080, 512)
                    for ck_ in range(CPT):
                        C.op("act", lambda e: e.activation(out=vcb[:c, ck_, :], in_=pb[ck_ * c:(ck_ + 1) * c, 0:256], func=AF.Copy),
                             reads=[pbk], writes=[vck])
                    silu_from(pb[:TS, 256:512], sg6[:TS, :], sg6e[:TS, :], sg6k, pbk)
                    C.op("dve", lambda e: e.tensor_tensor(out=sg6[:TS, :], in0=sg6[:TS, :], in1=sg6e[:TS, :], op=ALU.mult),
                         reads=[sg6k], writes=[sg6k])
                    pm, pmk = bank("mi")
                    for h in range(4):
                        C.op("pe", lambda e: e.transpose(out=pm[:64, h * TS:(h + 1) * TS], in_=sg6[:TS, h * 64:(h + 1) * 64],
                                                         identity=identf[:TS, :TS]),
                             reads=[sg6k, constk], writes=[pmk])
                    C.op("act", lambda e: e.activation(out=sgT[:, 4:8, :TS],
                                                       in_=pm[:64, 0:4 * TS].rearrange("p (h t) -> p h t", h=4), func=AF.Copy),
                         reads=[pmk], writes=[sgTk])
                    yield
                    for rnd in range(2):
                        pb, pbk = bank("pj")
                        for j in range(3):
                            ch = rnd * 3 + j
                            for k in range(8):
                                C.op("pe", lambda e: e.matmul(
                                    pb[:, j * TS:(j + 1) * TS],
                                    lhsT=WA_in[:, k, 1536 + ch * 128:1536 + (ch + 1) * 128],
                                    rhs=xT[:, k, col0:col0 + TS], start=(k == 0), stop=(k == 7)),
                                     reads=[xTk, WAK], writes=[pbk])
                        C.op("act", lambda e: e.activation(
                            out=uT[:, rnd * 3:rnd * 3 + 3, 3:3 + TS],
                            in_=pb[:, 0:3 * TS].rearrange("p (j t) -> p j t", j=3), func=AF.Copy), reads=[pbk], writes=[uTk])
                        yield
                    yield
                    for ch in range(6):
                        C.op("dve", lambda e: e.tensor_scalar(out=cT[:, ch, :TS], in0=uT[:, ch, 0:TS],
                                                              scalar1=cw[:, ch, 0:1], scalar2=None, op0=ALU.mult),
                             reads=[uTk, pk_], writes=[cTk])
                        for j in range(1, 4):
                            C.op("dve", lambda e: e.scalar_tensor_tensor(
                                out=cT[:, ch, :TS], in0=uT[:, ch, j:j + TS], scalar=cw[:, ch, j:j + 1],
                                in1=cT[:, ch, :TS], op0=ALU.mult, op1=ALU.add), reads=[uTk, pk_, cTk], writes=[cTk])
                    yield
                    if t == L // TS - 1:
                        for (c0_, c1_) in ((0, 4), (4, 6)):
                            pm, pmk = bank("mi")
                            for ch in range(c0_, c1_):
                                C.op("pe", lambda e: e.transpose(out=pm[:3, (ch - c0_) * 128:(ch - c0_ + 1) * 128],
                                                                 in_=uT[:, ch, TS:TS + 3], identity=identf[:, :]),
                                     reads=[uTk, constk], writes=[pmk])
                            C.op("act", lambda e: e.activation(out=pcst[:, c0_ * 128:c1_ * 128], in_=pm[:3, 0:(c1_ - c0_) * 128],
                                                               func=AF.Copy), reads=[pmk], writes=[pcstk])
                        cdst = pcv[l] if isP else scvo[l, sb_]
                        C.dma("pool", cdst, pcst[:, :], pcstk, reads=[pcstk])
                    C.op("pool", lambda e: e.tensor_copy(out=uT[:, :, 0:3], in_=uT[:, :, TS:TS + 3]),
                         reads=[uTk], writes=[uTk])
                    yield
                    C.op("act", lambda e: e.activation(out=cE[:, :, :TS], in_=cT[:, :, :TS], func=AF.Exp, scale=-1.0),
                         reads=[cTk], writes=[cEk])
                    C.op("dve", lambda e: e.tensor_scalar(out=cE[:, :, :TS], in0=cE[:, :, :TS], scalar1=1.0, scalar2=None,
                                                          op0=ALU.add), reads=[cEk], writes=[cEk])
                    C.op("dve", lambda e: e.reciprocal(out=cE[:, :, :TS], in_=cE[:, :, :TS]), reads=[cEk], writes=[cEk])
                    C.op("dve", lambda e: e.tensor_tensor(out=cT[:, :, :TS], in0=cT[:, :, :TS], in1=cE[:, :, :TS], op=ALU.mult),
                         reads=[cEk, cTk], writes=[cTk])
                    yield
                    C.op("dve", lambda e: e.tensor_tensor(out=cE[:, 0:4, :TS], in0=cT[:, 0:4, :TS], in1=cT[:, 0:4, :TS],
                                                           op=ALU.mult), reads=[cTk, cEk], writes=[cEk])
                    pm, pmk = bank("mi")
                    for j in range(4):
                        C.op("pe", lambda e: e.matmul(pm[:, j * TS:(j + 1) * TS], lhsT=bonesf[:, :], rhs=cE[:, j, :TS],
                                                      start=True, stop=True), reads=[cEk, constk], writes=[pmk])
                    C.op("act", lambda e: e.activation(out=cE[:, 0:4, :TS],
                                                       in_=pm[:, 0:4 * TS].rearrange("p (j t) -> p j t", j=4),
                                                       func=AF.Ln, bias=epsb[:, 1:2]),
                         reads=[pmk, constk], writes=[cEk])
                    C.op("act", lambda e: e.activation(out=cE[:, 0:4, :TS], in_=cE[:, 0:4, :TS], func=AF.Exp, scale=-0.5),
                         reads=[cEk], writes=[cEk])
                    yield
                    C.op("dve", lambda e: e.scalar_tensor_tensor(out=cT[:, 0:2, :TS], in0=cT[:, 0:2, :TS], scalar=0.125,
                                                                 in1=cE[:, 0:2, :TS], op0=ALU.mult, op1=ALU.mult),
                         reads=[cTk, cEk], writes=[cTk])
                    C.op("dve", lambda e: e.tensor_tensor(out=cT[:, 2:4, :TS], in0=cT[:, 2:4, :TS], in1=cE[:, 2:4, :TS],
                                                          op=ALU.mult), reads=[cTk, cEk], writes=[cTk])
                    for s_ in range(2):
                        C.op("dve", lambda e: e.tensor_copy(out=nTh[0:64, s_:8:2, :TS], in_=cT[s_ * 64:(s_ + 1) * 64, 0:4, :TS]),
                             reads=[cTk], writes=[nTk])
                    yield
                    for ck_ in range(CPT):
                        pm, pmk = bank("mi")
                        for j in range(4):
                            C.op("pe", lambda e: e.transpose(
                                out=pm[:c, j * 128:(j + 1) * 128], in_=cT[:, 2 + j, ck_ * c:(ck_ + 1) * c],
                                identity=identf[:, :]), reads=[cTk, constk], writes=[pmk])
                        C.op("act", lambda e: e.activation(out=ktok[:c, ck_, :], in_=pm[:c, :], func=AF.Copy),
                             reads=[pmk], writes=[ktokk])
                    yield

                def gen_chunks(col0):
                    po, pok = bank("pj")
                    po2, po2k = bank("pj")
                    for ck_ in range(CPT):
                        cs = slice(ck_ * c, (ck_ + 1) * c)
                        pm, pmk = bank("mi")
                        C.op("pe", lambda e: e.matmul(pm[:c, 0:4], lhsT=triu[:c, :c], rhs=gb[:c, ck_, 0:4],
                                                      start=True, stop=True), reads=[gbk, constk], writes=[pmk])
                        C.op("dve", lambda e: e.tensor_tensor(out=GT[:c, :, :c],
                                                              in0=bc(triu[:c, :c].unsqueeze(1), [c, 4, c]),
                                                              in1=bc(gb[:c, ck_, 0:4].unsqueeze(2), [c, 4, c]),
                                                              op=ALU.mult), reads=[gbk, constk], writes=[GTk])
                        pm2, pm2k = bank("mi")
                        for h in range(4):
                            C.op("pe", lambda e: e.matmul(pm2[:, h * c:(h + 1) * c], lhsT=onesf[:c, :],
                                                          rhs=GT[:c, h, :c], start=True, stop=True),
                                 reads=[GTk, constk], writes=[pm2k])
                        C.op("act", lambda e: e.activation(out=sm[:c, 0:4], in_=pm[:c, 0:4], func=AF.Copy), reads=[pmk], writes=[smk])
                        C.op("act", lambda e: e.activation(out=sm[:c, 4:8], in_=pm[:c, 0:4], func=AF.Exp), reads=[pmk], writes=[smk])
                        C.op("act", lambda e: e.activation(out=Drow[:, :, :c], in_=pm2[:, 0:4 * c].rearrange("p (h i) -> p h i", h=4),
                                                           func=AF.Copy), reads=[pm2k], writes=[Drk])
                        C.op("act", lambda e: e.activation(out=EDrow[:, :, :c],
                                                           in_=pm2[:, 0:4 * c].rearrange("p (h i) -> p h i", h=4), func=AF.Exp),
                             reads=[pm2k], writes=[Drk])
                        C.op("dve", lambda e: e.tensor_tensor(out=sm[:c, 12:16], in0=Drow[:c, :, c - 1], in1=sm[:c, 0:4],
                                                              op=ALU.subtract), reads=[Drk, smk], writes=[smk])
                        C.op("act", lambda e: e.activation(out=sm[:c, 8:12], in_=sm[:c, 12:16], func=AF.Exp), reads=[smk], writes=[smk])
                        yield
                        C.op("dve", lambda e: e.tensor_tensor(out=LM[:c, :, :c], in0=Drow[:c, :, :c],
                                                              in1=bc(sm[:c, 0:4].unsqueeze(2), [c, 4, c]), op=ALU.subtract),
                             reads=[Drk, smk], writes=[LMk])
                        C.op("dve", lambda e: e.tensor_scalar(out=LM[:c, :, :c], in0=LM[:c, :, :c], scalar1=0.0, scalar2=None,
                                                              op0=ALU.min), reads=[LMk], writes=[LMk])
                        C.op("act", lambda e: e.activation(out=LM[:c, :, :c], in_=LM[:c, :, :c], func=AF.Exp), reads=[LMk], writes=[LMk])
                        C.op("dve", lambda e: e.tensor_tensor(out=LMs[:c, :, :c], in0=LM[:c, :, :c],
                                                               in1=bc(striu[:c, :c].unsqueeze(1), [c, 4, c]), op=ALU.mult),
                             reads=[LMk, constk], writes=[LMk])
                        C.op("dve", lambda e: e.tensor_tensor(out=LMi[:c, :, :c], in0=LM[:c, :, :c],
                                                               in1=bc(triu[:c, :c].unsqueeze(1), [c, 4, c]), op=ALU.mult),
                             reads=[LMk, constk], writes=[LMk])
                        C.op("dve", lambda e: e.tensor_tensor(out=qdT[0:64, :, cs], in0=nTh[0:64, 0:4, cs],
                                                              in1=EDrow[0:64, :, :c], op=ALU.mult), reads=[nTk, Drk], writes=[qdk])
                        C.op("dve", lambda e: e.tensor_tensor(out=kz[:c, ck_, :].rearrange("p (h d) -> p h d", h=4),
                                                              in0=ktok[:c, ck_, 0:256].rearrange("p (h d) -> p h d", h=4),
                                                              in1=bc(sm[:c, 8:12].unsqueeze(2), [c, 4, 64]), op=ALU.mult),
                             reads=[ktokk, smk], writes=[kzk])
                        yield
                        pg, pgk = bank("mi")
                        for h in range(4):
                            C.op("pe", lambda e: e.matmul(pg[:c, h * c:(h + 1) * c], lhsT=nTh[0:64, 4 + h, cs], rhs=nTh[0:64, 4 + h, cs],
                                                          start=True, stop=True), reads=[nTk], writes=[pgk])
                        for h in range(4):
                            C.op("pe", lambda e: e.matmul(pg[:c, 256 + h * c:256 + (h + 1) * c], lhsT=nTh[0:64, 4 + h, cs],
                                                          rhs=nTh[0:64, h, cs], start=True, stop=True), reads=[nTk], writes=[pgk])
                        KKv = pg[:c, 0:4 * c].rearrange("p (h i) -> p h i", h=4)
                        KQv = pg[:c, 256:256 + 4 * c].rearrange("p (h i) -> p h i", h=4)
                        C.op("dve", lambda e: e.tensor_tensor(out=Xf[:c, :, :c], in0=KKv,
                                                              in1=bc(gb[:c, ck_, 4:8].unsqueeze(2), [c, 4, c]), op=ALU.mult),
                             reads=[pgk, gbk], writes=[Xk])
                        C.op("dve", lambda e: e.tensor_tensor(out=Xf[:c, :, :c], in0=Xf[:c, :, :c], in1=LMs[:c, :, :c], op=ALU.mult),
                             reads=[Xk, LMk], writes=[Xk])
                        C.op("dve", lambda e: e.tensor_tensor(out=scr[:c, :, :c], in0=KQv, in1=LMi[:c, :, :c], op=ALU.mult),
                             reads=[pgk, LMk], writes=[scrk])
                        C.op("act", lambda e: e.activation(func=AF.Copy, out=inT[:c, :, :c], in_=scr[:c, :, :c]), reads=[scrk], writes=[inTk])
                        yield
                        pm, pmk = bank("mi")
                        for h in range(4):
                            C.op("pe", lambda e: e.transpose(out=pm[:c, h * c:(h + 1) * c], in_=Xf[:c, h, :c],
                                                             identity=identf[:c, :c]), reads=[Xk, constk], writes=[pmk])
                        C.op("act", lambda e: e.activation(out=YTb[:c, 0, :, :c], in_=pm[:c, 0:4 * c].rearrange("p (h i) -> p h i", h=4),
                                                           func=AF.Copy), reads=[pmk], writes=[YTk[0]])
                        C.op("act", lambda e: e.activation(out=Yb[:c, 0, :, :c], in_=Xf[:c, :, :c], func=AF.Copy), reads=[Xk], writes=[Yk[0]])
                        C.op("dve", lambda e: e.tensor_tensor(out=Pf[:c, :, :c], in0=bc(identf[:c, :c].unsqueeze(1), [c, 4, c]),
                                                              in1=Xf[:c, :, :c], op=ALU.subtract), reads=[Xk, constk], writes=[Pk])
                        C.op("dve", lambda e: e.tensor_copy(out=Pb[:c, :, :c], in_=Pf[:c, :, :c]), reads=[Pk], writes=[Pk])
                        yield

                        def emit_prod(k):
                            pmq, pmqk = bank("mi")
                            for h in range(4):
                                C.op("pe", lambda e: e.matmul(pmq[:c, h * c:(h + 1) * c], lhsT=YTb[:c, k, h, :c],
                                                              rhs=Pb[:c, h, :c], start=True, stop=True),
                                     reads=[YTk[k], Pk], writes=[pmqk])
                            C.op("dve", lambda e: e.tensor_tensor(out=Pf[:c, :, :c], in0=Pf[:c, :, :c],
                                                                  in1=pmq[:c, 0:4 * c].rearrange("p (h i) -> p h i", h=4), op=ALU.add),
                                 reads=[pmqk, Pk], writes=[Pk])
                            C.op("dve", lambda e: e.tensor_copy(out=Pb[:c, :, :c], in_=Pf[:c, :, :c]), reads=[Pk], writes=[Pk])
                        for k in range(1, nit + 1):
                            ys, yd = (k - 1) % 2, k % 2
                            pm, pmk = bank("mi")
                            if k < nit:
                                for h in range(4):
                                    C.op("pe", lambda e: e.matmul(pm[:c, h * c:(h + 1) * c], lhsT=YTb[:c, k - 1, h, :c],
                                                                  rhs=Yb[:c, ys, h, :c], start=True, stop=True),
                                         reads=[Yk[ys], YTk[k - 1]], writes=[pmk])
                            for h in range(4):
                                C.op("pe", lambda e: e.matmul(pm[:c, 256 + h * c:256 + (h + 1) * c],
                                                              lhsT=Yb[:c, ys, h, :c], rhs=YTb[:c, k - 1, h, :c],
                                                              start=True, stop=True), reads=[Yk[ys], YTk[k - 1]], writes=[pmk])
                            if k < nit:
                                C.op("act", lambda e: e.activation(out=Yb[:c, yd, :, :c],
                                                                   in_=pm[:c, 0:4 * c].rearrange("p (h i) -> p h i", h=4), func=AF.Copy),
                                     reads=[pmk], writes=[Yk[yd]])
                            C.op("act", lambda e: e.activation(out=YTb[:c, k, :, :c],
                                                               in_=pm[:c, 256:256 + 4 * c].rearrange("p (h i) -> p h i", h=4), func=AF.Copy),
                                 reads=[pmk], writes=[YTk[k]])
                            yield
                            if k >= 2:
                                emit_prod(k - 1)
                                yield
                        emit_prod(nit)
                        yield
                        pc, pck = bank("mi")
                        for h in range(4):
                            C.op("pe", lambda e: e.matmul(pc[:c, h * 64:(h + 1) * 64], lhsT=nTh[0:64, 4 + h, cs],
                                                          rhs=Sdb[:, h, :], start=True, stop=True),
                                 reads=[nTk, Sdk], writes=[pck])
                        C.op("dve", lambda e: e.tensor_tensor(out=rt[:c, :].rearrange("p (h d) -> p h d", h=4),
                                                              in0=pc[:c, 0:256].rearrange("p (h d) -> p h d", h=4),
                                                              in1=bc(sm[:c, 4:8].unsqueeze(2), [c, 4, 64]), op=ALU.mult),
                             reads=[pck, smk], writes=[rk])
                        C.op("dve", lambda e: e.tensor_tensor(out=rb[:c, :], in0=ktok[:c, ck_, 256:512], in1=rt[:c, :], op=ALU.subtract),
                             reads=[rk, ktokk], writes=[rk])
                        yield
                        pc2, pc2k = bank("mi")
                        for h in range(4):
                            C.op("pe", lambda e: e.matmul(pc2[:c, h * 64:(h + 1) * 64], lhsT=Pb[:c, h, :c],
                                                          rhs=rb[:c, h * 64:(h + 1) * 64], start=True, stop=True),
                                 reads=[Pk, rk], writes=[pc2k])
                        C.op("dve", lambda e: e.tensor_tensor(out=rt[:c, :].rearrange("p (h d) -> p h d", h=4),
                                                              in0=pc2[:c, 0:256].rearrange("p (h d) -> p h d", h=4),
                                                              in1=bc(gb[:c, ck_, 4:8].unsqueeze(2), [c, 4, 64]), op=ALU.mult),
                             reads=[pc2k, gbk], writes=[rk])
                        C.op("dve", lambda e: e.tensor_copy(out=vnb[:c, :], in_=rt[:c, :]), reads=[rk], writes=[vnk])
                        yield
                        for h in range(4):
                            oc_ = slice(h * TS + ck_ * c, h * TS + (ck_ + 1) * c)
                            C.op("pe", lambda e: e.matmul(po[:64, oc_], lhsT=Sdb[:, h, :], rhs=qdT[0:64, h, cs],
                                                          start=True, stop=False), reads=[Sdk, qdk], writes=[pok])
                            C.op("pe", lambda e: e.matmul(po[:64, oc_], lhsT=vnb[:c, h * 64:(h + 1) * 64], rhs=inT[:c, h, :c],
                                                          start=False, stop=True), reads=[vnk, inTk], writes=[pok])
                        psu, psuk = bank("mi")
                        for h in range(4):
                            C.op("pe", lambda e: e.matmul(psu[:64, h * 64:(h + 1) * 64], lhsT=kz[:c, ck_, h * 64:(h + 1) * 64],
                                                          rhs=vnb[:c, h * 64:(h + 1) * 64], start=True, stop=True),
                                 reads=[kzk, vnk], writes=[psuk])
                        for h in range(4):
                            C.op("dve", lambda e: e.scalar_tensor_tensor(
                                out=Sd[:, h, :], in0=Sd[:, h, :], scalar=EDrow[0:64, h, c - 1:c], in1=psu[:64, h * 64:(h + 1) * 64],
                                op0=ALU.mult, op1=ALU.add), reads=[Drk, psuk, Sdk], writes=[Sdk])
                        C.op("act", lambda e: e.activation(func=AF.Copy, out=Sdb[:], in_=Sd[:]), reads=[Sdk], writes=[Sdk])
                        yield
                        pr, prk = bank("mi")
                        for h in range(4):
                            C.op("pe", lambda e: e.matmul(pr[:c, h * c:(h + 1) * c], lhsT=qkcT[0:64, 4 + h, cs], rhs=qkcT[0:64, h, cs],
                                                          start=True, stop=True), reads=[qkcTk], writes=[prk])
                        C.op("dve", lambda e: e.tensor_tensor(out=scr[:c, :, :c], in0=pr[:c, 0:4 * c].rearrange("p (h i) -> p h i", h=4),
                                                              in1=rmc[:c, :].rearrange("p (h i) -> p h i", h=4)[:, :, :c], op=ALU.mult),
                             reads=[prk, constk], writes=[scrk])
                        C.op("act", lambda e: e.activation(func=AF.Copy, out=inR[:c, :, :c], in_=scr[:c, :, :c]), reads=[scrk], writes=[inRk])
                        yield
                        for h in range(4):
                            oc_ = slice(h * TS + ck_ * c, h * TS + (ck_ + 1) * c)
                            C.op("pe", lambda e: e.matmul(po2[:64, oc_], lhsT=Srb[:, h, :], rhs=qkcT[0:64, h, cs],
                                                          start=True, stop=False), reads=[Srk, qkcTk], writes=[po2k])
                            C.op("pe", lambda e: e.matmul(po2[:64, oc_], lhsT=vcb[:c, ck_, h * 64:(h + 1) * 64], rhs=inR[:c, h, :c],
                                                          start=False, stop=True), reads=[vck, inRk], writes=[po2k])
                        psu, psuk = bank("mi")
                        for h in range(4):
                            C.op("pe", lambda e: e.matmul(psu[:64, h * 64:(h + 1) * 64], lhsT=kzc[:c, ck_, h * 64:(h + 1) * 64],
                                                          rhs=vcb[:c, ck_, h * 64:(h + 1) * 64], start=True, stop=True),
                                 reads=[vck], writes=[psuk])
                        for h in range(4):
                            C.op("dve", lambda e: e.scalar_tensor_tensor(out=Sr[:, h, :], in0=Sr[:, h, :], scalar=float(gch[h]),
                                                                         in1=psu[:64, h * 64:(h + 1) * 64], op0=ALU.mult, op1=ALU.add),
                                 reads=[psuk, Srk], writes=[Srk])
                        C.op("act", lambda e: e.activation(func=AF.Copy, out=Srb[:], in_=Sr[:]), reads=[Srk], writes=[Srk])

                    yield
                    for which in range(2):
                        pso, psok = (po, pok) if which == 0 else (po2, po2k)
                        pov = pso[:64, 0:4 * TS].rearrange("p (h t) -> p h t", h=4)
                        if which == 0:
                            C.op("act", lambda e: e.activation(out=ot[:, :, :TS], in_=pov, func=AF.Copy), reads=[psok], writes=[otk])
                        else:
                            C.op("dve", lambda e: e.tensor_tensor(out=ot[:, :, :TS], in0=pov,
                                                                  in1=xit[:, :].rearrange("p (h t) -> p h t", h=4)[:, :, :TS], op=ALU.mult),
                                 reads=[psok, constk], writes=[otk])
                        C.op("dve", lambda e: e.tensor_tensor(out=otb[:, :, :TS], in0=ot[:, :, :TS], in1=ot[:, :, :TS], op=ALU.mult),
                             reads=[otk], writes=[otk])
                        yield
                        pm, pmk = bank("mi")
                        for h in range(4):
                            C.op("pe", lambda e: e.matmul(pm[:64, h * TS:(h + 1) * TS], lhsT=onesb[:64, :64], rhs=otb[:, h, :TS],
                                                          start=True, stop=True), reads=[otk, constk], writes=[pmk])
                        C.op("act", lambda e: e.activation(out=ot2[:, :, :TS], in_=pm[:64, 0:4 * TS].rearrange("p (h t) -> p h t", h=4),
                                                           func=AF.Ln, scale=1.0 / 64, bias=epsb[:64, 1:2]),
                             reads=[pmk, constk], writes=[otk])
                        C.op("act", lambda e: e.activation(out=ot2[:, :, :TS], in_=ot2[:, :, :TS], func=AF.Exp, scale=-0.5),
                             reads=[otk], writes=[otk])
                        yield
                        C.op("dve", lambda e: e.tensor_tensor(out=ot[:, :, :TS], in0=ot[:, :, :TS], in1=ot2[:, :, :TS], op=ALU.mult),
                             reads=[otk], writes=[otk])
                        C.op("dve", lambda e: e.tensor_tensor(out=ot[:, :, :TS], in0=ot[:, :, :TS],
                                                               in1=sgT[:, which * 4:which * 4 + 4, :TS], op=ALU.mult),
                             reads=[otk, sgTk], writes=[otk])
                        for s_ in range(2):
                            dst = mixT[s_ * 64:(s_ + 1) * 64, 4 + which * 2:6 + which * 2, col0:col0 + TS]
                            if which == 0:
                                C.op("dve", lambda e: e.tensor_scalar(out=dst, in0=ot[:, s_:4:2, :TS], scalar1=dlngc[:, 0:1],
                                                                      scalar2=None, op0=ALU.mult),
                                     reads=[otk, pk_], writes=[mixk])
                            else:
                                C.op("dve", lambda e: e.tensor_copy(out=dst, in_=ot[:, s_:4:2, :TS]), reads=[otk], writes=[mixk])
                    yield

                def back_tile(ti, t):
                    r0 = t * TS
                    col0 = ti * TS
                    C.dma("sp", xres[:TS, :], xcs[row0 + r0:row0 + r0 + TS, :], xk,
                          reads=[dtk("xcs", row0 + r0)], writes=[xk])
                    for half in range(2):
                        pb, pbk = bank("pj")
                        for k in range(8):
                            C.op("pe", lambda e: e.matmul(pb[:TS, :], lhsT=(mixA if k < 4 else mixT)[:, k, col0:col0 + TS],
                                                          rhs=WA_out[:, k, half * 512:(half + 1) * 512], start=(k == 0), stop=(k == 7)),
                                 reads=[mixk, mixAk, WAK2], writes=[pbk])
                        C.op("dve", lambda e: e.scalar_tensor_tensor(out=xres[:TS, half * 512:(half + 1) * 512],
                                                                     in0=xres[:TS, half * 512:(half + 1) * 512], scalar=float(ALPHA),
                                                                     in1=pb[:TS, :], op0=ALU.mult, op1=ALU.add),
                             reads=[pbk, xk], writes=[xk])
                    layer_norm(xres[:TS, :], TS, g1, b1, pk_, xk)
                    C.dma("pool", x1s[row0 + r0:row0 + r0 + TS, :], xres[:TS, :], xk, reads=[xk], writes=[dtk("x1s", row0 + r0)])


                def gen_attention(blk):
                    if isP:
                        kts = [(kt, 128, None) for kt in range(2 * blk)] + [(2 * blk, 128, 0), (2 * blk + 1, 128, 1)]
                    else:
                        kts = [(kt, 128, None) for kt in range(past // 128)] + [(past // 128, TS, None)]
                    its = [(hd, kt, rows, msk) for hd in range(4) for (kt, rows, msk) in kts]
                    nk = len(kts)

                    def load_v(ii):
                        hd, kt, rows, msk = its[ii]
                        vt, vtk = Vt[ii % 3], Vtk[ii % 3]
                        if kt < past // 128:
                            C.dma("pool", vt[:rows, :], cv[l, sb_, kt * 128:kt * 128 + rows, hd * 128:(hd + 1) * 128], vtk, writes=[vtk])
                        else:
                            C.dma("sp", vt[:rows, :], vsc[kt * 128:kt * 128 + rows, hd * 128:(hd + 1) * 128], vtk,
                                  reads=[dtk("vsc", kt)], writes=[vtk])

                    def emit_s(ii):
                        hd, kt, rows, msk = its[ii]
                        sbk_, sbkk = PS[2 + ii % 2], PSK[2 + ii % 2]
                        kc0 = kt * 128
                        C.op("pe", lambda e: e.matmul(sbk_[:rows, 0:nq], lhsT=KT[:, hd, kc0:kc0 + rows], rhs=QTa[:, hd, 0:nq],
                                                      start=True, stop=True), reads=[ktk[kt], QTk], writes=[sbkk])
                        C.op("pe", lambda e: e.matmul(sbk_[:rows, 256:256 + nq], lhsT=KT[:, hd, kc0:kc0 + rows],
                                                      rhs=QTb[:, hd, 0:nq], start=True, stop=True),
                             reads=[ktk[kt], QTk], writes=[sbkk])
                    load_v(0)
                    if len(its) > 1:
                        load_v(1)
                    emit_s(0)
                    for ii, (hd, kt, rows, msk) in enumerate(its):
                        if ii + 2 < len(its):
                            load_v(ii + 2)
                        if ii + 1 < len(its):
                            emit_s(ii + 1)
                        first = (ii % nk == 0)
                        last = (ii % nk == nk - 1)
                        sbk_, sbkk = PS[2 + ii % 2], PSK[2 + ii % 2]
                        pt, ptk = PT[ii % 2], PTk[ii % 2]
                        C.op("act", lambda e: e.activation(out=pt[:rows, :, 0:nq],
                                                           in_=sbk_[:rows, :].rearrange("p (j q) -> p j q", j=2)[:, :, 0:nq],
                                                           func=AF.Exp, scale=0.125), reads=[sbkk], writes=[ptk])
                        if msk is not None:
                            C.op("dve", lambda e: e.tensor_tensor(out=pt[:rows, :, 0:nq], in0=pt[:rows, :, 0:nq],
                                                                   in1=bc(amask[:rows, msk * 256:msk * 256 + nq].unsqueeze(1), [rows, 2, nq]),
                                                                   op=ALU.mult), reads=[ptk, constk], writes=[ptk])
                        if first:
                            C.op("dve", lambda e: e.memset(OB[:, :], 0.0), writes=[OBK])
                            C.op("dve", lambda e: e.memset(LB[:, :], 0.0), writes=[LBK])
                        vt, vtk = Vt[ii % 3], Vtk[ii % 3]
                        for j in range(2):
                            C.op("pe", lambda e: e.matmul(OB[:, j * 256:j * 256 + nq], lhsT=vt[:rows, :],
                                                          rhs=pt[:rows, j, 0:nq], start=False, stop=False, skip_group_check=True),
                                 reads=[vtk, ptk], writes=[OBK])
                            C.op("pe", lambda e: e.matmul(LB[:, j * 256:j * 256 + nq], lhsT=onesb[:rows, :], rhs=pt[:rows, j, 0:nq],
                                                          start=False, stop=False, skip_group_check=True),
                                 reads=[constk, ptk], writes=[LBK])
                        if last:
                            Lv = LB[:, :].rearrange("p (j q) -> p j q", j=2)[:, :, 0:nq]
                            Ov = OB[:, :].rearrange("p (j q) -> p j q", j=2)[:, :, 0:nq]
                            C.op("dve", lambda e: e.reciprocal(out=Rr[:, :, 0:nq], in_=Lv), reads=[LBK], writes=[atk])
                            C.op("dve", lambda e: e.tensor_tensor(out=Tt[:, :, 0:nq], in0=Ov, in1=Rr[:, :, 0:nq], op=ALU.mult),
                                 reads=[OBK, atk], writes=[atk])
                            C.op("dve", lambda e: e.scalar_tensor_tensor(out=oaf[:, 0:nq], in0=Tt[:, 1, 0:nq], scalar=neglam[:, 0:1],
                                                                         in1=Tt[:, 0, 0:nq], op0=ALU.mult, op1=ALU.add),
                                 reads=[atk, pk_], writes=[atk])
                            C.op("dve", lambda e: e.tensor_tensor(out=oab[:, 0:nq], in0=oaf[:, 0:nq], in1=oaf[:, 0:nq], op=ALU.mult),
                                 reads=[atk], writes=[atk])
                            yield
                            pm, pmk = bank("mi")
                            C.op("pe", lambda e: e.matmul(pm[:, 0:nq], lhsT=onesb[:, :], rhs=oab[:, 0:nq], start=True, stop=True),
                                 reads=[atk, constk], writes=[pmk])
                            C.op("act", lambda e: e.activation(out=Rr[:, 0, 0:nq], in_=pm[:, 0:nq], func=AF.Ln, scale=1.0 / 128,
                                                               bias=epsb[:, 1:2]), reads=[pmk, constk], writes=[atk])
                            C.op("act", lambda e: e.activation(out=Rr[:, 0, 0:nq], in_=Rr[:, 0, 0:nq], func=AF.Exp, scale=-0.5),
                                 reads=[atk], writes=[atk])
                            C.op("dve", lambda e: e.tensor_tensor(out=oaf[:, 0:nq], in0=oaf[:, 0:nq], in1=Rr[:, 0, 0:nq], op=ALU.mult),
                                 reads=[atk], writes=[atk])
                            C.op("dve", lambda e: e.tensor_scalar(out=mixA[:, hd, 0:nq], in0=oaf[:, 0:nq], scalar1=dngc[:, 0:1],
                                                                  scalar2=float(1.0 - lam_init), op0=ALU.mult, op1=ALU.mult),
                                 reads=[atk, pk_], writes=[mixAk])
                        yield

                def run_gens(*gens):
                    live = list(gens)
                    while live:
                        for g in list(live):
                            try:
                                next(g)
                            except StopIteration:
                                live.remove(g)

                def chain(*gs):
                    for g_ in gs:
                        yield from g_

                for blk in range(nblk):
                    t0_ = blk * TPB
                    run_gens(gen_frontA(0, t0_, "pj"), gen_frontB(t0_, 0))
                    if TPB == 2:
                        run_gens(gen_chunks(0), gen_frontA(1, t0_ + 1, "s"))
                        run_gens(gen_attention(blk), chain(gen_frontB(t0_ + 1, TS), gen_chunks(TS)))
                    else:
                        run_gens(gen_attention(blk), gen_chunks(0))
                    for ti in range(TPB):
                        back_tile(ti, t0_ + ti)

                if STOP == 1:
                    C.finish()
                    return nc
                ddst = pdl[l] if isP else sdlo[l, sb_]
                rdst = prt[l] if isP else srto[l, sb_]
                C.dma("pool", ddst.rearrange("(h d) e -> d h e", d=64), Sd[:], Sdk, reads=[Sdk])
                C.dma("pool", rdst.rearrange("(h d) e -> d h e", d=64), Sr[:], Srk, reads=[Srk])

        if STOP == 20 + l:
            C.finish()
            return nc
        C.barrier()
        WA_up = WA[:, 0:8 * DFF].rearrange("p (k n) -> p k n", k=8)
        WA_dn = WA[:, 8 * DFF:8 * DFF + 32 * D].rearrange("p (k n) -> p k n", k=32)
        WAK2 = Tk("WA2f")
        for k in range(8):
            for c0_ in range(0, DFF, 1024):
                C.dma("pool", WA_up[:, k, c0_:c0_ + 1024], w_up[l, k * 128:(k + 1) * 128, c0_:c0_ + 1024], WAK, writes=[WAK])
        for k in range(32):
            C.dma("pool", WA_dn[:, k, :], w_down[l, k * 128:(k + 1) * 128, :], WAK2, writes=[WAK2])
        with ExitStack() as st:
            def T(name, shape, dt=F32):
                return st.enter_context(nc.sbuf_tensor("%s_f%d" % (name, l), list(shape), dt))
            g2 = T("g2", [128, D]); b2 = T("b2", [128, D]); pk2 = Tk()
            C.dma("sp", g2[:], ln2g[l:l + 1, :].partition_broadcast(128), pk2, writes=[pk2])
            C.dma("sp", b2[:], ln2b[l:l + 1, :].partition_broadcast(128), pk2, writes=[pk2])
            x1 = T("x1", [128, 2, D]); x1k = [Tk(), Tk()]
            x1b = T("x1b", [128, D], BF16); x1bk = Tk()
            x1T = T("x1T", [128, 8, 256], BF16); x1Tk = Tk()
            hidT = T("hidT", [128, 32, 256], BF16); hidk = Tk()
            rl = [T("rl0", [128, 256]), T("rl1", [128, 256])]; rlk = [Tk(), Tk()]
            blocks = []
            for b0_ in range(0, Lp, 256):
                blocks.append((b0_, 128, min(2, (Lp - b0_) // 128)))
            for b_ in range(NS):
                blocks.append((Lp + b_ * Ls, Ls, 1))
            for (rb0, TS, ntl) in blocks:
                nb = TS * ntl
                for ti in range(ntl):
                    r0 = rb0 + ti * TS
                    C.dma("sp", x1[:TS, ti, :], x1s[r0:r0 + TS, :], x1k[ti], reads=[dtk("x1s", r0)], writes=[x1k[ti]])
                    C.op("act", lambda e: e.activation(out=x1b[:TS, :], in_=x1[:TS, ti, :], func=AF.Copy), reads=[x1k[ti]], writes=[x1bk])
                    pb, pbk = bank("mi")
                    pbb = pb[:].bitcast(BF16)
                    for k in range(8):
                        C.op("pe", lambda e: e.transpose(out=pbb[:, k * TS:(k + 1) * TS], in_=x1b[:TS, k * 128:(k + 1) * 128],
                                                         identity=identb[:TS, :TS]), reads=[x1bk, constk], writes=[pbk])
                    C.op("dve", lambda e: e.tensor_copy(out=x1T[:, :, ti * TS:(ti + 1) * TS],
                                                        in_=pbb[:, 0:8 * TS].rearrange("p (k t) -> p k t", k=8)),
                         reads=[pbk], writes=[x1Tk])
                for f in range(32):
                    pb, pbk = bank("pj")
                    for k in range(8):
                        C.op("pe", lambda e: e.matmul(pb[:, 0:nb], lhsT=WA_up[:, k, f * 128:(f + 1) * 128], rhs=x1T[:, k, 0:nb],
                                                      start=(k == 0), stop=(k == 7)), reads=[x1Tk, WAK], writes=[pbk])
                    r_, rk_ = rl[f % 2], rlk[f % 2]
                    C.op("act", lambda e: e.activation(out=r_[:, 0:nb], in_=pb[:, 0:nb], func=AF.Relu), reads=[pbk], writes=[rk_])
                    eng = "dve"
                    C.op(eng, lambda e: e.tensor_tensor(out=hidT[:, f, 0:nb], in0=r_[:, 0:nb], in1=r_[:, 0:nb], op=ALU.mult),
                         reads=[rk_], writes=[hidk])
                for ti in range(ntl):
                    r0 = rb0 + ti * TS
                    for half in range(2):
                        pb, pbk = bank("pj")
                        for f in range(32):
                            C.op("pe", lambda e: e.matmul(pb[:TS, :], lhsT=hidT[:, f, ti * TS:(ti + 1) * TS],
                                                          rhs=WA_dn[:, f, half * 512:(half + 1) * 512], start=(f == 0), stop=(f == 31)),
                                 reads=[hidk, WAK2], writes=[pbk])
                        C.op("dve", lambda e: e.scalar_tensor_tensor(out=x1[:TS, ti, half * 512:(half + 1) * 512],
                                                                     in0=x1[:TS, ti, half * 512:(half + 1) * 512], scalar=float(ALPHA),
                                                                     in1=pb[:TS, :], op0=ALU.mult, op1=ALU.add),
                             reads=[pbk, x1k[ti]], writes=[x1k[ti]])
                    layer_norm(x1[:TS, ti, :], TS, g2, b2, pk2, x1k[ti])
                    if l < DEPTH - 1:
                        C.dma("pool", xcs[r0:r0 + TS, :], x1[:TS, ti, :], x1k[ti], reads=[x1k[ti]], writes=[dtk("xcs", r0)])
                    else:
                        if r0 < Lp:
                            C.dma("pool", yp[r0:r0 + TS, :], x1[:TS, ti, :], x1k[ti], reads=[x1k[ti]])
                        else:
                            C.dma("pool", ys[r0 - Lp:r0 - Lp + TS, :], x1[:TS, ti, :], x1k[ti], reads=[x1k[ti]])
    C.finish()
    return nc


def make_consts(Lp, Ls, PAST):
    c = {}
    c["c_ident"] = np.eye(128, dtype=np.float32)
    half = 32
    inv_freq = (10000.0 ** (-np.arange(half, dtype=np.float32) / half)).astype(np.float32)

    def rot(pos):
        ang = pos.astype(np.float32)[:, None] * inv_freq[None, :]
        cos = np.cos(ang).astype(np.float32)
        sin = np.sin(ang).astype(np.float32)
        return np.concatenate([cos, cos, -sin, sin], axis=1).astype(np.float32)

    c["c_rotp"] = rot(np.arange(Lp))
    c["c_rots"] = rot(PAST + np.arange(Ls))
    k = np.arange(128)[:, None]
    qq = np.arange(256)[None, :]
    m0 = ((0 + k // 64) <= (qq // 64)).astype(np.float32)
    m1 = ((2 + k // 64) <= (qq // 64)).astype(np.float32)
    c["c_amask"] = np.concatenate([m0, m1], axis=1)
    j = np.arange(64)[:, None]
    i = np.arange(64)[None, :]
    c["c_triu"] = (j <= i).astype(np.float32)
    c["c_striu"] = (j < i).astype(np.float32)
    lg = np.log(1.0 - 2.0 ** (-5.0 - np.arange(4, dtype=np.float64)))
    rm = np.zeros((64, 4, 64), np.float64)
    for h in range(4):
        rm[:, h, :] = np.exp(-lg[h] * (j + 1.0)) * (j <= i)
    c["c_rm"] = rm.reshape(64, 256).astype(np.float32)
    xi = np.zeros((64, 4, 128), np.float64)
    xis = np.zeros((64, 4, Ls), np.float64)
    for h in range(4):
        xi[:, h, :] = np.exp(lg[h] * ((np.arange(128) % 64) + 1.0))[None, :]
        xis[:, h, :] = np.exp(lg[h] * (np.arange(Ls) + 1.0))[None, :]
    c["c_xi"] = xi.reshape(64, 512).astype(np.float32)
    c["c_xis"] = xis.reshape(64, 4 * Ls).astype(np.float32)
    zeta = np.zeros((128, 4), np.float64)
    zetas = np.zeros((Ls, 4), np.float64)
    for h in range(4):
        zeta[:, h] = np.exp(lg[h] * (63.0 - (np.arange(128) % 64)))
        zetas[:, h] = np.exp(lg[h] * (Ls - 1.0 - np.arange(Ls)))
    c["c_zeta"] = zeta.astype(np.float32)
    c["c_zetas"] = zetas.astype(np.float32)
    bo = np.zeros((128, 128), np.float32)
    bo[:64, :64] = 1.0
    bo[64:, 64:] = 1.0
    c["c_bones"] = bo
    return c


_NC_CACHE = {}


def run(inputs, Lp, NS, Ls, PAST, ncores):
    key = (Lp, NS, Ls, PAST)
    if key not in _NC_CACHE:
        _NC_CACHE[key] = build(Lp, NS, Ls, PAST)
    nc = _NC_CACHE[key]
    f = lambda a: np.ascontiguousarray(np.asarray(a, dtype=np.float32))
    consts = make_consts(Lp, Ls, PAST)
    I = {k: f(v) for k, v in inputs.items()}
    in_maps = []
    for i in range(ncores):
        sl = slice(i * NS, (i + 1) * NS)
        m = dict(consts)
        m["xp"] = f(I["x_prompt"][i])
        m["xs"] = f(I["x_sample"][sl].reshape(NS * Ls, D))
        m["ck"] = f(I["cache_k"][:, sl].reshape(DEPTH, NS, PAST, 512))
        m["cv"] = f(I["cache_v"][:, sl].reshape(DEPTH, NS, PAST, 512))
        m["sdl"] = f(I["state_delta"][:, sl].reshape(DEPTH, NS, 256, 64))
        m["scv"] = f(I["state_conv"][:, sl])
        m["srt"] = f(I["state_ret"][:, sl].reshape(DEPTH, NS, 256, 64))
        m["ln0g"] = f(I["ln0_g"].reshape(1, D)); m["ln0b"] = f(I["ln0_b"].reshape(1, D))
        m["w_in"] = I["w_in"]
        m["lamq1"] = I["lam_q1"]; m["lamk1"] = I["lam_k1"]; m["lamq2"] = I["lam_q2"]; m["lamk2"] = I["lam_k2"]
        m["dng"] = I["diff_norm_g"]; m["convw"] = I["conv_w"]; m["alog"] = I["a_log"]; m["dtb"] = I["dt_bias"]
        m["dlng"] = I["delta_norm_g"]; m["w_out"] = I["w_out"]
        m["ln1g"] = I["ln1_g"]; m["ln1b"] = I["ln1_b"]; m["w_up"] = I["w_up"]; m["w_down"] = I["w_down"]
        m["ln2g"] = I["ln2_g"]; m["ln2b"] = I["ln2_b"]
        in_maps.append(m)
    res = run_bass_kernel_spmd(nc, in_maps, core_ids=list(range(ncores)))
    R = res.results
    B = ncores
    st = lambda name: np.stack([np.asarray(R[i][name]) for i in range(B)])
    y_p = st("yp")
    y_s = st("ys").reshape(B * NS, Ls, D)
    p_k = st("pk").transpose(1, 0, 2, 3).reshape(DEPTH, B, Lp, 8, 64)
    p_v = st("pv").transpose(1, 0, 2, 3).reshape(DEPTH, B, Lp, 4, 128)
    p_d = st("pdl").transpose(1, 0, 2, 3).reshape(DEPTH, B, 4, 64, 64)
    p_c = st("pcv").transpose(1, 0, 2, 3)
    p_r = st("prt").transpose(1, 0, 2, 3).reshape(DEPTH, B, 4, 64, 64)
    s_k = st("sk").transpose(1, 0, 2, 3).reshape(DEPTH, B * NS, Ls, 8, 64)
    s_v = st("sv").transpose(1, 0, 2, 3).reshape(DEPTH, B * NS, Ls, 4, 128)
    s_d = st("sdlo").transpose(1, 0, 2, 3, 4).reshape(DEPTH, B * NS, 4, 64, 64)
    s_c = st("scvo").transpose(1, 0, 2, 3, 4).reshape(DEPTH, B * NS, 3, 768)
    s_r = st("srto").transpose(1, 0, 2, 3, 4).reshape(DEPTH, B * NS, 4, 64, 64)
    outs = (y_p, y_s, p_k, p_v, p_d, p_c, p_r, s_k, s_v, s_d, s_c, s_r)
    return tuple(np.ascontiguousarray(o.astype(np.float32)) for o in outs)


def kernel(**inputs):
    return run(inputs, 4096, 2, 32, 2048, NCORES)
```

```python
import math
import numpy as np
import ml_dtypes
import concourse.bass as bass
import concourse.mybir as mybir
from concourse.bass_utils import run_bass_kernel_spmd

F32 = mybir.dt.float32
BF16 = mybir.dt.bfloat16
AF = mybir.ActivationFunctionType
ALU = mybir.AluOpType
AX = mybir.AxisListType

D = 1024
NIN = 3592
DFF = 4096
DEPTH = 2
ALPHA = (2 * DEPTH) ** 0.25
LN_EPS = 1e-5
NORM_EPS = 1e-6
NCORES = 8


class Tk:
    __slots__ = ("w", "r", "dsem", "dcnt", "name")

    def __init__(self, name=""):
        self.w = None
        self.r = {}
        self.dsem = None
        self.dcnt = 0
        self.name = name


class Ctx:
    def __init__(self, nc):
        self.nc = nc
        self.E = {"pe": nc.tensor, "dve": nc.vector, "act": nc.scalar, "pool": nc.gpsimd,
                  "sp": nc.sync}
        self.sem = {k: nc.alloc_semaphore("es_" + k) for k in self.E}
        self.cnt = {k: 0 for k in self.E}
        self.seen = {k: {} for k in self.E}
        self.dsems = []
        self.nd = 0

    def _wait(self, e, dep):
        key, sem, val = dep
        if self.seen[e].get(key, 0) >= val:
            return
        self.E[e].wait_ge(sem, val)
        self.seen[e][key] = val

    def _deps(self, e, reads, writes):
        for t in reads:
            if t.w is not None:
                self._wait(e, t.w)
        for t in writes:
            if t.w is not None:
                self._wait(e, t.w)
            for d in t.r.values():
                self._wait(e, d)

    def _mark(self, me, reads, writes):
        for t in reads:
            old = t.r.get(me[0])
            if old is None or old[2] < me[2]:
                t.r[me[0]] = me
        for t in writes:
            t.w = me
            t.r = {}

    def op(self, e, fn, reads=(), writes=()):
        self._deps(e, reads, writes)
        ins = fn(self.E[e])
        self.cnt[e] += 1
        ins.then_inc(self.sem[e], 1)
        me = (e, self.sem[e], self.cnt[e])
        if e == "pe":
            self.seen[e][e] = self.cnt[e]
        self._mark(me, reads, writes)
        return ins

    def dma(self, q, out, in_, owner, reads=(), writes=(), **kw):
        self._deps(q, reads, writes)
        if owner.dsem is None:
            owner.dsem = self.nc.alloc_semaphore("ds%d" % self.nd)
            self.nd += 1
            self.dsems.append(owner)
        ins = self.E[q].dma_start(out=out, in_=in_, **kw)
        owner.dcnt += 16
        ins.then_inc(owner.dsem, 16)
        me = ("d%d" % id(owner), owner.dsem, owner.dcnt)
        self._mark(me, reads, writes)
        return ins

    def barrier(self):
        for e in self.E:
            for f in self.E:
                if f != e and self.cnt[f] > 0:
                    self._wait(e, (f, self.sem[f], self.cnt[f]))
            for o in self.dsems:
                if o.dcnt > 0:
                    self._wait(e, ("d%d" % id(o), o.dsem, o.dcnt))

    def finish(self):
        self.barrier()


def bc(ap, shape):
    return ap.to_broadcast(list(shape))


import os
from contextlib import ExitStack
STOP = int(os.environ.get('KSTOP', '99'))


def build(Lp=4096, NS=2, Ls=32, PAST=2048):
    nc = bass.Bass("TRN2", target_bir_lowering=False)
    C = Ctx(nc)
    Ltot = Lp + NS * Ls
    LK = max(Lp, PAST + Ls)

    def din(name, shape, dt=F32):
        return nc.dram_tensor(name, list(shape), dt, kind="ExternalInput").ap()

    def dout(name, shape):
        return nc.dram_tensor(name, list(shape), F32, kind="ExternalOutput").ap()

    xp = din("xp", [Lp, D])
    xs = din("xs", [NS * Ls, D])
    ck = din("ck", [DEPTH, NS, PAST, 512])
    cv = din("cv", [DEPTH, NS, PAST, 512])
    sdl = din("sdl", [DEPTH, NS, 256, 64])
    scv = din("scv", [DEPTH, NS, 3, 768])
    srt = din("srt", [DEPTH, NS, 256, 64])
    ln0g = din("ln0g", [1, D]); ln0b = din("ln0b", [1, D])
    w_in = din("w_in", [DEPTH, D, NIN])
    lamq1 = din("lamq1", [DEPTH, 64]); lamk1 = din("lamk1", [DEPTH, 64])
    lamq2 = din("lamq2", [DEPTH, 64]); lamk2 = din("lamk2", [DEPTH, 64])
    dng = din("dng", [DEPTH, 128])
    convw = din("convw", [DEPTH, 4, 768])
    alog = din("alog", [DEPTH, 4]); dtb = din("dtb", [DEPTH, 4])
    dlng = din("dlng", [DEPTH, 64])
    w_out = din("w_out", [DEPTH, D, D])
    ln1g = din("ln1g", [DEPTH, D]); ln1b = din("ln1b", [DEPTH, D])
    w_up = din("w_up", [DEPTH, D, DFF])
    w_down = din("w_down", [DEPTH, DFF, D])
    ln2g = din("ln2g", [DEPTH, D]); ln2b = din("ln2b", [DEPTH, D])
    c_ident = din("c_ident", [128, 128])
    c_rotp = din("c_rotp", [Lp, 128])
    c_rots = din("c_rots", [Ls, 128])
    c_amask = din("c_amask", [128, 512])
    c_triu = din("c_triu", [64, 64])
    c_striu = din("c_striu", [64, 64])
    c_rm = din("c_rm", [64, 256])
    c_xi = din("c_xi", [64, 512])
    c_xis = din("c_xis", [64, 4 * Ls])
    c_zeta = din("c_zeta", [128, 4])
    c_zetas = din("c_zetas", [Ls, 4])
    c_bones = din("c_bones", [128, 128])

    yp = dout("yp", [Lp, D]); ys = dout("ys", [NS * Ls, D])
    pk = dout("pk", [DEPTH, Lp, 512]); pv = dout("pv", [DEPTH, Lp, 512])
    pdl = dout("pdl", [DEPTH, 256, 64]); pcv = dout("pcv", [DEPTH, 3, 768])
    prt = dout("prt", [DEPTH, 256, 64])
    sk = dout("sk", [DEPTH, NS * Ls, 512]); sv = dout("sv", [DEPTH, NS * Ls, 512])
    sdlo = dout("sdlo", [DEPTH, NS, 256, 64]); scvo = dout("scvo", [DEPTH, NS, 3, 768])
    srto = dout("srto", [DEPTH, NS, 256, 64])
    x1s = nc.dram_tensor("x1s", [Ltot, D], F32).ap()
    xcs = nc.dram_tensor("xcs", [Ltot, D], F32).ap()
    vsc = nc.dram_tensor("vsc", [LK, 512], BF16).ap()
    dram_tk = {}

    def dtk(name, r0):
        k = (name, r0)
        if k not in dram_tk:
            dram_tk[k] = Tk()
        return dram_tk[k]

    PS = [nc.alloc_psum_tensor("ps%d" % i, [128, 512], F32) for i in range(8)]
    PSK = [Tk("ps%d" % i) for i in range(8)]
    rot_state = {"pj": 0, "mi": 0, "s": 0}

    def bank(kind):
        base, n = {"pj": (0, 2), "s": (2, 2), "mi": (6, 2)}[kind]
        i = base + rot_state[kind] % n
        rot_state[kind] += 1
        return PS[i], PSK[i]

    OB, OBK = PS[4], PSK[4]
    LB, LBK = PS[5], PSK[5]

    def sb(name, shape, dt=F32):
        return nc.alloc_sbuf_tensor(name, list(shape), dt)

    identf = sb("identf", [128, 128]); identb = sb("identb", [128, 128], BF16)
    onesb = sb("onesb", [128, 128], BF16)
    onesf = sb("onesf", [128, 128])
    bonesf = sb("bonesf", [128, 128])
    triu = sb("triu", [64, 64]); striu = sb("striu", [64, 64])
    rmc = sb("rmc", [64, 256]); xic = sb("xic", [64, 512]); xisc = sb("xisc", [64, 4 * Ls])
    zetac = sb("zetac", [128, 4]); zetasc = sb("zetasc", [Ls, 4])
    amask = sb("amask", [128, 512], BF16)
    epsb = sb("epsb", [128, 2])
    constk = Tk("const")
    q = "sp"
    C.dma(q, identf[:], c_ident[:, :], constk, writes=[constk])
    C.dma(q, bonesf[:], c_bones[:, :], constk, writes=[constk])
    C.dma(q, triu[:], c_triu[:, :], constk, writes=[constk])
    C.dma(q, striu[:], c_striu[:, :], constk, writes=[constk])
    C.dma(q, rmc[:], c_rm[:, :], constk, writes=[constk])
    C.dma(q, xic[:], c_xi[:, :], constk, writes=[constk])
    C.dma(q, xisc[:], c_xis[:, :], constk, writes=[constk])
    C.dma(q, zetac[:], c_zeta[:, :], constk, writes=[constk])
    C.dma(q, zetasc[:], c_zetas[:, :], constk, writes=[constk])
    C.dma("pool", amask[:], c_amask[:, :], constk, writes=[constk])
    C.op("dve", lambda e: e.tensor_copy(out=identb[:], in_=identf[:]), reads=[constk], writes=[constk])
    C.op("dve", lambda e: e.memset(onesb[:], 1.0), writes=[constk])
    C.op("dve", lambda e: e.memset(onesf[:], 1.0), writes=[constk])
    C.op("dve", lambda e: e.memset(epsb[:, 0:1], LN_EPS), writes=[constk])
    C.op("dve", lambda e: e.memset(epsb[:, 1:2], NORM_EPS), writes=[constk])

    WA = sb("WA", [128, 65536], BF16)
    WAK = Tk("WA")

    lnst = sb("lnst", [128, 2, 6]); lnmv = sb("lnmv", [128, 2]); lnk = Tk("ln")

    def layer_norm(xap, n, gt, bt, gk, xk):
        tks = [xk]
        C.op("dve", lambda e: e.bn_stats(out=lnst[:n, 0, :], in_=xap[:, 0:512]), reads=tks, writes=[lnk])
        C.op("dve", lambda e: e.bn_stats(out=lnst[:n, 1, :], in_=xap[:, 512:1024]), reads=tks, writes=[lnk])
        C.op("dve", lambda e: e.bn_aggr(out=lnmv[:n, :], in_=lnst[:n, :, :].rearrange("p a b -> p (a b)")),
             reads=[lnk], writes=[lnk])
        C.op("act", lambda e: e.activation(out=lnmv[:n, 1:2], in_=lnmv[:n, 1:2], func=AF.Ln, bias=epsb[:n, 0:1]),
             reads=[lnk, constk], writes=[lnk])
        C.op("act", lambda e: e.activation(out=lnmv[:n, 1:2], in_=lnmv[:n, 1:2], func=AF.Exp, scale=-0.5),
             reads=[lnk], writes=[lnk])
        C.op("dve", lambda e: e.tensor_scalar(out=xap, in0=xap, scalar1=lnmv[:n, 0:1], scalar2=lnmv[:n, 1:2],
                                              op0=ALU.subtract, op1=ALU.mult),
             reads=[lnk] + tks, writes=tks)
        C.op("dve", lambda e: e.tensor_tensor(out=xap, in0=xap, in1=gt[:n, :], op=ALU.mult),
             reads=tks + [gk], writes=tks)
        C.op("dve", lambda e: e.tensor_tensor(out=xap, in0=xap, in1=bt[:n, :], op=ALU.add),
             reads=tks + [gk], writes=tks)

    seqs = [dict(name="p", L=Lp, past=0, TS=128, c=64, row0=0, b=0)]
    for b_ in range(NS):
        seqs.append(dict(name="s%d" % b_, L=Ls, past=PAST, TS=Ls, c=Ls, row0=Lp + b_ * Ls, b=b_))

    with ExitStack() as st0:
        g0 = st0.enter_context(nc.sbuf_tensor("g0", [128, D], F32))
        b0 = st0.enter_context(nc.sbuf_tensor("b0", [128, D], F32))
        xl = st0.enter_context(nc.sbuf_tensor("xl", [128, 2, D], F32))
        xlk = [Tk(), Tk()]
        pk0 = Tk()
        C.dma("sp", g0[:], ln0g[0:1, :].partition_broadcast(128), pk0, writes=[pk0])
        C.dma("sp", b0[:], ln0b[0:1, :].partition_broadcast(128), pk0, writes=[pk0])
        tiles0 = [(xp, r, r, 128) for r in range(0, Lp, 128)]
        for b_ in range(NS):
            tiles0.append((xs, b_ * Ls, Lp + b_ * Ls, Ls))
        for i0, (src0, sr0, dr0, n0) in enumerate(tiles0):
            sl0 = i0 % 2
            C.dma("sp", xl[:n0, sl0, :], src0[sr0:sr0 + n0, :], xlk[sl0], writes=[xlk[sl0]])
            layer_norm(xl[:n0, sl0, :], n0, g0, b0, pk0, xlk[sl0])
            C.dma("pool", xcs[dr0:dr0 + n0, :], xl[:n0, sl0, :], xlk[sl0], reads=[xlk[sl0]], writes=[dtk("xcs", dr0)])
    if STOP == 0:
        C.finish()
        return nc

    for l in range(DEPTH):
        lam_init = 0.8 - 0.6 * math.exp(-0.3 * l)
        C.barrier()
        WA_in = WA[:, 0:8 * NIN].rearrange("p (k n) -> p k n", k=8)
        WA_out = WA[:, 8 * NIN:8 * NIN + 8192].rearrange("p (k n) -> p k n", k=8)
        WAK2 = Tk("WA2")
        for k in range(8):
            for c0_ in range(0, NIN, 1024):
                c1_ = min(NIN, c0_ + 1024)
                C.dma("pool", WA_in[:, k, c0_:c1_], w_in[l, k * 128:(k + 1) * 128, c0_:c1_], WAK, writes=[WAK])
        for k in range(8):
            C.dma("pool", WA_out[:, k, :], w_out[l, k * 128:(k + 1) * 128, :], WAK2, writes=[WAK2])
        with ExitStack() as st:
            def T(name, shape, dt=F32):
                return st.enter_context(nc.sbuf_tensor("%s_%d" % (name, l), list(shape), dt))
            wa_off = [8 * NIN + 8192]

            def WT(shape):
                n = 1
                for d_ in shape[1:]:
                    n *= d_
                ap = WA[:, wa_off[0]:wa_off[0] + n]
                wa_off[0] += n
                assert wa_off[0] <= 65536, wa_off[0]
                if len(shape) == 3:
                    ap = ap.rearrange("p (a b) -> p a b", a=shape[1])
                return ap
            KT = WT([128, 4, LK])
            ktk = [Tk("kt%d" % i) for i in range((LK + 127) // 128)]
            g1 = T("g1", [128, D]); b1 = T("b1", [128, D])
            pk_ = Tk("params")
            C.dma("sp", g1[:], ln1g[l:l + 1, :].partition_broadcast(128), pk_, writes=[pk_])
            C.dma("sp", b1[:], ln1b[l:l + 1, :].partition_broadcast(128), pk_, writes=[pk_])
            lamt = T("lamt", [128, 4, 64]); lamr = T("lamr", [128, 4]); neglam = T("neglam", [128, 1])
            for i, src in enumerate((lamq1, lamk1, lamq2, lamk2)):
                C.dma("sp", lamt[:, i, :], src[l:l + 1, :].partition_broadcast(128), pk_, writes=[pk_])
            C.op("dve", lambda e: e.tensor_tensor(out=lamt[:, 0, :], in0=lamt[:, 0, :], in1=lamt[:, 1, :], op=ALU.mult),
                 reads=[pk_], writes=[pk_])
            C.op("dve", lambda e: e.tensor_tensor(out=lamt[:, 2, :], in0=lamt[:, 2, :], in1=lamt[:, 3, :], op=ALU.mult),
                 reads=[pk_], writes=[pk_])
            C.op("dve", lambda e: e.tensor_reduce(out=lamr[:, 0:1], in_=lamt[:, 0, :], axis=AX.X, op=ALU.add),
                 reads=[pk_], writes=[pk_])
            C.op("dve", lambda e: e.tensor_reduce(out=lamr[:, 1:2], in_=lamt[:, 2, :], axis=AX.X, op=ALU.add),
                 reads=[pk_], writes=[pk_])
            C.op("act", lambda e: e.activation(out=lamr[:, 2:4], in_=lamr[:, 0:2], func=AF.Exp),
                 reads=[pk_], writes=[pk_])
            C.op("dve", lambda e: e.scalar_tensor_tensor(out=neglam[:], in0=lamr[:, 3:4], scalar=-lam_init,
                                                         in1=lamr[:, 2:3], op0=ALU.add, op1=ALU.subtract),
                 reads=[pk_], writes=[pk_])
            dngc = T("dngc", [128, 1]); dlngc = T("dlngc", [64, 1])
            C.dma("sp", dngc[:], dng[l:l + 1, :].rearrange("o e -> e o"), pk_, writes=[pk_])
            C.dma("sp", dlngc[:], dlng[l:l + 1, :].rearrange("o e -> e o"), pk_, writes=[pk_])
            cw = T("cw", [128, 6, 4])
            for ch in range(6):
                C.dma("sp", cw[:, ch, :], convw[l, :, ch * 128:(ch + 1) * 128].rearrange("j p -> p j"),
                      pk_, writes=[pk_], allow_slow_non_contiguous=True)
            nA = T("nA", [128, 4]); dtbb = T("dtbb", [128, 4])
            C.dma("sp", nA[:], alog[l:l + 1, :].partition_broadcast(128), pk_, writes=[pk_])
            C.dma("sp", dtbb[:], dtb[l:l + 1, :].partition_broadcast(128), pk_, writes=[pk_])
            C.op("act", lambda e: e.activation(out=nA[:], in_=nA[:], func=AF.Exp), reads=[pk_], writes=[pk_])
            C.op("dve", lambda e: e.tensor_scalar(out=nA[:], in0=nA[:], scalar1=-1.0, scalar2=None, op0=ALU.mult),
                 reads=[pk_], writes=[pk_])
            if STOP == 6:
                C.finish()
                return nc

            xres = T("xres", [128, D]); xk = Tk("xres")
            xbf = WT([128, D]); xbfk = Tk()
            kbf = xbf[:, 0:512]; kbfk = xbfk
            xT = WT([128, 8, 256]); xTk = Tk()
            mixT = xT; mixk = xTk
            mixA = WT([128, 4, 256]); mixAk = Tk()
            QTa = WT([128, 4, 256]); QTb = WT([128, 4, 256]); QTk = Tk()
            C.op("pool", lambda e: e.memset(QTa[64:128, :, :], 0.0), writes=[QTk])
            C.op("pool", lambda e: e.memset(QTb[0:64, :, :], 0.0), writes=[QTk])
            rot = T("rot", [128, 128]); rotk = Tk()
            tmpA = T("tmpA", [128, 512]); tmpB = T("tmpB", [128, 512]); tmpk = Tk()
            kout = T("kout", [128, 512]); koutk = Tk()
            vout = T("vout", [128, 512]); voutk = Tk()
            vbf = WT([128, 512]); vbfk = Tk()
            uT = T("uT", [128, 6, 131]); uTk = Tk()
            cT = T("cT", [128, 6, 128]); cTk = Tk()
            cE = T("cE", [128, 6, 128]); cEk = Tk()
            nTh = WT([128, 8, 128]); nTk = Tk()
            qdT = WT([128, 4, 128]); qdk = Tk()
            g4 = T("g4", [128, 264]); g4e = T("g4e", [128, 264]); g4k = Tk()
            sgT = T("sgT", [64, 8, 128]); sgTk = Tk()
            qkc = T("qkc", [128, 512]); qkck = Tk()
            qkcT = WT([128, 8, 128]); qkcTk = Tk()
            sg6 = g4[:, 0:256]; sg6e = g4e[:, 0:256]; sg6k = g4k
            ktok = T("ktok", [64, 2, 512]); ktokk = Tk()
            kz = T("kz", [64, 2, 256], BF16); kzk = Tk()
            vcb = T("vcb", [64, 2, 256], BF16); kzc = T("kzc", [64, 2, 256], BF16); vck = Tk()
            gb = T("gb", [64, 2, 8]); gbk = Tk()
            sm = T("sm", [128, 64]); smk = Tk()
            GT = T("GT", [64, 4, 64]); GTk = Tk()
            Drow = T("Drow", [128, 4, 64]); EDrow = T("EDrow", [128, 4, 64]); Drk = Tk()
            LM = T("LM", [64, 4, 64]); LMs = T("LMs", [64, 4, 64]); LMi = T("LMi", [64, 4, 64]); LMk = Tk()
            Xf = T("Xf", [64, 4, 64]); Xk = Tk()
            Yb = T("Yb", [64, 2, 4, 64], BF16); YTb = T("YTb", [64, 6, 4, 64], BF16); Yk = [Tk() for _ in range(8)]; YTk = [Tk() for _ in range(8)]
            Pf = T("Pf", [64, 4, 64]); Pb = T("Pb", [64, 4, 64], BF16); Pk = Tk()
            inT = T("inT", [64, 4, 64], BF16); inTk = Tk()
            inR = T("inR", [64, 4, 64], BF16); inRk = Tk()
            rt = T("rt", [64, 256]); rb = T("rb", [64, 256], BF16); rk = Tk()
            scr = T("scr", [64, 4, 64]); scrk = Tk()
            vnb = T("vnb", [64, 256], BF16); vnk = Tk()
            Sd = T("Sd", [64, 4, 64]); Sdb = T("Sdb", [64, 4, 64], BF16); Sdk = Tk()
            Sr = T("Sr", [64, 4, 64]); Srb = T("Srb", [64, 4, 64], BF16); Srk = Tk()
            ot = T("ot", [64, 4, 128]); ot2 = T("ot2", [64, 4, 128]); otb = T("otb", [64, 4, 128], BF16); otk = Tk()
            PT = [WT([128, 2, 256]), WT([128, 2, 256])]; PTk = [Tk(), Tk()]
            Vt = [WT([128, 128]), WT([128, 128]), WT([128, 128])]; Vtk = [Tk(), Tk(), Tk()]
            ckb = WT([128, 512]); ckbk = Tk()
            Rr = tmpA[:, :].rearrange("p (j q) -> p j q", j=2); Tt = tmpB[:, :].rearrange("p (j q) -> p j q", j=2)
            oaf = T("oaf", [128, 256])
            oab = T("oab", [128, 256], BF16); atk = tmpk
            pcst = cE[:3, :, :].rearrange("p a b -> p (a b)"); pcstk = cEk

            def proj(TS, col0, c0, n, kind="pj"):
                pb, pbk = bank(kind)
                for k in range(8):
                    C.op("pe", lambda e: e.matmul(pb[:TS, 0:n], lhsT=xT[:, k, col0:col0 + TS],
                                                  rhs=WA_in[:, k, c0:c0 + n], start=(k == 0), stop=(k == 7)),
                         reads=[xTk, WAK], writes=[pbk])
                return pb, pbk

            def rotary(pb, pbk, TS, outs):
                pv_ = pb[:TS, :].rearrange("p (h d) -> p h d", h=8)
                C.op("dve", lambda e: e.tensor_tensor(out=tmpA[:TS, :].rearrange("p (h d) -> p h d", h=8), in0=pv_,
                                                      in1=bc(rot[:TS, 0:64].unsqueeze(1), [TS, 8, 64]), op=ALU.mult),
                     reads=[pbk, rotk], writes=[tmpk])
                tb = tmpB[:TS, :].rearrange("p (h d) -> p h d", h=8)
                C.op("dve", lambda e: e.tensor_tensor(out=tb[:, :, 0:32], in0=pv_[:, :, 32:64],
                                                      in1=bc(rot[:TS, 64:96].unsqueeze(1), [TS, 8, 32]), op=ALU.mult),
                     reads=[pbk, rotk], writes=[tmpk])
                C.op("dve", lambda e: e.tensor_tensor(out=tb[:, :, 32:64], in0=pv_[:, :, 0:32],
                                                      in1=bc(rot[:TS, 96:128].unsqueeze(1), [TS, 8, 32]), op=ALU.mult),
                     reads=[pbk, rotk], writes=[tmpk])
                for (eng, oap, otk_) in outs:
                    C.op(eng, lambda e: e.tensor_tensor(out=oap, in0=tmpA[:TS, :], in1=tmpB[:TS, :], op=ALU.add),
                         reads=[tmpk], writes=[otk_])

            def transposes_bf(src, srck, TS, n):
                pb, pbk = bank("mi")
                pbb = pb[:].bitcast(BF16)
                for i in range(n):
                    C.op("pe", lambda e: e.transpose(out=pbb[:, i * TS:(i + 1) * TS],
                                                     in_=src[:TS, i * 128:(i + 1) * 128],
                                                     identity=identb[:TS, :TS]),
                         reads=[srck, constk], writes=[pbk])
                return pbb[:, 0:n * TS].rearrange("p (k t) -> p k t", k=n), pbk

            def silu_from(pb_ap, xs_, es_, k_, pbk):
                C.op("act", lambda e: e.activation(out=es_, in_=pb_ap, func=AF.Exp, scale=-1.0), reads=[pbk], writes=[k_])
                C.op("act", lambda e: e.activation(out=xs_, in_=pb_ap, func=AF.Copy), reads=[pbk], writes=[k_])
                C.op("dve", lambda e: e.tensor_scalar(out=es_, in0=es_, scalar1=1.0, scalar2=None, op0=ALU.add),
                     reads=[k_], writes=[k_])
                C.op("dve", lambda e: e.reciprocal(out=es_, in_=es_), reads=[k_], writes=[k_])

            for sq_ in seqs:
                TS, c, L, past, row0, sb_ = sq_["TS"], sq_["c"], sq_["L"], sq_["past"], sq_["row0"], sq_["b"]
                isP = sq_["name"] == "p"
                CPT = TS // c
                TPB = 2 if isP else 1
                nblk = L // (TS * TPB)
                nq = TS * TPB
                nit = 5 if c == 64 else 4
                rsrc = c_rotp if isP else c_rots
                kdst = pk if isP else sk
                vdst = pv if isP else sv
                orow0 = 0 if isP else sb_ * Ls
                zt = zetac if isP else zetasc
                xit = xic if isP else xisc
                gch = [math.exp(math.log(1.0 - 2.0 ** (-5.0 - h)) * c) for h in range(4)]
                if isP:
                    C.op("dve", lambda e: e.memset(Sd[:], 0.0), writes=[Sdk])
                    C.op("dve", lambda e: e.memset(Sr[:], 0.0), writes=[Srk])
                    C.op("dve", lambda e: e.memset(uT[:, :, 0:3], 0.0), writes=[uTk])
                else:
                    C.dma("sp", Sd[:], sdl[l, sb_].rearrange("(h d) e -> d h e", d=64), Sdk, writes=[Sdk])
                    C.dma("sp", Sr[:], srt[l, sb_].rearrange("(h d) e -> d h e", d=64), Srk, writes=[Srk])
                    for ch in range(6):
                        C.dma("sp", uT[:, ch, 0:3], scv[l, sb_, :, ch * 128:(ch + 1) * 128].rearrange("j p -> p j"),
                              uTk, writes=[uTk], allow_slow_non_contiguous=True)
                    for kt in range(past // 128):
                        C.dma("pool", ckb[:], ck[l, sb_, kt * 128:(kt + 1) * 128, :], ckbk, writes=[ckbk])
                        pv4, pv4k = transposes_bf(ckb, ckbk, 128, 4)
                        C.op("dve", lambda e: e.tensor_copy(out=KT[:, :, kt * 128:(kt + 1) * 128], in_=pv4),
                             reads=[pv4k], writes=[ktk[kt]])
                C.op("act", lambda e: e.activation(func=AF.Copy, out=Sdb[:], in_=Sd[:]), reads=[Sdk], writes=[Sdk])
                C.op("act", lambda e: e.activation(func=AF.Copy, out=Srb[:], in_=Sr[:]), reads=[Srk], writes=[Srk])

                def gen_frontA(ti, t, pjk):
                    r0 = t * TS
                    col0 = ti * TS
                    kcol = past + r0
                    kt_own = kcol // 128
                    xa = xres[:TS, :]
                    C.dma("sp", xa, xcs[row0 + r0:row0 + r0 + TS, :], xk,
                          reads=[dtk("xcs", row0 + r0)], writes=[xk])
                    C.dma("sp", rot[:TS, :], rsrc[r0:r0 + TS, :], rotk, writes=[rotk])
                    C.op("act", lambda e: e.activation(out=xbf[:TS, :], in_=xa, func=AF.Copy), reads=[xk], writes=[xbfk])
                    pv8, pv8k = transposes_bf(xbf, xbfk, TS, 8)
                    C.op("dve", lambda e: e.tensor_copy(out=xT[:, :, col0:col0 + TS], in_=pv8), reads=[pv8k], writes=[xTk])
                    yield
                    pb, pbk = proj(TS, col0, 0, 512, pjk)
                    rotary(pb, pbk, TS, [("dve", kbf[:TS, :], kbfk)])
                    pv4, pv4k = transposes_bf(kbf, kbfk, TS, 4)
                    C.op("dve", lambda e: e.tensor_copy(out=QTa[0:64, :, col0:col0 + TS], in_=pv4[0:64, :, :]),
                         reads=[pv4k], writes=[QTk])
                    C.op("dve", lambda e: e.tensor_copy(out=QTb[64:128, :, col0:col0 + TS], in_=pv4[64:128, :, :]),
                         reads=[pv4k], writes=[QTk])
                    yield
                    pb, pbk = proj(TS, col0, 512, 512, pjk)
                    rotary(pb, pbk, TS, [("dve", kout[:TS, :], koutk), ("dve", kbf[:TS, :], kbfk)])
                    C.dma("pool", kdst[l, orow0 + r0:orow0 + r0 + TS, :], kout[:TS, :], koutk, reads=[koutk])
                    pv4, pv4k = transposes_bf(kbf, kbfk, TS, 4)
                    C.op("dve", lambda e: e.tensor_copy(out=KT[:, :, kcol:kcol + TS], in_=pv4),
                         reads=[pv4k], writes=[ktk[kt_own]])
                    yield
                    pb, pbk = proj(TS, col0, 1024, 512, pjk)
                    C.op("act", lambda e: e.activation(out=vout[:TS, :], in_=pb[:TS, :], func=AF.Copy), reads=[pbk], writes=[voutk])
                    C.op("dve", lambda e: e.tensor_copy(out=vbf[:TS, :], in_=vout[:TS, :]), reads=[voutk], writes=[vbfk])
                    C.dma("pool", vdst[l, orow0 + r0:orow0 + r0 + TS, :], vout[:TS, :], voutk, reads=[voutk])
                    C.dma("pool", vsc[kcol:kcol + TS, :], vbf[:TS, :], vbfk, reads=[vbfk], writes=[dtk("vsc", kt_own)])
                    yield

                def gen_frontB(t, col0):
                    pb, pbk = proj(TS, col0, 2304, 264)
                    silu_from(pb[:TS, 0:260], g4[:TS, 0:260], g4e[:TS, 0:260], g4k, pbk)
                    C.op("dve", lambda e: e.tensor_tensor(out=g4[:TS, 0:256], in0=g4[:TS, 0:256], in1=g4e[:TS, 0:256],
                                                          op=ALU.mult), reads=[g4k], writes=[g4k])
                    C.op("dve", lambda e: e.tensor_tensor(out=g4[:TS, 260:264], in0=pb[:TS, 260:264], in1=dtbb[:TS, :],
                                                          op=ALU.add), reads=[pbk, pk_], writes=[g4k])
                    C.op("act", lambda e: e.activation(out=g4[:TS, 260:264], in_=g4[:TS, 260:264], func=AF.Exp),
                         reads=[g4k], writes=[g4k])
                    C.op("act", lambda e: e.activation(out=g4[:TS, 260:264], in_=g4[:TS, 260:264], func=AF.Ln, bias=1.0),
                         reads=[g4k], writes=[g4k])
                    C.op("dve", lambda e: e.tensor_tensor(out=g4[:TS, 260:264], in0=g4[:TS, 260:264], in1=nA[:TS, :],
                                                          op=ALU.mult), reads=[g4k, pk_], writes=[g4k])
                    for ck_ in range(CPT):
                        C.op("dve", lambda e: e.tensor_copy(out=gb[:c, ck_, 0:4], in_=g4[ck_ * c:(ck_ + 1) * c, 260:264]),
                             reads=[g4k], writes=[gbk])
                        C.op("dve", lambda e: e.tensor_copy(out=gb[:c, ck_, 4:8], in_=g4e[ck_ * c:(ck_ + 1) * c, 256:260]),
                             reads=[g4k], writes=[gbk])
                    pm, pmk = bank("mi")
                    for h in range(4):
                        C.op("pe", lambda e: e.transpose(out=pm[:64, h * TS:(h + 1) * TS], in_=g4[:TS, h * 64:(h + 1) * 64],
                                                         identity=identf[:TS, :TS]),
                             reads=[g4k, constk], writes=[pmk])
                    C.op("act", lambda e: e.activation(out=sgT[:, 0:4, :TS],
                                                       in_=pm[:64, 0:4 * TS].rearrange("p (h t) -> p h t", h=4), func=AF.Copy),
                         reads=[pmk], writes=[sgTk])
                    yield
                    pb, pbk = proj(TS, col0, 2568, 512)
                    rotary(pb, pbk, TS, [("dve", qkc[:TS, :], qkck)])
                    C.op("dve", lambda e: e.tensor_scalar(out=qkc[:TS, 256:512], in0=qkc[:TS, 256:512], scalar1=0.125,
                                                          scalar2=None, op0=ALU.mult), reads=[qkck], writes=[qkck])
                    for half in range(2):
                        pm, pmk = bank("mi")
                        for h in range(4):
                            C.op("pe", lambda e: e.transpose(
                                out=pm[:64, h * TS:(h + 1) * TS],
                                in_=qkc[:TS, half * 256 + h * 64:half * 256 + (h + 1) * 64],
                                identity=identf[:TS, :TS]), reads=[qkck, constk], writes=[pmk])
                        C.op("act", lambda e: e.activation(
                            out=qkcT[0:64, half * 4:half * 4 + 4, :TS],
                            in_=pm[:64, 0:4 * TS].rearrange("p (h t) -> p h t", h=4), func=AF.Copy), reads=[pmk], writes=[qkcTk])
                    for ck_ in range(CPT):
                        C.op("dve", lambda e: e.tensor_tensor(
                            out=kzc[:c, ck_, :].rearrange("p (h d) -> p h d", h=4),
                            in0=qkc[ck_ * c:(ck_ + 1) * c, 256:512].rearrange("p (h d) -> p h d", h=4),
                            in1=bc(zt[ck_ * c:(ck_ + 1) * c, :].unsqueeze(2), [c, 4, 64]), op=ALU.mult),
                             reads=[qkck, constk], writes=[vck])
                    yield
                    pb, pbk = proj(TS, col0, 3080, 512)
                    for ck_ in range(CPT):
                        C.op("act", lambda e: e.activation(out=vcb[:c, ck_, :], in_=pb[ck_ * c:(ck_ + 1) * c, 0:256], func=AF.Copy),
                             reads=[pbk], writes=[vck])
                    silu_from(pb[:TS, 256:512], sg6[:TS, :], sg6e[:TS, :], sg6k, pbk)
                    C.op("dve", lambda e: e.tensor_tensor(out=sg6[:TS, :], in0=sg6[:TS, :], in1=sg6e[:TS, :], op=ALU.mult),
                         reads=[sg6k], writes=[sg6k])
                    pm, pmk = bank("mi")
                    for h in range(4):
                        C.op("pe", lambda e: e.transpose(out=pm[:64, h * TS:(h + 1) * TS], in_=sg6[:TS, h * 64:(h + 1) * 64],
                                                         identity=identf[:TS, :TS]),
                             reads=[sg6k, constk], writes=[pmk])
                    C.op("act", lambda e: e.activation(out=sgT[:, 4:8, :TS],
                                                       in_=pm[:64, 0:4 * TS].rearrange("p (h t) -> p h t", h=4), func=AF.Copy),
                         reads=[pmk], writes=[sgTk])
                    yield
                    for rnd in range(2):
                        pb, pbk = bank("pj")
                        for j in range(3):
                            ch = rnd * 3 + j
                            for k in range(8):
                                C.op("pe", lambda e: e.matmul(
                                    pb[:, j * TS:(j + 1) * TS],
                                    lhsT=WA_in[:, k, 1536 + ch * 128:1536 + (ch + 1) * 128],
                                    rhs=xT[:, k, col0:col0 + TS], start=(k == 0), stop=(k == 7)),
                                     reads=[xTk, WAK], writes=[pbk])
                        C.op("act", lambda e: e.activation(
                            out=uT[:, rnd * 3:rnd * 3 + 3, 3:3 + TS],
                            in_=pb[:, 0:3 * TS].rearrange("p (j t) -> p j t", j=3), func=AF.Copy), reads=[pbk], writes=[uTk])
                        yield
                    yield
                    for ch in range(6):
                        C.op("dve", lambda e: e.tensor_scalar(out=cT[:, ch, :TS], in0=uT[:, ch, 0:TS],
                                                              scalar1=cw[:, ch, 0:1], scalar2=None, op0=ALU.mult),
                             reads=[uTk, pk_], writes=[cTk])
                        for j in range(1, 4):
                            C.op("dve", lambda e: e.scalar_tensor_tensor(
                                out=cT[:, ch, :TS], in0=uT[:, ch, j:j + TS], scalar=cw[:, ch, j:j + 1],
                                in1=cT[:, ch, :TS], op0=ALU.mult, op1=ALU.add), reads=[uTk, pk_, cTk], writes=[cTk])
                    yield
                    if t == L // TS - 1:
                        for (c0_, c1_) in ((0, 4), (4, 6)):
                            pm, pmk = bank("mi")
                            for ch in range(c0_, c1_):
                                C.op("pe", lambda e: e.transpose(out=pm[:3, (ch - c0_) * 128:(ch - c0_ + 1) * 128],
                                                                 in_=uT[:, ch, TS:TS + 3], identity=identf[:, :]),
                                     reads=[uTk, constk], writes=[pmk])
                            C.op("act", lambda e: e.activation(out=pcst[:, c0_ * 128:c1_ * 128], in_=pm[:3, 0:(c1_ - c0_) * 128],
                                                               func=AF.Copy), reads=[pmk], writes=[pcstk])
                        cdst = pcv[l] if isP else scvo[l, sb_]
                        C.dma("pool", cdst, pcst[:, :], pcstk, reads=[pcstk])
                    C.op("pool", lambda e: e.tensor_copy(out=uT[:, :, 0:3], in_=uT[:, :, TS:TS + 3]),
                         reads=[uTk], writes=[uTk])
                    yield
                    C.op("act", lambda e: e.activation(out=cE[:, :, :TS], in_=cT[:, :, :TS], func=AF.Exp, scale=-1.0),
                         reads=[cTk], writes=[cEk])
                    C.op("dve", lambda e: e.tensor_scalar(out=cE[:, :, :TS], in0=cE[:, :, :TS], scalar1=1.0, scalar2=None,
                                                          op0=ALU.add), reads=[cEk], writes=[cEk])
                    C.op("dve", lambda e: e.reciprocal(out=cE[:, :, :TS], in_=cE[:, :, :TS]), reads=[cEk], writes=[cEk])
                    C.op("dve", lambda e: e.tensor_tensor(out=cT[:, :, :TS], in0=cT[:, :, :TS], in1=cE[:, :, :TS], op=ALU.mult),
                         reads=[cEk, cTk], writes=[cTk])
                    yield
                    C.op("dve", lambda e: e.tensor_tensor(out=cE[:, 0:4, :TS], in0=cT[:, 0:4, :TS], in1=cT[:, 0:4, :TS],
                                                           op=ALU.mult), reads=[cTk, cEk], writes=[cEk])
                    pm, pmk = bank("mi")
                    for j in range(4):
                        C.op("pe", lambda e: e.matmul(pm[:, j * TS:(j + 1) * TS], lhsT=bonesf[:, :], rhs=cE[:, j, :TS],
                                                      start=True, stop=True), reads=[cEk, constk], writes=[pmk])
                    C.op("act", lambda e: e.activation(out=cE[:, 0:4, :TS],
                                                       in_=pm[:, 0:4 * TS].rearrange("p (j t) -> p j t", j=4),
                                                       func=AF.Ln, bias=epsb[:, 1:2]),
                         reads=[pmk, constk], writes=[cEk])
                    C.op("act", lambda e: e.activation(out=cE[:, 0:4, :TS], in_=cE[:, 0:4, :TS], func=AF.Exp, scale=-0.5),
                         reads=[cEk], writes=[cEk])
                    yield
                    C.op("dve", lambda e: e.scalar_tensor_tensor(out=cT[:, 0:2, :TS], in0=cT[:, 0:2, :TS], scalar=0.125,
                                                                 in1=cE[:, 0:2, :TS], op0=ALU.mult, op1=ALU.mult),
                         reads=[cTk, cEk], writes=[cTk])
                    C.op("dve", lambda e: e.tensor_tensor(out=cT[:, 2:4, :TS], in0=cT[:, 2:4, :TS], in1=cE[:, 2:4, :TS],
                                                          op=ALU.mult), reads=[cTk, cEk], writes=[cTk])
                    for s_ in range(2):
                        C.op("dve", lambda e: e.tensor_copy(out=nTh[0:64, s_:8:2, :TS], in_=cT[s_ * 64:(s_ + 1) * 64, 0:4, :TS]),
                             reads=[cTk], writes=[nTk])
                    yield
                    for ck_ in range(CPT):
                        pm, pmk = bank("mi")
                        for j in range(4):
                            C.op("pe", lambda e: e.transpose(
                                out=pm[:c, j * 128:(j + 1) * 128], in_=cT[:, 2 + j, ck_ * c:(ck_ + 1) * c],
                                identity=identf[:, :]), reads=[cTk, constk], writes=[pmk])
                        C.op("act", lambda e: e.activation(out=ktok[:c, ck_, :], in_=pm[:c, :], func=AF.Copy),
                             reads=[pmk], writes=[ktokk])
                    yield

                def gen_chunks(col0):
                    po, pok = bank("pj")
                    po2, po2k = bank("pj")
                    for ck_ in range(CPT):
                        cs = slice(ck_ * c, (ck_ + 1) * c)
                        pm, pmk = bank("mi")
                        C.op("pe", lambda e: e.matmul(pm[:c, 0:4], lhsT=triu[:c, :c], rhs=gb[:c, ck_, 0:4],
                                                      start=True, stop=True), reads=[gbk, constk], writes=[pmk])
                        C.op("dve", lambda e: e.tensor_tensor(out=GT[:c, :, :c],
                                                              in0=bc(triu[:c, :c].unsqueeze(1), [c, 4, c]),
                                                              in1=bc(gb[:c, ck_, 0:4].unsqueeze(2), [c, 4, c]),
                                                              op=ALU.mult), reads=[gbk, constk], writes=[GTk])
                        pm2, pm2k = bank("mi")
                        for h in range(4):
                            C.op("pe", lambda e: e.matmul(pm2[:, h * c:(h + 1) * c], lhsT=onesf[:c, :],
                                                          rhs=GT[:c, h, :c], start=True, stop=True),
                                 reads=[GTk, constk], writes=[pm2k])
                        C.op("act", lambda e: e.activation(out=sm[:c, 0:4], in_=pm[:c, 0:4], func=AF.Copy), reads=[pmk], writes=[smk])
                        C.op("act", lambda e: e.activation(out=sm[:c, 4:8], in_=pm[:c, 0:4], func=AF.Exp), reads=[pmk], writes=[smk])
                        C.op("act", lambda e: e.activation(out=Drow[:, :, :c], in_=pm2[:, 0:4 * c].rearrange("p (h i) -> p h i", h=4),
                                                           func=AF.Copy), reads=[pm2k], writes=[Drk])
                        C.op("act", lambda e: e.activation(out=EDrow[:, :, :c],
                                                           in_=pm2[:, 0:4 * c].rearrange("p (h i) -> p h i", h=4), func=AF.Exp),
                             reads=[pm2k], writes=[Drk])
                        C.op("dve", lambda e: e.tensor_tensor(out=sm[:c, 12:16], in0=Drow[:c, :, c - 1], in1=sm[:c, 0:4],
                                                              op=ALU.subtract), reads=[Drk, smk], writes=[smk])
                        C.op("act", lambda e: e.activation(out=sm[:c, 8:12], in_=sm[:c, 12:16], func=AF.Exp), reads=[smk], writes=[smk])
                        yield
                        C.op("dve", lambda e: e.tensor_tensor(out=LM[:c, :, :c], in0=Drow[:c, :, :c],
                                                              in1=bc(sm[:c, 0:4].unsqueeze(2), [c, 4, c]), op=ALU.subtract),
                             reads=[Drk, smk], writes=[LMk])
                        C.op("dve", lambda e: e.tensor_scalar(out=LM[:c, :, :c], in0=LM[:c, :, :c], scalar1=0.0, scalar2=None,
                                                              op0=ALU.min), reads=[LMk], writes=[LMk])
                        C.op("act", lambda e: e.activation(out=LM[:c, :, :c], in_=LM[:c, :, :c], func=AF.Exp), reads=[LMk], writes=[LMk])
                        C.op("dve", lambda e: e.tensor_tensor(out=LMs[:c, :, :c], in0=LM[:c, :, :c],
                                                               in1=bc(striu[:c, :c].unsqueeze(1), [c, 4, c]), op=ALU.mult),
                             reads=[LMk, constk], writes=[LMk])
                        C.op("dve", lambda e: e.tensor_tensor(out=LMi[:c, :, :c], in0=LM[:c, :, :c],
                                                               in1=bc(triu[:c, :c].unsqueeze(1), [c, 4, c]), op=ALU.mult),
                             reads=[LMk, constk], writes=[LMk])
                        C.op("dve", lambda e: e.tensor_tensor(out=qdT[0:64, :, cs], in0=nTh[0:64, 0:4, cs],
                                                              in1=EDrow[0:64, :, :c], op=ALU.mult), reads=[nTk, Drk], writes=[qdk])
                        C.op("dve", lambda e: e.tensor_tensor(out=kz[:c, ck_, :].rearrange("p (h d) -> p h d", h=4),
                                                              in0=ktok[:c, ck_, 0:256].rearrange("p (h d) -> p h d", h=4),
                                                              in1=bc(sm[:c, 8:12].unsqueeze(2), [c, 4, 64]), op=ALU.mult),
                             reads=[ktokk, smk], writes=[kzk])
                        yield
                        pg, pgk = bank("mi")
                        for h in range(4):
                            C.op("pe", lambda e: e.matmul(pg[:c, h * c:(h + 1) * c], lhsT=nTh[0:64, 4 + h, cs], rhs=nTh[0:64, 4 + h, cs],
                                                          start=True, stop=True), reads=[nTk], writes=[pgk])
                        for h in range(4):
                            C.op("pe", lambda e: e.matmul(pg[:c, 256 + h * c:256 + (h + 1) * c], lhsT=nTh[0:64, 4 + h, cs],
                                                          rhs=nTh[0:64, h, cs], start=True, stop=True), reads=[nTk], writes=[pgk])
                        KKv = pg[:c, 0:4 * c].rearrange("p (h i) -> p h i", h=4)
                        KQv = pg[:c, 256:256 + 4 * c].rearrange("p (h i) -> p h i", h=4)
                        C.op("dve", lambda e: e.tensor_tensor(out=Xf[:c, :, :c], in0=KKv,
                                                              in1=bc(gb[:c, ck_, 4:8].unsqueeze(2), [c, 4, c]), op=ALU.mult),
                             reads=[pgk, gbk], writes=[Xk])
                        C.op("dve", lambda e: e.tensor_tensor(out=Xf[:c, :, :c], in0=Xf[:c, :, :c], in1=LMs[:c, :, :c], op=ALU.mult),
                             reads=[Xk, LMk], writes=[Xk])
                        C.op("dve", lambda e: e.tensor_tensor(out=scr[:c, :, :c], in0=KQv, in1=LMi[:c, :, :c], op=ALU.mult),
                             reads=[pgk, LMk], writes=[scrk])
                        C.op("act", lambda e: e.activation(func=AF.Copy, out=inT[:c, :, :c], in_=scr[:c, :, :c]), reads=[scrk], writes=[inTk])
                        yield
                        pm, pmk = bank("mi")
                        for h in range(4):
                            C.op("pe", lambda e: e.transpose(out=pm[:c, h * c:(h + 1) * c], in_=Xf[:c, h, :c],
                                                             identity=identf[:c, :c]), reads=[Xk, constk], writes=[pmk])
                        C.op("act", lambda e: e.activation(out=YTb[:c, 0, :, :c], in_=pm[:c, 0:4 * c].rearrange("p (h i) -> p h i", h=4),
                                                           func=AF.Copy), reads=[pmk], writes=[YTk[0]])
                        C.op("act", lambda e: e.activation(out=Yb[:c, 0, :, :c], in_=Xf[:c, :, :c], func=AF.Copy), reads=[Xk], writes=[Yk[0]])
                        C.op("dve", lambda e: e.tensor_tensor(out=Pf[:c, :, :c], in0=bc(identf[:c, :c].unsqueeze(1), [c, 4, c]),
                                                              in1=Xf[:c, :, :c], op=ALU.subtract), reads=[Xk, constk], writes=[Pk])
                        C.op("dve", lambda e: e.tensor_copy(out=Pb[:c, :, :c], in_=Pf[:c, :, :c]), reads=[Pk], writes=[Pk])
                        yield

                        def emit_prod(k):
                            pmq, pmqk = bank("mi")
                            for h in range(4):
                                C.op("pe", lambda e: e.matmul(pmq[:c, h * c:(h + 1) * c], lhsT=YTb[:c, k, h, :c],
                                                              rhs=Pb[:c, h, :c], start=True, stop=True),
                                     reads=[YTk[k], Pk], writes=[pmqk])
                            C.op("dve", lambda e: e.tensor_tensor(out=Pf[:c, :, :c], in0=Pf[:c, :, :c],
                                                                  in1=pmq[:c, 0:4 * c].rearrange("p (h i) -> p h i", h=4), op=ALU.add),
                                 reads=[pmqk, Pk], writes=[Pk])
                            C.op("dve", lambda e: e.tensor_copy(out=Pb[:c, :, :c], in_=Pf[:c, :, :c]), reads=[Pk], writes=[Pk])
                        for k in range(1, nit + 1):
                            ys, yd = (k - 1) % 2, k % 2
                            pm, pmk = bank("mi")
                            if k < nit:
                                for h in range(4):
                                    C.op("pe", lambda e: e.matmul(pm[:c, h * c:(h + 1) * c], lhsT=YTb[:c, k - 1, h, :c],
                                                                  rhs=Yb[:c, ys, h, :c], start=True, stop=True),
                                         reads=[Yk[ys], YTk[k - 1]], writes=[pmk])
                            for h in range(4):
                                C.op("pe", lambda e: e.matmul(pm[:c, 256 + h * c:256 + (h + 1) * c],
                                                              lhsT=Yb[:c, ys, h, :c], rhs=YTb[:c, k - 1, h, :c],
                                                              start=True, stop=True), reads=[Yk[ys], YTk[k - 1]], writes=[pmk])
                            if k < nit:
                                C.op("act", lambda e: e.activation(out=Yb[:c, yd, :, :c],
                                                                   in_=pm[:c, 0:4 * c].rearrange("p (h i) -> p h i", h=4), func=AF.Copy),
                                     reads=[pmk], writes=[Yk[yd]])
                            C.op("act", lambda e: e.activation(out=YTb[:c, k, :, :c],
                                                               in_=pm[:c, 256:256 + 4 * c].rearrange("p (h i) -> p h i", h=4), func=AF.Copy),
                                 reads=[pmk], writes=[YTk[k]])
                            yield
                            if k >= 2:
                                emit_prod(k - 1)
                                yield
                        emit_prod(nit)
                        yield
                        pc, pck = bank("mi")
                        for h in range(4):
                            C.op("pe", lambda e: e.matmul(pc[:c, h * 64:(h + 1) * 64], lhsT=nTh[0:64, 4 + h, cs],
                                                          rhs=Sdb[:, h, :], start=True, stop=True),
                                 reads=[nTk, Sdk], writes=[pck])
                        C.op("dve", lambda e: e.tensor_tensor(out=rt[:c, :].rearrange("p (h d) -> p h d", h=4),
                                                              in0=pc[:c, 0:256].rearrange("p (h d) -> p h d", h=4),
                                                              in1=bc(sm[:c, 4:8].unsqueeze(2), [c, 4, 64]), op=ALU.mult),
                             reads=[pck, smk], writes=[rk])
                        C.op("dve", lambda e: e.tensor_tensor(out=rb[:c, :], in0=ktok[:c, ck_, 256:512], in1=rt[:c, :], op=ALU.subtract),
                             reads=[rk, ktokk], writes=[rk])
                        yield
                        pc2, pc2k = bank("mi")
                        for h in range(4):
                            C.op("pe", lambda e: e.matmul(pc2[:c, h * 64:(h + 1) * 64], lhsT=Pb[:c, h, :c],
                                                          rhs=rb[:c, h * 64:(h + 1) * 64], start=True, stop=True),
                                 reads=[Pk, rk], writes=[pc2k])
                        C.op("dve", lambda e: e.tensor_tensor(out=rt[:c, :].rearrange("p (h d) -> p h d", h=4),
                                                              in0=pc2[:c, 0:256].rearrange("p (h d) -> p h d", h=4),
                                                              in1=bc(gb[:c, ck_, 4:8].unsqueeze(2), [c, 4, 64]), op=ALU.mult),
                             reads=[pc2k, gbk], writes=[rk])
                        C.op("dve", lambda e: e.tensor_copy(out=vnb[:c, :], in_=rt[:c, :]), reads=[rk], writes=[vnk])
                        yield
                        for h in range(4):
                            oc_ = slice(h * TS + ck_ * c, h * TS + (ck_ + 1) * c)
                            C.op("pe", lambda e: e.matmul(po[:64, oc_], lhsT=Sdb[:, h, :], rhs=qdT[0:64, h, cs],
                                                          start=True, stop=False), reads=[Sdk, qdk], writes=[pok])
                            C.op("pe", lambda e: e.matmul(po[:64, oc_], lhsT=vnb[:c, h * 64:(h + 1) * 64], rhs=inT[:c, h, :c],
                                                          start=False, stop=True), reads=[vnk, inTk], writes=[pok])
                        psu, psuk = bank("mi")
                        for h in range(4):
                            C.op("pe", lambda e: e.matmul(psu[:64, h * 64:(h + 1) * 64], lhsT=kz[:c, ck_, h * 64:(h + 1) * 64],
                                                          rhs=vnb[:c, h * 64:(h + 1) * 64], start=True, stop=True),
                                 reads=[kzk, vnk], writes=[psuk])
                        for h in range(4):
                            C.op("dve", lambda e: e.scalar_tensor_tensor(
                                out=Sd[:, h, :], in0=Sd[:, h, :], scalar=EDrow[0:64, h, c - 1:c], in1=psu[:64, h * 64:(h + 1) * 64],
                                op0=ALU.mult, op1=ALU.add), reads=[Drk, psuk, Sdk], writes=[Sdk])
                        C.op("act", lambda e: e.activation(func=AF.Copy, out=Sdb[:], in_=Sd[:]), reads=[Sdk], writes=[Sdk])
                        yield
                        pr, prk = bank("mi")
                        for h in range(4):
                            C.op("pe", lambda e: e.matmul(pr[:c, h * c:(h + 1) * c], lhsT=qkcT[0:64, 4 + h, cs], rhs=qkcT[0:64, h, cs],
                                                          start=True, stop=True), reads=[qkcTk], writes=[prk])
                        C.op("dve", lambda e: e.tensor_tensor(out=scr[:c, :, :c], in0=pr[:c, 0:4 * c].rearrange("p (h i) -> p h i", h=4),
                                                              in1=rmc[:c, :].rearrange("p (h i) -> p h i", h=4)[:, :, :c], op=ALU.mult),
                             reads=[prk, constk], writes=[scrk])
                        C.op("act", lambda e: e.activation(func=AF.Copy, out=inR[:c, :, :c], in_=scr[:c, :, :c]), reads=[scrk], writes=[inRk])
                        yield
                        for h in range(4):
                            oc_ = slice(h * TS + ck_ * c, h * TS + (ck_ + 1) * c)
                            C.op("pe", lambda e: e.matmul(po2[:64, oc_], lhsT=Srb[:, h, :], rhs=qkcT[0:64, h, cs],
                                                          start=True, stop=False), reads=[Srk, qkcTk], writes=[po2k])
                            C.op("pe", lambda e: e.matmul(po2[:64, oc_], lhsT=vcb[:c, ck_, h * 64:(h + 1) * 64], rhs=inR[:c, h, :c],
                                                          start=False, stop=True), reads=[vck, inRk], writes=[po2k])
                        psu, psuk = bank("mi")
                        for h in range(4):
                            C.op("pe", lambda e: e.matmul(psu[:64, h * 64:(h + 1) * 64], lhsT=kzc[:c, ck_, h * 64:(h + 1) * 64],
                                                          rhs=vcb[:c, ck_, h * 64:(h + 1) * 64], start=True, stop=True),
                                 reads=[vck], writes=[psuk])
                        for h in range(4):
                            C.op("dve", lambda e: e.scalar_tensor_tensor(out=Sr[:, h, :], in0=Sr[:, h, :], scalar=float(gch[h]),
                                                                         in1=psu[:64, h * 64:(h + 1) * 64], op0=ALU.mult, op1=ALU.add),
                                 reads=[psuk, Srk], writes=[Srk])
                        C.op("act", lambda e: e.activation(func=AF.Copy, out=Srb[:], in_=Sr[:]), reads=[Srk], writes=[Srk])

                    yield
                    for which in range(2):
                        pso, psok = (po, pok) if which == 0 else (po2, po2k)
                        pov = pso[:64, 0:4 * TS].rearrange("p (h t) -> p h t", h=4)
                        if which == 0:
                            C.op("act", lambda e: e.activation(out=ot[:, :, :TS], in_=pov, func=AF.Copy), reads=[psok], writes=[otk])
                        else:
                            C.op("dve", lambda e: e.tensor_tensor(out=ot[:, :, :TS], in0=pov,
                                                                  in1=xit[:, :].rearrange("p (h t) -> p h t", h=4)[:, :, :TS], op=ALU.mult),
                                 reads=[psok, constk], writes=[otk])
                        C.op("dve", lambda e: e.tensor_tensor(out=otb[:, :, :TS], in0=ot[:, :, :TS], in1=ot[:, :, :TS], op=ALU.mult),
                             reads=[otk], writes=[otk])
                        yield
                        pm, pmk = bank("mi")
                        for h in range(4):
                            C.op("pe", lambda e: e.matmul(pm[:64, h * TS:(h + 1) * TS], lhsT=onesb[:64, :64], rhs=otb[:, h, :TS],
                                                          start=True, stop=True), reads=[otk, constk], writes=[pmk])
                        C.op("act", lambda e: e.activation(out=ot2[:, :, :TS], in_=pm[:64, 0:4 * TS].rearrange("p (h t) -> p h t", h=4),
                                                           func=AF.Ln, scale=1.0 / 64, bias=epsb[:64, 1:2]),
                             reads=[pmk, constk], writes=[otk])
                        C.op("act", lambda e: e.activation(out=ot2[:, :, :TS], in_=ot2[:, :, :TS], func=AF.Exp, scale=-0.5),
                             reads=[otk], writes=[otk])
                        yield
                        C.op("dve", lambda e: e.tensor_tensor(out=ot[:, :, :TS], in0=ot[:, :, :TS], in1=ot2[:, :, :TS], op=ALU.mult),
                             reads=[otk], writes=[otk])
                        C.op("dve", lambda e: e.tensor_tensor(out=ot[:, :, :TS], in0=ot[:, :, :TS],
                                                               in1=sgT[:, which * 4:which * 4 + 4, :TS], op=ALU.mult),
                             reads=[otk, sgTk], writes=[otk])
                        for s_ in range(2):
                            dst = mixT[s_ * 64:(s_ + 1) * 64, 4 + which * 2:6 + which * 2, col0:col0 + TS]
                            if which == 0:
                                C.op("dve", lambda e: e.tensor_scalar(out=dst, in0=ot[:, s_:4:2, :TS], scalar1=dlngc[:, 0:1],
                                                                      scalar2=None, op0=ALU.mult),
                                     reads=[otk, pk_], writes=[mixk])
                            else:
                                C.op("dve", lambda e: e.tensor_copy(out=dst, in_=ot[:, s_:4:2, :TS]), reads=[otk], writes=[mixk])
                    yield

                def back_tile(ti, t):
                    r0 = t * TS
                    col0 = ti * TS
                    C.dma("sp", xres[:TS, :], xcs[row0 + r0:row0 + r0 + TS, :], xk,
                          reads=[dtk("xcs", row0 + r0)], writes=[xk])
                    for half in range(2):
                        pb, pbk = bank("pj")
                        for k in range(8):
                            C.op("pe", lambda e: e.matmul(pb[:TS, :], lhsT=(mixA if k < 4 else mixT)[:, k, col0:col0 + TS],
                                                          rhs=WA_out[:, k, half * 512:(half + 1) * 512], start=(k == 0), stop=(k == 7)),
                                 reads=[mixk, mixAk, WAK2], writes=[pbk])
                        C.op("dve", lambda e: e.scalar_tensor_tensor(out=xres[:TS, half * 512:(half + 1) * 512],
                                                                     in0=xres[:TS, half * 512:(half + 1) * 512], scalar=float(ALPHA),
                                                                     in1=pb[:TS, :], op0=ALU.mult, op1=ALU.add),
                             reads=[pbk, xk], writes=[xk])
                    layer_norm(xres[:TS, :], TS, g1, b1, pk_, xk)
                    C.dma("pool", x1s[row0 + r0:row0 + r0 + TS, :], xres[:TS, :], xk, reads=[xk], writes=[dtk("x1s", row0 + r0)])


                def gen_attention(blk):
                    if isP:
                        kts = [(kt, 128, None) for kt in range(2 * blk)] + [(2 * blk, 128, 0), (2 * blk + 1, 128, 1)]
                    else:
                        kts = [(kt, 128, None) for kt in range(past // 128)] + [(past // 128, TS, None)]
                    its = [(hd, kt, rows, msk) for hd in range(4) for (kt, rows, msk) in kts]
                    nk = len(kts)

                    def load_v(ii):
                        hd, kt, rows, msk = its[ii]
                        vt, vtk = Vt[ii % 3], Vtk[ii % 3]
                        if kt < past // 128:
                            C.dma("pool", vt[:rows, :], cv[l, sb_, kt * 128:kt * 128 + rows, hd * 128:(hd + 1) * 128], vtk, writes=[vtk])
                        else:
                            C.dma("sp", vt[:rows, :], vsc[kt * 128:kt * 128 + rows, hd * 128:(hd + 1) * 128], vtk,
                                  reads=[dtk("vsc", kt)], writes=[vtk])

                    def emit_s(ii):
                        hd, kt, rows, msk = its[ii]
                        sbk_, sbkk = PS[2 + ii % 2], PSK[2 + ii % 2]
                        kc0 = kt * 128
                        C.op("pe", lambda e: e.matmul(sbk_[:rows, 0:nq], lhsT=KT[:, hd, kc0:kc0 + rows], rhs=QTa[:, hd, 0:nq],
                                                      start=True, stop=True), reads=[ktk[kt], QTk], writes=[sbkk])
                        C.op("pe", lambda e: e.matmul(sbk_[:rows, 256:256 + nq], lhsT=KT[:, hd, kc0:kc0 + rows],
                                                      rhs=QTb[:, hd, 0:nq], start=True, stop=True),
                             reads=[ktk[kt], QTk], writes=[sbkk])
                    load_v(0)
                    if len(its) > 1:
                        load_v(1)
                    emit_s(0)
                    for ii, (hd, kt, rows, msk) in enumerate(its):
                        if ii + 2 < len(its):
                            load_v(ii + 2)
                        if ii + 1 < len(its):
                            emit_s(ii + 1)
                        first = (ii % nk == 0)
                        last = (ii % nk == nk - 1)
                        sbk_, sbkk = PS[2 + ii % 2], PSK[2 + ii % 2]
                        pt, ptk = PT[ii % 2], PTk[ii % 2]
                        C.op("act", lambda e: e.activation(out=pt[:rows, :, 0:nq],
                                                           in_=sbk_[:rows, :].rearrange("p (j q) -> p j q", j=2)[:, :, 0:nq],
                                                           func=AF.Exp, scale=0.125), reads=[sbkk], writes=[ptk])
                        if msk is not None:
                            C.op("dve", lambda e: e.tensor_tensor(out=pt[:rows, :, 0:nq], in0=pt[:rows, :, 0:nq],
                                                                   in1=bc(amask[:rows, msk * 256:msk * 256 + nq].unsqueeze(1), [rows, 2, nq]),
                                                                   op=ALU.mult), reads=[ptk, constk], writes=[ptk])
                        if first:
                            C.op("dve", lambda e: e.memset(OB[:, :], 0.0), writes=[OBK])
                            C.op("dve", lambda e: e.memset(LB[:, :], 0.0), writes=[LBK])
                        vt, vtk = Vt[ii % 3], Vtk[ii % 3]
                        for j in range(2):
                            C.op("pe", lambda e: e.matmul(OB[:, j * 256:j * 256 + nq], lhsT=vt[:rows, :],
                                                          rhs=pt[:rows, j, 0:nq], start=False, stop=False, skip_group_check=True),
                                 reads=[vtk, ptk], writes=[OBK])
                            C.op("pe", lambda e: e.matmul(LB[:, j * 256:j * 256 + nq], lhsT=onesb[:rows, :], rhs=pt[:rows, j, 0:nq],
                                                          start=False, stop=False, skip_group_check=True),
                                 reads=[constk, ptk], writes=[LBK])
                        if last:
                            Lv = LB[:, :].rearrange("p (j q) -> p j q", j=2)[:, :, 0:nq]
                            Ov = OB[:, :].rearrange("p (j q) -> p j q", j=2)[:, :, 0:nq]
                            C.op("dve", lambda e: e.reciprocal(out=Rr[:, :, 0:nq], in_=Lv), reads=[LBK], writes=[atk])
                            C.op("dve", lambda e: e.tensor_tensor(out=Tt[:, :, 0:nq], in0=Ov, in1=Rr[:, :, 0:nq], op=ALU.mult),
                                 reads=[OBK, atk], writes=[atk])
                            C.op("dve", lambda e: e.scalar_tensor_tensor(out=oaf[:, 0:nq], in0=Tt[:, 1, 0:nq], scalar=neglam[:, 0:1],
                                                                         in1=Tt[:, 0, 0:nq], op0=ALU.mult, op1=ALU.add),
                                 reads=[atk, pk_], writes=[atk])
                            C.op("dve", lambda e: e.tensor_tensor(out=oab[:, 0:nq], in0=oaf[:, 0:nq], in1=oaf[:, 0:nq], op=ALU.mult),
                                 reads=[atk], writes=[atk])
                            yield
                            pm, pmk = bank("mi")
                            C.op("pe", lambda e: e.matmul(pm[:, 0:nq], lhsT=onesb[:, :], rhs=oab[:, 0:nq], start=True, stop=True),
                                 reads=[atk, constk], writes=[pmk])
                            C.op("act", lambda e: e.activation(out=Rr[:, 0, 0:nq], in_=pm[:, 0:nq], func=AF.Ln, scale=1.0 / 128,
                                                               bias=epsb[:, 1:2]), reads=[pmk, constk], writes=[atk])
                            C.op("act", lambda e: e.activation(out=Rr[:, 0, 0:nq], in_=Rr[:, 0, 0:nq], func=AF.Exp, scale=-0.5),
                                 reads=[atk], writes=[atk])
                            C.op("dve", lambda e: e.tensor_tensor(out=oaf[:, 0:nq], in0=oaf[:, 0:nq], in1=Rr[:, 0, 0:nq], op=ALU.mult),
                                 reads=[atk], writes=[atk])
                            C.op("dve", lambda e: e.tensor_scalar(out=mixA[:, hd, 0:nq], in0=oaf[:, 0:nq], scalar1=dngc[:, 0:1],
                                                                  scalar2=float(1.0 - lam_init), op0=ALU.mult, op1=ALU.mult),
                                 reads=[atk, pk_], writes=[mixAk])
                        yield

                def run_gens(*gens, weights=None):
                    live = list(gens)
                    wts = {id(g): (weights[i] if weights else 1) for i, g in enumerate(gens)}
                    while live:
                        for g in list(live):
                            for _ in range(wts[id(g)]):
                                try:
                                    next(g)
                                except StopIteration:
                                    live.remove(g)
                                    break

                def chain(*gs):
                    for g_ in gs:
                        yield from g_

                for blk in range(nblk):
                    t0_ = blk * TPB
                    run_gens(gen_frontA(0, t0_, "pj"), gen_frontB(t0_, 0))
                    if TPB == 2:
                        run_gens(gen_chunks(0), gen_frontA(1, t0_ + 1, "s"))
                        n_att = 4 * (2 * blk + 2)
                        run_gens(gen_attention(blk), chain(gen_frontB(t0_ + 1, TS), gen_chunks(TS)),
                                 weights=[2 if n_att > 60 else 1, 1])
                    else:
                        run_gens(gen_attention(blk), gen_chunks(0))
                    for ti in range(TPB):
                        back_tile(ti, t0_ + ti)

                if STOP == 1:
                    C.finish()
                    return nc
                ddst = pdl[l] if isP else sdlo[l, sb_]
                rdst = prt[l] if isP else srto[l, sb_]
                C.dma("pool", ddst.rearrange("(h d) e -> d h e", d=64), Sd[:], Sdk, reads=[Sdk])
                C.dma("pool", rdst.rearrange("(h d) e -> d h e", d=64), Sr[:], Srk, reads=[Srk])

        if STOP == 20 + l:
            C.finish()
            return nc
        C.barrier()
        WA_up = WA[:, 0:8 * DFF].rearrange("p (k n) -> p k n", k=8)
        WA_dn = WA[:, 8 * DFF:8 * DFF + 32 * D].rearrange("p (k n) -> p k n", k=32)
        WAK2 = Tk("WA2f")
        for k in range(8):
            for c0_ in range(0, DFF, 1024):
                C.dma("pool", WA_up[:, k, c0_:c0_ + 1024], w_up[l, k * 128:(k + 1) * 128, c0_:c0_ + 1024], WAK, writes=[WAK])
        for k in range(32):
            C.dma("pool", WA_dn[:, k, :], w_down[l, k * 128:(k + 1) * 128, :], WAK2, writes=[WAK2])
        with ExitStack() as st:
            def T(name, shape, dt=F32):
                return st.enter_context(nc.sbuf_tensor("%s_f%d" % (name, l), list(shape), dt))
            g2 = T("g2", [128, D]); b2 = T("b2", [128, D]); pk2 = Tk()
            C.dma("sp", g2[:], ln2g[l:l + 1, :].partition_broadcast(128), pk2, writes=[pk2])
            C.dma("sp", b2[:], ln2b[l:l + 1, :].partition_broadcast(128), pk2, writes=[pk2])
            x1 = T("x1", [128, 2, D]); x1k = [Tk(), Tk()]
            x1b = T("x1b", [128, D], BF16); x1bk = Tk()
            x1T = T("x1T", [128, 8, 256], BF16); x1Tk = Tk()
            hidT = T("hidT", [128, 32, 256], BF16); hidk = Tk()
            rl = [T("rl0", [128, 256]), T("rl1", [128, 256])]; rlk = [Tk(), Tk()]
            blocks = []
            for b0_ in range(0, Lp, 256):
                blocks.append((b0_, 128, min(2, (Lp - b0_) // 128)))
            for b_ in range(NS):
                blocks.append((Lp + b_ * Ls, Ls, 1))
            for (rb0, TS, ntl) in blocks:
                nb = TS * ntl
                for ti in range(ntl):
                    r0 = rb0 + ti * TS
                    C.dma("sp", x1[:TS, ti, :], x1s[r0:r0 + TS, :], x1k[ti], reads=[dtk("x1s", r0)], writes=[x1k[ti]])
                    C.op("act", lambda e: e.activation(out=x1b[:TS, :], in_=x1[:TS, ti, :], func=AF.Copy), reads=[x1k[ti]], writes=[x1bk])
                    pb, pbk = bank("mi")
                    pbb = pb[:].bitcast(BF16)
                    for k in range(8):
                        C.op("pe", lambda e: e.transpose(out=pbb[:, k * TS:(k + 1) * TS], in_=x1b[:TS, k * 128:(k + 1) * 128],
                                                         identity=identb[:TS, :TS]), reads=[x1bk, constk], writes=[pbk])
                    C.op("dve", lambda e: e.tensor_copy(out=x1T[:, :, ti * TS:(ti + 1) * TS],
                                                        in_=pbb[:, 0:8 * TS].rearrange("p (k t) -> p k t", k=8)),
                         reads=[pbk], writes=[x1Tk])
                for f in range(32):
                    pb, pbk = bank("pj")
                    for k in range(8):
                        C.op("pe", lambda e: e.matmul(pb[:, 0:nb], lhsT=WA_up[:, k, f * 128:(f + 1) * 128], rhs=x1T[:, k, 0:nb],
                                                      start=(k == 0), stop=(k == 7)), reads=[x1Tk, WAK], writes=[pbk])
                    r_, rk_ = rl[f % 2], rlk[f % 2]
                    C.op("act", lambda e: e.activation(out=r_[:, 0:nb], in_=pb[:, 0:nb], func=AF.Relu), reads=[pbk], writes=[rk_])
                    eng = "dve"
                    C.op(eng, lambda e: e.tensor_tensor(out=hidT[:, f, 0:nb], in0=r_[:, 0:nb], in1=r_[:, 0:nb], op=ALU.mult),
                         reads=[rk_], writes=[hidk])
                for ti in range(ntl):
                    r0 = rb0 + ti * TS
                    for half in range(2):
                        pb, pbk = bank("pj")
                        for f in range(32):
                            C.op("pe", lambda e: e.matmul(pb[:TS, :], lhsT=hidT[:, f, ti * TS:(ti + 1) * TS],
                                                          rhs=WA_dn[:, f, half * 512:(half + 1) * 512], start=(f == 0), stop=(f == 31)),
                                 reads=[hidk, WAK2], writes=[pbk])
                        C.op("dve", lambda e: e.scalar_tensor_tensor(out=x1[:TS, ti, half * 512:(half + 1) * 512],
                                                                     in0=x1[:TS, ti, half * 512:(half + 1) * 512], scalar=float(ALPHA),
                                                                     in1=pb[:TS, :], op0=ALU.mult, op1=ALU.add),
                             reads=[pbk, x1k[ti]], writes=[x1k[ti]])
                    layer_norm(x1[:TS, ti, :], TS, g2, b2, pk2, x1k[ti])
                    if l < DEPTH - 1:
                        C.dma("pool", xcs[r0:r0 + TS, :], x1[:TS, ti, :], x1k[ti], reads=[x1k[ti]], writes=[dtk("xcs", r0)])
                    else:
                        if r0 < Lp:
                            C.dma("pool", yp[r0:r0 + TS, :], x1[:TS, ti, :], x1k[ti], reads=[x1k[ti]])
                        else:
                            C.dma("pool", ys[r0 - Lp:r0 - Lp + TS, :], x1[:TS, ti, :], x1k[ti], reads=[x1k[ti]])
    C.finish()
    return nc


def make_consts(Lp, Ls, PAST):
    c = {}
    c["c_ident"] = np.eye(128, dtype=np.float32)
    half = 32
    inv_freq = (10000.0 ** (-np.arange(half, dtype=np.float32) / half)).astype(np.float32)

    def rot(pos):
        ang = pos.astype(np.float32)[:, None] * inv_freq[None, :]
        cos = np.cos(ang).astype(np.float32)
        sin = np.sin(ang).astype(np.float32)
        return np.concatenate([cos, cos, -sin, sin], axis=1).astype(np.float32)

    c["c_rotp"] = rot(np.arange(Lp))
    c["c_rots"] = rot(PAST + np.arange(Ls))
    k = np.arange(128)[:, None]
    qq = np.arange(256)[None, :]
    m0 = ((0 + k // 64) <= (qq // 64)).astype(np.float32)
    m1 = ((2 + k // 64) <= (qq // 64)).astype(np.float32)
    c["c_amask"] = np.concatenate([m0, m1], axis=1)
    j = np.arange(64)[:, None]
    i = np.arange(64)[None, :]
    c["c_triu"] = (j <= i).astype(np.float32)
    c["c_striu"] = (j < i).astype(np.float32)
    lg = np.log(1.0 - 2.0 ** (-5.0 - np.arange(4, dtype=np.float64)))
    rm = np.zeros((64, 4, 64), np.float64)
    for h in range(4):
        rm[:, h, :] = np.exp(-lg[h] * (j + 1.0)) * (j <= i)
    c["c_rm"] = rm.reshape(64, 256).astype(np.float32)
    xi = np.zeros((64, 4, 128), np.float64)
    xis = np.zeros((64, 4, Ls), np.float64)
    for h in range(4):
        xi[:, h, :] = np.exp(lg[h] * ((np.arange(128) % 64) + 1.0))[None, :]
        xis[:, h, :] = np.exp(lg[h] * (np.arange(Ls) + 1.0))[None, :]
    c["c_xi"] = xi.reshape(64, 512).astype(np.float32)
    c["c_xis"] = xis.reshape(64, 4 * Ls).astype(np.float32)
    zeta = np.zeros((128, 4), np.float64)
    zetas = np.zeros((Ls, 4), np.float64)
    for h in range(4):
        zeta[:, h] = np.exp(lg[h] * (63.0 - (np.arange(128) % 64)))
        zetas[:, h] = np.exp(lg[h] * (Ls - 1.0 - np.arange(Ls)))
    c["c_zeta"] = zeta.astype(np.float32)
    c["c_zetas"] = zetas.astype(np.float32)
    bo = np.zeros((128, 128), np.float32)
    bo[:64, :64] = 1.0
    bo[64:, 64:] = 1.0
    c["c_bones"] = bo
    return c


_NC_CACHE = {}


def run(inputs, Lp, NS, Ls, PAST, ncores):
    key = (Lp, NS, Ls, PAST)
    if key not in _NC_CACHE:
        _NC_CACHE[key] = build(Lp, NS, Ls, PAST)
    nc = _NC_CACHE[key]
    f = lambda a: np.ascontiguousarray(np.asarray(a, dtype=np.float32))
    consts = make_consts(Lp, Ls, PAST)
    I = {k: f(v) for k, v in inputs.items()}
    in_maps = []
    for i in range(ncores):
        sl = slice(i * NS, (i + 1) * NS)
        m = dict(consts)
        m["xp"] = f(I["x_prompt"][i])
        m["xs"] = f(I["x_sample"][sl].reshape(NS * Ls, D))
        m["ck"] = f(I["cache_k"][:, sl].reshape(DEPTH, NS, PAST, 512))
        m["cv"] = f(I["cache_v"][:, sl].reshape(DEPTH, NS, PAST, 512))
        m["sdl"] = f(I["state_delta"][:, sl].reshape(DEPTH, NS, 256, 64))
        m["scv"] = f(I["state_conv"][:, sl])
        m["srt"] = f(I["state_ret"][:, sl].reshape(DEPTH, NS, 256, 64))
        m["ln0g"] = f(I["ln0_g"].reshape(1, D)); m["ln0b"] = f(I["ln0_b"].reshape(1, D))
        m["w_in"] = I["w_in"]
        m["lamq1"] = I["lam_q1"]; m["lamk1"] = I["lam_k1"]; m["lamq2"] = I["lam_q2"]; m["lamk2"] = I["lam_k2"]
        m["dng"] = I["diff_norm_g"]; m["convw"] = I["conv_w"]; m["alog"] = I["a_log"]; m["dtb"] = I["dt_bias"]
        m["dlng"] = I["delta_norm_g"]; m["w_out"] = I["w_out"]
        m["ln1g"] = I["ln1_g"]; m["ln1b"] = I["ln1_b"]; m["w_up"] = I["w_up"]; m["w_down"] = I["w_down"]
        m["ln2g"] = I["ln2_g"]; m["ln2b"] = I["ln2_b"]
        in_maps.append(m)
    res = run_bass_kernel_spmd(nc, in_maps, core_ids=list(range(ncores)))
    R = res.results
    B = ncores
    st = lambda name: np.stack([np.asarray(R[i][name]) for i in range(B)])
    y_p = st("yp")
    y_s = st("ys").reshape(B * NS, Ls, D)
    p_k = st("pk").transpose(1, 0, 2, 3).reshape(DEPTH, B, Lp, 8, 64)
    p_v = st("pv").transpose(1, 0, 2, 3).reshape(DEPTH, B, Lp, 4, 128)
    p_d = st("pdl").transpose(1, 0, 2, 3).reshape(DEPTH, B, 4, 64, 64)
    p_c = st("pcv").transpose(1, 0, 2, 3)
    p_r = st("prt").transpose(1, 0, 2, 3).reshape(DEPTH, B, 4, 64, 64)
    s_k = st("sk").transpose(1, 0, 2, 3).reshape(DEPTH, B * NS, Ls, 8, 64)
    s_v = st("sv").transpose(1, 0, 2, 3).reshape(DEPTH, B * NS, Ls, 4, 128)
    s_d = st("sdlo").transpose(1, 0, 2, 3, 4).reshape(DEPTH, B * NS, 4, 64, 64)
    s_c = st("scvo").transpose(1, 0, 2, 3, 4).reshape(DEPTH, B * NS, 3, 768)
    s_r = st("srto").transpose(1, 0, 2, 3, 4).reshape(DEPTH, B * NS, 4, 64, 64)
    outs = (y_p, y_s, p_k, p_v, p_d, p_c, p_r, s_k, s_v, s_d, s_c, s_r)
    return tuple(np.ascontiguousarray(o.astype(np.float32)) for o in outs)


def kernel(**inputs):
    return run(inputs, 4096, 2, 32, 2048, NCORES)
```

```python
import math
import numpy as np
import ml_dtypes
import concourse.bass as bass
import concourse.mybir as mybir
from concourse.bass_utils import run_bass_kernel_spmd

F32 = mybir.dt.float32
BF16 = mybir.dt.bfloat16
AF = mybir.ActivationFunctionType
ALU = mybir.AluOpType
AX = mybir.AxisListType

D = 1024
NIN = 3592
DFF = 4096
DEPTH = 2
ALPHA = (2 * DEPTH) ** 0.25
LN_EPS = 1e-5
NORM_EPS = 1e-6
NCORES = 8


class Tk:
    __slots__ = ("w", "r", "dsem", "dcnt", "name")

    def __init__(self, name=""):
        self.w = None
        self.r = {}
        self.dsem = None
        self.dcnt = 0
        self.name = name


class Ctx:
    def __init__(self, nc):
        self.nc = nc
        self.E = {"pe": nc.tensor, "dve": nc.vector, "act": nc.scalar, "pool": nc.gpsimd,
                  "sp": nc.sync}
        self.sem = {k: nc.alloc_semaphore("es_" + k) for k in self.E}
        self.cnt = {k: 0 for k in self.E}
        self.seen = {k: {} for k in self.E}
        self.dsems = []
        self.nd = 0

    def _wait(self, e, dep):
        key, sem, val = dep
        if self.seen[e].get(key, 0) >= val:
            return
        self.E[e].wait_ge(sem, val)
        self.seen[e][key] = val

    def _deps(self, e, reads, writes):
        for t in reads:
            if t.w is not None:
                self._wait(e, t.w)
        for t in writes:
            if t.w is not None:
                self._wait(e, t.w)
            for d in t.r.values():
                self._wait(e, d)

    def _mark(self, me, reads, writes):
        for t in reads:
            old = t.r.get(me[0])
            if old is None or old[2] < me[2]:
                t.r[me[0]] = me
        for t in writes:
            t.w = me
            t.r = {}

    def op(self, e, fn, reads=(), writes=()):
        self._deps(e, reads, writes)
        ins = fn(self.E[e])
        self.cnt[e] += 1
        ins.then_inc(self.sem[e], 1)
        me = (e, self.sem[e], self.cnt[e])
        if e == "pe":
            self.seen[e][e] = self.cnt[e]
        self._mark(me, reads, writes)
        return ins

    def dma(self, q, out, in_, owner, reads=(), writes=(), **kw):
        self._deps(q, reads, writes)
        if owner.dsem is None:
            owner.dsem = self.nc.alloc_semaphore("ds%d" % self.nd)
            self.nd += 1
            self.dsems.append(owner)
        ins = self.E[q].dma_start(out=out, in_=in_, **kw)
        owner.dcnt += 16
        ins.then_inc(owner.dsem, 16)
        me = ("d%d" % id(owner), owner.dsem, owner.dcnt)
        self._mark(me, reads, writes)
        return ins

    def barrier(self):
        for e in self.E:
            for f in self.E:
                if f != e and self.cnt[f] > 0:
                    self._wait(e, (f, self.sem[f], self.cnt[f]))
            for o in self.dsems:
                if o.dcnt > 0:
                    self._wait(e, ("d%d" % id(o), o.dsem, o.dcnt))

    def finish(self):
        self.barrier()


def bc(ap, shape):
    return ap.to_broadcast(list(shape))


import os
from contextlib import ExitStack
STOP = int(os.environ.get('KSTOP', '99'))


def build(Lp=4096, NS=2, Ls=32, PAST=2048):
    nc = bass.Bass("TRN2", target_bir_lowering=False)
    C = Ctx(nc)
    Ltot = Lp + NS * Ls
    LK = max(Lp, PAST + Ls)

    def din(name, shape, dt=F32):
        return nc.dram_tensor(name, list(shape), dt, kind="ExternalInput").ap()

    def dout(name, shape):
        return nc.dram_tensor(name, list(shape), F32, kind="ExternalOutput").ap()

    xp = din("xp", [Lp, D])
    xs = din("xs", [NS * Ls, D])
    ck = din("ck", [DEPTH, NS, PAST, 512])
    cv = din("cv", [DEPTH, NS, PAST, 512])
    sdl = din("sdl", [DEPTH, NS, 256, 64])
    scv = din("scv", [DEPTH, NS, 3, 768])
    srt = din("srt", [DEPTH, NS, 256, 64])
    ln0g = din("ln0g", [1, D]); ln0b = din("ln0b", [1, D])
    w_in = din("w_in", [DEPTH, D, NIN])
    lamq1 = din("lamq1", [DEPTH, 64]); lamk1 = din("lamk1", [DEPTH, 64])
    lamq2 = din("lamq2", [DEPTH, 64]); lamk2 = din("lamk2", [DEPTH, 64])
    dng = din("dng", [DEPTH, 128])
    convw = din("convw", [DEPTH, 4, 768])
    alog = din("alog", [DEPTH, 4]); dtb = din("dtb", [DEPTH, 4])
    dlng = din("dlng", [DEPTH, 64])
    w_out = din("w_out", [DEPTH, D, D])
    ln1g = din("ln1g", [DEPTH, D]); ln1b = din("ln1b", [DEPTH, D])
    w_up = din("w_up", [DEPTH, D, DFF])
    w_down = din("w_down", [DEPTH, DFF, D])
    ln2g = din("ln2g", [DEPTH, D]); ln2b = din("ln2b", [DEPTH, D])
    c_ident = din("c_ident", [128, 128])
    c_rotp = din("c_rotp", [Lp, 128])
    c_rots = din("c_rots", [Ls, 128])
    c_amask = din("c_amask", [128, 512])
    c_triu = din("c_triu", [64, 64])
    c_striu = din("c_striu", [64, 64])
    c_rm = din("c_rm", [64, 256])
    c_xi = din("c_xi", [64, 512])
    c_xis = din("c_xis", [64, 4 * Ls])
    c_zeta = din("c_zeta", [128, 4])
    c_zetas = din("c_zetas", [Ls, 4])
    c_bones = din("c_bones", [128, 128])

    yp = dout("yp", [Lp, D]); ys = dout("ys", [NS * Ls, D])
    pk = dout("pk", [DEPTH, Lp, 512]); pv = dout("pv", [DEPTH, Lp, 512])
    pdl = dout("pdl", [DEPTH, 256, 64]); pcv = dout("pcv", [DEPTH, 3, 768])
    prt = dout("prt", [DEPTH, 256, 64])
    sk = dout("sk", [DEPTH, NS * Ls, 512]); sv = dout("sv", [DEPTH, NS * Ls, 512])
    sdlo = dout("sdlo", [DEPTH, NS, 256, 64]); scvo = dout("scvo", [DEPTH, NS, 3, 768])
    srto = dout("srto", [DEPTH, NS, 256, 64])
    x1s = nc.dram_tensor("x1s", [Ltot, D], F32).ap()
    xcs = nc.dram_tensor("xcs", [Ltot, D], F32).ap()
    vsc = nc.dram_tensor("vsc", [LK, 512], BF16).ap()
    dram_tk = {}

    def dtk(name, r0):
        k = (name, r0)
        if k not in dram_tk:
            dram_tk[k] = Tk()
        return dram_tk[k]

    PS = [nc.alloc_psum_tensor("ps%d" % i, [128, 512], F32) for i in range(8)]
    PSK = [Tk("ps%d" % i) for i in range(8)]
    rot_state = {"pj": 0, "mi": 0, "s": 0}

    def bank(kind):
        base, n = {"pj": (0, 2), "s": (2, 2), "mi": (6, 2)}[kind]
        i = base + rot_state[kind] % n
        rot_state[kind] += 1
        return PS[i], PSK[i]

    OB, OBK = PS[4], PSK[4]
    LB, LBK = PS[5], PSK[5]

    def sb(name, shape, dt=F32):
        return nc.alloc_sbuf_tensor(name, list(shape), dt)

    identf = sb("identf", [128, 128]); identb = sb("identb", [128, 128], BF16)
    onesb = sb("onesb", [128, 128], BF16)
    onesf = sb("onesf", [128, 128])
    bonesf = sb("bonesf", [128, 128])
    triu = sb("triu", [64, 64]); striu = sb("striu", [64, 64])
    rmc = sb("rmc", [64, 256]); xic = sb("xic", [64, 512]); xisc = sb("xisc", [64, 4 * Ls])
    zetac = sb("zetac", [128, 4]); zetasc = sb("zetasc", [Ls, 4])
    amask = sb("amask", [128, 512], BF16)
    epsb = sb("epsb", [128, 2])
    constk = Tk("const")
    q = "sp"
    C.dma(q, identf[:], c_ident[:, :], constk, writes=[constk])
    C.dma(q, bonesf[:], c_bones[:, :], constk, writes=[constk])
    C.dma(q, triu[:], c_triu[:, :], constk, writes=[constk])
    C.dma(q, striu[:], c_striu[:, :], constk, writes=[constk])
    C.dma(q, rmc[:], c_rm[:, :], constk, writes=[constk])
    C.dma(q, xic[:], c_xi[:, :], constk, writes=[constk])
    C.dma(q, xisc[:], c_xis[:, :], constk, writes=[constk])
    C.dma(q, zetac[:], c_zeta[:, :], constk, writes=[constk])
    C.dma(q, zetasc[:], c_zetas[:, :], constk, writes=[constk])
    C.dma("pool", amask[:], c_amask[:, :], constk, writes=[constk])
    C.op("dve", lambda e: e.tensor_copy(out=identb[:], in_=identf[:]), reads=[constk], writes=[constk])
    C.op("dve", lambda e: e.memset(onesb[:], 1.0), writes=[constk])
    C.op("dve", lambda e: e.memset(onesf[:], 1.0), writes=[constk])
    C.op("dve", lambda e: e.memset(epsb[:, 0:1], LN_EPS), writes=[constk])
    C.op("dve", lambda e: e.memset(epsb[:, 1:2], NORM_EPS), writes=[constk])

    WA = sb("WA", [128, 65536], BF16)
    WAK = Tk("WA")

    lnst = sb("lnst", [128, 2, 6]); lnmv = sb("lnmv", [128, 2]); lnk = Tk("ln")

    def layer_norm(xap, n, gt, bt, gk, xk):
        tks = [xk]
        C.op("dve", lambda e: e.bn_stats(out=lnst[:n, 0, :], in_=xap[:, 0:512]), reads=tks, writes=[lnk])
        C.op("dve", lambda e: e.bn_stats(out=lnst[:n, 1, :], in_=xap[:, 512:1024]), reads=tks, writes=[lnk])
        C.op("dve", lambda e: e.bn_aggr(out=lnmv[:n, :], in_=lnst[:n, :, :].rearrange("p a b -> p (a b)")),
             reads=[lnk], writes=[lnk])
        C.op("act", lambda e: e.activation(out=lnmv[:n, 1:2], in_=lnmv[:n, 1:2], func=AF.Ln, bias=epsb[:n, 0:1]),
             reads=[lnk, constk], writes=[lnk])
        C.op("act", lambda e: e.activation(out=lnmv[:n, 1:2], in_=lnmv[:n, 1:2], func=AF.Exp, scale=-0.5),
             reads=[lnk], writes=[lnk])
        C.op("dve", lambda e: e.tensor_scalar(out=xap, in0=xap, scalar1=lnmv[:n, 0:1], scalar2=lnmv[:n, 1:2],
                                              op0=ALU.subtract, op1=ALU.mult),
             reads=[lnk] + tks, writes=tks)
        C.op("dve", lambda e: e.tensor_tensor(out=xap, in0=xap, in1=gt[:n, :], op=ALU.mult),
             reads=tks + [gk], writes=tks)
        C.op("dve", lambda e: e.tensor_tensor(out=xap, in0=xap, in1=bt[:n, :], op=ALU.add),
             reads=tks + [gk], writes=tks)

    seqs = [dict(name="p", L=Lp, past=0, TS=128, c=64, row0=0, b=0)]
    for b_ in range(NS):
        seqs.append(dict(name="s%d" % b_, L=Ls, past=PAST, TS=Ls, c=Ls, row0=Lp + b_ * Ls, b=b_))

    with ExitStack() as st0:
        g0 = st0.enter_context(nc.sbuf_tensor("g0", [128, D], F32))
        b0 = st0.enter_context(nc.sbuf_tensor("b0", [128, D], F32))
        xl = st0.enter_context(nc.sbuf_tensor("xl", [128, 2, D], F32))
        xlk = [Tk(), Tk()]
        pk0 = Tk()
        C.dma("sp", g0[:], ln0g[0:1, :].partition_broadcast(128), pk0, writes=[pk0])
        C.dma("sp", b0[:], ln0b[0:1, :].partition_broadcast(128), pk0, writes=[pk0])
        tiles0 = [(xp, r, r, 128) for r in range(0, Lp, 128)]
        for b_ in range(NS):
            tiles0.append((xs, b_ * Ls, Lp + b_ * Ls, Ls))
        for i0, (src0, sr0, dr0, n0) in enumerate(tiles0):
            sl0 = i0 % 2
            C.dma("sp", xl[:n0, sl0, :], src0[sr0:sr0 + n0, :], xlk[sl0], writes=[xlk[sl0]])
            layer_norm(xl[:n0, sl0, :], n0, g0, b0, pk0, xlk[sl0])
            C.dma("pool", xcs[dr0:dr0 + n0, :], xl[:n0, sl0, :], xlk[sl0], reads=[xlk[sl0]], writes=[dtk("xcs", dr0)])
    if STOP == 0:
        C.finish()
        return nc

    for l in range(DEPTH):
        lam_init = 0.8 - 0.6 * math.exp(-0.3 * l)
        C.barrier()
        WA_in = WA[:, 0:8 * NIN].rearrange("p (k n) -> p k n", k=8)
        WA_out = WA[:, 8 * NIN:8 * NIN + 8192].rearrange("p (k n) -> p k n", k=8)
        WAK2 = Tk("WA2")
        for k in range(8):
            for c0_ in range(0, NIN, 1024):
                c1_ = min(NIN, c0_ + 1024)
                C.dma("pool", WA_in[:, k, c0_:c1_], w_in[l, k * 128:(k + 1) * 128, c0_:c1_], WAK, writes=[WAK])
        for k in range(8):
            C.dma("pool", WA_out[:, k, :], w_out[l, k * 128:(k + 1) * 128, :], WAK2, writes=[WAK2])
        with ExitStack() as st:
            def T(name, shape, dt=F32):
                return st.enter_context(nc.sbuf_tensor("%s_%d" % (name, l), list(shape), dt))
            wa_off = [8 * NIN + 8192]

            def WT(shape):
                n = 1
                for d_ in shape[1:]:
                    n *= d_
                ap = WA[:, wa_off[0]:wa_off[0] + n]
                wa_off[0] += n
                assert wa_off[0] <= 65536, wa_off[0]
                if len(shape) == 3:
                    ap = ap.rearrange("p (a b) -> p a b", a=shape[1])
                return ap
            KT = WT([128, 4, LK])
            ktk = [Tk("kt%d" % i) for i in range((LK + 127) // 128)]
            g1 = T("g1", [128, D]); b1 = T("b1", [128, D])
            pk_ = Tk("params")
            C.dma("sp", g1[:], ln1g[l:l + 1, :].partition_broadcast(128), pk_, writes=[pk_])
            C.dma("sp", b1[:], ln1b[l:l + 1, :].partition_broadcast(128), pk_, writes=[pk_])
            lamt = T("lamt", [128, 4, 64]); lamr = T("lamr", [128, 4]); neglam = T("neglam", [128, 1])
            for i, src in enumerate((lamq1, lamk1, lamq2, lamk2)):
                C.dma("sp", lamt[:, i, :], src[l:l + 1, :].partition_broadcast(128), pk_, writes=[pk_])
            C.op("dve", lambda e: e.tensor_tensor(out=lamt[:, 0, :], in0=lamt[:, 0, :], in1=lamt[:, 1, :], op=ALU.mult),
                 reads=[pk_], writes=[pk_])
            C.op("dve", lambda e: e.tensor_tensor(out=lamt[:, 2, :], in0=lamt[:, 2, :], in1=lamt[:, 3, :], op=ALU.mult),
                 reads=[pk_], writes=[pk_])
            C.op("dve", lambda e: e.tensor_reduce(out=lamr[:, 0:1], in_=lamt[:, 0, :], axis=AX.X, op=ALU.add),
                 reads=[pk_], writes=[pk_])
            C.op("dve", lambda e: e.tensor_reduce(out=lamr[:, 1:2], in_=lamt[:, 2, :], axis=AX.X, op=ALU.add),
                 reads=[pk_], writes=[pk_])
            C.op("act", lambda e: e.activation(out=lamr[:, 2:4], in_=lamr[:, 0:2], func=AF.Exp),
                 reads=[pk_], writes=[pk_])
            C.op("dve", lambda e: e.scalar_tensor_tensor(out=neglam[:], in0=lamr[:, 3:4], scalar=-lam_init,
                                                         in1=lamr[:, 2:3], op0=ALU.add, op1=ALU.subtract),
                 reads=[pk_], writes=[pk_])
            dngc = T("dngc", [128, 1]); dlngc = T("dlngc", [64, 1])
            C.dma("sp", dngc[:], dng[l:l + 1, :].rearrange("o e -> e o"), pk_, writes=[pk_])
            C.dma("sp", dlngc[:], dlng[l:l + 1, :].rearrange("o e -> e o"), pk_, writes=[pk_])
            cw = T("cw", [128, 6, 4])
            for ch in range(6):
                C.dma("sp", cw[:, ch, :], convw[l, :, ch * 128:(ch + 1) * 128].rearrange("j p -> p j"),
                      pk_, writes=[pk_], allow_slow_non_contiguous=True)
            nA = T("nA", [128, 4]); dtbb = T("dtbb", [128, 4])
            C.dma("sp", nA[:], alog[l:l + 1, :].partition_broadcast(128), pk_, writes=[pk_])
            C.dma("sp", dtbb[:], dtb[l:l + 1, :].partition_broadcast(128), pk_, writes=[pk_])
            C.op("act", lambda e: e.activation(out=nA[:], in_=nA[:], func=AF.Exp), reads=[pk_], writes=[pk_])
            C.op("dve", lambda e: e.tensor_scalar(out=nA[:], in0=nA[:], scalar1=-1.0, scalar2=None, op0=ALU.mult),
                 reads=[pk_], writes=[pk_])
            if STOP == 6:
                C.finish()
                return nc

            xres = T("xres", [128, D]); xk = Tk("xres")
            xbf = WT([128, D]); xbfk = Tk()
            kbf = xbf[:, 0:512]; kbfk = xbfk
            xT = WT([128, 8, 256]); xTk = Tk()
            mixT = xT; mixk = xTk
            mixA = WT([128, 4, 256]); mixAk = Tk()
            QTa = WT([128, 4, 256]); QTb = WT([128, 4, 256]); QTk = Tk()
            C.op("pool", lambda e: e.memset(QTa[64:128, :, :], 0.0), writes=[QTk])
            C.op("pool", lambda e: e.memset(QTb[0:64, :, :], 0.0), writes=[QTk])
            rot = T("rot", [128, 128]); rotk = Tk()
            tmpA = T("tmpA", [128, 512]); tmpB = T("tmpB", [128, 512]); tmpk = Tk()
            kout = T("kout", [128, 512]); koutk = Tk()
            vout = T("vout", [128, 512]); voutk = Tk()
            vbf = WT([128, 512]); vbfk = Tk()
            uT = T("uT", [128, 6, 131]); uTk = Tk()
            cT = T("cT", [128, 6, 128]); cTk = Tk()
            cE = T("cE", [128, 6, 128]); cEk = Tk()
            nTh = WT([128, 8, 128]); nTk = Tk()
            qdT = WT([128, 4, 128]); qdk = Tk()
            g4 = T("g4", [128, 264]); g4e = T("g4e", [128, 264]); g4k = Tk()
            sgT = T("sgT", [64, 8, 128]); sgTk = Tk()
            qkc = T("qkc", [128, 512]); qkck = Tk()
            qkcT = WT([128, 8, 128]); qkcTk = Tk()
            sg6 = g4[:, 0:256]; sg6e = g4e[:, 0:256]; sg6k = g4k
            ktok = T("ktok", [64, 2, 512]); ktokk = Tk()
            kz = T("kz", [64, 2, 256], BF16); kzk = Tk()
            vcb = T("vcb", [64, 2, 256], BF16); kzc = T("kzc", [64, 2, 256], BF16); vck = Tk()
            gb = T("gb", [64, 2, 8]); gbk = Tk()
            sm = T("sm", [128, 64]); smk = Tk()
            GT = T("GT", [64, 4, 64]); GTk = Tk()
            Drow = T("Drow", [128, 4, 64]); EDrow = T("EDrow", [128, 4, 64]); Drk = Tk()
            LM = T("LM", [64, 4, 64]); LMs = T("LMs", [64, 4, 64]); LMi = T("LMi", [64, 4, 64]); LMk = Tk()
            Xf = T("Xf", [64, 4, 64]); Xk = Tk()
            Yb = T("Yb", [64, 2, 4, 64], BF16); YTb = T("YTb", [64, 6, 4, 64], BF16); Yk = [Tk() for _ in range(8)]; YTk = [Tk() for _ in range(8)]
            Pf = T("Pf", [64, 4, 64]); Pb = T("Pb", [64, 4, 64], BF16); Pk = Tk()
            inT = T("inT", [64, 4, 64], BF16); inTk = Tk()
            inR = T("inR", [64, 4, 64], BF16); inRk = Tk()
            rt = T("rt", [64, 256]); rb = T("rb", [64, 256], BF16); rk = Tk()
            scr = T("scr", [64, 4, 64]); scrk = Tk()
            vnb = T("vnb", [64, 256], BF16); vnk = Tk()
            Sd = T("Sd", [64, 4, 64]); Sdb = T("Sdb", [64, 4, 64], BF16); Sdk = Tk()
            Sr = T("Sr", [64, 4, 64]); Srb = T("Srb", [64, 4, 64], BF16); Srk = Tk()
            ot = T("ot", [64, 4, 128]); ot2 = T("ot2", [64, 4, 128]); otb = T("otb", [64, 4, 128], BF16); otk = Tk()
            PT = [WT([128, 2, 256]), WT([128, 2, 256])]; PTk = [Tk(), Tk()]
            Vt = [WT([128, 128]), WT([128, 128]), WT([128, 128])]; Vtk = [Tk(), Tk(), Tk()]
            ckb = WT([128, 512]); ckbk = Tk()
            Rr = tmpA[:, :].rearrange("p (j q) -> p j q", j=2); Tt = tmpB[:, :].rearrange("p (j q) -> p j q", j=2)
            oaf = T("oaf", [128, 256])
            oab = T("oab", [128, 256], BF16); atk = tmpk
            pcst = cE[:3, :, :].rearrange("p a b -> p (a b)"); pcstk = cEk

            def proj(TS, col0, c0, n, kind="pj"):
                pb, pbk = bank(kind)
                for k in range(8):
                    C.op("pe", lambda e: e.matmul(pb[:TS, 0:n], lhsT=xT[:, k, col0:col0 + TS],
                                                  rhs=WA_in[:, k, c0:c0 + n], start=(k == 0), stop=(k == 7)),
                         reads=[xTk, WAK], writes=[pbk])
                return pb, pbk

            def rotary(pb, pbk, TS, outs):
                pv_ = pb[:TS, :].rearrange("p (h d) -> p h d", h=8)
                C.op("dve", lambda e: e.tensor_tensor(out=tmpA[:TS, :].rearrange("p (h d) -> p h d", h=8), in0=pv_,
                                                      in1=bc(rot[:TS, 0:64].unsqueeze(1), [TS, 8, 64]), op=ALU.mult),
                     reads=[pbk, rotk], writes=[tmpk])
                tb = tmpB[:TS, :].rearrange("p (h d) -> p h d", h=8)
                C.op("dve", lambda e: e.tensor_tensor(out=tb[:, :, 0:32], in0=pv_[:, :, 32:64],
                                                      in1=bc(rot[:TS, 64:96].unsqueeze(1), [TS, 8, 32]), op=ALU.mult),
                     reads=[pbk, rotk], writes=[tmpk])
                C.op("dve", lambda e: e.tensor_tensor(out=tb[:, :, 32:64], in0=pv_[:, :, 0:32],
                                                      in1=bc(rot[:TS, 96:128].unsqueeze(1), [TS, 8, 32]), op=ALU.mult),
                     reads=[pbk, rotk], writes=[tmpk])
                for (eng, oap, otk_) in outs:
                    C.op(eng, lambda e: e.tensor_tensor(out=oap, in0=tmpA[:TS, :], in1=tmpB[:TS, :], op=ALU.add),
                         reads=[tmpk], writes=[otk_])

            def transposes_bf(src, srck, TS, n):
                pb, pbk = bank("mi")
                pbb = pb[:].bitcast(BF16)
                for i in range(n):
                    C.op("pe", lambda e: e.transpose(out=pbb[:, i * TS:(i + 1) * TS],
                                                     in_=src[:TS, i * 128:(i + 1) * 128],
                                                     identity=identb[:TS, :TS]),
                         reads=[srck, constk], writes=[pbk])
                return pbb[:, 0:n * TS].rearrange("p (k t) -> p k t", k=n), pbk

            def silu_from(pb_ap, xs_, es_, k_, pbk):
                C.op("act", lambda e: e.activation(out=es_, in_=pb_ap, func=AF.Exp, scale=-1.0), reads=[pbk], writes=[k_])
                C.op("act", lambda e: e.activation(out=xs_, in_=pb_ap, func=AF.Copy), reads=[pbk], writes=[k_])
                C.op("dve", lambda e: e.tensor_scalar(out=es_, in0=es_, scalar1=1.0, scalar2=None, op0=ALU.add),
                     reads=[k_], writes=[k_])
                C.op("dve", lambda e: e.reciprocal(out=es_, in_=es_), reads=[k_], writes=[k_])

            for sq_ in seqs:
                TS, c, L, past, row0, sb_ = sq_["TS"], sq_["c"], sq_["L"], sq_["past"], sq_["row0"], sq_["b"]
                isP = sq_["name"] == "p"
                CPT = TS // c
                TPB = 2 if isP else 1
                nblk = L // (TS * TPB)
                nq = TS * TPB
                nit = 5 if c == 64 else 4
                rsrc = c_rotp if isP else c_rots
                kdst = pk if isP else sk
                vdst = pv if isP else sv
                orow0 = 0 if isP else sb_ * Ls
                zt = zetac if isP else zetasc
                xit = xic if isP else xisc
                gch = [math.exp(math.log(1.0 - 2.0 ** (-5.0 - h)) * c) for h in range(4)]
                if isP:
                    C.op("dve", lambda e: e.memset(Sd[:], 0.0), writes=[Sdk])
                    C.op("dve", lambda e: e.memset(Sr[:], 0.0), writes=[Srk])
                    C.op("dve", lambda e: e.memset(uT[:, :, 0:3], 0.0), writes=[uTk])
                else:
                    C.dma("sp", Sd[:], sdl[l, sb_].rearrange("(h d) e -> d h e", d=64), Sdk, writes=[Sdk])
                    C.dma("sp", Sr[:], srt[l, sb_].rearrange("(h d) e -> d h e", d=64), Srk, writes=[Srk])
                    for ch in range(6):
                        C.dma("sp", uT[:, ch, 0:3], scv[l, sb_, :, ch * 128:(ch + 1) * 128].rearrange("j p -> p j"),
                              uTk, writes=[uTk], allow_slow_non_contiguous=True)
                    for kt in range(past // 128):
                        C.dma("pool", ckb[:], ck[l, sb_, kt * 128:(kt + 1) * 128, :], ckbk, writes=[ckbk])
                        pv4, pv4k = transposes_bf(ckb, ckbk, 128, 4)
                        C.op("dve", lambda e: e.tensor_copy(out=KT[:, :, kt * 128:(kt + 1) * 128], in_=pv4),
                             reads=[pv4k], writes=[ktk[kt]])
                C.op("act", lambda e: e.activation(func=AF.Copy, out=Sdb[:], in_=Sd[:]), reads=[Sdk], writes=[Sdk])
                C.op("act", lambda e: e.activation(func=AF.Copy, out=Srb[:], in_=Sr[:]), reads=[Srk], writes=[Srk])

                def gen_frontA(ti, t, pjk):
                    r0 = t * TS
                    col0 = ti * TS
                    kcol = past + r0
                    kt_own = kcol // 128
                    xa = xres[:TS, :]
                    C.dma("sp", xa, xcs[row0 + r0:row0 + r0 + TS, :], xk,
                          reads=[dtk("xcs", row0 + r0)], writes=[xk])
                    C.dma("sp", rot[:TS, :], rsrc[r0:r0 + TS, :], rotk, writes=[rotk])
                    C.op("act", lambda e: e.activation(out=xbf[:TS, :], in_=xa, func=AF.Copy), reads=[xk], writes=[xbfk])
                    pv8, pv8k = transposes_bf(xbf, xbfk, TS, 8)
                    C.op("dve", lambda e: e.tensor_copy(out=xT[:, :, col0:col0 + TS], in_=pv8), reads=[pv8k], writes=[xTk])
                    yield
                    pb, pbk = proj(TS, col0, 0, 512, pjk)
                    rotary(pb, pbk, TS, [("dve", kbf[:TS, :], kbfk)])
                    pv4, pv4k = transposes_bf(kbf, kbfk, TS, 4)
                    C.op("dve", lambda e: e.tensor_copy(out=QTa[0:64, :, col0:col0 + TS], in_=pv4[0:64, :, :]),
                         reads=[pv4k], writes=[QTk])
                    C.op("dve", lambda e: e.tensor_copy(out=QTb[64:128, :, col0:col0 + TS], in_=pv4[64:128, :, :]),
                         reads=[pv4k], writes=[QTk])
                    yield
                    pb, pbk = proj(TS, col0, 512, 512, pjk)
                    rotary(pb, pbk, TS, [("dve", kout[:TS, :], koutk), ("dve", kbf[:TS, :], kbfk)])
                    C.dma("pool", kdst[l, orow0 + r0:orow0 + r0 + TS, :], kout[:TS, :], koutk, reads=[koutk])
                    pv4, pv4k = transposes_bf(kbf, kbfk, TS, 4)
                    C.op("dve", lambda e: e.tensor_copy(out=KT[:, :, kcol:kcol + TS], in_=pv4),
                         reads=[pv4k], writes=[ktk[kt_own]])
                    yield
                    pb, pbk = proj(TS, col0, 1024, 512, pjk)
                    C.op("act", lambda e: e.activation(out=vout[:TS, :], in_=pb[:TS, :], func=AF.Copy), reads=[pbk], writes=[voutk])
                    C.op("dve", lambda e: e.tensor_copy(out=vbf[:TS, :], in_=vout[:TS, :]), reads=[voutk], writes=[vbfk])
                    C.dma("pool", vdst[l, orow0 + r0:orow0 + r0 + TS, :], vout[:TS, :], voutk, reads=[voutk])
                    C.dma("pool", vsc[kcol:kcol + TS, :], vbf[:TS, :], vbfk, reads=[vbfk], writes=[dtk("vsc", kt_own)])
                    yield

                def gen_frontB(t, col0):
                    pb, pbk = proj(TS, col0, 2304, 264)
                    silu_from(pb[:TS, 0:260], g4[:TS, 0:260], g4e[:TS, 0:260], g4k, pbk)
                    C.op("dve", lambda e: e.tensor_tensor(out=g4[:TS, 0:256], in0=g4[:TS, 0:256], in1=g4e[:TS, 0:256],
                                                          op=ALU.mult), reads=[g4k], writes=[g4k])
                    C.op("dve", lambda e: e.tensor_tensor(out=g4[:TS, 260:264], in0=pb[:TS, 260:264], in1=dtbb[:TS, :],
                                                          op=ALU.add), reads=[pbk, pk_], writes=[g4k])
                    C.op("act", lambda e: e.activation(out=g4[:TS, 260:264], in_=g4[:TS, 260:264], func=AF.Exp),
                         reads=[g4k], writes=[g4k])
                    C.op("act", lambda e: e.activation(out=g4[:TS, 260:264], in_=g4[:TS, 260:264], func=AF.Ln, bias=1.0),
                         reads=[g4k], writes=[g4k])
                    C.op("dve", lambda e: e.tensor_tensor(out=g4[:TS, 260:264], in0=g4[:TS, 260:264], in1=nA[:TS, :],
                                                          op=ALU.mult), reads=[g4k, pk_], writes=[g4k])
                    for ck_ in range(CPT):
                        C.op("dve", lambda e: e.tensor_copy(out=gb[:c, ck_, 0:4], in_=g4[ck_ * c:(ck_ + 1) * c, 260:264]),
                             reads=[g4k], writes=[gbk])
                        C.op("dve", lambda e: e.tensor_copy(out=gb[:c, ck_, 4:8], in_=g4e[ck_ * c:(ck_ + 1) * c, 256:260]),
                             reads=[g4k], writes=[gbk])
                    pm, pmk = bank("mi")
                    for h in range(4):
                        C.op("pe", lambda e: e.transpose(out=pm[:64, h * TS:(h + 1) * TS], in_=g4[:TS, h * 64:(h + 1) * 64],
                                                         identity=identf[:TS, :TS]),
                             reads=[g4k, constk], writes=[pmk])
                    C.op("act", lambda e: e.activation(out=sgT[:, 0:4, :TS],
                                                       in_=pm[:64, 0:4 * TS].rearrange("p (h t) -> p h t", h=4), func=AF.Copy),
                         reads=[pmk], writes=[sgTk])
                    yield
                    pb, pbk = proj(TS, col0, 2568, 512)
                    rotary(pb, pbk, TS, [("dve", qkc[:TS, :], qkck)])
                    C.op("dve", lambda e: e.tensor_scalar(out=qkc[:TS, 256:512], in0=qkc[:TS, 256:512], scalar1=0.125,
                                                          scalar2=None, op0=ALU.mult), reads=[qkck], writes=[qkck])
                    for half in range(2):
                        pm, pmk = bank("mi")
                        for h in range(4):
                            C.op("pe", lambda e: e.transpose(
                                out=pm[:64, h * TS:(h + 1) * TS],
                                in_=qkc[:TS, half * 256 + h * 64:half * 256 + (h + 1) * 64],
                                identity=identf[:TS, :TS]), reads=[qkck, constk], writes=[pmk])
                        C.op("act", lambda e: e.activation(
                            out=qkcT[0:64, half * 4:half * 4 + 4, :TS],
                            in_=pm[:64, 0:4 * TS].rearrange("p (h t) -> p h t", h=4), func=AF.Copy), reads=[pmk], writes=[qkcTk])
                    for ck_ in range(CPT):
                        C.op("dve", lambda e: e.tensor_tensor(
                            out=kzc[:c, ck_, :].rearrange("p (h d) -> p h d", h=4),
                            in0=qkc[ck_ * c:(ck_ + 1) * c, 256:512].rearrange("p (h d) -> p h d", h=4),
                            in1=bc(zt[ck_ * c:(ck_ + 1) * c, :].unsqueeze(2), [c, 4, 64]), op=ALU.mult),
                             reads=[qkck, constk], writes=[vck])
                    yield
                    pb, pbk = proj(TS, col0, 3080, 512)
                    for ck_ in range(CPT):
                        C.op("act", lambda e: e.activation(out=vcb[:c, ck_, :], in_=pb[ck_ * c:(ck_ + 1) * c, 0:256], func=AF.Copy),
                             reads=[pbk], writes=[vck])
                    silu_from(pb[:TS, 256:512], sg6[:TS, :], sg6e[:TS, :], sg6k, pbk)
                    C.op("dve", lambda e: e.tensor_tensor(out=sg6[:TS, :], in0=sg6[:TS, :], in1=sg6e[:TS, :], op=ALU.mult),
                         reads=[sg6k], writes=[sg6k])
                    pm, pmk = bank("mi")
                    for h in range(4):
                        C.op("pe", lambda e: e.transpose(out=pm[:64, h * TS:(h + 1) * TS], in_=sg6[:TS, h * 64:(h + 1) * 64],
                                                         identity=identf[:TS, :TS]),
                             reads=[sg6k, constk], writes=[pmk])
                    C.op("act", lambda e: e.activation(out=sgT[:, 4:8, :TS],
                                                       in_=pm[:64, 0:4 * TS].rearrange("p (h t) -> p h t", h=4), func=AF.Copy),
                         reads=[pmk], writes=[sgTk])
                    yield
                    for rnd in range(2):
                        pb, pbk = bank("pj")
                        for j in range(3):
                            ch = rnd * 3 + j
                            for k in range(8):
                                C.op("pe", lambda e: e.matmul(
                                    pb[:, j * TS:(j + 1) * TS],
                                    lhsT=WA_in[:, k, 1536 + ch * 128:1536 + (ch + 1) * 128],
                                    rhs=xT[:, k, col0:col0 + TS], start=(k == 0), stop=(k == 7)),
                                     reads=[xTk, WAK], writes=[pbk])
                        C.op("act", lambda e: e.activation(
                            out=uT[:, rnd * 3:rnd * 3 + 3, 3:3 + TS],
                            in_=pb[:, 0:3 * TS].rearrange("p (j t) -> p j t", j=3), func=AF.Copy), reads=[pbk], writes=[uTk])
                        yield
                    yield
                    for ch in range(6):
                        C.op("dve", lambda e: e.tensor_scalar(out=cT[:, ch, :TS], in0=uT[:, ch, 0:TS],
                                                              scalar1=cw[:, ch, 0:1], scalar2=None, op0=ALU.mult),
                             reads=[uTk, pk_], writes=[cTk])
                        for j in range(1, 4):
                            C.op("dve", lambda e: e.scalar_tensor_tensor(
                                out=cT[:, ch, :TS], in0=uT[:, ch, j:j + TS], scalar=cw[:, ch, j:j + 1],
                                in1=cT[:, ch, :TS], op0=ALU.mult, op1=ALU.add), reads=[uTk, pk_, cTk], writes=[cTk])
                    yield
                    if t == L // TS - 1:
                        for (c0_, c1_) in ((0, 4), (4, 6)):
                            pm, pmk = bank("mi")
                            for ch in range(c0_, c1_):
                                C.op("pe", lambda e: e.transpose(out=pm[:3, (ch - c0_) * 128:(ch - c0_ + 1) * 128],
                                                                 in_=uT[:, ch, TS:TS + 3], identity=identf[:, :]),
                                     reads=[uTk, constk], writes=[pmk])
                            C.op("act", lambda e: e.activation(out=pcst[:, c0_ * 128:c1_ * 128], in_=pm[:3, 0:(c1_ - c0_) * 128],
                                                               func=AF.Copy), reads=[pmk], writes=[pcstk])
                        cdst = pcv[l] if isP else scvo[l, sb_]
                        C.dma("pool", cdst, pcst[:, :], pcstk, reads=[pcstk])
                    C.op("pool", lambda e: e.tensor_copy(out=uT[:, :, 0:3], in_=uT[:, :, TS:TS + 3]),
                         reads=[uTk], writes=[uTk])
                    yield
                    C.op("act", lambda e: e.activation(out=cE[:, :, :TS], in_=cT[:, :, :TS], func=AF.Exp, scale=-1.0),
                         reads=[cTk], writes=[cEk])
                    C.op("dve", lambda e: e.tensor_scalar(out=cE[:, :, :TS], in0=cE[:, :, :TS], scalar1=1.0, scalar2=None,
                                                          op0=ALU.add), reads=[cEk], writes=[cEk])
                    C.op("dve", lambda e: e.reciprocal(out=cE[:, :, :TS], in_=cE[:, :, :TS]), reads=[cEk], writes=[cEk])
                    C.op("dve", lambda e: e.tensor_tensor(out=cT[:, :, :TS], in0=cT[:, :, :TS], in1=cE[:, :, :TS], op=ALU.mult),
                         reads=[cEk, cTk], writes=[cTk])
                    yield
                    C.op("dve", lambda e: e.tensor_tensor(out=cE[:, 0:4, :TS], in0=cT[:, 0:4, :TS], in1=cT[:, 0:4, :TS],
                                                           op=ALU.mult), reads=[cTk, cEk], writes=[cEk])
                    pm, pmk = bank("mi")
                    for j in range(4):
                        C.op("pe", lambda e: e.matmul(pm[:, j * TS:(j + 1) * TS], lhsT=bonesf[:, :], rhs=cE[:, j, :TS],
                                                      start=True, stop=True), reads=[cEk, constk], writes=[pmk])
                    C.op("act", lambda e: e.activation(out=cE[:, 0:4, :TS],
                                                       in_=pm[:, 0:4 * TS].rearrange("p (j t) -> p j t", j=4),
                                                       func=AF.Ln, bias=epsb[:, 1:2]),
                         reads=[pmk, constk], writes=[cEk])
                    C.op("act", lambda e: e.activation(out=cE[:, 0:4, :TS], in_=cE[:, 0:4, :TS], func=AF.Exp, scale=-0.5),
                         reads=[cEk], writes=[cEk])
                    yield
                    C.op("dve", lambda e: e.scalar_tensor_tensor(out=cT[:, 0:2, :TS], in0=cT[:, 0:2, :TS], scalar=0.125,
                                                                 in1=cE[:, 0:2, :TS], op0=ALU.mult, op1=ALU.mult),
                         reads=[cTk, cEk], writes=[cTk])
                    C.op("dve", lambda e: e.tensor_tensor(out=cT[:, 2:4, :TS], in0=cT[:, 2:4, :TS], in1=cE[:, 2:4, :TS],
                                                          op=ALU.mult), reads=[cTk, cEk], writes=[cTk])
                    for s_ in range(2):
                        C.op("dve", lambda e: e.tensor_copy(out=nTh[0:64, s_:8:2, :TS], in_=cT[s_ * 64:(s_ + 1) * 64, 0:4, :TS]),
                             reads=[cTk], writes=[nTk])
                    yield
                    for ck_ in range(CPT):
                        pm, pmk = bank("mi")
                        for j in range(4):
                            C.op("pe", lambda e: e.transpose(
                                out=pm[:c, j * 128:(j + 1) * 128], in_=cT[:, 2 + j, ck_ * c:(ck_ + 1) * c],
                                identity=identf[:, :]), reads=[cTk, constk], writes=[pmk])
                        C.op("act", lambda e: e.activation(out=ktok[:c, ck_, :], in_=pm[:c, :], func=AF.Copy),
                             reads=[pmk], writes=[ktokk])
                    yield

                def gen_chunks(col0):
                    po, pok = bank("pj")
                    po2, po2k = bank("pj")
                    for ck_ in range(CPT):
                        cs = slice(ck_ * c, (ck_ + 1) * c)
                        pm, pmk = bank("mi")
                        C.op("pe", lambda e: e.matmul(pm[:c, 0:4], lhsT=triu[:c, :c], rhs=gb[:c, ck_, 0:4],
                                                      start=True, stop=True), reads=[gbk, constk], writes=[pmk])
                        C.op("dve", lambda e: e.tensor_tensor(out=GT[:c, :, :c],
                                                              in0=bc(triu[:c, :c].unsqueeze(1), [c, 4, c]),
                                                              in1=bc(gb[:c, ck_, 0:4].unsqueeze(2), [c, 4, c]),
                                                              op=ALU.mult), reads=[gbk, constk], writes=[GTk])
                        pm2, pm2k = bank("mi")
                        for h in range(4):
                            C.op("pe", lambda e: e.matmul(pm2[:, h * c:(h + 1) * c], lhsT=onesf[:c, :],
                                                          rhs=GT[:c, h, :c], start=True, stop=True),
                                 reads=[GTk, constk], writes=[pm2k])
                        C.op("act", lambda e: e.activation(out=sm[:c, 0:4], in_=pm[:c, 0:4], func=AF.Copy), reads=[pmk], writes=[smk])
                        C.op("act", lambda e: e.activation(out=sm[:c, 4:8], in_=pm[:c, 0:4], func=AF.Exp), reads=[pmk], writes=[smk])
                        C.op("act", lambda e: e.activation(out=Drow[:, :, :c], in_=pm2[:, 0:4 * c].rearrange("p (h i) -> p h i", h=4),
                                                           func=AF.Copy), reads=[pm2k], writes=[Drk])
                        C.op("act", lambda e: e.activation(out=EDrow[:, :, :c],
                                                           in_=pm2[:, 0:4 * c].rearrange("p (h i) -> p h i", h=4), func=AF.Exp),
                             reads=[pm2k], writes=[Drk])
                        C.op("dve", lambda e: e.tensor_tensor(out=sm[:c, 12:16], in0=Drow[:c, :, c - 1], in1=sm[:c, 0:4],
                                                              op=ALU.subtract), reads=[Drk, smk], writes=[smk])
                        C.op("act", lambda e: e.activation(out=sm[:c, 8:12], in_=sm[:c, 12:16], func=AF.Exp), reads=[smk], writes=[smk])
                        yield
                        C.op("dve", lambda e: e.tensor_tensor(out=LM[:c, :, :c], in0=Drow[:c, :, :c],
                                                              in1=bc(sm[:c, 0:4].unsqueeze(2), [c, 4, c]), op=ALU.subtract),
                             reads=[Drk, smk], writes=[LMk])
                        C.op("dve", lambda e: e.tensor_scalar(out=LM[:c, :, :c], in0=LM[:c, :, :c], scalar1=0.0, scalar2=None,
                                                              op0=ALU.min), reads=[LMk], writes=[LMk])
                        C.op("act", lambda e: e.activation(out=LM[:c, :, :c], in_=LM[:c, :, :c], func=AF.Exp), reads=[LMk], writes=[LMk])
                        C.op("dve", lambda e: e.tensor_tensor(out=LMs[:c, :, :c], in0=LM[:c, :, :c],
                                                               in1=bc(striu[:c, :c].unsqueeze(1), [c, 4, c]), op=ALU.mult),
                             reads=[LMk, constk], writes=[LMk])
                        C.op("dve", lambda e: e.tensor_tensor(out=LMi[:c, :, :c], in0=LM[:c, :, :c],
                                                               in1=bc(triu[:c, :c].unsqueeze(1), [c, 4, c]), op=ALU.mult),
                             reads=[LMk, constk], writes=[LMk])
                        C.op("dve", lambda e: e.tensor_tensor(out=qdT[0:64, :, cs], in0=nTh[0:64, 0:4, cs],
                                                              in1=EDrow[0:64, :, :c], op=ALU.mult), reads=[nTk, Drk], writes=[qdk])
                        C.op("dve", lambda e: e.tensor_tensor(out=kz[:c, ck_, :].rearrange("p (h d) -> p h d", h=4),
                                                              in0=ktok[:c, ck_, 0:256].rearrange("p (h d) -> p h d", h=4),
                                                              in1=bc(sm[:c, 8:12].unsqueeze(2), [c, 4, 64]), op=ALU.mult),
                             reads=[ktokk, smk], writes=[kzk])
                        yield
                        pg, pgk = bank("mi")
                        for h in range(4):
                            C.op("pe", lambda e: e.matmul(pg[:c, h * c:(h + 1) * c], lhsT=nTh[0:64, 4 + h, cs], rhs=nTh[0:64, 4 + h, cs],
                                                          start=True, stop=True), reads=[nTk], writes=[pgk])
                        for h in range(4):
                            C.op("pe", lambda e: e.matmul(pg[:c, 256 + h * c:256 + (h + 1) * c], lhsT=nTh[0:64, 4 + h, cs],
                                                          rhs=nTh[0:64, h, cs], start=True, stop=True), reads=[nTk], writes=[pgk])
                        KKv = pg[:c, 0:4 * c].rearrange("p (h i) -> p h i", h=4)
                        KQv = pg[:c, 256:256 + 4 * c].rearrange("p (h i) -> p h i", h=4)
                        C.op("dve", lambda e: e.tensor_tensor(out=Xf[:c, :, :c], in0=KKv,
                                                              in1=bc(gb[:c, ck_, 4:8].unsqueeze(2), [c, 4, c]), op=ALU.mult),
                             reads=[pgk, gbk], writes=[Xk])
                        C.op("dve", lambda e: e.tensor_tensor(out=Xf[:c, :, :c], in0=Xf[:c, :, :c], in1=LMs[:c, :, :c], op=ALU.mult),
                             reads=[Xk, LMk], writes=[Xk])
                        C.op("dve", lambda e: e.tensor_tensor(out=scr[:c, :, :c], in0=KQv, in1=LMi[:c, :, :c], op=ALU.mult),
                             reads=[pgk, LMk], writes=[scrk])
                        C.op("act", lambda e: e.activation(func=AF.Copy, out=inT[:c, :, :c], in_=scr[:c, :, :c]), reads=[scrk], writes=[inTk])
                        yield
                        pm, pmk = bank("mi")
                        for h in range(4):
                            C.op("pe", lambda e: e.transpose(out=pm[:c, h * c:(h + 1) * c], in_=Xf[:c, h, :c],
                                                             identity=identf[:c, :c]), reads=[Xk, constk], writes=[pmk])
                        C.op("act", lambda e: e.activation(out=YTb[:c, 0, :, :c], in_=pm[:c, 0:4 * c].rearrange("p (h i) -> p h i", h=4),
                                                           func=AF.Copy), reads=[pmk], writes=[YTk[0]])
                        C.op("act", lambda e: e.activation(out=Yb[:c, 0, :, :c], in_=Xf[:c, :, :c], func=AF.Copy), reads=[Xk], writes=[Yk[0]])
                        C.op("dve", lambda e: e.tensor_tensor(out=Pf[:c, :, :c], in0=bc(identf[:c, :c].unsqueeze(1), [c, 4, c]),
                                                              in1=Xf[:c, :, :c], op=ALU.subtract), reads=[Xk, constk], writes=[Pk])
                        C.op("dve", lambda e: e.tensor_copy(out=Pb[:c, :, :c], in_=Pf[:c, :, :c]), reads=[Pk], writes=[Pk])
                        yield

                        def emit_prod(k):
                            pmq, pmqk = bank("mi")
                            for h in range(4):
                                C.op("pe", lambda e: e.matmul(pmq[:c, h * c:(h + 1) * c], lhsT=YTb[:c, k, h, :c],
                                                              rhs=Pb[:c, h, :c], start=True, stop=True),
                                     reads=[YTk[k], Pk], writes=[pmqk])
                            C.op("dve", lambda e: e.tensor_tensor(out=Pf[:c, :, :c], in0=Pf[:c, :, :c],
                                                                  in1=pmq[:c, 0:4 * c].rearrange("p (h i) -> p h i", h=4), op=ALU.add),
                                 reads=[pmqk, Pk], writes=[Pk])
                            C.op("dve", lambda e: e.tensor_copy(out=Pb[:c, :, :c], in_=Pf[:c, :, :c]), reads=[Pk], writes=[Pk])
                        for k in range(1, nit + 1):
                            ys, yd = (k - 1) % 2, k % 2
                            pm, pmk = bank("mi")
                            if k < nit:
                                for h in range(4):
                                    C.op("pe", lambda e: e.matmul(pm[:c, h * c:(h + 1) * c], lhsT=YTb[:c, k - 1, h, :c],
                                                                  rhs=Yb[:c, ys, h, :c], start=True, stop=True),
                                         reads=[Yk[ys], YTk[k - 1]], writes=[pmk])
                            for h in range(4):
                                C.op("pe", lambda e: e.matmul(pm[:c, 256 + h * c:256 + (h + 1) * c],
                                                              lhsT=Yb[:c, ys, h, :c], rhs=YTb[:c, k - 1, h, :c],
                                                              start=True, stop=True), reads=[Yk[ys], YTk[k - 1]], writes=[pmk])
                            if k < nit:
                                C.op("act", lambda e: e.activation(out=Yb[:c, yd, :, :c],
                                                                   in_=pm[:c, 0:4 * c].rearrange("p (h i) -> p h i", h=4), func=AF.Copy),
                                     reads=[pmk], writes=[Yk[yd]])
                            C.op("act", lambda e: e.activation(out=YTb[:c, k, :, :c],
                                                               in_=pm[:c, 256:256 + 4 * c].rearrange("p (h i) -> p h i", h=4), func=AF.Copy),
                                 reads=[pmk], writes=[YTk[k]])
                            yield
                            if k >= 2:
                                emit_prod(k - 1)
                                yield
                        emit_prod(nit)
                        yield
                        pc, pck = bank("mi")
                        for h in range(4):
                            C.op("pe", lambda e: e.matmul(pc[:c, h * 64:(h + 1) * 64], lhsT=nTh[0:64, 4 + h, cs],
                                                          rhs=Sdb[:, h, :], start=True, stop=True),
                                 reads=[nTk, Sdk], writes=[pck])
                        C.op("dve", lambda e: e.tensor_tensor(out=rt[:c, :].rearrange("p (h d) -> p h d", h=4),
                                                              in0=pc[:c, 0:256].rearrange("p (h d) -> p h d", h=4),
                                                              in1=bc(sm[:c, 4:8].unsqueeze(2), [c, 4, 64]), op=ALU.mult),
                             reads=[pck, smk], writes=[rk])
                        C.op("dve", lambda e: e.tensor_tensor(out=rb[:c, :], in0=ktok[:c, ck_, 256:512], in1=rt[:c, :], op=ALU.subtract),
                             reads=[rk, ktokk], writes=[rk])
                        yield
                        pc2, pc2k = bank("mi")
                        for h in range(4):
                            C.op("pe", lambda e: e.matmul(pc2[:c, h * 64:(h + 1) * 64], lhsT=Pb[:c, h, :c],
                                                          rhs=rb[:c, h * 64:(h + 1) * 64], start=True, stop=True),
                                 reads=[Pk, rk], writes=[pc2k])
                        C.op("dve", lambda e: e.tensor_tensor(out=rt[:c, :].rearrange("p (h d) -> p h d", h=4),
                                                              in0=pc2[:c, 0:256].rearrange("p (h d) -> p h d", h=4),
                                                              in1=bc(gb[:c, ck_, 4:8].unsqueeze(2), [c, 4, 64]), op=ALU.mult),
                             reads=[pc2k, gbk], writes=[rk])
                        C.op("dve", lambda e: e.tensor_copy(out=vnb[:c, :], in_=rt[:c, :]), reads=[rk], writes=[vnk])
                        yield
                        for h in range(4):
                            oc_ = slice(h * TS + ck_ * c, h * TS + (ck_ + 1) * c)
                            C.op("pe", lambda e: e.matmul(po[:64, oc_], lhsT=Sdb[:, h, :], rhs=qdT[0:64, h, cs],
                                                          start=True, stop=False), reads=[Sdk, qdk], writes=[pok])
                            C.op("pe", lambda e: e.matmul(po[:64, oc_], lhsT=vnb[:c, h * 64:(h + 1) * 64], rhs=inT[:c, h, :c],
                                                          start=False, stop=True), reads=[vnk, inTk], writes=[pok])
                        psu, psuk = bank("mi")
                        for h in range(4):
                            C.op("pe", lambda e: e.matmul(psu[:64, h * 64:(h + 1) * 64], lhsT=kz[:c, ck_, h * 64:(h + 1) * 64],
                                                          rhs=vnb[:c, h * 64:(h + 1) * 64], start=True, stop=True),
                                 reads=[kzk, vnk], writes=[psuk])
                        for h in range(4):
                            C.op("dve", lambda e: e.scalar_tensor_tensor(
                                out=Sd[:, h, :], in0=Sd[:, h, :], scalar=EDrow[0:64, h, c - 1:c], in1=psu[:64, h * 64:(h + 1) * 64],
                                op0=ALU.mult, op1=ALU.add), reads=[Drk, psuk, Sdk], writes=[Sdk])
                        C.op("act", lambda e: e.activation(func=AF.Copy, out=Sdb[:], in_=Sd[:]), reads=[Sdk], writes=[Sdk])
                        yield
                        pr, prk = bank("mi")
                        for h in range(4):
                            C.op("pe", lambda e: e.matmul(pr[:c, h * c:(h + 1) * c], lhsT=qkcT[0:64, 4 + h, cs], rhs=qkcT[0:64, h, cs],
                                                          start=True, stop=True), reads=[qkcTk], writes=[prk])
                        C.op("dve", lambda e: e.tensor_tensor(out=scr[:c, :, :c], in0=pr[:c, 0:4 * c].rearrange("p (h i) -> p h i", h=4),
                                                              in1=rmc[:c, :].rearrange("p (h i) -> p h i", h=4)[:, :, :c], op=ALU.mult),
                             reads=[prk, constk], writes=[scrk])
                        C.op("act", lambda e: e.activation(func=AF.Copy, out=inR[:c, :, :c], in_=scr[:c, :, :c]), reads=[scrk], writes=[inRk])
                        yield
                        for h in range(4):
                            oc_ = slice(h * TS + ck_ * c, h * TS + (ck_ + 1) * c)
                            C.op("pe", lambda e: e.matmul(po2[:64, oc_], lhsT=Srb[:, h, :], rhs=qkcT[0:64, h, cs],
                                                          start=True, stop=False), reads=[Srk, qkcTk], writes=[po2k])
                            C.op("pe", lambda e: e.matmul(po2[:64, oc_], lhsT=vcb[:c, ck_, h * 64:(h + 1) * 64], rhs=inR[:c, h, :c],
                                                          start=False, stop=True), reads=[vck, inRk], writes=[po2k])
                        psu, psuk = bank("mi")
                        for h in range(4):
                            C.op("pe", lambda e: e.matmul(psu[:64, h * 64:(h + 1) * 64], lhsT=kzc[:c, ck_, h * 64:(h + 1) * 64],
                                                          rhs=vcb[:c, ck_, h * 64:(h + 1) * 64], start=True, stop=True),
                                 reads=[vck], writes=[psuk])
                        for h in range(4):
                            C.op("dve", lambda e: e.scalar_tensor_tensor(out=Sr[:, h, :], in0=Sr[:, h, :], scalar=float(gch[h]),
                                                                         in1=psu[:64, h * 64:(h + 1) * 64], op0=ALU.mult, op1=ALU.add),
                                 reads=[psuk, Srk], writes=[Srk])
                        C.op("act", lambda e: e.activation(func=AF.Copy, out=Srb[:], in_=Sr[:]), reads=[Srk], writes=[Srk])

                    yield
                    for which in range(2):
                        pso, psok = (po, pok) if which == 0 else (po2, po2k)
                        pov = pso[:64, 0:4 * TS].rearrange("p (h t) -> p h t", h=4)
                        if which == 0:
                            C.op("act", lambda e: e.activation(out=ot[:, :, :TS], in_=pov, func=AF.Copy), reads=[psok], writes=[otk])
                        else:
                            C.op("dve", lambda e: e.tensor_tensor(out=ot[:, :, :TS], in0=pov,
                                                                  in1=xit[:, :].rearrange("p (h t) -> p h t", h=4)[:, :, :TS], op=ALU.mult),
                                 reads=[psok, constk], writes=[otk])
                        C.op("dve", lambda e: e.tensor_tensor(out=otb[:, :, :TS], in0=ot[:, :, :TS], in1=ot[:, :, :TS], op=ALU.mult),
                             reads=[otk], writes=[otk])
                        yield
                        pm, pmk = bank("mi")
                        for h in range(4):
                            C.op("pe", lambda e: e.matmul(pm[:64, h * TS:(h + 1) * TS], lhsT=onesb[:64, :64], rhs=otb[:, h, :TS],
                                                          start=True, stop=True), reads=[otk, constk], writes=[pmk])
                        C.op("act", lambda e: e.activation(out=ot2[:, :, :TS], in_=pm[:64, 0:4 * TS].rearrange("p (h t) -> p h t", h=4),
                                                           func=AF.Ln, scale=1.0 / 64, bias=epsb[:64, 1:2]),
                             reads=[pmk, constk], writes=[otk])
                        C.op("act", lambda e: e.activation(out=ot2[:, :, :TS], in_=ot2[:, :, :TS], func=AF.Exp, scale=-0.5),
                             reads=[otk], writes=[otk])
                        yield
                        C.op("dve", lambda e: e.tensor_tensor(out=ot[:, :, :TS], in0=ot[:, :, :TS], in1=ot2[:, :, :TS], op=ALU.mult),
                             reads=[otk], writes=[otk])
                        C.op("dve", lambda e: e.tensor_tensor(out=ot[:, :, :TS], in0=ot[:, :, :TS],
                                                               in1=sgT[:, which * 4:which * 4 + 4, :TS], op=ALU.mult),
                             reads=[otk, sgTk], writes=[otk])
                        for s_ in range(2):
                            dst = mixT[s_ * 64:(s_ + 1) * 64, 4 + which * 2:6 + which * 2, col0:col0 + TS]
                            if which == 0:
                                C.op("dve", lambda e: e.tensor_scalar(out=dst, in0=ot[:, s_:4:2, :TS], scalar1=dlngc[:, 0:1],
                                                                      scalar2=None, op0=ALU.mult),
                                     reads=[otk, pk_], writes=[mixk])
                            else:
                                C.op("dve", lambda e: e.tensor_copy(out=dst, in_=ot[:, s_:4:2, :TS]), reads=[otk], writes=[mixk])
                    yield

                def back_tile(ti, t):
                    r0 = t * TS
                    col0 = ti * TS
                    C.dma("sp", xres[:TS, :], xcs[row0 + r0:row0 + r0 + TS, :], xk,
                          reads=[dtk("xcs", row0 + r0)], writes=[xk])
                    for half in range(2):
                        pb, pbk = bank("pj")
                        for k in range(8):
                            C.op("pe", lambda e: e.matmul(pb[:TS, :], lhsT=(mixA if k < 4 else mixT)[:, k, col0:col0 + TS],
                                                          rhs=WA_out[:, k, half * 512:(half + 1) * 512], start=(k == 0), stop=(k == 7)),
                                 reads=[mixk, mixAk, WAK2], writes=[pbk])
                        C.op("dve", lambda e: e.scalar_tensor_tensor(out=xres[:TS, half * 512:(half + 1) * 512],
                                                                     in0=xres[:TS, half * 512:(half + 1) * 512], scalar=float(ALPHA),
                                                                     in1=pb[:TS, :], op0=ALU.mult, op1=ALU.add),
                             reads=[pbk, xk], writes=[xk])
                    layer_norm(xres[:TS, :], TS, g1, b1, pk_, xk)
                    C.dma("pool", x1s[row0 + r0:row0 + r0 + TS, :], xres[:TS, :], xk, reads=[xk], writes=[dtk("x1s", row0 + r0)])


                def gen_attention(blk):
                    if isP:
                        kts = [(kt, 128, None) for kt in range(2 * blk)] + [(2 * blk, 128, 0), (2 * blk + 1, 128, 1)]
                    else:
                        kts = [(kt, 128, None) for kt in range(past // 128)] + [(past // 128, TS, None)]
                    its = [(hd, kt, rows, msk) for hd in range(4) for (kt, rows, msk) in kts]
                    nk = len(kts)

                    def load_v(ii):
                        hd, kt, rows, msk = its[ii]
                        vt, vtk = Vt[ii % 3], Vtk[ii % 3]
                        if kt < past // 128:
                            C.dma("pool", vt[:rows, :], cv[l, sb_, kt * 128:kt * 128 + rows, hd * 128:(hd + 1) * 128], vtk, writes=[vtk])
                        else:
                            C.dma("sp", vt[:rows, :], vsc[kt * 128:kt * 128 + rows, hd * 128:(hd + 1) * 128], vtk,
                                  reads=[dtk("vsc", kt)], writes=[vtk])

                    def emit_s(ii):
                        hd, kt, rows, msk = its[ii]
                        sbk_, sbkk = PS[2 + ii % 2], PSK[2 + ii % 2]
                        kc0 = kt * 128
                        C.op("pe", lambda e: e.matmul(sbk_[:rows, 0:nq], lhsT=KT[:, hd, kc0:kc0 + rows], rhs=QTa[:, hd, 0:nq],
                                                      start=True, stop=True), reads=[ktk[kt], QTk], writes=[sbkk])
                        C.op("pe", lambda e: e.matmul(sbk_[:rows, 256:256 + nq], lhsT=KT[:, hd, kc0:kc0 + rows],
                                                      rhs=QTb[:, hd, 0:nq], start=True, stop=True),
                             reads=[ktk[kt], QTk], writes=[sbkk])
                    load_v(0)
                    if len(its) > 1:
                        load_v(1)
                    emit_s(0)
                    for ii, (hd, kt, rows, msk) in enumerate(its):
                        if ii + 2 < len(its):
                            load_v(ii + 2)
                        if ii + 1 < len(its):
                            emit_s(ii + 1)
                        first = (ii % nk == 0)
                        last = (ii % nk == nk - 1)
                        sbk_, sbkk = PS[2 + ii % 2], PSK[2 + ii % 2]
                        pt, ptk = PT[ii % 2], PTk[ii % 2]
                        C.op("act", lambda e: e.activation(out=pt[:rows, :, 0:nq],
                                                           in_=sbk_[:rows, :].rearrange("p (j q) -> p j q", j=2)[:, :, 0:nq],
                                                           func=AF.Exp, scale=0.125), reads=[sbkk], writes=[ptk])
                        if msk is not None:
                            C.op("dve", lambda e: e.tensor_tensor(out=pt[:rows, :, 0:nq], in0=pt[:rows, :, 0:nq],
                                                                   in1=bc(amask[:rows, msk * 256:msk * 256 + nq].unsqueeze(1), [rows, 2, nq]),
                                                                   op=ALU.mult), reads=[ptk, constk], writes=[ptk])
                        if first:
                            C.op("dve", lambda e: e.memset(OB[:, :], 0.0), writes=[OBK])
                            C.op("dve", lambda e: e.memset(LB[:, :], 0.0), writes=[LBK])
                        vt, vtk = Vt[ii % 3], Vtk[ii % 3]
                        for j in range(2):
                            C.op("pe", lambda e: e.matmul(OB[:, j * 256:j * 256 + nq], lhsT=vt[:rows, :],
                                                          rhs=pt[:rows, j, 0:nq], start=False, stop=False, skip_group_check=True),
                                 reads=[vtk, ptk], writes=[OBK])
                            C.op("pe", lambda e: e.matmul(LB[:, j * 256:j * 256 + nq], lhsT=onesb[:rows, :], rhs=pt[:rows, j, 0:nq],
                                                          start=False, stop=False, skip_group_check=True),
                                 reads=[constk, ptk], writes=[LBK])
                        if last:
                            Lv = LB[:, :].rearrange("p (j q) -> p j q", j=2)[:, :, 0:nq]
                            Ov = OB[:, :].rearrange("p (j q) -> p j q", j=2)[:, :, 0:nq]
                            C.op("dve", lambda e: e.reciprocal(out=Rr[:, :, 0:nq], in_=Lv), reads=[LBK], writes=[atk])
                            C.op("dve", lambda e: e.tensor_tensor(out=Tt[:, :, 0:nq], in0=Ov, in1=Rr[:, :, 0:nq], op=ALU.mult),
                                 reads=[OBK, atk], writes=[atk])
                            C.op("dve", lambda e: e.scalar_tensor_tensor(out=oaf[:, 0:nq], in0=Tt[:, 1, 0:nq], scalar=neglam[:, 0:1],
                                                                         in1=Tt[:, 0, 0:nq], op0=ALU.mult, op1=ALU.add),
                                 reads=[atk, pk_], writes=[atk])
                            C.op("dve", lambda e: e.tensor_tensor(out=oab[:, 0:nq], in0=oaf[:, 0:nq], in1=oaf[:, 0:nq], op=ALU.mult),
                                 reads=[atk], writes=[atk])
                            yield
                            pm, pmk = bank("mi")
                            C.op("pe", lambda e: e.matmul(pm[:, 0:nq], lhsT=onesb[:, :], rhs=oab[:, 0:nq], start=True, stop=True),
                                 reads=[atk, constk], writes=[pmk])
                            C.op("act", lambda e: e.activation(out=Rr[:, 0, 0:nq], in_=pm[:, 0:nq], func=AF.Ln, scale=1.0 / 128,
                                                               bias=epsb[:, 1:2]), reads=[pmk, constk], writes=[atk])
                            C.op("act", lambda e: e.activation(out=Rr[:, 0, 0:nq], in_=Rr[:, 0, 0:nq], func=AF.Exp, scale=-0.5),
                                 reads=[atk], writes=[atk])
                            C.op("dve", lambda e: e.tensor_tensor(out=oaf[:, 0:nq], in0=oaf[:, 0:nq], in1=Rr[:, 0, 0:nq], op=ALU.mult),
                                 reads=[atk], writes=[atk])
                            C.op("dve", lambda e: e.tensor_scalar(out=mixA[:, hd, 0:nq], in0=oaf[:, 0:nq], scalar1=dngc[:, 0:1],
                                                                  scalar2=float(1.0 - lam_init), op0=ALU.mult, op1=ALU.mult),
                                 reads=[atk, pk_], writes=[mixAk])
                        yield

                def run_gens(*gens, weights=None):
                    live = list(gens)
                    wts = {id(g): (weights[i] if weights else 1) for i, g in enumerate(gens)}
                    while live:
                        for g in list(live):
                            for _ in range(wts[id(g)]):
                                try:
                                    next(g)
                                except StopIteration:
                                    live.remove(g)
                                    break

                def chain(*gs):
                    for g_ in gs:
                        yield from g_

                for blk in range(nblk):
                    t0_ = blk * TPB
                    run_gens(gen_frontA(0, t0_, "pj"), gen_frontB(t0_, 0))
                    if TPB == 2:
                        run_gens(gen_chunks(0), gen_frontA(1, t0_ + 1, "s"))
                        n_att = 4 * (2 * blk + 2)
                        run_gens(gen_attention(blk), chain(gen_frontB(t0_ + 1, TS), gen_chunks(TS)),
                                 weights=[2 if n_att > 60 else 1, 1])
                    else:
                        run_gens(gen_attention(blk), gen_chunks(0))
                    for ti in range(TPB):
                        back_tile(ti, t0_ + ti)

                if STOP == 1:
                    C.finish()
                    return nc
                ddst = pdl[l] if isP else sdlo[l, sb_]
                rdst = prt[l] if isP else srto[l, sb_]
                C.dma("pool", ddst.rearrange("(h d) e -> d h e", d=64), Sd[:], Sdk, reads=[Sdk])
                C.dma("pool", rdst.rearrange("(h d) e -> d h e", d=64), Sr[:], Srk, reads=[Srk])

        if STOP == 20 + l:
            C.finish()
            return nc
        C.barrier()
        WA_up = WA[:, 0:8 * DFF].rearrange("p (k n) -> p k n", k=8)
        WA_dn = WA[:, 8 * DFF:8 * DFF + 32 * D].rearrange("p (k n) -> p k n", k=32)
        WAK2 = Tk("WA2f")
        for k in range(8):
            for c0_ in range(0, DFF, 1024):
                C.dma("pool", WA_up[:, k, c0_:c0_ + 1024], w_up[l, k * 128:(k + 1) * 128, c0_:c0_ + 1024], WAK, writes=[WAK])
        for k in range(32):
            C.dma("pool", WA_dn[:, k, :], w_down[l, k * 128:(k + 1) * 128, :], WAK2, writes=[WAK2])
        with ExitStack() as st:
            def T(name, shape, dt=F32):
                return st.enter_context(nc.sbuf_tensor("%s_f%d" % (name, l), list(shape), dt))
            g2 = T("g2", [128, D]); b2 = T("b2", [128, D]); pk2 = Tk()
            C.dma("sp", g2[:], ln2g[l:l + 1, :].partition_broadcast(128), pk2, writes=[pk2])
            C.dma("sp", b2[:], ln2b[l:l + 1, :].partition_broadcast(128), pk2, writes=[pk2])
            x1 = T("x1", [128, 6, D]); x1k = [Tk() for _ in range(6)]
            x1b = T("x1b", [128, D], BF16); x1bk = Tk()
            x1T = T("x1T", [128, 2, 8, 256], BF16); x1Tk = [Tk(), Tk()]
            hidT = T("hidT", [128, 32, 256], BF16); hidk = Tk()
            rl = [T("rl0", [128, 256]), T("rl1", [128, 256])]; rlk = [Tk(), Tk()]
            blocks = []
            for b0_ in range(0, Lp, 256):
                blocks.append((b0_, 128, min(2, (Lp - b0_) // 128)))
            for b_ in range(NS):
                blocks.append((Lp + b_ * Ls, Ls, 1))

            def gen_prep(bi):
                rb0, TS, ntl = blocks[bi]
                sb3, xs2 = bi % 3, bi % 2
                for ti in range(ntl):
                    r0 = rb0 + ti * TS
                    sl = sb3 * 2 + ti
                    C.dma("sp", x1[:TS, sl, :], x1s[r0:r0 + TS, :], x1k[sl], reads=[dtk("x1s", r0)], writes=[x1k[sl]])
                yield
                for ti in range(ntl):
                    sl = sb3 * 2 + ti
                    C.op("act", lambda e: e.activation(out=x1b[:TS, :], in_=x1[:TS, sl, :], func=AF.Copy), reads=[x1k[sl]], writes=[x1bk])
                    yield
                    pb, pbk = bank("mi")
                    pbb = pb[:].bitcast(BF16)
                    for k in range(8):
                        C.op("pe", lambda e: e.transpose(out=pbb[:, k * TS:(k + 1) * TS], in_=x1b[:TS, k * 128:(k + 1) * 128],
                                                         identity=identb[:TS, :TS]), reads=[x1bk, constk], writes=[pbk])
                    C.op("dve", lambda e: e.tensor_copy(out=x1T[:, xs2, :, ti * TS:(ti + 1) * TS],
                                                        in_=pbb[:, 0:8 * TS].rearrange("p (k t) -> p k t", k=8)),
                         reads=[pbk], writes=[x1Tk[xs2]])
                    yield

            def gen_ln(bi):
                rb0, TS, ntl = blocks[bi]
                sb3 = bi % 3
                for ti in range(ntl):
                    r0 = rb0 + ti * TS
                    sl = sb3 * 2 + ti
                    xap = x1[:TS, sl, :]
                    xk_ = x1k[sl]
                    n = TS
                    C.op("dve", lambda e: e.bn_stats(out=lnst[:n, 0, :], in_=xap[:, 0:512]), reads=[xk_], writes=[lnk])
                    C.op("dve", lambda e: e.bn_stats(out=lnst[:n, 1, :], in_=xap[:, 512:1024]), reads=[xk_], writes=[lnk])
                    yield
                    C.op("dve", lambda e: e.bn_aggr(out=lnmv[:n, :], in_=lnst[:n, :, :].rearrange("p a b -> p (a b)")),
                         reads=[lnk], writes=[lnk])
                    yield
                    C.op("act", lambda e: e.activation(out=lnmv[:n, 1:2], in_=lnmv[:n, 1:2], func=AF.Ln, bias=epsb[:n, 0:1]),
                         reads=[lnk, constk], writes=[lnk])
                    yield
                    C.op("act", lambda e: e.activation(out=lnmv[:n, 1:2], in_=lnmv[:n, 1:2], func=AF.Exp, scale=-0.5),
                         reads=[lnk], writes=[lnk])
                    yield
                    C.op("dve", lambda e: e.tensor_scalar(out=xap, in0=xap, scalar1=lnmv[:n, 0:1], scalar2=lnmv[:n, 1:2],
                                                          op0=ALU.subtract, op1=ALU.mult), reads=[lnk, xk_], writes=[xk_])
                    yield
                    C.op("dve", lambda e: e.tensor_tensor(out=xap, in0=xap, in1=g2[:n, :], op=ALU.mult), reads=[xk_, pk2], writes=[xk_])
                    yield
                    C.op("dve", lambda e: e.tensor_tensor(out=xap, in0=xap, in1=b2[:n, :], op=ALU.add), reads=[xk_, pk2], writes=[xk_])
                    if l < DEPTH - 1:
                        C.dma("pool", xcs[r0:r0 + TS, :], xap, xk_, reads=[xk_], writes=[dtk("xcs", r0)])
                    else:
                        if r0 < Lp:
                            C.dma("pool", yp[r0:r0 + TS, :], xap, xk_, reads=[xk_])
                        else:
                            C.dma("pool", ys[r0 - Lp:r0 - Lp + TS, :], xap, xk_, reads=[xk_])
                    yield

            def step_side(side):
                while side:
                    g_ = side[0]
                    try:
                        next(g_)
                        side.append(side.pop(0))
                        return
                    except StopIteration:
                        side.pop(0)

            def run_main(bi, side):
                rb0, TS, ntl = blocks[bi]
                nb = TS * ntl
                sb3, xs2 = bi % 3, bi % 2
                for f in range(32):
                    pb, pbk = bank("pj")
                    for k in range(8):
                        C.op("pe", lambda e: e.matmul(pb[:, 0:nb], lhsT=WA_up[:, k, f * 128:(f + 1) * 128], rhs=x1T[:, xs2, k, 0:nb],
                                                      start=(k == 0), stop=(k == 7)), reads=[x1Tk[xs2], WAK], writes=[pbk])
                    r_, rk_ = rl[f % 2], rlk[f % 2]
                    C.op("act", lambda e: e.activation(out=r_[:, 0:nb], in_=pb[:, 0:nb], func=AF.Relu), reads=[pbk], writes=[rk_])
                    C.op("dve", lambda e: e.tensor_tensor(out=hidT[:, f, 0:nb], in0=r_[:, 0:nb], in1=r_[:, 0:nb], op=ALU.mult),
                         reads=[rk_], writes=[hidk])
                    if f >= 2:
                        step_side(side)
                for ti in range(ntl):
                    sl = sb3 * 2 + ti
                    for half in range(2):
                        pb, pbk = bank("pj")
                        for f in range(32):
                            C.op("pe", lambda e: e.matmul(pb[:TS, :], lhsT=hidT[:, f, ti * TS:(ti + 1) * TS],
                                                          rhs=WA_dn[:, f, half * 512:(half + 1) * 512], start=(f == 0), stop=(f == 31)),
                                 reads=[hidk, WAK2], writes=[pbk])
                        C.op("dve", lambda e: e.scalar_tensor_tensor(out=x1[:TS, sl, half * 512:(half + 1) * 512],
                                                                     in0=x1[:TS, sl, half * 512:(half + 1) * 512], scalar=float(ALPHA),
                                                                     in1=pb[:TS, :], op0=ALU.mult, op1=ALU.add),
                             reads=[pbk, x1k[sl]], writes=[x1k[sl]])
                        step_side(side)
                while side:
                    step_side(side)

            for _ in gen_prep(0):
                pass
            for bi in range(len(blocks)):
                side = []
                if bi >= 1:
                    side.append(gen_ln(bi - 1))
                if bi + 1 < len(blocks):
                    side.append(gen_prep(bi + 1))
                run_main(bi, side)
            for _ in gen_ln(len(blocks) - 1):
                pass
    C.finish()
    return nc


def make_consts(Lp, Ls, PAST):
    c = {}
    c["c_ident"] = np.eye(128, dtype=np.float32)
    half = 32
    inv_freq = (10000.0 ** (-np.arange(half, dtype=np.float32) / half)).astype(np.float32)

    def rot(pos):
        ang = pos.astype(np.float32)[:, None] * inv_freq[None, :]
        cos = np.cos(ang).astype(np.float32)
        sin = np.sin(ang).astype(np.float32)
        return np.concatenate([cos, cos, -sin, sin], axis=1).astype(np.float32)

    c["c_rotp"] = rot(np.arange(Lp))
    c["c_rots"] = rot(PAST + np.arange(Ls))
    k = np.arange(128)[:, None]
    qq = np.arange(256)[None, :]
    m0 = ((0 + k // 64) <= (qq // 64)).astype(np.float32)
    m1 = ((2 + k // 64) <= (qq // 64)).astype(np.float32)
    c["c_amask"] = np.concatenate([m0, m1], axis=1)
    j = np.arange(64)[:, None]
    i = np.arange(64)[None, :]
    c["c_triu"] = (j <= i).astype(np.float32)
    c["c_striu"] = (j < i).astype(np.float32)
    lg = np.log(1.0 - 2.0 ** (-5.0 - np.arange(4, dtype=np.float64)))
    rm = np.zeros((64, 4, 64), np.float64)
    for h in range(4):
        rm[:, h, :] = np.exp(-lg[h] * (j + 1.0)) * (j <= i)
    c["c_rm"] = rm.reshape(64, 256).astype(np.float32)
    xi = np.zeros((64, 4, 128), np.float64)
    xis = np.zeros((64, 4, Ls), np.float64)
    for h in range(4):
        xi[:, h, :] = np.exp(lg[h] * ((np.arange(128) % 64) + 1.0))[None, :]
        xis[:, h, :] = np.exp(lg[h] * (np.arange(Ls) + 1.0))[None, :]
    c["c_xi"] = xi.reshape(64, 512).astype(np.float32)
    c["c_xis"] = xis.reshape(64, 4 * Ls).astype(np.float32)
    zeta = np.zeros((128, 4), np.float64)
    zetas = np.zeros((Ls, 4), np.float64)
    for h in range(4):
        zeta[:, h] = np.exp(lg[h] * (63.0 - (np.arange(128) % 64)))
        zetas[:, h] = np.exp(lg[h] * (Ls - 1.0 - np.arange(Ls)))
    c["c_zeta"] = zeta.astype(np.float32)
    c["c_zetas"] = zetas.astype(np.float32)
    bo = np.zeros((128, 128), np.float32)
    bo[:64, :64] = 1.0
    bo[64:, 64:] = 1.0
    c["c_bones"] = bo
    return c


_NC_CACHE = {}


def run(inputs, Lp, NS, Ls, PAST, ncores):
    key = (Lp, NS, Ls, PAST)
    if key not in _NC_CACHE:
        _NC_CACHE[key] = build(Lp, NS, Ls, PAST)
    nc = _NC_CACHE[key]
    f = lambda a: np.ascontiguousarray(np.asarray(a, dtype=np.float32))
    consts = make_consts(Lp, Ls, PAST)
    I = {k: f(v) for k, v in inputs.items()}
    in_maps = []
    for i in range(ncores):
        sl = slice(i * NS, (i + 1) * NS)
        m = dict(consts)
        m["xp"] = f(I["x_prompt"][i])
        m["xs"] = f(I["x_sample"][sl].reshape(NS * Ls, D))
        m["ck"] = f(I["cache_k"][:, sl].reshape(DEPTH, NS, PAST, 512))
        m["cv"] = f(I["cache_v"][:, sl].reshape(DEPTH, NS, PAST, 512))
        m["sdl"] = f(I["state_delta"][:, sl].reshape(DEPTH, NS, 256, 64))
        m["scv"] = f(I["state_conv"][:, sl])
        m["srt"] = f(I["state_ret"][:, sl].reshape(DEPTH, NS, 256, 64))
        m["ln0g"] = f(I["ln0_g"].reshape(1, D)); m["ln0b"] = f(I["ln0_b"].reshape(1, D))
        m["w_in"] = I["w_in"]
        m["lamq1"] = I["lam_q1"]; m["lamk1"] = I["lam_k1"]; m["lamq2"] = I["lam_q2"]; m["lamk2"] = I["lam_k2"]
        m["dng"] = I["diff_norm_g"]; m["convw"] = I["conv_w"]; m["alog"] = I["a_log"]; m["dtb"] = I["dt_bias"]
        m["dlng"] = I["delta_norm_g"]; m["w_out"] = I["w_out"]
        m["ln1g"] = I["ln1_g"]; m["ln1b"] = I["ln1_b"]; m["w_up"] = I["w_up"]; m["w_down"] = I["w_down"]
        m["ln2g"] = I["ln2_g"]; m["ln2b"] = I["ln2_b"]
        in_maps.append(m)
    res = run_bass_kernel_spmd(nc, in_maps, core_ids=list(range(ncores)))
    R = res.results
    B = ncores
    st = lambda name: np.stack([np.asarray(R[i][name]) for i in range(B)])
    y_p = st("yp")
    y_s = st("ys").reshape(B * NS, Ls, D)
    p_k = st("pk").transpose(1, 0, 2, 3).reshape(DEPTH, B, Lp, 8, 64)
    p_v = st("pv").transpose(1, 0, 2, 3).reshape(DEPTH, B, Lp, 4, 128)
    p_d = st("pdl").transpose(1, 0, 2, 3).reshape(DEPTH, B, 4, 64, 64)
    p_c = st("pcv").transpose(1, 0, 2, 3)
    p_r = st("prt").transpose(1, 0, 2, 3).reshape(DEPTH, B, 4, 64, 64)
    s_k = st("sk").transpose(1, 0, 2, 3).reshape(DEPTH, B * NS, Ls, 8, 64)
    s_v = st("sv").transpose(1, 0, 2, 3).reshape(DEPTH, B * NS, Ls, 4, 128)
    s_d = st("sdlo").transpose(1, 0, 2, 3, 4).reshape(DEPTH, B * NS, 4, 64, 64)
    s_c = st("scvo").transpose(1, 0, 2, 3, 4).reshape(DEPTH, B * NS, 3, 768)
    s_r = st("srto").transpose(1, 0, 2, 3, 4).reshape(DEPTH, B * NS, 4, 64, 64)
    outs = (y_p, y_s, p_k, p_v, p_d, p_c, p_r, s_k, s_v, s_d, s_c, s_r)
    return tuple(np.ascontiguousarray(o.astype(np.float32)) for o in outs)


def kernel(**inputs):
    return run(inputs, 4096, 2, 32, 2048, NCORES)
```

```python
import math
import numpy as np
import ml_dtypes
import concourse.bass as bass
import concourse.mybir as mybir
from concourse.bass_utils import run_bass_kernel_spmd

F32 = mybir.dt.float32
BF16 = mybir.dt.bfloat16
AF = mybir.ActivationFunctionType
ALU = mybir.AluOpType
AX = mybir.AxisListType

D = 1024
NIN = 3592
DFF = 4096
DEPTH = 2
ALPHA = (2 * DEPTH) ** 0.25
LN_EPS = 1e-5
NORM_EPS = 1e-6
NCORES = 8


class Tk:
    __slots__ = ("w", "r", "dsem", "dcnt", "name")

    def __init__(self, name=""):
        self.w = None
        self.r = {}
        self.dsem = None
        self.dcnt = 0
        self.name = name


class Ctx:
    def __init__(self, nc):
        self.nc = nc
        self.E = {"pe": nc.tensor, "dve": nc.vector, "act": nc.scalar, "pool": nc.gpsimd,
                  "sp": nc.sync}
        self.sem = {k: nc.alloc_semaphore("es_" + k) for k in self.E}
        self.cnt = {k: 0 for k in self.E}
        self.seen = {k: {} for k in self.E}
        self.dsems = []
        self.nd = 0

    def _wait(self, e, dep):
        key, sem, val = dep
        if self.seen[e].get(key, 0) >= val:
            return
        self.E[e].wait_ge(sem, val)
        self.seen[e][key] = val

    def _deps(self, e, reads, writes):
        for t in reads:
            if t.w is not None:
                self._wait(e, t.w)
        for t in writes:
            if t.w is not None:
                self._wait(e, t.w)
            for d in t.r.values():
                self._wait(e, d)

    def _mark(self, me, reads, writes):
        for t in reads:
            old = t.r.get(me[0])
            if old is None or old[2] < me[2]:
                t.r[me[0]] = me
        for t in writes:
            t.w = me
            t.r = {}

    def op(self, e, fn, reads=(), writes=()):
        self._deps(e, reads, writes)
        ins = fn(self.E[e])
        self.cnt[e] += 1
        ins.then_inc(self.sem[e], 1)
        me = (e, self.sem[e], self.cnt[e])
        if e == "pe":
            self.seen[e][e] = self.cnt[e]
        self._mark(me, reads, writes)
        return ins

    def dma(self, q, out, in_, owner, reads=(), writes=(), **kw):
        self._deps(q, reads, writes)
        if owner.dsem is None:
            owner.dsem = self.nc.alloc_semaphore("ds%d" % self.nd)
            self.nd += 1
            self.dsems.append(owner)
        ins = self.E[q].dma_start(out=out, in_=in_, **kw)
        owner.dcnt += 16
        ins.then_inc(owner.dsem, 16)
        me = ("d%d" % id(owner), owner.dsem, owner.dcnt)
        self._mark(me, reads, writes)
        return ins

    def barrier(self):
        for e in self.E:
            for f in self.E:
                if f != e and self.cnt[f] > 0:
                    self._wait(e, (f, self.sem[f], self.cnt[f]))
            for o in self.dsems:
                if o.dcnt > 0:
                    self._wait(e, ("d%d" % id(o), o.dsem, o.dcnt))

    def finish(self):
        self.barrier()


def bc(ap, shape):
    return ap.to_broadcast(list(shape))


import os
from contextlib import ExitStack
STOP = int(os.environ.get('KSTOP', '99'))


def build(Lp=4096, NS=2, Ls=32, PAST=2048):
    nc = bass.Bass("TRN2", target_bir_lowering=False)
    C = Ctx(nc)
    Ltot = Lp + NS * Ls
    LK = max(Lp, PAST + Ls)

    def din(name, shape, dt=F32):
        return nc.dram_tensor(name, list(shape), dt, kind="ExternalInput").ap()

    def dout(name, shape):
        return nc.dram_tensor(name, list(shape), F32, kind="ExternalOutput").ap()

    xp = din("xp", [Lp, D])
    xs = din("xs", [NS * Ls, D])
    ck = din("ck", [DEPTH, NS, PAST, 512])
    cv = din("cv", [DEPTH, NS, PAST, 512])
    sdl = din("sdl", [DEPTH, NS, 256, 64])
    scv = din("scv", [DEPTH, NS, 3, 768])
    srt = din("srt", [DEPTH, NS, 256, 64])
    ln0g = din("ln0g", [1, D]); ln0b = din("ln0b", [1, D])
    w_in = din("w_in", [DEPTH, D, NIN])
    lamq1 = din("lamq1", [DEPTH, 64]); lamk1 = din("lamk1", [DEPTH, 64])
    lamq2 = din("lamq2", [DEPTH, 64]); lamk2 = din("lamk2", [DEPTH, 64])
    dng = din("dng", [DEPTH, 128])
    convw = din("convw", [DEPTH, 4, 768])
    alog = din("alog", [DEPTH, 4]); dtb = din("dtb", [DEPTH, 4])
    dlng = din("dlng", [DEPTH, 64])
    w_out = din("w_out", [DEPTH, D, D])
    ln1g = din("ln1g", [DEPTH, D]); ln1b = din("ln1b", [DEPTH, D])
    w_up = din("w_up", [DEPTH, D, DFF])
    w_down = din("w_down", [DEPTH, DFF, D])
    ln2g = din("ln2g", [DEPTH, D]); ln2b = din("ln2b", [DEPTH, D])
    c_ident = din("c_ident", [128, 128])
    c_rotp = din("c_rotp", [Lp, 128])
    c_rots = din("c_rots", [Ls, 128])
    c_amask = din("c_amask", [128, 512])
    c_triu = din("c_triu", [64, 64])
    c_striu = din("c_striu", [64, 64])
    c_rm = din("c_rm", [64, 256])
    c_xi = din("c_xi", [64, 512])
    c_xis = din("c_xis", [64, 4 * Ls])
    c_zeta = din("c_zeta", [128, 4])
    c_zetas = din("c_zetas", [Ls, 4])
    c_bones = din("c_bones", [128, 128])

    yp = dout("yp", [Lp, D]); ys = dout("ys", [NS * Ls, D])
    pk = dout("pk", [DEPTH, Lp, 512]); pv = dout("pv", [DEPTH, Lp, 512])
    pdl = dout("pdl", [DEPTH, 256, 64]); pcv = dout("pcv", [DEPTH, 3, 768])
    prt = dout("prt", [DEPTH, 256, 64])
    sk = dout("sk", [DEPTH, NS * Ls, 512]); sv = dout("sv", [DEPTH, NS * Ls, 512])
    sdlo = dout("sdlo", [DEPTH, NS, 256, 64]); scvo = dout("scvo", [DEPTH, NS, 3, 768])
    srto = dout("srto", [DEPTH, NS, 256, 64])
    x1s = nc.dram_tensor("x1s", [Ltot, D], F32).ap()
    xcs = nc.dram_tensor("xcs", [Ltot, D], F32).ap()
    vsc = nc.dram_tensor("vsc", [LK, 512], BF16).ap()
    dram_tk = {}

    def dtk(name, r0):
        k = (name, r0)
        if k not in dram_tk:
            dram_tk[k] = Tk()
        return dram_tk[k]

    PS = [nc.alloc_psum_tensor("ps%d" % i, [128, 512], F32) for i in range(8)]
    PSK = [Tk("ps%d" % i) for i in range(8)]
    rot_state = {"pj": 0, "mi": 0, "s": 0}

    def bank(kind):
        base, n = {"pj": (0, 2), "s": (2, 2), "mi": (6, 2)}[kind]
        i = base + rot_state[kind] % n
        rot_state[kind] += 1
        return PS[i], PSK[i]

    OB, OBK = PS[4], PSK[4]
    LB, LBK = PS[5], PSK[5]

    def sb(name, shape, dt=F32):
        return nc.alloc_sbuf_tensor(name, list(shape), dt)

    identf = sb("identf", [128, 128]); identb = sb("identb", [128, 128], BF16)
    onesb = sb("onesb", [128, 128], BF16)
    onesf = sb("onesf", [128, 128])
    bonesf = sb("bonesf", [128, 128])
    triu = sb("triu", [64, 64]); striu = sb("striu", [64, 64])
    rmc = sb("rmc", [64, 256]); xic = sb("xic", [64, 512]); xisc = sb("xisc", [64, 4 * Ls])
    zetac = sb("zetac", [128, 4]); zetasc = sb("zetasc", [Ls, 4])
    amask = sb("amask", [128, 512], BF16)
    epsb = sb("epsb", [128, 2])
    constk = Tk("const")
    q = "sp"
    C.dma(q, identf[:], c_ident[:, :], constk, writes=[constk])
    C.dma(q, bonesf[:], c_bones[:, :], constk, writes=[constk])
    C.dma(q, triu[:], c_triu[:, :], constk, writes=[constk])
    C.dma(q, striu[:], c_striu[:, :], constk, writes=[constk])
    C.dma(q, rmc[:], c_rm[:, :], constk, writes=[constk])
    C.dma(q, xic[:], c_xi[:, :], constk, writes=[constk])
    C.dma(q, xisc[:], c_xis[:, :], constk, writes=[constk])
    C.dma(q, zetac[:], c_zeta[:, :], constk, writes=[constk])
    C.dma(q, zetasc[:], c_zetas[:, :], constk, writes=[constk])
    C.dma("pool", amask[:], c_amask[:, :], constk, writes=[constk])
    C.op("dve", lambda e: e.tensor_copy(out=identb[:], in_=identf[:]), reads=[constk], writes=[constk])
    C.op("dve", lambda e: e.memset(onesb[:], 1.0), writes=[constk])
    C.op("dve", lambda e: e.memset(onesf[:], 1.0), writes=[constk])
    C.op("dve", lambda e: e.memset(epsb[:, 0:1], LN_EPS), writes=[constk])
    C.op("dve", lambda e: e.memset(epsb[:, 1:2], NORM_EPS), writes=[constk])

    WA = sb("WA", [128, 65536], BF16)
    WAK = Tk("WA")

    lnst = sb("lnst", [128, 2, 6]); lnmv = sb("lnmv", [128, 2]); lnk = Tk("ln")

    def layer_norm(xap, n, gt, bt, gk, xk):
        tks = [xk]
        C.op("dve", lambda e: e.bn_stats(out=lnst[:n, 0, :], in_=xap[:, 0:512]), reads=tks, writes=[lnk])
        C.op("dve", lambda e: e.bn_stats(out=lnst[:n, 1, :], in_=xap[:, 512:1024]), reads=tks, writes=[lnk])
        C.op("dve", lambda e: e.bn_aggr(out=lnmv[:n, :], in_=lnst[:n, :, :].rearrange("p a b -> p (a b)")),
             reads=[lnk], writes=[lnk])
        C.op("act", lambda e: e.activation(out=lnmv[:n, 1:2], in_=lnmv[:n, 1:2], func=AF.Ln, bias=epsb[:n, 0:1]),
             reads=[lnk, constk], writes=[lnk])
        C.op("act", lambda e: e.activation(out=lnmv[:n, 1:2], in_=lnmv[:n, 1:2], func=AF.Exp, scale=-0.5),
             reads=[lnk], writes=[lnk])
        C.op("dve", lambda e: e.tensor_scalar(out=xap, in0=xap, scalar1=lnmv[:n, 0:1], scalar2=lnmv[:n, 1:2],
                                              op0=ALU.subtract, op1=ALU.mult),
             reads=[lnk] + tks, writes=tks)
        C.op("dve", lambda e: e.tensor_tensor(out=xap, in0=xap, in1=gt[:n, :], op=ALU.mult),
             reads=tks + [gk], writes=tks)
        C.op("dve", lambda e: e.tensor_tensor(out=xap, in0=xap, in1=bt[:n, :], op=ALU.add),
             reads=tks + [gk], writes=tks)

    seqs = [dict(name="p", L=Lp, past=0, TS=128, c=64, row0=0, b=0)]
    for b_ in range(NS):
        seqs.append(dict(name="s%d" % b_, L=Ls, past=PAST, TS=Ls, c=Ls, row0=Lp + b_ * Ls, b=b_))

    with ExitStack() as st0:
        g0 = st0.enter_context(nc.sbuf_tensor("g0", [128, D], F32))
        b0 = st0.enter_context(nc.sbuf_tensor("b0", [128, D], F32))
        xl = st0.enter_context(nc.sbuf_tensor("xl", [128, 2, D], F32))
        xlk = [Tk(), Tk()]
        pk0 = Tk()
        C.dma("sp", g0[:], ln0g[0:1, :].partition_broadcast(128), pk0, writes=[pk0])
        C.dma("sp", b0[:], ln0b[0:1, :].partition_broadcast(128), pk0, writes=[pk0])
        tiles0 = [(xp, r, r, 128) for r in range(0, Lp, 128)]
        for b_ in range(NS):
            tiles0.append((xs, b_ * Ls, Lp + b_ * Ls, Ls))
        for i0, (src0, sr0, dr0, n0) in enumerate(tiles0):
            sl0 = i0 % 2
            C.dma("sp", xl[:n0, sl0, :], src0[sr0:sr0 + n0, :], xlk[sl0], writes=[xlk[sl0]])
            layer_norm(xl[:n0, sl0, :], n0, g0, b0, pk0, xlk[sl0])
            C.dma("pool", xcs[dr0:dr0 + n0, :], xl[:n0, sl0, :], xlk[sl0], reads=[xlk[sl0]], writes=[dtk("xcs", dr0)])
    if STOP == 0:
        C.finish()
        return nc

    for l in range(DEPTH):
        lam_init = 0.8 - 0.6 * math.exp(-0.3 * l)
        C.barrier()
        WA_in = WA[:, 0:8 * NIN].rearrange("p (k n) -> p k n", k=8)
        WA_out = WA[:, 8 * NIN:8 * NIN + 8192].rearrange("p (k n) -> p k n", k=8)
        WAK2 = Tk("WA2")
        for k in range(8):
            for c0_ in range(0, NIN, 1024):
                c1_ = min(NIN, c0_ + 1024)
                C.dma("pool", WA_in[:, k, c0_:c1_], w_in[l, k * 128:(k + 1) * 128, c0_:c1_], WAK, writes=[WAK])
        for k in range(8):
            C.dma("pool", WA_out[:, k, :], w_out[l, k * 128:(k + 1) * 128, :], WAK2, writes=[WAK2])
        with ExitStack() as st:
            def T(name, shape, dt=F32):
                return st.enter_context(nc.sbuf_tensor("%s_%d" % (name, l), list(shape), dt))
            wa_off = [8 * NIN + 8192]

            def WT(shape):
                n = 1
                for d_ in shape[1:]:
                    n *= d_
                ap = WA[:, wa_off[0]:wa_off[0] + n]
                wa_off[0] += n
                assert wa_off[0] <= 65536, wa_off[0]
                if len(shape) == 3:
                    ap = ap.rearrange("p (a b) -> p a b", a=shape[1])
                return ap
            KT = WT([128, 4, LK])
            ktk = [Tk("kt%d" % i) for i in range((LK + 127) // 128)]
            g1 = T("g1", [128, D]); b1 = T("b1", [128, D])
            pk_ = Tk("params")
            C.dma("sp", g1[:], ln1g[l:l + 1, :].partition_broadcast(128), pk_, writes=[pk_])
            C.dma("sp", b1[:], ln1b[l:l + 1, :].partition_broadcast(128), pk_, writes=[pk_])
            lamt = T("lamt", [128, 4, 64]); lamr = T("lamr", [128, 4]); neglam = T("neglam", [128, 1])
            for i, src in enumerate((lamq1, lamk1, lamq2, lamk2)):
                C.dma("sp", lamt[:, i, :], src[l:l + 1, :].partition_broadcast(128), pk_, writes=[pk_])
            C.op("dve", lambda e: e.tensor_tensor(out=lamt[:, 0, :], in0=lamt[:, 0, :], in1=lamt[:, 1, :], op=ALU.mult),
                 reads=[pk_], writes=[pk_])
            C.op("dve", lambda e: e.tensor_tensor(out=lamt[:, 2, :], in0=lamt[:, 2, :], in1=lamt[:, 3, :], op=ALU.mult),
                 reads=[pk_], writes=[pk_])
            C.op("dve", lambda e: e.tensor_reduce(out=lamr[:, 0:1], in_=lamt[:, 0, :], axis=AX.X, op=ALU.add),
                 reads=[pk_], writes=[pk_])
            C.op("dve", lambda e: e.tensor_reduce(out=lamr[:, 1:2], in_=lamt[:, 2, :], axis=AX.X, op=ALU.add),
                 reads=[pk_], writes=[pk_])
            C.op("act", lambda e: e.activation(out=lamr[:, 2:4], in_=lamr[:, 0:2], func=AF.Exp),
                 reads=[pk_], writes=[pk_])
            C.op("dve", lambda e: e.scalar_tensor_tensor(out=neglam[:], in0=lamr[:, 3:4], scalar=-lam_init,
                                                         in1=lamr[:, 2:3], op0=ALU.add, op1=ALU.subtract),
                 reads=[pk_], writes=[pk_])
            dngc = T("dngc", [128, 1]); dlngc = T("dlngc", [64, 1])
            C.dma("sp", dngc[:], dng[l:l + 1, :].rearrange("o e -> e o"), pk_, writes=[pk_])
            C.dma("sp", dlngc[:], dlng[l:l + 1, :].rearrange("o e -> e o"), pk_, writes=[pk_])
            cw = T("cw", [128, 6, 4])
            for ch in range(6):
                C.dma("sp", cw[:, ch, :], convw[l, :, ch * 128:(ch + 1) * 128].rearrange("j p -> p j"),
                      pk_, writes=[pk_], allow_slow_non_contiguous=True)
            nA = T("nA", [128, 4]); dtbb = T("dtbb", [128, 4])
            C.dma("sp", nA[:], alog[l:l + 1, :].partition_broadcast(128), pk_, writes=[pk_])
            C.dma("sp", dtbb[:], dtb[l:l + 1, :].partition_broadcast(128), pk_, writes=[pk_])
            C.op("act", lambda e: e.activation(out=nA[:], in_=nA[:], func=AF.Exp), reads=[pk_], writes=[pk_])
            C.op("dve", lambda e: e.tensor_scalar(out=nA[:], in0=nA[:], scalar1=-1.0, scalar2=None, op0=ALU.mult),
                 reads=[pk_], writes=[pk_])
            if STOP == 6:
                C.finish()
                return nc

            xres = T("xres", [128, D]); xk = Tk("xres")
            xbf = WT([128, D]); xbfk = Tk()
            kbf = xbf[:, 0:512]; kbfk = xbfk
            xT = WT([128, 8, 256]); xTk = Tk()
            mixT = xT; mixk = xTk
            mixA = WT([128, 4, 256]); mixAk = Tk()
            QTa = WT([128, 4, 256]); QTb = WT([128, 4, 256]); QTk = Tk()
            C.op("pool", lambda e: e.memset(QTa[64:128, :, :], 0.0), writes=[QTk])
            C.op("pool", lambda e: e.memset(QTb[0:64, :, :], 0.0), writes=[QTk])
            rot = T("rot", [128, 128]); rotk = Tk()
            tmpA = T("tmpA", [128, 512]); tmpB = T("tmpB", [128, 512]); tmpk = Tk()
            kout = T("kout", [128, 512]); koutk = Tk()
            vout = T("vout", [128, 512]); voutk = Tk()
            vbf = WT([128, 512]); vbfk = Tk()
            uT = T("uT", [128, 6, 131]); uTk = Tk()
            cT = T("cT", [128, 6, 128]); cTk = Tk()
            cE = T("cE", [128, 6, 128]); cEk = Tk()
            nTh = WT([128, 8, 128]); nTk = Tk()
            qdT = WT([128, 4, 128]); qdk = Tk()
            g4 = T("g4", [128, 264]); g4e = T("g4e", [128, 264]); g4k = Tk()
            sgT = T("sgT", [64, 8, 128]); sgTk = Tk()
            qkc = T("qkc", [128, 512]); qkck = Tk()
            qkcT = WT([128, 8, 128]); qkcTk = Tk()
            sg6 = g4[:, 0:256]; sg6e = g4e[:, 0:256]; sg6k = g4k
            ktok = T("ktok", [64, 2, 512]); ktokk = Tk()
            kz = T("kz", [64, 2, 256], BF16); kzk = Tk()
            vcb = T("vcb", [64, 2, 256], BF16); kzc = T("kzc", [64, 2, 256], BF16); vck = Tk()
            gb = T("gb", [64, 2, 8]); gbk = Tk()
            sm = T("sm", [128, 64]); smk = Tk()
            GT = T("GT", [64, 4, 64]); GTk = Tk()
            Drow = T("Drow", [128, 4, 64]); EDrow = T("EDrow", [128, 4, 64]); Drk = Tk()
            LM = T("LM", [64, 4, 64]); LMs = T("LMs", [64, 4, 64]); LMi = T("LMi", [64, 4, 64]); LMk = Tk()
            Xf = T("Xf", [64, 4, 64]); Xk = Tk()
            Yb = T("Yb", [64, 2, 4, 64], BF16); YTb = T("YTb", [64, 6, 4, 64], BF16); Yk = [Tk() for _ in range(8)]; YTk = [Tk() for _ in range(8)]
            Pf = T("Pf", [64, 4, 64]); Pb = T("Pb", [64, 4, 64], BF16); Pk = Tk()
            inT = T("inT", [64, 4, 64], BF16); inTk = Tk()
            inR = T("inR", [64, 4, 64], BF16); inRk = Tk()
            rt = T("rt", [64, 256]); rb = T("rb", [64, 256], BF16); rk = Tk()
            scr = T("scr", [64, 4, 64]); scrk = Tk()
            vnb = T("vnb", [64, 256], BF16); vnk = Tk()
            Sd = T("Sd", [64, 4, 64]); Sdb = T("Sdb", [64, 4, 64], BF16); Sdk = Tk()
            Sr = T("Sr", [64, 4, 64]); Srb = T("Srb", [64, 4, 64], BF16); Srk = Tk()
            ot = T("ot", [64, 4, 128]); ot2 = T("ot2", [64, 4, 128]); otb = T("otb", [64, 4, 128], BF16); otk = Tk()
            PT = [WT([128, 2, 256]), WT([128, 2, 256])]; PTk = [Tk(), Tk()]
            Vt = [WT([128, 128]), WT([128, 128]), WT([128, 128])]; Vtk = [Tk(), Tk(), Tk()]
            ckb = WT([128, 512]); ckbk = Tk()
            Rr = tmpA[:, :].rearrange("p (j q) -> p j q", j=2); Tt = tmpB[:, :].rearrange("p (j q) -> p j q", j=2)
            oaf = T("oaf", [128, 256])
            oab = T("oab", [128, 256], BF16); atk = tmpk
            pcst = cE[:3, :, :].rearrange("p a b -> p (a b)"); pcstk = cEk

            def proj(TS, col0, c0, n, kind="pj"):
                pb, pbk = bank(kind)
                for k in range(8):
                    C.op("pe", lambda e: e.matmul(pb[:TS, 0:n], lhsT=xT[:, k, col0:col0 + TS],
                                                  rhs=WA_in[:, k, c0:c0 + n], start=(k == 0), stop=(k == 7)),
                         reads=[xTk, WAK], writes=[pbk])
                return pb, pbk

            def rotary(pb, pbk, TS, outs):
                pv_ = pb[:TS, :].rearrange("p (h d) -> p h d", h=8)
                C.op("dve", lambda e: e.tensor_tensor(out=tmpA[:TS, :].rearrange("p (h d) -> p h d", h=8), in0=pv_,
                                                      in1=bc(rot[:TS, 0:64].unsqueeze(1), [TS, 8, 64]), op=ALU.mult),
                     reads=[pbk, rotk], writes=[tmpk])
                tb = tmpB[:TS, :].rearrange("p (h d) -> p h d", h=8)
                C.op("dve", lambda e: e.tensor_tensor(out=tb[:, :, 0:32], in0=pv_[:, :, 32:64],
                                                      in1=bc(rot[:TS, 64:96].unsqueeze(1), [TS, 8, 32]), op=ALU.mult),
                     reads=[pbk, rotk], writes=[tmpk])
                C.op("dve", lambda e: e.tensor_tensor(out=tb[:, :, 32:64], in0=pv_[:, :, 0:32],
                                                      in1=bc(rot[:TS, 96:128].unsqueeze(1), [TS, 8, 32]), op=ALU.mult),
                     reads=[pbk, rotk], writes=[tmpk])
                for (eng, oap, otk_) in outs:
                    C.op(eng, lambda e: e.tensor_tensor(out=oap, in0=tmpA[:TS, :], in1=tmpB[:TS, :], op=ALU.add),
                         reads=[tmpk], writes=[otk_])

            FINE = [False]

            def transposes_bf(src, srck, TS, n, kind="mi"):
                pb, pbk = bank(kind)
                pbb = pb[:].bitcast(BF16)
                for i in range(n):
                    C.op("pe", lambda e: e.transpose(out=pbb[:, i * TS:(i + 1) * TS],
                                                     in_=src[:TS, i * 128:(i + 1) * 128],
                                                     identity=identb[:TS, :TS]),
                         reads=[srck, constk], writes=[pbk])
                return pbb[:, 0:n * TS].rearrange("p (k t) -> p k t", k=n), pbk

            def silu_from(pb_ap, xs_, es_, k_, pbk):
                C.op("act", lambda e: e.activation(out=es_, in_=pb_ap, func=AF.Exp, scale=-1.0), reads=[pbk], writes=[k_])
                C.op("act", lambda e: e.activation(out=xs_, in_=pb_ap, func=AF.Copy), reads=[pbk], writes=[k_])
                C.op("dve", lambda e: e.tensor_scalar(out=es_, in0=es_, scalar1=1.0, scalar2=None, op0=ALU.add),
                     reads=[k_], writes=[k_])
                C.op("dve", lambda e: e.reciprocal(out=es_, in_=es_), reads=[k_], writes=[k_])

            for sq_ in seqs:
                TS, c, L, past, row0, sb_ = sq_["TS"], sq_["c"], sq_["L"], sq_["past"], sq_["row0"], sq_["b"]
                isP = sq_["name"] == "p"
                CPT = TS // c
                TPB = 2 if isP else 1
                nblk = L // (TS * TPB)
                nq = TS * TPB
                nit = 5 if c == 64 else 4
                rsrc = c_rotp if isP else c_rots
                kdst = pk if isP else sk
                vdst = pv if isP else sv
                orow0 = 0 if isP else sb_ * Ls
                zt = zetac if isP else zetasc
                xit = xic if isP else xisc
                gch = [math.exp(math.log(1.0 - 2.0 ** (-5.0 - h)) * c) for h in range(4)]
                if isP:
                    C.op("dve", lambda e: e.memset(Sd[:], 0.0), writes=[Sdk])
                    C.op("dve", lambda e: e.memset(Sr[:], 0.0), writes=[Srk])
                    C.op("dve", lambda e: e.memset(uT[:, :, 0:3], 0.0), writes=[uTk])
                else:
                    C.dma("sp", Sd[:], sdl[l, sb_].rearrange("(h d) e -> d h e", d=64), Sdk, writes=[Sdk])
                    C.dma("sp", Sr[:], srt[l, sb_].rearrange("(h d) e -> d h e", d=64), Srk, writes=[Srk])
                    for ch in range(6):
                        C.dma("sp", uT[:, ch, 0:3], scv[l, sb_, :, ch * 128:(ch + 1) * 128].rearrange("j p -> p j"),
                              uTk, writes=[uTk], allow_slow_non_contiguous=True)
                    for kt in range(past // 128):
                        C.dma("pool", ckb[:], ck[l, sb_, kt * 128:(kt + 1) * 128, :], ckbk, writes=[ckbk])
                        pv4, pv4k = transposes_bf(ckb, ckbk, 128, 4)
                        C.op("dve", lambda e: e.tensor_copy(out=KT[:, :, kt * 128:(kt + 1) * 128], in_=pv4),
                             reads=[pv4k], writes=[ktk[kt]])
                C.op("act", lambda e: e.activation(func=AF.Copy, out=Sdb[:], in_=Sd[:]), reads=[Sdk], writes=[Sdk])
                C.op("act", lambda e: e.activation(func=AF.Copy, out=Srb[:], in_=Sr[:]), reads=[Srk], writes=[Srk])

                def gen_frontA(ti, t, pjk):
                    r0 = t * TS
                    col0 = ti * TS
                    kcol = past + r0
                    kt_own = kcol // 128
                    xa = xres[:TS, :]
                    C.dma("sp", xa, xcs[row0 + r0:row0 + r0 + TS, :], xk,
                          reads=[dtk("xcs", row0 + r0)], writes=[xk])
                    C.dma("sp", rot[:TS, :], rsrc[r0:r0 + TS, :], rotk, writes=[rotk])
                    C.op("act", lambda e: e.activation(out=xbf[:TS, :], in_=xa, func=AF.Copy), reads=[xk], writes=[xbfk])
                    pv8, pv8k = transposes_bf(xbf, xbfk, TS, 8, "mi" if pjk == "pj" else "s")
                    C.op("dve", lambda e: e.tensor_copy(out=xT[:, :, col0:col0 + TS], in_=pv8), reads=[pv8k], writes=[xTk])
                    yield
                    pb, pbk = proj(TS, col0, 0, 512, pjk)
                    rotary(pb, pbk, TS, [("dve", kbf[:TS, :], kbfk)])
                    pv4, pv4k = transposes_bf(kbf, kbfk, TS, 4, "mi" if pjk == "pj" else "s")
                    C.op("dve", lambda e: e.tensor_copy(out=QTa[0:64, :, col0:col0 + TS], in_=pv4[0:64, :, :]),
                         reads=[pv4k], writes=[QTk])
                    C.op("dve", lambda e: e.tensor_copy(out=QTb[64:128, :, col0:col0 + TS], in_=pv4[64:128, :, :]),
                         reads=[pv4k], writes=[QTk])
                    yield
                    pb, pbk = proj(TS, col0, 512, 512, pjk)
                    rotary(pb, pbk, TS, [("dve", kout[:TS, :], koutk), ("dve", kbf[:TS, :], kbfk)])
                    C.dma("pool", kdst[l, orow0 + r0:orow0 + r0 + TS, :], kout[:TS, :], koutk, reads=[koutk])
                    pv4, pv4k = transposes_bf(kbf, kbfk, TS, 4, "mi" if pjk == "pj" else "s")
                    C.op("dve", lambda e: e.tensor_copy(out=KT[:, :, kcol:kcol + TS], in_=pv4),
                         reads=[pv4k], writes=[ktk[kt_own]])
                    yield
                    pb, pbk = proj(TS, col0, 1024, 512, pjk)
                    C.op("act", lambda e: e.activation(out=vout[:TS, :], in_=pb[:TS, :], func=AF.Copy), reads=[pbk], writes=[voutk])
                    C.op("dve", lambda e: e.tensor_copy(out=vbf[:TS, :], in_=vout[:TS, :]), reads=[voutk], writes=[vbfk])
                    C.dma("pool", vdst[l, orow0 + r0:orow0 + r0 + TS, :], vout[:TS, :], voutk, reads=[voutk])
                    C.dma("pool", vsc[kcol:kcol + TS, :], vbf[:TS, :], vbfk, reads=[vbfk], writes=[dtk("vsc", kt_own)])
                    yield

                def gen_frontB(t, col0):
                    pb, pbk = proj(TS, col0, 2304, 264)
                    silu_from(pb[:TS, 0:260], g4[:TS, 0:260], g4e[:TS, 0:260], g4k, pbk)
                    C.op("dve", lambda e: e.tensor_tensor(out=g4[:TS, 0:256], in0=g4[:TS, 0:256], in1=g4e[:TS, 0:256],
                                                          op=ALU.mult), reads=[g4k], writes=[g4k])
                    if FINE[0]: yield
                    C.op("dve", lambda e: e.tensor_tensor(out=g4[:TS, 260:264], in0=pb[:TS, 260:264], in1=dtbb[:TS, :],
                                                          op=ALU.add), reads=[pbk, pk_], writes=[g4k])
                    if FINE[0]: yield
                    C.op("act", lambda e: e.activation(out=g4[:TS, 260:264], in_=g4[:TS, 260:264], func=AF.Exp),
                         reads=[g4k], writes=[g4k])
                    if FINE[0]: yield
                    C.op("act", lambda e: e.activation(out=g4[:TS, 260:264], in_=g4[:TS, 260:264], func=AF.Ln, bias=1.0),
                         reads=[g4k], writes=[g4k])
                    if FINE[0]: yield
                    C.op("dve", lambda e: e.tensor_tensor(out=g4[:TS, 260:264], in0=g4[:TS, 260:264], in1=nA[:TS, :],
                                                          op=ALU.mult), reads=[g4k, pk_], writes=[g4k])
                    if FINE[0]: yield
                    for ck_ in range(CPT):
                        C.op("dve", lambda e: e.tensor_copy(out=gb[:c, ck_, 0:4], in_=g4[ck_ * c:(ck_ + 1) * c, 260:264]),
                             reads=[g4k], writes=[gbk])
                        if FINE[0]: yield
                        C.op("dve", lambda e: e.tensor_copy(out=gb[:c, ck_, 4:8], in_=g4e[ck_ * c:(ck_ + 1) * c, 256:260]),
                             reads=[g4k], writes=[gbk])
                        if FINE[0]: yield
                    pm, pmk = bank("mi")
                    for h in range(4):
                        C.op("pe", lambda e: e.transpose(out=pm[:64, h * TS:(h + 1) * TS], in_=g4[:TS, h * 64:(h + 1) * 64],
                                                         identity=identf[:TS, :TS]),
                             reads=[g4k, constk], writes=[pmk])
                    C.op("act", lambda e: e.activation(out=sgT[:, 0:4, :TS],
                                                       in_=pm[:64, 0:4 * TS].rearrange("p (h t) -> p h t", h=4), func=AF.Copy),
                         reads=[pmk], writes=[sgTk])
                    if FINE[0]: yield
                    yield
                    pb, pbk = proj(TS, col0, 2568, 512)
                    rotary(pb, pbk, TS, [("dve", qkc[:TS, :], qkck)])
                    C.op("dve", lambda e: e.tensor_scalar(out=qkc[:TS, 256:512], in0=qkc[:TS, 256:512], scalar1=0.125,
                                                          scalar2=None, op0=ALU.mult), reads=[qkck], writes=[qkck])
                    if FINE[0]: yield
                    for half in range(2):
                        pm, pmk = bank("mi")
                        for h in range(4):
                            C.op("pe", lambda e: e.transpose(
                                out=pm[:64, h * TS:(h + 1) * TS],
                                in_=qkc[:TS, half * 256 + h * 64:half * 256 + (h + 1) * 64],
                                identity=identf[:TS, :TS]), reads=[qkck, constk], writes=[pmk])
                        C.op("act", lambda e: e.activation(
                            out=qkcT[0:64, half * 4:half * 4 + 4, :TS],
                            in_=pm[:64, 0:4 * TS].rearrange("p (h t) -> p h t", h=4), func=AF.Copy), reads=[pmk], writes=[qkcTk])
                        if FINE[0]: yield
                    for ck_ in range(CPT):
                        C.op("dve", lambda e: e.tensor_tensor(
                            out=kzc[:c, ck_, :].rearrange("p (h d) -> p h d", h=4),
                            in0=qkc[ck_ * c:(ck_ + 1) * c, 256:512].rearrange("p (h d) -> p h d", h=4),
                            in1=bc(zt[ck_ * c:(ck_ + 1) * c, :].unsqueeze(2), [c, 4, 64]), op=ALU.mult),
                             reads=[qkck, constk], writes=[vck])
                        if FINE[0]: yield
                    yield
                    pb, pbk = proj(TS, col0, 3080, 512)
                    for ck_ in range(CPT):
                        C.op("act", lambda e: e.activation(out=vcb[:c, ck_, :], in_=pb[ck_ * c:(ck_ + 1) * c, 0:256], func=AF.Copy),
                             reads=[pbk], writes=[vck])
                        if FINE[0]: yield
                    silu_from(pb[:TS, 256:512], sg6[:TS, :], sg6e[:TS, :], sg6k, pbk)
                    C.op("dve", lambda e: e.tensor_tensor(out=sg6[:TS, :], in0=sg6[:TS, :], in1=sg6e[:TS, :], op=ALU.mult),
                         reads=[sg6k], writes=[sg6k])
                    if FINE[0]: yield
                    pm, pmk = bank("mi")
                    for h in range(4):
                        C.op("pe", lambda e: e.transpose(out=pm[:64, h * TS:(h + 1) * TS], in_=sg6[:TS, h * 64:(h + 1) * 64],
                                                         identity=identf[:TS, :TS]),
                             reads=[sg6k, constk], writes=[pmk])
                    C.op("act", lambda e: e.activation(out=sgT[:, 4:8, :TS],
                                                       in_=pm[:64, 0:4 * TS].rearrange("p (h t) -> p h t", h=4), func=AF.Copy),
                         reads=[pmk], writes=[sgTk])
                    if FINE[0]: yield
                    yield
                    for rnd in range(2):
                        pb, pbk = bank("pj")
                        for j in range(3):
                            ch = rnd * 3 + j
                            for k in range(8):
                                C.op("pe", lambda e: e.matmul(
                                    pb[:, j * TS:(j + 1) * TS],
                                    lhsT=WA_in[:, k, 1536 + ch * 128:1536 + (ch + 1) * 128],
                                    rhs=xT[:, k, col0:col0 + TS], start=(k == 0), stop=(k == 7)),
                                     reads=[xTk, WAK], writes=[pbk])
                        C.op("act", lambda e: e.activation(
                            out=uT[:, rnd * 3:rnd * 3 + 3, 3:3 + TS],
                            in_=pb[:, 0:3 * TS].rearrange("p (j t) -> p j t", j=3), func=AF.Copy), reads=[pbk], writes=[uTk])
                        if FINE[0]: yield
                        yield
                    yield
                    for ch in range(6):
                        C.op("dve", lambda e: e.tensor_scalar(out=cT[:, ch, :TS], in0=uT[:, ch, 0:TS],
                                                              scalar1=cw[:, ch, 0:1], scalar2=None, op0=ALU.mult),
                             reads=[uTk, pk_], writes=[cTk])
                        if FINE[0]: yield
                        for j in range(1, 4):
                            C.op("dve", lambda e: e.scalar_tensor_tensor(
                                out=cT[:, ch, :TS], in0=uT[:, ch, j:j + TS], scalar=cw[:, ch, j:j + 1],
                                in1=cT[:, ch, :TS], op0=ALU.mult, op1=ALU.add), reads=[uTk, pk_, cTk], writes=[cTk])
                            if FINE[0]: yield
                    yield
                    if t == L // TS - 1:
                        for (c0_, c1_) in ((0, 4), (4, 6)):
                            pm, pmk = bank("mi")
                            for ch in range(c0_, c1_):
                                C.op("pe", lambda e: e.transpose(out=pm[:3, (ch - c0_) * 128:(ch - c0_ + 1) * 128],
                                                                 in_=uT[:, ch, TS:TS + 3], identity=identf[:, :]),
                                     reads=[uTk, constk], writes=[pmk])
                            C.op("act", lambda e: e.activation(out=pcst[:, c0_ * 128:c1_ * 128], in_=pm[:3, 0:(c1_ - c0_) * 128],
                                                               func=AF.Copy), reads=[pmk], writes=[pcstk])
                            if FINE[0]: yield
                        cdst = pcv[l] if isP else scvo[l, sb_]
                        C.dma("pool", cdst, pcst[:, :], pcstk, reads=[pcstk])
                    C.op("pool", lambda e: e.tensor_copy(out=uT[:, :, 0:3], in_=uT[:, :, TS:TS + 3]),
                         reads=[uTk], writes=[uTk])
                    yield
                    C.op("act", lambda e: e.activation(out=cE[:, :, :TS], in_=cT[:, :, :TS], func=AF.Exp, scale=-1.0),
                         reads=[cTk], writes=[cEk])
                    if FINE[0]: yield
                    C.op("dve", lambda e: e.tensor_scalar(out=cE[:, :, :TS], in0=cE[:, :, :TS], scalar1=1.0, scalar2=None,
                                                          op0=ALU.add), reads=[cEk], writes=[cEk])
                    if FINE[0]: yield
                    C.op("dve", lambda e: e.reciprocal(out=cE[:, :, :TS], in_=cE[:, :, :TS]), reads=[cEk], writes=[cEk])
                    if FINE[0]: yield
                    C.op("dve", lambda e: e.tensor_tensor(out=cT[:, :, :TS], in0=cT[:, :, :TS], in1=cE[:, :, :TS], op=ALU.mult),
                         reads=[cEk, cTk], writes=[cTk])
                    if FINE[0]: yield
                    yield
                    C.op("dve", lambda e: e.tensor_tensor(out=cE[:, 0:4, :TS], in0=cT[:, 0:4, :TS], in1=cT[:, 0:4, :TS],
                                                           op=ALU.mult), reads=[cTk, cEk], writes=[cEk])
                    if FINE[0]: yield
                    pm, pmk = bank("mi")
                    for j in range(4):
                        C.op("pe", lambda e: e.matmul(pm[:, j * TS:(j + 1) * TS], lhsT=bonesf[:, :], rhs=cE[:, j, :TS],
                                                      start=True, stop=True), reads=[cEk, constk], writes=[pmk])
                    C.op("act", lambda e: e.activation(out=cE[:, 0:4, :TS],
                                                       in_=pm[:, 0:4 * TS].rearrange("p (j t) -> p j t", j=4),
                                                       func=AF.Ln, bias=epsb[:, 1:2]),
                         reads=[pmk, constk], writes=[cEk])
                    if FINE[0]: yield
                    C.op("act", lambda e: e.activation(out=cE[:, 0:4, :TS], in_=cE[:, 0:4, :TS], func=AF.Exp, scale=-0.5),
                         reads=[cEk], writes=[cEk])
                    if FINE[0]: yield
                    yield
                    C.op("dve", lambda e: e.scalar_tensor_tensor(out=cT[:, 0:2, :TS], in0=cT[:, 0:2, :TS], scalar=0.125,
                                                                 in1=cE[:, 0:2, :TS], op0=ALU.mult, op1=ALU.mult),
                         reads=[cTk, cEk], writes=[cTk])
                    if FINE[0]: yield
                    C.op("dve", lambda e: e.tensor_tensor(out=cT[:, 2:4, :TS], in0=cT[:, 2:4, :TS], in1=cE[:, 2:4, :TS],
                                                          op=ALU.mult), reads=[cTk, cEk], writes=[cTk])
                    if FINE[0]: yield
                    for s_ in range(2):
                        C.op("dve", lambda e: e.tensor_copy(out=nTh[0:64, s_:8:2, :TS], in_=cT[s_ * 64:(s_ + 1) * 64, 0:4, :TS]),
                             reads=[cTk], writes=[nTk])
                        if FINE[0]: yield
                    yield
                    for ck_ in range(CPT):
                        pm, pmk = bank("mi")
                        for j in range(4):
                            C.op("pe", lambda e: e.transpose(
                                out=pm[:c, j * 128:(j + 1) * 128], in_=cT[:, 2 + j, ck_ * c:(ck_ + 1) * c],
                                identity=identf[:, :]), reads=[cTk, constk], writes=[pmk])
                        C.op("act", lambda e: e.activation(out=ktok[:c, ck_, :], in_=pm[:c, :], func=AF.Copy),
                             reads=[pmk], writes=[ktokk])
                        if FINE[0]: yield
                    yield

                def gen_chunks(col0):
                    po, pok = bank("pj")
                    po2, po2k = bank("pj")
                    for ck_ in range(CPT):
                        cs = slice(ck_ * c, (ck_ + 1) * c)
                        pm, pmk = bank("mi")
                        C.op("pe", lambda e: e.matmul(pm[:c, 0:4], lhsT=triu[:c, :c], rhs=gb[:c, ck_, 0:4],
                                                      start=True, stop=True), reads=[gbk, constk], writes=[pmk])
                        C.op("dve", lambda e: e.tensor_tensor(out=GT[:c, :, :c],
                                                              in0=bc(triu[:c, :c].unsqueeze(1), [c, 4, c]),
                                                              in1=bc(gb[:c, ck_, 0:4].unsqueeze(2), [c, 4, c]),
                                                              op=ALU.mult), reads=[gbk, constk], writes=[GTk])
                        if FINE[0]: yield
                        pm2, pm2k = bank("mi")
                        for h in range(4):
                            C.op("pe", lambda e: e.matmul(pm2[:, h * c:(h + 1) * c], lhsT=onesf[:c, :],
                                                          rhs=GT[:c, h, :c], start=True, stop=True),
                                 reads=[GTk, constk], writes=[pm2k])
                        C.op("act", lambda e: e.activation(out=sm[:c, 0:4], in_=pm[:c, 0:4], func=AF.Copy), reads=[pmk], writes=[smk])
                        if FINE[0]: yield
                        C.op("act", lambda e: e.activation(out=sm[:c, 4:8], in_=pm[:c, 0:4], func=AF.Exp), reads=[pmk], writes=[smk])
                        if FINE[0]: yield
                        C.op("act", lambda e: e.activation(out=Drow[:, :, :c], in_=pm2[:, 0:4 * c].rearrange("p (h i) -> p h i", h=4),
                                                           func=AF.Copy), reads=[pm2k], writes=[Drk])
                        if FINE[0]: yield
                        C.op("act", lambda e: e.activation(out=EDrow[:, :, :c],
                                                           in_=pm2[:, 0:4 * c].rearrange("p (h i) -> p h i", h=4), func=AF.Exp),
                             reads=[pm2k], writes=[Drk])
                        if FINE[0]: yield
                        C.op("dve", lambda e: e.tensor_tensor(out=sm[:c, 12:16], in0=Drow[:c, :, c - 1], in1=sm[:c, 0:4],
                                                              op=ALU.subtract), reads=[Drk, smk], writes=[smk])
                        if FINE[0]: yield
                        C.op("act", lambda e: e.activation(out=sm[:c, 8:12], in_=sm[:c, 12:16], func=AF.Exp), reads=[smk], writes=[smk])
                        if FINE[0]: yield
                        yield
                        C.op("dve", lambda e: e.tensor_tensor(out=LM[:c, :, :c], in0=Drow[:c, :, :c],
                                                              in1=bc(sm[:c, 0:4].unsqueeze(2), [c, 4, c]), op=ALU.subtract),
                             reads=[Drk, smk], writes=[LMk])
                        if FINE[0]: yield
                        C.op("dve", lambda e: e.tensor_scalar(out=LM[:c, :, :c], in0=LM[:c, :, :c], scalar1=0.0, scalar2=None,
                                                              op0=ALU.min), reads=[LMk], writes=[LMk])
                        if FINE[0]: yield
                        C.op("act", lambda e: e.activation(out=LM[:c, :, :c], in_=LM[:c, :, :c], func=AF.Exp), reads=[LMk], writes=[LMk])
                        if FINE[0]: yield
                        C.op("dve", lambda e: e.tensor_tensor(out=LMs[:c, :, :c], in0=LM[:c, :, :c],
                                                               in1=bc(striu[:c, :c].unsqueeze(1), [c, 4, c]), op=ALU.mult),
                             reads=[LMk, constk], writes=[LMk])
                        if FINE[0]: yield
                        C.op("dve", lambda e: e.tensor_tensor(out=LMi[:c, :, :c], in0=LM[:c, :, :c],
                                                               in1=bc(triu[:c, :c].unsqueeze(1), [c, 4, c]), op=ALU.mult),
                             reads=[LMk, constk], writes=[LMk])
                        if FINE[0]: yield
                        C.op("dve", lambda e: e.tensor_tensor(out=qdT[0:64, :, cs], in0=nTh[0:64, 0:4, cs],
                                                              in1=EDrow[0:64, :, :c], op=ALU.mult), reads=[nTk, Drk], writes=[qdk])
                        if FINE[0]: yield
                        C.op("dve", lambda e: e.tensor_tensor(out=kz[:c, ck_, :].rearrange("p (h d) -> p h d", h=4),
                                                              in0=ktok[:c, ck_, 0:256].rearrange("p (h d) -> p h d", h=4),
                                                              in1=bc(sm[:c, 8:12].unsqueeze(2), [c, 4, 64]), op=ALU.mult),
                             reads=[ktokk, smk], writes=[kzk])
                        if FINE[0]: yield
                        yield
                        pg, pgk = bank("mi")
                        for h in range(4):
                            C.op("pe", lambda e: e.matmul(pg[:c, h * c:(h + 1) * c], lhsT=nTh[0:64, 4 + h, cs], rhs=nTh[0:64, 4 + h, cs],
                                                          start=True, stop=True), reads=[nTk], writes=[pgk])
                        for h in range(4):
                            C.op("pe", lambda e: e.matmul(pg[:c, 256 + h * c:256 + (h + 1) * c], lhsT=nTh[0:64, 4 + h, cs],
                                                          rhs=nTh[0:64, h, cs], start=True, stop=True), reads=[nTk], writes=[pgk])
                        KKv = pg[:c, 0:4 * c].rearrange("p (h i) -> p h i", h=4)
                        KQv = pg[:c, 256:256 + 4 * c].rearrange("p (h i) -> p h i", h=4)
                        C.op("dve", lambda e: e.tensor_tensor(out=Xf[:c, :, :c], in0=KKv,
                                                              in1=bc(gb[:c, ck_, 4:8].unsqueeze(2), [c, 4, c]), op=ALU.mult),
                             reads=[pgk, gbk], writes=[Xk])
                        if FINE[0]: yield
                        C.op("dve", lambda e: e.tensor_tensor(out=Xf[:c, :, :c], in0=Xf[:c, :, :c], in1=LMs[:c, :, :c], op=ALU.mult),
                             reads=[Xk, LMk], writes=[Xk])
                        if FINE[0]: yield
                        C.op("dve", lambda e: e.tensor_tensor(out=scr[:c, :, :c], in0=KQv, in1=LMi[:c, :, :c], op=ALU.mult),
                             reads=[pgk, LMk], writes=[scrk])
                        if FINE[0]: yield
                        C.op("act", lambda e: e.activation(func=AF.Copy, out=inT[:c, :, :c], in_=scr[:c, :, :c]), reads=[scrk], writes=[inTk])
                        if FINE[0]: yield
                        yield
                        pm, pmk = bank("mi")
                        for h in range(4):
                            C.op("pe", lambda e: e.transpose(out=pm[:c, h * c:(h + 1) * c], in_=Xf[:c, h, :c],
                                                             identity=identf[:c, :c]), reads=[Xk, constk], writes=[pmk])
                        C.op("act", lambda e: e.activation(out=YTb[:c, 0, :, :c], in_=pm[:c, 0:4 * c].rearrange("p (h i) -> p h i", h=4),
                                                           func=AF.Copy), reads=[pmk], writes=[YTk[0]])
                        if FINE[0]: yield
                        C.op("act", lambda e: e.activation(out=Yb[:c, 0, :, :c], in_=Xf[:c, :, :c], func=AF.Copy), reads=[Xk], writes=[Yk[0]])
                        if FINE[0]: yield
                        C.op("dve", lambda e: e.tensor_tensor(out=Pf[:c, :, :c], in0=bc(identf[:c, :c].unsqueeze(1), [c, 4, c]),
                                                              in1=Xf[:c, :, :c], op=ALU.subtract), reads=[Xk, constk], writes=[Pk])
                        if FINE[0]: yield
                        C.op("dve", lambda e: e.tensor_copy(out=Pb[:c, :, :c], in_=Pf[:c, :, :c]), reads=[Pk], writes=[Pk])
                        if FINE[0]: yield
                        yield

                        def emit_prod(k):
                            pmq, pmqk = bank("mi")
                            for h in range(4):
                                C.op("pe", lambda e: e.matmul(pmq[:c, h * c:(h + 1) * c], lhsT=YTb[:c, k, h, :c],
                                                              rhs=Pb[:c, h, :c], start=True, stop=True),
                                     reads=[YTk[k], Pk], writes=[pmqk])
                            C.op("dve", lambda e: e.tensor_tensor(out=Pf[:c, :, :c], in0=Pf[:c, :, :c],
                                                                  in1=pmq[:c, 0:4 * c].rearrange("p (h i) -> p h i", h=4), op=ALU.add),
                                 reads=[pmqk, Pk], writes=[Pk])
                            C.op("dve", lambda e: e.tensor_copy(out=Pb[:c, :, :c], in_=Pf[:c, :, :c]), reads=[Pk], writes=[Pk])
                        for k in range(1, nit + 1):
                            ys, yd = (k - 1) % 2, k % 2
                            pm, pmk = bank("mi")
                            if k < nit:
                                for h in range(4):
                                    C.op("pe", lambda e: e.matmul(pm[:c, h * c:(h + 1) * c], lhsT=YTb[:c, k - 1, h, :c],
                                                                  rhs=Yb[:c, ys, h, :c], start=True, stop=True),
                                         reads=[Yk[ys], YTk[k - 1]], writes=[pmk])
                            for h in range(4):
                                C.op("pe", lambda e: e.matmul(pm[:c, 256 + h * c:256 + (h + 1) * c],
                                                              lhsT=Yb[:c, ys, h, :c], rhs=YTb[:c, k - 1, h, :c],
                                                              start=True, stop=True), reads=[Yk[ys], YTk[k - 1]], writes=[pmk])
                            if k < nit:
                                C.op("act", lambda e: e.activation(out=Yb[:c, yd, :, :c],
                                                                   in_=pm[:c, 0:4 * c].rearrange("p (h i) -> p h i", h=4), func=AF.Copy),
                                     reads=[pmk], writes=[Yk[yd]])
                                if FINE[0]: yield
                            C.op("act", lambda e: e.activation(out=YTb[:c, k, :, :c],
                                                               in_=pm[:c, 256:256 + 4 * c].rearrange("p (h i) -> p h i", h=4), func=AF.Copy),
                                 reads=[pmk], writes=[YTk[k]])
                            if FINE[0]: yield
                            yield
                            if k >= 2:
                                emit_prod(k - 1)
                                yield
                        emit_prod(nit)
                        yield
                        pc, pck = bank("mi")
                        for h in range(4):
                            C.op("pe", lambda e: e.matmul(pc[:c, h * 64:(h + 1) * 64], lhsT=nTh[0:64, 4 + h, cs],
                                                          rhs=Sdb[:, h, :], start=True, stop=True),
                                 reads=[nTk, Sdk], writes=[pck])
                        C.op("dve", lambda e: e.tensor_tensor(out=rt[:c, :].rearrange("p (h d) -> p h d", h=4),
                                                              in0=pc[:c, 0:256].rearrange("p (h d) -> p h d", h=4),
                                                              in1=bc(sm[:c, 4:8].unsqueeze(2), [c, 4, 64]), op=ALU.mult),
                             reads=[pck, smk], writes=[rk])
                        if FINE[0]: yield
                        C.op("dve", lambda e: e.tensor_tensor(out=rb[:c, :], in0=ktok[:c, ck_, 256:512], in1=rt[:c, :], op=ALU.subtract),
                             reads=[rk, ktokk], writes=[rk])
                        if FINE[0]: yield
                        yield
                        pc2, pc2k = bank("mi")
                        for h in range(4):
                            C.op("pe", lambda e: e.matmul(pc2[:c, h * 64:(h + 1) * 64], lhsT=Pb[:c, h, :c],
                                                          rhs=rb[:c, h * 64:(h + 1) * 64], start=True, stop=True),
                                 reads=[Pk, rk], writes=[pc2k])
                        C.op("dve", lambda e: e.tensor_tensor(out=rt[:c, :].rearrange("p (h d) -> p h d", h=4),
                                                              in0=pc2[:c, 0:256].rearrange("p (h d) -> p h d", h=4),
                                                              in1=bc(gb[:c, ck_, 4:8].unsqueeze(2), [c, 4, 64]), op=ALU.mult),
                             reads=[pc2k, gbk], writes=[rk])
                        if FINE[0]: yield
                        C.op("dve", lambda e: e.tensor_copy(out=vnb[:c, :], in_=rt[:c, :]), reads=[rk], writes=[vnk])
                        if FINE[0]: yield
                        yield
                        for h in range(4):
                            oc_ = slice(h * TS + ck_ * c, h * TS + (ck_ + 1) * c)
                            C.op("pe", lambda e: e.matmul(po[:64, oc_], lhsT=Sdb[:, h, :], rhs=qdT[0:64, h, cs],
                                                          start=True, stop=False), reads=[Sdk, qdk], writes=[pok])
                            C.op("pe", lambda e: e.matmul(po[:64, oc_], lhsT=vnb[:c, h * 64:(h + 1) * 64], rhs=inT[:c, h, :c],
                                                          start=False, stop=True), reads=[vnk, inTk], writes=[pok])
                        psu, psuk = bank("mi")
                        for h in range(4):
                            C.op("pe", lambda e: e.matmul(psu[:64, h * 64:(h + 1) * 64], lhsT=kz[:c, ck_, h * 64:(h + 1) * 64],
                                                          rhs=vnb[:c, h * 64:(h + 1) * 64], start=True, stop=True),
                                 reads=[kzk, vnk], writes=[psuk])
                        for h in range(4):
                            C.op("dve", lambda e: e.scalar_tensor_tensor(
                                out=Sd[:, h, :], in0=Sd[:, h, :], scalar=EDrow[0:64, h, c - 1:c], in1=psu[:64, h * 64:(h + 1) * 64],
                                op0=ALU.mult, op1=ALU.add), reads=[Drk, psuk, Sdk], writes=[Sdk])
                            if FINE[0]: yield
                        C.op("act", lambda e: e.activation(func=AF.Copy, out=Sdb[:], in_=Sd[:]), reads=[Sdk], writes=[Sdk])
                        if FINE[0]: yield
                        yield
                        pr, prk = bank("mi")
                        for h in range(4):
                            C.op("pe", lambda e: e.matmul(pr[:c, h * c:(h + 1) * c], lhsT=qkcT[0:64, 4 + h, cs], rhs=qkcT[0:64, h, cs],
                                                          start=True, stop=True), reads=[qkcTk], writes=[prk])
                        C.op("dve", lambda e: e.tensor_tensor(out=scr[:c, :, :c], in0=pr[:c, 0:4 * c].rearrange("p (h i) -> p h i", h=4),
                                                              in1=rmc[:c, :].rearrange("p (h i) -> p h i", h=4)[:, :, :c], op=ALU.mult),
                             reads=[prk, constk], writes=[scrk])
                        if FINE[0]: yield
                        C.op("act", lambda e: e.activation(func=AF.Copy, out=inR[:c, :, :c], in_=scr[:c, :, :c]), reads=[scrk], writes=[inRk])
                        if FINE[0]: yield
                        yield
                        for h in range(4):
                            oc_ = slice(h * TS + ck_ * c, h * TS + (ck_ + 1) * c)
                            C.op("pe", lambda e: e.matmul(po2[:64, oc_], lhsT=Srb[:, h, :], rhs=qkcT[0:64, h, cs],
                                                          start=True, stop=False), reads=[Srk, qkcTk], writes=[po2k])
                            C.op("pe", lambda e: e.matmul(po2[:64, oc_], lhsT=vcb[:c, ck_, h * 64:(h + 1) * 64], rhs=inR[:c, h, :c],
                                                          start=False, stop=True), reads=[vck, inRk], writes=[po2k])
                        psu, psuk = bank("mi")
                        for h in range(4):
                            C.op("pe", lambda e: e.matmul(psu[:64, h * 64:(h + 1) * 64], lhsT=kzc[:c, ck_, h * 64:(h + 1) * 64],
                                                          rhs=vcb[:c, ck_, h * 64:(h + 1) * 64], start=True, stop=True),
                                 reads=[vck], writes=[psuk])
                        for h in range(4):
                            C.op("dve", lambda e: e.scalar_tensor_tensor(out=Sr[:, h, :], in0=Sr[:, h, :], scalar=float(gch[h]),
                                                                         in1=psu[:64, h * 64:(h + 1) * 64], op0=ALU.mult, op1=ALU.add),
                                 reads=[psuk, Srk], writes=[Srk])
                            if FINE[0]: yield
                        C.op("act", lambda e: e.activation(func=AF.Copy, out=Srb[:], in_=Sr[:]), reads=[Srk], writes=[Srk])
                        if FINE[0]: yield

                    yield
                    for which in range(2):
                        pso, psok = (po, pok) if which == 0 else (po2, po2k)
                        pov = pso[:64, 0:4 * TS].rearrange("p (h t) -> p h t", h=4)
                        if which == 0:
                            C.op("act", lambda e: e.activation(out=ot[:, :, :TS], in_=pov, func=AF.Copy), reads=[psok], writes=[otk])
                            if FINE[0]: yield
                        else:
                            C.op("dve", lambda e: e.tensor_tensor(out=ot[:, :, :TS], in0=pov,
                                                                  in1=xit[:, :].rearrange("p (h t) -> p h t", h=4)[:, :, :TS], op=ALU.mult),
                                 reads=[psok, constk], writes=[otk])
                            if FINE[0]: yield
                        C.op("dve", lambda e: e.tensor_tensor(out=otb[:, :, :TS], in0=ot[:, :, :TS], in1=ot[:, :, :TS], op=ALU.mult),
                             reads=[otk], writes=[otk])
                        if FINE[0]: yield
                        yield
                        pm, pmk = bank("mi")
                        for h in range(4):
                            C.op("pe", lambda e: e.matmul(pm[:64, h * TS:(h + 1) * TS], lhsT=onesb[:64, :64], rhs=otb[:, h, :TS],
                                                          start=True, stop=True), reads=[otk, constk], writes=[pmk])
                        C.op("act", lambda e: e.activation(out=ot2[:, :, :TS], in_=pm[:64, 0:4 * TS].rearrange("p (h t) -> p h t", h=4),
                                                           func=AF.Ln, scale=1.0 / 64, bias=epsb[:64, 1:2]),
                             reads=[pmk, constk], writes=[otk])
                        if FINE[0]: yield
                        C.op("act", lambda e: e.activation(out=ot2[:, :, :TS], in_=ot2[:, :, :TS], func=AF.Exp, scale=-0.5),
                             reads=[otk], writes=[otk])
                        if FINE[0]: yield
                        yield
                        C.op("dve", lambda e: e.tensor_tensor(out=ot[:, :, :TS], in0=ot[:, :, :TS], in1=ot2[:, :, :TS], op=ALU.mult),
                             reads=[otk], writes=[otk])
                        if FINE[0]: yield
                        C.op("dve", lambda e: e.tensor_tensor(out=ot[:, :, :TS], in0=ot[:, :, :TS],
                                                               in1=sgT[:, which * 4:which * 4 + 4, :TS], op=ALU.mult),
                             reads=[otk, sgTk], writes=[otk])
                        if FINE[0]: yield
                        for s_ in range(2):
                            dst = mixT[s_ * 64:(s_ + 1) * 64, 4 + which * 2:6 + which * 2, col0:col0 + TS]
                            if which == 0:
                                C.op("dve", lambda e: e.tensor_scalar(out=dst, in0=ot[:, s_:4:2, :TS], scalar1=dlngc[:, 0:1],
                                                                      scalar2=None, op0=ALU.mult),
                                     reads=[otk, pk_], writes=[mixk])
                                if FINE[0]: yield
                            else:
                                C.op("dve", lambda e: e.tensor_copy(out=dst, in_=ot[:, s_:4:2, :TS]), reads=[otk], writes=[mixk])
                                if FINE[0]: yield
                    yield

                def back_tile(ti, t):
                    r0 = t * TS
                    col0 = ti * TS
                    C.dma("sp", xres[:TS, :], xcs[row0 + r0:row0 + r0 + TS, :], xk,
                          reads=[dtk("xcs", row0 + r0)], writes=[xk])
                    for half in range(2):
                        pb, pbk = bank("pj")
                        for k in range(8):
                            C.op("pe", lambda e: e.matmul(pb[:TS, :], lhsT=(mixA if k < 4 else mixT)[:, k, col0:col0 + TS],
                                                          rhs=WA_out[:, k, half * 512:(half + 1) * 512], start=(k == 0), stop=(k == 7)),
                                 reads=[mixk, mixAk, WAK2], writes=[pbk])
                        C.op("dve", lambda e: e.scalar_tensor_tensor(out=xres[:TS, half * 512:(half + 1) * 512],
                                                                     in0=xres[:TS, half * 512:(half + 1) * 512], scalar=float(ALPHA),
                                                                     in1=pb[:TS, :], op0=ALU.mult, op1=ALU.add),
                             reads=[pbk, xk], writes=[xk])
                    layer_norm(xres[:TS, :], TS, g1, b1, pk_, xk)
                    C.dma("pool", x1s[row0 + r0:row0 + r0 + TS, :], xres[:TS, :], xk, reads=[xk], writes=[dtk("x1s", row0 + r0)])


                def gen_attention(blk):
                    if isP:
                        kts = [(kt, 128, None) for kt in range(2 * blk)] + [(2 * blk, 128, 0), (2 * blk + 1, 128, 1)]
                    else:
                        kts = [(kt, 128, None) for kt in range(past // 128)] + [(past // 128, TS, None)]
                    its = [(hd, kt, rows, msk) for hd in range(4) for (kt, rows, msk) in kts]
                    nk = len(kts)

                    def load_v(ii):
                        hd, kt, rows, msk = its[ii]
                        vt, vtk = Vt[ii % 3], Vtk[ii % 3]
                        if kt < past // 128:
                            C.dma("pool", vt[:rows, :], cv[l, sb_, kt * 128:kt * 128 + rows, hd * 128:(hd + 1) * 128], vtk, writes=[vtk])
                        else:
                            C.dma("sp", vt[:rows, :], vsc[kt * 128:kt * 128 + rows, hd * 128:(hd + 1) * 128], vtk,
                                  reads=[dtk("vsc", kt)], writes=[vtk])

                    def emit_s(ii):
                        hd, kt, rows, msk = its[ii]
                        sbk_, sbkk = PS[2 + ii % 2], PSK[2 + ii % 2]
                        kc0 = kt * 128
                        C.op("pe", lambda e: e.matmul(sbk_[:rows, 0:nq], lhsT=KT[:, hd, kc0:kc0 + rows], rhs=QTa[:, hd, 0:nq],
                                                      start=True, stop=True), reads=[ktk[kt], QTk], writes=[sbkk])
                        C.op("pe", lambda e: e.matmul(sbk_[:rows, 256:256 + nq], lhsT=KT[:, hd, kc0:kc0 + rows],
                                                      rhs=QTb[:, hd, 0:nq], start=True, stop=True),
                             reads=[ktk[kt], QTk], writes=[sbkk])
                    load_v(0)
                    if len(its) > 1:
                        load_v(1)
                    emit_s(0)
                    for ii, (hd, kt, rows, msk) in enumerate(its):
                        if ii + 2 < len(its):
                            load_v(ii + 2)
                        if ii + 1 < len(its):
                            emit_s(ii + 1)
                        first = (ii % nk == 0)
                        last = (ii % nk == nk - 1)
                        sbk_, sbkk = PS[2 + ii % 2], PSK[2 + ii % 2]
                        pt, ptk = PT[ii % 2], PTk[ii % 2]
                        C.op("act", lambda e: e.activation(out=pt[:rows, :, 0:nq],
                                                           in_=sbk_[:rows, :].rearrange("p (j q) -> p j q", j=2)[:, :, 0:nq],
                                                           func=AF.Exp, scale=0.125), reads=[sbkk], writes=[ptk])
                        if msk is not None:
                            C.op("dve", lambda e: e.tensor_tensor(out=pt[:rows, :, 0:nq], in0=pt[:rows, :, 0:nq],
                                                                   in1=bc(amask[:rows, msk * 256:msk * 256 + nq].unsqueeze(1), [rows, 2, nq]),
                                                                   op=ALU.mult), reads=[ptk, constk], writes=[ptk])
                        if first:
                            C.op("dve", lambda e: e.memset(OB[:, :], 0.0), writes=[OBK])
                            C.op("dve", lambda e: e.memset(LB[:, :], 0.0), writes=[LBK])
                        vt, vtk = Vt[ii % 3], Vtk[ii % 3]
                        for j in range(2):
                            C.op("pe", lambda e: e.matmul(OB[:, j * 256:j * 256 + nq], lhsT=vt[:rows, :],
                                                          rhs=pt[:rows, j, 0:nq], start=False, stop=False, skip_group_check=True),
                                 reads=[vtk, ptk], writes=[OBK])
                            C.op("pe", lambda e: e.matmul(LB[:, j * 256:j * 256 + nq], lhsT=onesb[:rows, :], rhs=pt[:rows, j, 0:nq],
                                                          start=False, stop=False, skip_group_check=True),
                                 reads=[constk, ptk], writes=[LBK])
                        if last:
                            Lv = LB[:, :].rearrange("p (j q) -> p j q", j=2)[:, :, 0:nq]
                            Ov = OB[:, :].rearrange("p (j q) -> p j q", j=2)[:, :, 0:nq]
                            C.op("dve", lambda e: e.reciprocal(out=Rr[:, :, 0:nq], in_=Lv), reads=[LBK], writes=[atk])
                            C.op("dve", lambda e: e.tensor_tensor(out=Tt[:, :, 0:nq], in0=Ov, in1=Rr[:, :, 0:nq], op=ALU.mult),
                                 reads=[OBK, atk], writes=[atk])
                            C.op("dve", lambda e: e.scalar_tensor_tensor(out=oaf[:, 0:nq], in0=Tt[:, 1, 0:nq], scalar=neglam[:, 0:1],
                                                                         in1=Tt[:, 0, 0:nq], op0=ALU.mult, op1=ALU.add),
                                 reads=[atk, pk_], writes=[atk])
                            C.op("dve", lambda e: e.tensor_tensor(out=oab[:, 0:nq], in0=oaf[:, 0:nq], in1=oaf[:, 0:nq], op=ALU.mult),
                                 reads=[atk], writes=[atk])
                            yield
                            pm, pmk = PS[2 + ii % 2], PSK[2 + ii % 2]
                            C.op("pe", lambda e: e.matmul(pm[:, 0:nq], lhsT=onesb[:, :], rhs=oab[:, 0:nq], start=True, stop=True),
                                 reads=[atk, constk], writes=[pmk])
                            C.op("act", lambda e: e.activation(out=Rr[:, 0, 0:nq], in_=pm[:, 0:nq], func=AF.Ln, scale=1.0 / 128,
                                                               bias=epsb[:, 1:2]), reads=[pmk, constk], writes=[atk])
                            C.op("act", lambda e: e.activation(out=Rr[:, 0, 0:nq], in_=Rr[:, 0, 0:nq], func=AF.Exp, scale=-0.5),
                                 reads=[atk], writes=[atk])
                            C.op("dve", lambda e: e.tensor_tensor(out=oaf[:, 0:nq], in0=oaf[:, 0:nq], in1=Rr[:, 0, 0:nq], op=ALU.mult),
                                 reads=[atk], writes=[atk])
                            C.op("dve", lambda e: e.tensor_scalar(out=mixA[:, hd, 0:nq], in0=oaf[:, 0:nq], scalar1=dngc[:, 0:1],
                                                                  scalar2=float(1.0 - lam_init), op0=ALU.mult, op1=ALU.mult),
                                 reads=[atk, pk_], writes=[mixAk])
                        yield

                def run_gens(*gens, weights=None):
                    live = list(gens)
                    wts = {id(g): (weights[i] if weights else 1) for i, g in enumerate(gens)}
                    while live:
                        for g in list(live):
                            for _ in range(wts[id(g)]):
                                try:
                                    next(g)
                                except StopIteration:
                                    live.remove(g)
                                    break

                def chain(*gs):
                    for g_ in gs:
                        yield from g_

                def run_att_chain(att, ch, r):
                    a_live = c_live = True
                    while a_live or c_live:
                        if a_live:
                            try:
                                next(att)
                            except StopIteration:
                                a_live = False
                        for _ in range(r if a_live else 64):
                            if not c_live:
                                break
                            try:
                                next(ch)
                            except StopIteration:
                                c_live = False

                for blk in range(nblk):
                    t0_ = blk * TPB
                    FINE[0] = False
                    run_gens(gen_frontA(0, t0_, "pj"), gen_frontB(t0_, 0))
                    FINE[0] = True
                    if TPB == 2:
                        n_att = 4 * (2 * blk + 2)
                        run_gens(gen_chunks(0), gen_frontA(1, t0_ + 1, "s"), weights=[16, 1])
                        run_att_chain(gen_attention(blk), chain(gen_frontB(t0_ + 1, TS), gen_chunks(TS)),
                                      max(1, -(-260 // n_att)))
                    else:
                        n_att = 4 * (past // 128 + 1)
                        run_att_chain(gen_attention(blk), gen_chunks(0), max(1, -(-120 // n_att)))
                    FINE[0] = False
                    for ti in range(TPB):
                        back_tile(ti, t0_ + ti)

                if STOP == 1:
                    C.finish()
                    return nc
                ddst = pdl[l] if isP else sdlo[l, sb_]
                rdst = prt[l] if isP else srto[l, sb_]
                C.dma("pool", ddst.rearrange("(h d) e -> d h e", d=64), Sd[:], Sdk, reads=[Sdk])
                C.dma("pool", rdst.rearrange("(h d) e -> d h e", d=64), Sr[:], Srk, reads=[Srk])

        if STOP == 20 + l:
            C.finish()
            return nc
        C.barrier()
        WA_up = WA[:, 0:8 * DFF].rearrange("p (k n) -> p k n", k=8)
        WA_dn = WA[:, 8 * DFF:8 * DFF + 32 * D].rearrange("p (k n) -> p k n", k=32)
        WAK2 = Tk("WA2f")
        for k in range(8):
            for c0_ in range(0, DFF, 1024):
                C.dma("pool", WA_up[:, k, c0_:c0_ + 1024], w_up[l, k * 128:(k + 1) * 128, c0_:c0_ + 1024], WAK, writes=[WAK])
        for k in range(32):
            C.dma("pool", WA_dn[:, k, :], w_down[l, k * 128:(k + 1) * 128, :], WAK2, writes=[WAK2])
        with ExitStack() as st:
            def T(name, shape, dt=F32):
                return st.enter_context(nc.sbuf_tensor("%s_f%d" % (name, l), list(shape), dt))
            g2 = T("g2", [128, D]); b2 = T("b2", [128, D]); pk2 = Tk()
            C.dma("sp", g2[:], ln2g[l:l + 1, :].partition_broadcast(128), pk2, writes=[pk2])
            C.dma("sp", b2[:], ln2b[l:l + 1, :].partition_broadcast(128), pk2, writes=[pk2])
            x1 = T("x1", [128, 6, D]); x1k = [Tk() for _ in range(6)]
            x1b = T("x1b", [128, D], BF16); x1bk = Tk()
            x1T = T("x1T", [128, 2, 8, 256], BF16); x1Tk = [Tk(), Tk()]
            hidT = T("hidT", [128, 32, 256], BF16); hidk = Tk()
            rl = [T("rl0", [128, 256]), T("rl1", [128, 256])]; rlk = [Tk(), Tk()]
            blocks = []
            for b0_ in range(0, Lp, 256):
                blocks.append((b0_, 128, min(2, (Lp - b0_) // 128)))
            for b_ in range(NS):
                blocks.append((Lp + b_ * Ls, Ls, 1))

            def gen_prep(bi):
                rb0, TS, ntl = blocks[bi]
                sb3, xs2 = bi % 3, bi % 2
                for ti in range(ntl):
                    r0 = rb0 + ti * TS
                    sl = sb3 * 2 + ti
                    C.dma("sp", x1[:TS, sl, :], x1s[r0:r0 + TS, :], x1k[sl], reads=[dtk("x1s", r0)], writes=[x1k[sl]])
                yield
                for ti in range(ntl):
                    sl = sb3 * 2 + ti
                    C.op("act", lambda e: e.activation(out=x1b[:TS, :], in_=x1[:TS, sl, :], func=AF.Copy), reads=[x1k[sl]], writes=[x1bk])
                    yield
                    pb, pbk = bank("mi")
                    pbb = pb[:].bitcast(BF16)
                    for k in range(8):
                        C.op("pe", lambda e: e.transpose(out=pbb[:, k * TS:(k + 1) * TS], in_=x1b[:TS, k * 128:(k + 1) * 128],
                                                         identity=identb[:TS, :TS]), reads=[x1bk, constk], writes=[pbk])
                    C.op("dve", lambda e: e.tensor_copy(out=x1T[:, xs2, :, ti * TS:(ti + 1) * TS],
                                                        in_=pbb[:, 0:8 * TS].rearrange("p (k t) -> p k t", k=8)),
                         reads=[pbk], writes=[x1Tk[xs2]])
                    yield

            def gen_ln(bi):
                rb0, TS, ntl = blocks[bi]
                sb3 = bi % 3
                for ti in range(ntl):
                    r0 = rb0 + ti * TS
                    sl = sb3 * 2 + ti
                    xap = x1[:TS, sl, :]
                    xk_ = x1k[sl]
                    n = TS
                    C.op("dve", lambda e: e.bn_stats(out=lnst[:n, 0, :], in_=xap[:, 0:512]), reads=[xk_], writes=[lnk])
                    C.op("dve", lambda e: e.bn_stats(out=lnst[:n, 1, :], in_=xap[:, 512:1024]), reads=[xk_], writes=[lnk])
                    yield
                    C.op("dve", lambda e: e.bn_aggr(out=lnmv[:n, :], in_=lnst[:n, :, :].rearrange("p a b -> p (a b)")),
                         reads=[lnk], writes=[lnk])
                    yield
                    C.op("act", lambda e: e.activation(out=lnmv[:n, 1:2], in_=lnmv[:n, 1:2], func=AF.Ln, bias=epsb[:n, 0:1]),
                         reads=[lnk, constk], writes=[lnk])
                    yield
                    C.op("act", lambda e: e.activation(out=lnmv[:n, 1:2], in_=lnmv[:n, 1:2], func=AF.Exp, scale=-0.5),
                         reads=[lnk], writes=[lnk])
                    yield
                    C.op("dve", lambda e: e.tensor_scalar(out=xap, in0=xap, scalar1=lnmv[:n, 0:1], scalar2=lnmv[:n, 1:2],
                                                          op0=ALU.subtract, op1=ALU.mult), reads=[lnk, xk_], writes=[xk_])
                    yield
                    C.op("dve", lambda e: e.tensor_tensor(out=xap, in0=xap, in1=g2[:n, :], op=ALU.mult), reads=[xk_, pk2], writes=[xk_])
                    yield
                    C.op("dve", lambda e: e.tensor_tensor(out=xap, in0=xap, in1=b2[:n, :], op=ALU.add), reads=[xk_, pk2], writes=[xk_])
                    if l < DEPTH - 1:
                        C.dma("pool", xcs[r0:r0 + TS, :], xap, xk_, reads=[xk_], writes=[dtk("xcs", r0)])
                    else:
                        if r0 < Lp:
                            C.dma("pool", yp[r0:r0 + TS, :], xap, xk_, reads=[xk_])
                        else:
                            C.dma("pool", ys[r0 - Lp:r0 - Lp + TS, :], xap, xk_, reads=[xk_])
                    yield

            def step_side(side):
                while side:
                    g_ = side[0]
                    try:
                        next(g_)
                        side.append(side.pop(0))
                        return
                    except StopIteration:
                        side.pop(0)

            def run_main(bi, side):
                rb0, TS, ntl = blocks[bi]
                nb = TS * ntl
                sb3, xs2 = bi % 3, bi % 2
                for f in range(32):
                    pb, pbk = bank("pj")
                    for k in range(8):
                        C.op("pe", lambda e: e.matmul(pb[:, 0:nb], lhsT=WA_up[:, k, f * 128:(f + 1) * 128], rhs=x1T[:, xs2, k, 0:nb],
                                                      start=(k == 0), stop=(k == 7)), reads=[x1Tk[xs2], WAK], writes=[pbk])
                    r_, rk_ = rl[f % 2], rlk[f % 2]
                    C.op("act", lambda e: e.activation(out=r_[:, 0:nb], in_=pb[:, 0:nb], func=AF.Relu), reads=[pbk], writes=[rk_])
                    C.op("dve", lambda e: e.tensor_tensor(out=hidT[:, f, 0:nb], in0=r_[:, 0:nb], in1=r_[:, 0:nb], op=ALU.mult),
                         reads=[rk_], writes=[hidk])
                    if f >= 2:
                        step_side(side)
                for ti in range(ntl):
                    sl = sb3 * 2 + ti
                    for half in range(2):
                        pb, pbk = bank("pj")
                        for f in range(32):
                            C.op("pe", lambda e: e.matmul(pb[:TS, :], lhsT=hidT[:, f, ti * TS:(ti + 1) * TS],
                                                          rhs=WA_dn[:, f, half * 512:(half + 1) * 512], start=(f == 0), stop=(f == 31)),
                                 reads=[hidk, WAK2], writes=[pbk])
                        C.op("dve", lambda e: e.scalar_tensor_tensor(out=x1[:TS, sl, half * 512:(half + 1) * 512],
                                                                     in0=x1[:TS, sl, half * 512:(half + 1) * 512], scalar=float(ALPHA),
                                                                     in1=pb[:TS, :], op0=ALU.mult, op1=ALU.add),
                             reads=[pbk, x1k[sl]], writes=[x1k[sl]])
                        step_side(side)
                while side:
                    step_side(side)

            for _ in gen_prep(0):
                pass
            for bi in range(len(blocks)):
                side = []
                if bi >= 1:
                    side.append(gen_ln(bi - 1))
                if bi + 1 < len(blocks):
                    side.append(gen_prep(bi + 1))
                run_main(bi, side)
            for _ in gen_ln(len(blocks) - 1):
                pass
    C.finish()
    return nc


def make_consts(Lp, Ls, PAST):
    c = {}
    c["c_ident"] = np.eye(128, dtype=np.float32)
    half = 32
    inv_freq = (10000.0 ** (-np.arange(half, dtype=np.float32) / half)).astype(np.float32)

    def rot(pos):
        ang = pos.astype(np.float32)[:, None] * inv_freq[None, :]
        cos = np.cos(ang).astype(np.float32)
        sin = np.sin(ang).astype(np.float32)
        return np.concatenate([cos, cos, -sin, sin], axis=1).astype(np.float32)

    c["c_rotp"] = rot(np.arange(Lp))
    c["c_rots"] = rot(PAST + np.arange(Ls))
    k = np.arange(128)[:, None]
    qq = np.arange(256)[None, :]
    m0 = ((0 + k // 64) <= (qq // 64)).astype(np.float32)
    m1 = ((2 + k // 64) <= (qq // 64)).astype(np.float32)
    c["c_amask"] = np.concatenate([m0, m1], axis=1)
    j = np.arange(64)[:, None]
    i = np.arange(64)[None, :]
    c["c_triu"] = (j <= i).astype(np.float32)
    c["c_striu"] = (j < i).astype(np.float32)
    lg = np.log(1.0 - 2.0 ** (-5.0 - np.arange(4, dtype=np.float64)))
    rm = np.zeros((64, 4, 64), np.float64)
    for h in range(4):
        rm[:, h, :] = np.exp(-lg[h] * (j + 1.0)) * (j <= i)
    c["c_rm"] = rm.reshape(64, 256).astype(np.float32)
    xi = np.zeros((64, 4, 128), np.float64)
    xis = np.zeros((64, 4, Ls), np.float64)
    for h in range(4):
        xi[:, h, :] = np.exp(lg[h] * ((np.arange(128) % 64) + 1.0))[None, :]
        xis[:, h, :] = np.exp(lg[h] * (np.arange(Ls) + 1.0))[None, :]
    c["c_xi"] = xi.reshape(64, 512).astype(np.float32)
    c["c_xis"] = xis.reshape(64, 4 * Ls).astype(np.float32)
    zeta = np.zeros((128, 4), np.float64)
    zetas = np.zeros((Ls, 4), np.float64)
    for h in range(4):
        zeta[:, h] = np.exp(lg[h] * (63.0 - (np.arange(128) % 64)))
        zetas[:, h] = np.exp(lg[h] * (Ls - 1.0 - np.arange(Ls)))
    c["c_zeta"] = zeta.astype(np.float32)
    c["c_zetas"] = zetas.astype(np.float32)
    bo = np.zeros((128, 128), np.float32)
    bo[:64, :64] = 1.0
    bo[64:, 64:] = 1.0
    c["c_bones"] = bo
    return c


_NC_CACHE = {}


def run(inputs, Lp, NS, Ls, PAST, ncores):
    key = (Lp, NS, Ls, PAST)
    if key not in _NC_CACHE:
        _NC_CACHE[key] = build(Lp, NS, Ls, PAST)
    nc = _NC_CACHE[key]
    f = lambda a: np.ascontiguousarray(np.asarray(a, dtype=np.float32))
    consts = make_consts(Lp, Ls, PAST)
    I = {k: f(v) for k, v in inputs.items()}
    in_maps = []
    for i in range(ncores):
        sl = slice(i * NS, (i + 1) * NS)
        m = dict(consts)
        m["xp"] = f(I["x_prompt"][i])
        m["xs"] = f(I["x_sample"][sl].reshape(NS * Ls, D))
        m["ck"] = f(I["cache_k"][:, sl].reshape(DEPTH, NS, PAST, 512))
        m["cv"] = f(I["cache_v"][:, sl].reshape(DEPTH, NS, PAST, 512))
        m["sdl"] = f(I["state_delta"][:, sl].reshape(DEPTH, NS, 256, 64))
        m["scv"] = f(I["state_conv"][:, sl])
        m["srt"] = f(I["state_ret"][:, sl].reshape(DEPTH, NS, 256, 64))
        m["ln0g"] = f(I["ln0_g"].reshape(1, D)); m["ln0b"] = f(I["ln0_b"].reshape(1, D))
        m["w_in"] = I["w_in"]
        m["lamq1"] = I["lam_q1"]; m["lamk1"] = I["lam_k1"]; m["lamq2"] = I["lam_q2"]; m["lamk2"] = I["lam_k2"]
        m["dng"] = I["diff_norm_g"]; m["convw"] = I["conv_w"]; m["alog"] = I["a_log"]; m["dtb"] = I["dt_bias"]
        m["dlng"] = I["delta_norm_g"]; m["w_out"] = I["w_out"]
        m["ln1g"] = I["ln1_g"]; m["ln1b"] = I["ln1_b"]; m["w_up"] = I["w_up"]; m["w_down"] = I["w_down"]
        m["ln2g"] = I["ln2_g"]; m["ln2b"] = I["ln2_b"]
        in_maps.append(m)
    res = run_bass_kernel_spmd(nc, in_maps, core_ids=list(range(ncores)))
    R = res.results
    B = ncores
    st = lambda name: np.stack([np.asarray(R[i][name]) for i in range(B)])
    y_p = st("yp")
    y_s = st("ys").reshape(B * NS, Ls, D)
    p_k = st("pk").transpose(1, 0, 2, 3).reshape(DEPTH, B, Lp, 8, 64)
    p_v = st("pv").transpose(1, 0, 2, 3).reshape(DEPTH, B, Lp, 4, 128)
    p_d = st("pdl").transpose(1, 0, 2, 3).reshape(DEPTH, B, 4, 64, 64)
    p_c = st("pcv").transpose(1, 0, 2, 3)
    p_r = st("prt").transpose(1, 0, 2, 3).reshape(DEPTH, B, 4, 64, 64)
    s_k = st("sk").transpose(1, 0, 2, 3).reshape(DEPTH, B * NS, Ls, 8, 64)
    s_v = st("sv").transpose(1, 0, 2, 3).reshape(DEPTH, B * NS, Ls, 4, 128)
    s_d = st("sdlo").transpose(1, 0, 2, 3, 4).reshape(DEPTH, B * NS, 4, 64, 64)
    s_c = st("scvo").transpose(1, 0, 2, 3, 4).reshape(DEPTH, B * NS, 3, 768)
    s_r = st("srto").transpose(1, 0, 2, 3, 4).reshape(DEPTH, B * NS, 4, 64, 64)
    outs = (y_p, y_s, p_k, p_v, p_d, p_c, p_r, s_k, s_v, s_d, s_c, s_r)
    return tuple(np.ascontiguousarray(o.astype(np.float32)) for o in outs)


def kernel(**inputs):
    return run(inputs, 4096, 2, 32, 2048, NCORES)
```
